# Optimizing a Trainium2 kernel written in Bass

```python
import math
import jax, jax.numpy as jnp
from jax import lax


D_MODEL = 1024
BATCH = 8
SEQ = 2048
DEPTH = 2
DEC_BATCH = 128
DEC_SEQ = 4
PAST_LEN = 16384
PAGE_SIZE = 128

BRANCH_WIDTH = D_MODEL // 2
HEAD_DIM = 64
N_Q_HEADS = BRANCH_WIDTH // HEAD_DIM
N_KV_HEADS = max(N_Q_HEADS // 4, 1)
GROUP = N_Q_HEADS // N_KV_HEADS
WINDOW = 128
BLOCK = WINDOW
N_BUCKETS = 32
MAX_DISTANCE = 128
SSM_WIDTH = BRANCH_WIDTH
SSM_GROUP = 16
N_SSM_GROUPS = SSM_WIDTH // SSM_GROUP
SSM_STATE = 64
LRU_WIDTH = BRANCH_WIDTH
LRU_BLOCKS = 8
LRU_BLOCK_DIM = LRU_WIDTH // LRU_BLOCKS
CONV_WIDTH = 4
LRU_C = 8.0
DN_ALPHA = (2 * DEPTH) ** 0.25
DN_BETA = (8 * DEPTH) ** -0.25
LN_EPS = 1e-5
NEG = -1e30

SPLITS = (N_Q_HEADS * HEAD_DIM, N_KV_HEADS * HEAD_DIM, N_KV_HEADS * HEAD_DIM, BRANCH_WIDTH,
          SSM_WIDTH, SSM_WIDTH, LRU_WIDTH, LRU_WIDTH, 3 * D_MODEL)
IN_COLS = sum(SPLITS)

kernel_name = 'hybrid_gated_swa_s5_rglru_deepnorm_step'

f32 = jnp.float32


def _split_points():
    pts, acc = [], 0
    for s in SPLITS[:-1]:
        acc += s
        pts.append(acc)
    return pts


def _t5_bucket(dist):
    max_exact = N_BUCKETS // 2
    d = jnp.maximum(dist, 0)
    large = max_exact + (jnp.log(jnp.maximum(d, 1).astype(f32) / max_exact)
                         / math.log(MAX_DISTANCE / max_exact) * (N_BUCKETS - max_exact)).astype(jnp.int32)
    large = jnp.minimum(large, N_BUCKETS - 1)
    return jnp.where(d < max_exact, d, large)


def _band_attend(q, k, v, key_valid, rel_bias, sinks):
    T, S = q.shape[2], k.shape[2]
    dist = jnp.arange(T)[:, None] + (S - T) - jnp.arange(S)[None, :]
    in_band = (dist >= 0) & (dist < WINDOW)
    bias = rel_bias.astype(f32)[_t5_bucket(dist)]
    bias = bias.reshape(T, S, N_KV_HEADS, GROUP).transpose(2, 3, 0, 1)
    mask = in_band[None] & key_valid[:, None, :]
    scores = jnp.einsum('bntkgd,bnskd->bnkgts', q.astype(f32), k.astype(f32)) * (HEAD_DIM ** -0.5) + bias
    scores = jnp.where(mask[None, :, None, None], scores, NEG)
    sink = sinks.astype(f32).reshape(N_KV_HEADS, GROUP)[:, :, None, None]
    m = jnp.maximum(scores.max(-1, keepdims=True), sink)
    p = jnp.exp(scores - m)
    denom = p.sum(-1, keepdims=True) + jnp.exp(sink - m)
    return jnp.einsum('bnkgts,bnskd->bntkgd', p / denom, v.astype(f32))


def _attn_prompt(q, k, v, rel_bias, sinks):
    B, T, _ = q.shape
    nb = T // BLOCK
    qb = q.reshape(B, nb, BLOCK, N_KV_HEADS, GROUP, HEAD_DIM)
    kb = k.reshape(B, nb, BLOCK, N_KV_HEADS, HEAD_DIM)
    vb = v.reshape(B, nb, BLOCK, N_KV_HEADS, HEAD_DIM)
    pad = ((0, 0), (1, 0), (0, 0), (0, 0), (0, 0))
    kk = jnp.concatenate([jnp.pad(kb, pad)[:, :-1], kb], axis=2)
    vv = jnp.concatenate([jnp.pad(vb, pad)[:, :-1], vb], axis=2)
    prev_ok = jnp.broadcast_to((jnp.arange(nb) > 0)[:, None], (nb, BLOCK))
    key_valid = jnp.concatenate([prev_ok, jnp.ones((nb, BLOCK), bool)], axis=1)
    out = _band_attend(qb, kk, vv, key_valid, rel_bias, sinks)
    return out.reshape(B, T, N_Q_HEADS * HEAD_DIM)


def _attn_sample(q, k, v, k_cache, v_cache, rel_bias, sinks):
    B, T, _ = q.shape
    kk = jnp.concatenate([k_cache, k.reshape(B, T, N_KV_HEADS, HEAD_DIM)], axis=1)
    vv = jnp.concatenate([v_cache, v.reshape(B, T, N_KV_HEADS, HEAD_DIM)], axis=1)
    key_valid = jnp.ones((1, kk.shape[1]), bool)
    out = _band_attend(q.reshape(B, 1, T, N_KV_HEADS, GROUP, HEAD_DIM), kk[:, None], vv[:, None],
                       key_valid, rel_bias, sinks)
    return out.reshape(B, T, N_Q_HEADS * HEAD_DIM), kk[:, T:], vv[:, T:]


def _lin_scan(a, b):
    def comb(c1, c2):
        a1, b1 = c1
        a2, b2 = c2
        return a1 * a2, a2 * b1 + b2
    return lax.associative_scan(comb, (a, b), axis=1)[1]


def _clin_scan(a_re, a_im, b_re, b_im):
    def comb(c1, c2):
        ar1, ai1, br1, bi1 = c1
        ar2, ai2, br2, bi2 = c2
        return (ar2 * ar1 - ai2 * ai1, ar2 * ai1 + ai2 * ar1,
                ar2 * br1 - ai2 * bi1 + br2, ar2 * bi1 + ai2 * br1 + bi2)
    out = lax.associative_scan(comb, (a_re, a_im, b_re, b_im), axis=1)
    return out[2], out[3]


def _s5_branch(u, h0_re, h0_im, lam_re, lam_im, log_step, b_re, b_im, c_re, c_im, d, w_glu):
    B, T, _ = u.shape
    uf = u.astype(f32).reshape(B, T, N_SSM_GROUPS, SSM_GROUP)
    lr = jnp.minimum(lam_re.astype(f32), -1e-4)
    li = lam_im.astype(f32)
    step = jnp.exp(log_step.astype(f32))[:, None]
    mag = jnp.exp(lr * step)
    ab_re = mag * jnp.cos(li * step)
    ab_im = mag * jnp.sin(li * step)
    den = lr * lr + li * li
    f_re = ((ab_re - 1.0) * lr + ab_im * li) / den
    f_im = (ab_im * lr - (ab_re - 1.0) * li) / den
    bre, bim = b_re.astype(f32), b_im.astype(f32)
    bb_re = f_re[..., None] * bre - f_im[..., None] * bim
    bb_im = f_re[..., None] * bim + f_im[..., None] * bre
    x_re = jnp.einsum('btgc,gpc->btgp', uf, bb_re)
    x_im = jnp.einsum('btgc,gpc->btgp', uf, bb_im)
    x_re = x_re.at[:, 0].add(ab_re * h0_re - ab_im * h0_im)
    x_im = x_im.at[:, 0].add(ab_re * h0_im + ab_im * h0_re)
    h_re, h_im = _clin_scan(jnp.broadcast_to(ab_re, x_re.shape), jnp.broadcast_to(ab_im, x_im.shape), x_re, x_im)
    y = (jnp.einsum('btgp,gcp->btgc', h_re, c_re.astype(f32))
         - jnp.einsum('btgp,gcp->btgc', h_im, c_im.astype(f32)))
    y = y.reshape(B, T, SSM_WIDTH) + d.astype(f32) * uf.reshape(B, T, SSM_WIDTH)
    z = jax.nn.gelu(y)
    z = z * jax.nn.sigmoid(z @ w_glu.astype(f32))
    return z, h_re[:, -1], h_im[:, -1]


def _rglru_branch(xc, conv_buf, h0, conv_w, conv_b, w_a, b_a, w_x, b_x, lam):
    B, T, _ = xc.shape
    xp = jnp.concatenate([conv_buf.astype(xc.dtype), xc], axis=1)
    conv = conv_b + conv_w[0] * xp[:, 0:T]
    for j in range(1, CONV_WIDTH):
        conv = conv + conv_w[j] * xp[:, j:j + T]
    xf = conv.astype(f32)
    xb = xf.reshape(B, T, LRU_BLOCKS, LRU_BLOCK_DIM)
    r = jax.nn.sigmoid(jnp.einsum('btnd,nde->btne', xb, w_a.astype(f32)).reshape(B, T, LRU_WIDTH) + b_a.astype(f32))
    i = jax.nn.sigmoid(jnp.einsum('btnd,nde->btne', xb, w_x.astype(f32)).reshape(B, T, LRU_WIDTH) + b_x.astype(f32))
    log_a = LRU_C * r * jax.nn.log_sigmoid(lam.astype(f32))
    a = jnp.exp(log_a)
    b = xf * i * jnp.sqrt(-jnp.expm1(2.0 * log_a))
    b = b.at[:, 0].add(a[:, 0] * h0.astype(f32))
    h = _lin_scan(a, b)
    return h, h[:, -1], xp[:, -(CONV_WIDTH - 1):]


def _layernorm(h, g, b):
    hf = h.astype(f32)
    mu = hf.mean(-1, keepdims=True)
    var = jnp.square(hf - mu).mean(-1, keepdims=True)
    return (hf - mu) * lax.rsqrt(var + LN_EPS) * g.astype(f32) + b.astype(f32)


def _layer(x, p, rel_bias, state):
    B, T, _ = x.shape
    dt = x.dtype
    proj = x @ p['w_in']
    q, k, v, g_a, u_b, g_b, x_c, g_c, g_m = jnp.split(proj, _split_points(), axis=-1)
    if state is None:
        attn = _attn_prompt(q, k, v, rel_bias, p['sinks'])
        k_buf = k[:, -WINDOW:].reshape(B, WINDOW, N_KV_HEADS, HEAD_DIM)
        v_buf = v[:, -WINDOW:].reshape(B, WINDOW, N_KV_HEADS, HEAD_DIM)
        h0_re = jnp.zeros((B, N_SSM_GROUPS, SSM_STATE), f32)
        h0_im = jnp.zeros((B, N_SSM_GROUPS, SSM_STATE), f32)
        lru_h0 = jnp.zeros((B, LRU_WIDTH), f32)
        conv_buf = jnp.zeros((B, CONV_WIDTH - 1, LRU_WIDTH), dt)
    else:
        k_cache, v_cache, h0_re, h0_im, lru_h0, conv_buf = state
        attn, k_buf, v_buf = _attn_sample(q, k, v, k_cache, v_cache, rel_bias, p['sinks'])
    y_b, h_re, h_im = _s5_branch(u_b, h0_re.astype(f32), h0_im.astype(f32), p['lam_re'], p['lam_im'],
                                 p['log_step'], p['b_re'], p['b_im'], p['c_re'], p['c_im'], p['d'], p['w_glu'])
    y_c, lru_h, conv_new = _rglru_branch(x_c, conv_buf, lru_h0, p['conv_w'], p['conv_b'], p['w_a'],
                                         p['b_a'], p['w_x'], p['b_x'], p['lam'])
    ya = (attn * jax.nn.silu(g_a.astype(f32))).astype(dt) @ p['w_br_a']
    yb = (y_b * jax.nn.silu(g_b.astype(f32))).astype(dt) @ p['w_br_b']
    yc = (y_c * jax.nn.silu(g_c.astype(f32))).astype(dt) @ p['w_br_c']
    ga, gb, gc = jnp.split(jax.nn.sigmoid(g_m.astype(f32)), 3, axis=-1)
    merged = (ga * ya.astype(f32) + gb * yb.astype(f32) + gc * yc.astype(f32)).astype(dt)
    out = merged @ p['w_out']
    y = _layernorm(DN_ALPHA * x.astype(f32) + out.astype(f32), p['ln_g'], p['ln_b']).astype(dt)
    return y, (k_buf, v_buf, h_re, h_im, lru_h, conv_new)


def setup_inputs(seed: int = 0) -> dict:
    key = jax.random.key(seed)
    ks = iter(jax.random.split(key, 40))

    def nrm(shape, scale):
        return scale * jax.random.normal(next(ks), shape, f32)

    win = min(WINDOW, PAST_LEN)
    x_prompt = nrm((BATCH, SEQ, D_MODEL), 1.0)
    x_sample = nrm((DEC_BATCH, DEC_SEQ, D_MODEL), 1.0)
    cache_k = nrm((DEPTH, DEC_BATCH, win, N_KV_HEADS, HEAD_DIM), 1.0)
    cache_v = nrm((DEPTH, DEC_BATCH, win, N_KV_HEADS, HEAD_DIM), 1.0)
    state_ssm_re = nrm((DEPTH, DEC_BATCH, N_SSM_GROUPS, SSM_STATE), 0.1)
    state_ssm_im = nrm((DEPTH, DEC_BATCH, N_SSM_GROUPS, SSM_STATE), 0.1)
    state_lru = nrm((DEPTH, DEC_BATCH, LRU_WIDTH), 0.5)
    state_conv = nrm((DEPTH, DEC_BATCH, CONV_WIDTH - 1, LRU_WIDTH), 1.0)
    rel_bias = nrm((N_BUCKETS, N_Q_HEADS), 0.5)
    w_in = nrm((DEPTH, D_MODEL, IN_COLS), D_MODEL ** -0.5)
    sinks = nrm((DEPTH, N_Q_HEADS), 0.5)
    w_branch_a = nrm((DEPTH, BRANCH_WIDTH, D_MODEL), BRANCH_WIDTH ** -0.5)
    ssm_lambda_re = -0.5 + nrm((DEPTH, N_SSM_GROUPS, SSM_STATE), 0.01)
    ssm_lambda_im = math.pi * jnp.arange(SSM_STATE, dtype=f32) + nrm((DEPTH, N_SSM_GROUPS, SSM_STATE), 0.01)
    ssm_log_step = jax.random.uniform(next(ks), (DEPTH, N_SSM_GROUPS), f32, math.log(1e-3), math.log(1e-1))
    ssm_b_re = nrm((DEPTH, N_SSM_GROUPS, SSM_STATE, SSM_GROUP), (2 * SSM_GROUP) ** -0.5)
    ssm_b_im = nrm((DEPTH, N_SSM_GROUPS, SSM_STATE, SSM_GROUP), (2 * SSM_GROUP) ** -0.5)
    ssm_c_re = nrm((DEPTH, N_SSM_GROUPS, SSM_GROUP, SSM_STATE), (2 * SSM_STATE) ** -0.5)
    ssm_c_im = nrm((DEPTH, N_SSM_GROUPS, SSM_GROUP, SSM_STATE), (2 * SSM_STATE) ** -0.5)
    ssm_d = nrm((DEPTH, SSM_WIDTH), 1.0)
    ssm_w_glu = nrm((DEPTH, SSM_WIDTH, SSM_WIDTH), SSM_WIDTH ** -0.5)
    w_branch_b = nrm((DEPTH, SSM_WIDTH, D_MODEL), SSM_WIDTH ** -0.5)
    conv_w = nrm((DEPTH, CONV_WIDTH, LRU_WIDTH), CONV_WIDTH ** -0.5)
    conv_b = nrm((DEPTH, LRU_WIDTH), 0.01)
    lru_w_a = nrm((DEPTH, LRU_BLOCKS, LRU_BLOCK_DIM, LRU_BLOCK_DIM), LRU_BLOCK_DIM ** -0.5)
    lru_b_a = nrm((DEPTH, LRU_WIDTH), 0.01)
    lru_w_x = nrm((DEPTH, LRU_BLOCKS, LRU_BLOCK_DIM, LRU_BLOCK_DIM), LRU_BLOCK_DIM ** -0.5)
    lru_b_x = nrm((DEPTH, LRU_WIDTH), 0.01)
    a_c = jax.random.uniform(next(ks), (DEPTH, LRU_WIDTH), f32, 0.9, 0.999)
    s = a_c ** (1.0 / LRU_C)
    lru_lambda = jnp.log(s) - jnp.log1p(-s)
    w_branch_c = nrm((DEPTH, LRU_WIDTH, D_MODEL), LRU_WIDTH ** -0.5)
    w_out = nrm((DEPTH, D_MODEL, D_MODEL), DN_BETA * D_MODEL ** -0.5)
    ln_g = 1.0 + nrm((DEPTH, D_MODEL), 0.01)
    ln_b = nrm((DEPTH, D_MODEL), 0.01)
    return {'x_prompt': x_prompt, 'x_sample': x_sample, 'cache_k': cache_k, 'cache_v': cache_v,
            'state_ssm_re': state_ssm_re, 'state_ssm_im': state_ssm_im, 'state_lru': state_lru,
            'state_conv': state_conv, 'rel_bias': rel_bias, 'w_in': w_in, 'sinks': sinks,
            'w_branch_a': w_branch_a, 'ssm_lambda_re': ssm_lambda_re, 'ssm_lambda_im': ssm_lambda_im,
            'ssm_log_step': ssm_log_step, 'ssm_b_re': ssm_b_re, 'ssm_b_im': ssm_b_im,
            'ssm_c_re': ssm_c_re, 'ssm_c_im': ssm_c_im, 'ssm_d': ssm_d, 'ssm_w_glu': ssm_w_glu,
            'w_branch_b': w_branch_b, 'conv_w': conv_w, 'conv_b': conv_b, 'lru_w_a': lru_w_a,
            'lru_b_a': lru_b_a, 'lru_w_x': lru_w_x, 'lru_b_x': lru_b_x, 'lru_lambda': lru_lambda,
            'w_branch_c': w_branch_c, 'w_out': w_out, 'ln_g': ln_g, 'ln_b': ln_b}


def reference(x_prompt, x_sample, cache_k, cache_v, state_ssm_re, state_ssm_im, state_lru, state_conv,
              rel_bias, w_in, sinks, w_branch_a, ssm_lambda_re, ssm_lambda_im, ssm_log_step,
              ssm_b_re, ssm_b_im, ssm_c_re, ssm_c_im, ssm_d, ssm_w_glu, w_branch_b, conv_w, conv_b,
              lru_w_a, lru_b_a, lru_w_x, lru_b_x, lru_lambda, w_branch_c, w_out, ln_g, ln_b):
    y_prompt, y_sample = x_prompt, x_sample
    st_p, st_s = [], []
    for l in range(DEPTH):
        prm = {'w_in': w_in[l], 'sinks': sinks[l], 'w_br_a': w_branch_a[l],
               'lam_re': ssm_lambda_re[l], 'lam_im': ssm_lambda_im[l], 'log_step': ssm_log_step[l],
               'b_re': ssm_b_re[l], 'b_im': ssm_b_im[l], 'c_re': ssm_c_re[l], 'c_im': ssm_c_im[l],
               'd': ssm_d[l], 'w_glu': ssm_w_glu[l], 'w_br_b': w_branch_b[l],
               'conv_w': conv_w[l], 'conv_b': conv_b[l], 'w_a': lru_w_a[l], 'b_a': lru_b_a[l],
               'w_x': lru_w_x[l], 'b_x': lru_b_x[l], 'lam': lru_lambda[l], 'w_br_c': w_branch_c[l],
               'w_out': w_out[l], 'ln_g': ln_g[l], 'ln_b': ln_b[l]}
        y_prompt, sp = _layer(y_prompt, prm, rel_bias, None)
        y_sample, ss = _layer(y_sample, prm, rel_bias,
                              (cache_k[l], cache_v[l], state_ssm_re[l], state_ssm_im[l], state_lru[l], state_conv[l]))
        st_p.append(sp)
        st_s.append(ss)
    new_k_prompt = jnp.stack([s[0] for s in st_p])
    new_v_prompt = jnp.stack([s[1] for s in st_p])
    new_ssm_re_prompt = jnp.stack([s[2] for s in st_p])
    new_ssm_im_prompt = jnp.stack([s[3] for s in st_p])
    new_lru_prompt = jnp.stack([s[4] for s in st_p])
    new_conv_prompt = jnp.stack([s[5] for s in st_p])
    new_k_sample = jnp.stack([s[0] for s in st_s])
    new_v_sample = jnp.stack([s[1] for s in st_s])
    new_ssm_re_sample = jnp.stack([s[2] for s in st_s])
    new_ssm_im_sample = jnp.stack([s[3] for s in st_s])
    new_lru_sample = jnp.stack([s[4] for s in st_s])
    new_conv_sample = jnp.stack([s[5] for s in st_s])
    return (y_prompt, y_sample, new_k_prompt, new_v_prompt, new_ssm_re_prompt, new_ssm_im_prompt,
            new_lru_prompt, new_conv_prompt, new_k_sample, new_v_sample, new_ssm_re_sample,
            new_ssm_im_sample, new_lru_sample, new_conv_sample)
```

```python
import contextlib
import math
import os
import numpy as np
import concourse.bass as bass
import concourse.mybir as mybir
from concourse.bass_utils import run_bass_kernel_spmd

F32 = mybir.dt.float32
BF16 = mybir.dt.bfloat16
I32 = mybir.dt.int32
AF = mybir.ActivationFunctionType
ALU = mybir.AluOpType
AX = mybir.AxisListType

NCORES = 8
D = 1024
DEPTH = 2
SEQ = 2048
NS = 16
TS = 4
NTS = NS * TS
TILE = 512
NPT = SEQ // TILE
NEG = -1e30
DN_ALPHA = (2 * DEPTH) ** 0.25
LN_EPS = 1e-5
SJ = 64
COMPUTE = ("pe", "dve", "act", "pool")


class KB:
    def __init__(self, nc, n_dma_sems=16):
        self.nc = nc
        self.eng = {"pe": nc.tensor, "dve": nc.vector, "act": nc.scalar, "pool": nc.gpsimd, "sp": nc.sync}
        self.streams = {k: [] for k in self.eng}
        self.sem = {k: nc.alloc_semaphore(name=f"c_{k}") for k in COMPUTE}
        self.cnt = {k: 0 for k in COMPUTE}
        self.waited = {}
        self.lastw = {}
        self.readers = {}
        self.dpool = {}
        for q in ("sp", "pool"):
            self.dpool[q] = {"sems": [nc.alloc_semaphore(name=f"d_{q}{i}") for i in range(n_dma_sems)],
                             "uses": [0] * n_dma_sems, "next": 0}
        self.final_tokens = []
        self.dma_since_barrier = []
        self.nrec = 0
        self.alias_pending = {}
        self.maxops = int(os.environ.get("KMAXOPS", "1000000000"))
        self.dummy = (self.sem["pe"], 0, "pe")

    def _wait(self, e, tok):
        sem, val, owner = tok
        if owner == "pe" and e == "pe":
            return
        key = (e, sem.name)
        if self.waited.get(key, 0) >= val:
            return
        self.waited[key] = val
        eng = self.eng[e]
        self.streams[e].append(lambda eng=eng, sem=sem, val=val: eng.wait_ge(sem, val))

    def retire(self, keys):
        for k in keys:
            toks = []
            if k in self.lastw:
                toks.append(self.lastw.pop(k))
            toks.extend(self.readers.pop(k, []))
            for t in toks:
                cur = self.alias_pending.get(t[0].name)
                if cur is None or cur[1] < t[1]:
                    self.alias_pending[t[0].name] = t

    def _deps(self, e, reads, writes):
        for k in writes:
            if k not in self.lastw and k not in self.readers:
                for t in self.alias_pending.values():
                    self._wait(e, t)
                break
        toks = []
        for k in reads:
            if k in self.lastw:
                toks.append(self.lastw[k])
        for k in writes:
            if k in self.lastw:
                toks.append(self.lastw[k])
            toks.extend(self.readers.get(k, []))
        for t in toks:
            self._wait(e, t)

    def _commit(self, tok, reads, writes):
        for k in writes:
            self.lastw[k] = tok
            self.readers[k] = []
        for k in reads:
            self.readers.setdefault(k, []).append(tok)

    def op(self, e, fn, reads=(), writes=()):
        self.nrec += 1
        if self.nrec > self.maxops:
            return self.dummy
        if self.nrec == int(os.environ.get("KTRACE", "-1")):
            import traceback
            traceback.print_stack()
            for k in list(reads) + list(writes):
                print("KEY", k, "lastw", (self.lastw[k][0].name, self.lastw[k][1]) if k in self.lastw else None,
                      "readers", [(t[0].name, t[1]) for t in self.readers.get(k, [])])
            print("CNT", self.cnt, {q: p["uses"] for q, p in self.dpool.items()})
        self._deps(e, reads, writes)
        self.cnt[e] += 1
        tok = (self.sem[e], self.cnt[e], e)
        sem = self.sem[e]
        eng = self.eng[e]
        self.streams[e].append(lambda eng=eng, fn=fn, sem=sem: fn(eng).then_inc(sem, 1))
        self._commit(tok, reads, writes)
        return tok

    def dma(self, q, out, in_, reads=(), writes=(), final=False, **kw):
        self.nrec += 1
        if self.nrec > self.maxops:
            return self.dummy
        self._deps(q, reads, writes)
        p = self.dpool[q]
        i = p["next"]
        p["next"] = (i + 1) % len(p["sems"])
        sem = p["sems"][i]
        if p["uses"][i] > 0:
            self._wait(q, (sem, 16 * p["uses"][i], None))
        p["uses"][i] += 1
        tok = (sem, 16 * p["uses"][i], None)
        eng = self.eng[q]
        self.streams[q].append(
            lambda eng=eng, out=out, in_=in_, sem=sem, kw=kw: eng.dma_start(out=out, in_=in_, **kw).then_inc(sem, 16))
        self._commit(tok, reads, writes)
        self.dma_since_barrier.append(tok)
        if final:
            self.final_tokens.append(tok)
        return tok

    def barrier(self):
        toks = [(self.sem[k], self.cnt[k], k) for k in COMPUTE if self.cnt[k] > 0] + list(self.dma_since_barrier)
        for e in self.eng:
            for t in toks:
                if t[2] == e:
                    continue
                self._wait(e, t)
        self.dma_since_barrier = []

    def emit(self):
        for t in self.final_tokens:
            self._wait("sp", t)
        nc = self.nc
        st = self.streams
        with nc.Block() as block:
            @block.sync
            def _(e):
                for f in st["sp"]:
                    f()

            @block.tensor
            def _(e):
                for f in st["pe"]:
                    f()

            @block.vector
            def _(e):
                for f in st["dve"]:
                    f()

            @block.scalar
            def _(e):
                for f in st["act"]:
                    f()

            @block.gpsimd
            def _(e):
                for f in st["pool"]:
                    f()


def _t5_bucket(d):
    max_exact = 16
    d = max(d, 0)
    if d < max_exact:
        return d
    v = np.float32(np.log(np.float32(max(d, 1)) / np.float32(max_exact))) / np.float32(math.log(128 / max_exact)) * np.float32(16)
    return min(max_exact + int(np.float32(v)), 31)


def _bias_consts():
    oh = np.zeros((32, 383), np.float32)
    neg = np.zeros((8, 383), np.float32)
    for i in range(383):
        d = 255 - i
        if 0 <= d < 128:
            oh[_t5_bucket(d), i] = 1.0
        else:
            neg[:, i] = NEG
    return oh, neg


def build_nc(passes=None, dbg=False):
    nc = bass.Bass("TRN2", target_bir_lowering=False)

    def din(name, shape):
        return nc.dram_tensor(name, list(shape), F32, kind="ExternalInput").ap()

    def dout(name, shape):
        return nc.dram_tensor(name, list(shape), F32, kind="ExternalOutput").ap()

    xp = din("xp", [SEQ, D]); xs = din("xs", [NTS, D])
    ck = din("ck", [2, NS, 128, 128]); cv = din("cv", [2, NS, 128, 128])
    sre = din("sre", [2, NS, 2048]); sim = din("sim", [2, NS, 2048])
    slru = din("slru", [2, NS, 512]); sconv = din("sconv", [2, NS, 3, 512])
    rel_bias = din("rel_bias", [32, 8]); w_in = din("w_in", [2, D, 6400]); sinks = din("sinks", [2, 8])
    w_br = [din("w_br_a", [2, 512, D]), din("w_br_b", [2, 512, D]), din("w_br_c", [2, 512, D])]
    lam_re = din("lam_re", [2, 2048]); lam_im = din("lam_im", [2, 2048]); log_step = din("log_step", [2, 32])
    b_re = din("b_re", [2, 32, 64, 16]); b_im = din("b_im", [2, 32, 64, 16])
    c_re = din("c_re", [2, 32, 16, 64]); c_im = din("c_im", [2, 32, 16, 64])
    ssm_d = din("ssm_d", [2, 512]); w_glu = din("w_glu", [2, 512, 512])
    conv_w = din("conv_w", [2, 4, 512]); conv_b = din("conv_b", [2, 512])
    lru_w_a = din("lru_w_a", [2, 8, 64, 64]); lru_b_a = din("lru_b_a", [2, 512])
    lru_w_x = din("lru_w_x", [2, 8, 64, 64]); lru_b_x = din("lru_b_x", [2, 512])
    lru_lam = din("lru_lam", [2, 512]); w_out = din("w_out", [2, D, D])
    ln_g = din("ln_g", [2, D]); ln_b = din("ln_b", [2, D])
    oh2 = din("oh2", [32, 383]); negmask = din("negmask", [8, 383]); pmask = din("pmask", [128, 2])

    yp = dout("yp", [SEQ, D]); ys = dout("ys", [NTS, D])
    nkp = dout("nkp", [2, 128, 128]); nvp = dout("nvp", [2, 128, 128])
    nrep = dout("nrep", [2, 2048]); nimp = dout("nimp", [2, 2048])
    nlrup = dout("nlrup", [2, 512]); ncvp = dout("ncvp", [2, 3, 512])
    nks = dout("nks", [2, NS, 128, 128]); nvs = dout("nvs", [2, NS, 128, 128])
    nres = dout("nres", [2, NS, 2048]); nims = dout("nims", [2, NS, 2048])
    nlrus = dout("nlrus", [2, NS, 512]); ncvs = dout("ncvs", [2, NS, 3, 512])
    KDBG = bool(os.environ.get("KDBG"))
    dbg_m = [dout(f"dbg_m{b}", [128, 8 * TILE]) for b in range(2)] if KDBG else None
    dbg_x = dout("dbg_x", [128, 4 * D]) if KDBG else None
    rv_t = nc.dram_tensor("rv_scr", [8, 383], F32, kind="Internal")
    rv = rv_t.ap()
    tb_scr = nc.dram_tensor("tb_scr", [128, 2048], F32, kind="Internal").ap()

    kb = KB(nc)
    ncd = nc.allow_non_contiguous_dma(reason="small strided parameter/state transfers")

    def mm(out, lhsT, rhs, start, stop, reads, writes):
        kb.op("pe", lambda e: e.matmul(out, lhsT=lhsT, rhs=rhs, start=start, stop=stop), reads, writes)

    def tr(out, in_, ident, reads, writes):
        kb.op("pe", lambda e: e.transpose(out, in_, ident), reads, writes)

    def act(out, in_, func, reads, writes, **kw):
        kb.op("act", lambda e: e.activation(out=out, in_=in_, func=func, **kw), reads, writes)

    def acopy(out, in_, reads, writes):
        kb.op("act", lambda e: e.copy(out=out, in_=in_), reads, writes)

    def vcopy(out, in_, reads, writes, eng="dve"):
        kb.op(eng, lambda e: e.tensor_copy(out, in_), reads, writes)

    def tt(out, in0, in1, op, reads, writes, eng="dve"):
        kb.op(eng, lambda e: e.tensor_tensor(out=out, in0=in0, in1=in1, op=op), reads, writes)

    def ts(out, in0, s1, op0, reads, writes, s2=None, op1=None, eng="dve"):
        if op1 is None:
            kb.op(eng, lambda e: e.tensor_scalar(out=out, in0=in0, scalar1=s1, scalar2=None, op0=op0), reads, writes)
        else:
            kb.op(eng, lambda e: e.tensor_scalar(out=out, in0=in0, scalar1=s1, scalar2=s2, op0=op0, op1=op1), reads, writes)

    def stt(out, in0, scalar, in1, op0, op1, reads, writes):
        kb.op("dve", lambda e: e.scalar_tensor_tensor(out=out, in0=in0, scalar=scalar, in1=in1, op0=op0, op1=op1), reads, writes)

    def memset(ap, val, writes, eng="dve"):
        kb.op(eng, lambda e: e.memset(ap, val), (), writes)

    with contextlib.ExitStack() as es:
        es.enter_context(ncd)

        uniq = [0]

        def S(name, shape, dt=F32, stack=es):
            uniq[0] += 1
            return stack.enter_context(nc.sbuf_tensor(f"{name}_{uniq[0]}", list(shape), dt))

        pst = [es.enter_context(nc.psum_tensor(f"ps{i}", [128, 1024], F32)) for i in range(4)]
        bank_rr = [0]
        nrot = [8]

        def bank():
            i = bank_rr[0]
            bank_rr[0] = (i + 1) % nrot[0]
            return pst[i // 2][:, (i % 2) * 512:(i % 2) * 512 + 512], f"pb{i}"

        def bank2():
            i = bank_rr[0]
            if i % 2:
                i = (i + 1) % nrot[0]
            bank_rr[0] = (i + 2) % nrot[0]
            return pst[i // 2][:, :], [f"pb{i}", f"pb{i + 1}"]

        ident = S("ident", [128, 128]); identb = S("identb", [128, 128], BF16)
        xtok = [S("xtokA", [128, 4, D]), S("xtokB", [128, 4, D])]
        xT = S("xT", [128, 8, TILE], BF16)
        RING = 6
        wr = [S(f"wr{i}", [128, 2048], BF16) for i in range(RING)]
        kTd = S("kTd", [128, 2, 2, 128 + TILE], BF16)
        vtok = S("vtok", [128, 2, 5, 128], BF16)
        merged = S("merged", [128, 8, TILE])
        skb = S("skb", [128, 2, 8])
        COS = S("COS", [128, 2, 16, SJ + 1]); SIN = S("SIN", [128, 2, 16, SJ + 1])
        MAGT = S("MAGT", [128, 2, 16, SJ])
        WBre = S("WBre", [128, 2, 4, 2, 128], BF16); WBim = S("WBim", [128, 2, 4, 2, 128], BF16)
        WCre = S("WCre", [128, 2, 16, 128], BF16); WCim = S("WCim", [128, 2, 16, 128], BF16)
        AbR = S("AbR", [128, 2, 16]); AbI = S("AbI", [128, 2, 16])
        CaR = S("CaR", [128, 2, 16]); CaI = S("CaI", [128, 2, 16])
        dS5 = S("dS5", [128, 2, 4])
        Gre = S("Gre", [128, 2, 16]); Gim = S("Gim", [128, 2, 16])
        WA = S("WA", [128, 2, 4, 128], BF16); WX = S("WX", [128, 2, 4, 128], BF16)
        cw = S("cw", [128, 2, 4, 4]); cb = S("cb", [128, 2, 4]); ba = S("ba", [128, 2, 4]); bx = S("bx", [128, 2, 4])
        c1 = S("c1", [128, 2, 4]); c2 = S("c2", [128, 2, 4])
        hl = S("hl", [128, 2, 4]); chist = S("chist", [128, 2, 4, 3])
        epsc = S("epsc", [128, 1])

        kb.op("pool", lambda e: e.memset(ident[:], 0.0), (), ["ident"])
        kb.op("pool", lambda e: e.affine_select(out=ident[:], in_=ident[:], pattern=[[-1, 128]], compare_op=ALU.not_equal,
                                                fill=1.0, base=0, channel_multiplier=1), ["ident"], ["ident"])
        vcopy(identb[:], ident[:], ["ident"], ["identb"])
        memset(epsc[:], LN_EPS, ["epsc"])
        memset(Gre[:], 0.0, ["G"]); memset(Gim[:], 0.0, ["G"])
        memset(hl[:], 0.0, ["hl"]); memset(chist[:], 0.0, ["chist"])
        kb.dma("sp", skb[:].rearrange("p l h -> p (l h)"), sinks.rearrange("l h -> (l h)").partition_broadcast(128), (), ["skb"])

        class _Stop(Exception):
            pass
        STAGE = float(os.environ.get("KSTAGE", "99"))

        def stage(n):
            if STAGE < n:
                raise _Stop()
        try:
          with contextlib.ExitStack() as ss:
            def SS(name, shape, dt=F32):
                return S(name, shape, dt, stack=ss)
            stage(1)
            rb = SS("rb", [32, 8]); ohs = SS("ohs", [32, 383]); ngs = SS("ngs", [8, 383]); rvs = SS("rvs", [8, 383])
            kb.dma("sp", rb[:], rel_bias, (), ["rb"])
            kb.dma("sp", ohs[:], oh2, (), ["ohs"])
            kb.dma("sp", ngs[:], negmask, (), ["ngs"])
            pb, pk = bank()
            mm(pb[0:8, 0:383], rb[:], ohs[:], True, True, ["rb", "ohs"], [pk])
            tt(rvs[:], pb[0:8, 0:383], ngs[:], ALU.add, [pk, "ngs"], ["rvs"])
            kb.dma("sp", rv, rvs[:], ["rvs"], ["rv_dram"])
            stage(1.5)
            Tq = SS("Tq", [128, 2048]); Tf = SS("Tf", [128, 2048]); Jm = SS("Jm", [128, 128])
            kb.dma("sp", Tq[:].rearrange("p (h s) -> p h s", h=8), bass.AP(rv_t, 0, [[1, 128], [383, 8], [1, 256]]), ["rv_dram"], ["Tq"])
            kb.op("pool", lambda e: e.memset(Jm[:], 0.0), (), ["Jm"])
            kb.op("pool", lambda e: e.affine_select(out=Jm[:], in_=Jm[:], pattern=[[1, 128]], compare_op=ALU.not_equal,
                                                    fill=1.0, base=-127, channel_multiplier=1), ["Jm"], ["Jm"])
            for c4 in range(4):
                pb, pk = bank()
                mm(pb[:, :], Jm[:], Tq[:, 512 * c4:512 * c4 + 512], True, True, ["Jm", "Tq"], [pk])
                acopy(Tf[:, 512 * c4:512 * c4 + 512], pb[:, :], [pk], ["Tf"])
            kb.dma("sp", tb_scr, Tf[:], ["Tf"], ["tb_dram"])

            stage(2)
            jj_i = SS("jj_i", [128, SJ + 1], I32); jj = SS("jj", [128, SJ + 1])
            kb.op("pool", lambda e: e.iota(jj_i[:], pattern=[[1, SJ + 1]], base=0, channel_multiplier=0), (), ["jj_i"])
            vcopy(jj[:], jj_i[:], ["jj_i"], ["jj"])
            lre = SS("lre", [128, 2, 16]); lim = SS("lim", [128, 2, 16]); lst = SS("lst", [128, 2, 16])
            Bre = SS("Bre", [128, 16, 16]); Bim = SS("Bim", [128, 16, 16]); Cn = SS("Cn", [128, 4, 64]); Cn2 = SS("Cn2", [128, 4, 128])
            pmk = SS("pmk", [128, 2])
            kb.dma("sp", pmk[:], pmask, (), ["pmk"])
            BBr = SS("BBr", [128, 16, 16]); BBi = SS("BBi", [128, 16, 16]); tB = SS("tB", [128, 16, 16])
            ZB = SS("ZB", [128, 16, 128])
            phi = SS("phi", [128, 16, SJ + 1]); rr = SS("rr", [128, 16, SJ + 1]); ri = SS("ri", [128, 16, SJ + 1], I32)
            rf = SS("rf", [128, 16, SJ + 1]); gg = SS("gg", [128, 16, SJ + 1])
            sm = [SS(f"sm{i}", [128, 16]) for i in range(12)]
            wst = SS("wst", [128, 4, 128]); lamt = SS("lamt", [128, 4])
            for l in range(2):
                kb.dma("sp", lre[:, l, :], lam_re[l].rearrange("(k p) -> p k", p=128), (), ["lre"])
                kb.dma("sp", lim[:, l, :], lam_im[l].rearrange("(k p) -> p k", p=128), (), ["lim"])
                for g2 in range(2):
                    kb.dma("sp", lst[64 * g2:64 * g2 + 64, l, :],
                           log_step[l].rearrange("(k t) -> t k", t=2)[g2].partition_broadcast(64), (), ["lst"])
                kb.dma("sp", dS5[:, l, :], ssm_d[l].rearrange("(q p) -> p q", p=128), (), ["dS5"])
            stage(2.5)
            for l in range(2):
                step, lr, th, mag, den, a1, t0, t1, fre, fim, rden, t2 = sm
                act(step[:], lst[:, l, :], AF.Exp, ["lst"], ["sm0"])
                ts(lr[:], lre[:, l, :], -1e-4, ALU.min, ["lre"], ["sm1"])
                tt(th[:], lim[:, l, :], step[:], ALU.mult, ["lim", "sm0"], ["sm2"])
                tt(t0[:], lr[:], step[:], ALU.mult, ["sm1", "sm0"], ["sm6"])
                act(mag[:], t0[:], AF.Exp, ["sm6"], ["sm3"])
                tt(phi[:], th[:].unsqueeze(2).to_broadcast([128, 16, SJ + 1]), jj[:].unsqueeze(1).to_broadcast([128, 16, SJ + 1]),
                   ALU.mult, ["sm2", "jj"], ["phi"])
                for which, TAB in ((0, SIN), (1, COS)):
                    ts(rr[:], phi[:], 1.0 / (2 * math.pi), ALU.mult, ["phi"], ["rr"], s2=0.25 * which, op1=ALU.add)
                    vcopy(ri[:], rr[:], ["rr"], ["ri"])
                    vcopy(rf[:], ri[:], ["ri"], ["rf"])
                    tt(rr[:], rr[:], rf[:], ALU.subtract, ["rr", "rf"], ["rr"])
                    ts(gg[:], rr[:], 0.5, ALU.is_gt, ["rr"], ["gg"])
                    tt(rr[:], rr[:], gg[:], ALU.subtract, ["rr", "gg"], ["rr"])
                    ts(gg[:], rr[:], -0.5, ALU.is_lt, ["rr"], ["gg"])
                    tt(rr[:], rr[:], gg[:], ALU.add, ["rr", "gg"], ["rr"])
                    ts(rr[:], rr[:], 0.5, ALU.min, ["rr"], ["rr"], s2=-0.5, op1=ALU.max)
                    act(TAB[:, l, :, :], rr[:], AF.Sin, ["rr"], ["COS" if which else "SIN"], scale=6.283185)
                tt(AbR[:, l, :], mag[:], COS[:, l, :, 1], ALU.mult, ["sm3", "COS"], ["Ab"])
                tt(AbI[:, l, :], mag[:], SIN[:, l, :, 1], ALU.mult, ["sm3", "SIN"], ["Ab"])
                tt(CaR[:, l, :], mag[:], COS[:, l, :, SJ], ALU.mult, ["sm3", "COS"], ["Ca"])
                tt(CaI[:, l, :], mag[:], SIN[:, l, :, SJ], ALU.mult, ["sm3", "SIN"], ["Ca"])
                vcopy(MAGT[:, l, :, :], mag[:].unsqueeze(2).to_broadcast([128, 16, SJ]), ["sm3"], ["MAGT"])
                memset(MAGT[:, l, :, 0:1], 0.0, ["MAGT"])
                li_ = lim[:, l, :]
                tt(den[:], lr[:], lr[:], ALU.mult, ["sm1"], ["sm4"])
                tt(t1[:], li_, li_, ALU.mult, ["lim"], ["sm7"])
                tt(den[:], den[:], t1[:], ALU.add, ["sm4", "sm7"], ["sm4"])
                kb.op("dve", lambda e, rden=rden, den=den: e.reciprocal(rden[:], den[:]), ["sm4"], ["sm10"])
                ts(a1[:], AbR[:, l, :], -1.0, ALU.add, ["Ab"], ["sm5"])
                tt(t0[:], a1[:], lr[:], ALU.mult, ["sm5", "sm1"], ["sm6"])
                tt(t1[:], AbI[:, l, :], li_, ALU.mult, ["Ab", "lim"], ["sm7"])
                tt(fre[:], t0[:], t1[:], ALU.add, ["sm6", "sm7"], ["sm8"])
                tt(fre[:], fre[:], rden[:], ALU.mult, ["sm8", "sm10"], ["sm8"])
                tt(t0[:], AbI[:, l, :], lr[:], ALU.mult, ["Ab", "sm1"], ["sm6"])
                tt(t1[:], a1[:], li_, ALU.mult, ["sm5", "lim"], ["sm7"])
                tt(fim[:], t0[:], t1[:], ALU.subtract, ["sm6", "sm7"], ["sm9"])
                tt(fim[:], fim[:], rden[:], ALU.mult, ["sm9", "sm10"], ["sm9"])
                stage(3)
                kb.dma("sp", Bre[:], b_re[l].rearrange("(k t) p c -> (t p) k c", t=2), ["Bre"], ["Bre"])
                kb.dma("sp", Bim[:], b_im[l].rearrange("(k t) p c -> (t p) k c", t=2), ["Bim"], ["Bim"])
                frb = fre[:].unsqueeze(2).to_broadcast([128, 16, 16]); fib = fim[:].unsqueeze(2).to_broadcast([128, 16, 16])
                tt(BBr[:], Bre[:], frb, ALU.mult, ["Bre", "sm8"], ["BBr"])
                tt(tB[:], Bim[:], fib, ALU.mult, ["Bim", "sm9"], ["tB"])
                tt(BBr[:], BBr[:], tB[:], ALU.subtract, ["BBr", "tB"], ["BBr"])
                tt(BBi[:], Bim[:], frb, ALU.mult, ["Bim", "sm8"], ["BBi"])
                tt(tB[:], Bre[:], fib, ALU.mult, ["Bre", "sm9"], ["tB"])
                tt(BBi[:], BBi[:], tB[:], ALU.add, ["BBi", "tB"], ["BBi"])
                for src, skey_, dst, nm in ((BBr, "BBr", WBre, "WBre"), (BBi, "BBi", WBim, "WBim")):
                    memset(ZB[:], 0.0, ["ZB"])
                    for km in range(4):
                        for g2 in range(2):
                            c0 = 16 * (2 * km + g2)
                            vcopy(ZB[64 * g2:64 * g2 + 64, km::4, c0:c0 + 16], src[64 * g2:64 * g2 + 64, km::4, :], [skey_], ["ZB"])
                    for k4 in range(4):
                        pb, pk = bank()
                        for kq in range(4):
                            k = 4 * k4 + kq
                            tr(pb[:, kq * 128:(kq + 1) * 128], ZB[:, k, :], ident[:], ["ZB", "ident"], [pk])
                        for kq in range(4):
                            h_, kk_ = kq // 2, kq % 2
                            acopy(dst[64 * h_:64 * h_ + 64, l, k4, kk_, :], pb[64 * h_:64 * h_ + 64, kq * 128:(kq + 1) * 128], [pk], [nm])
                for csrc, sgn, wdst, nm in ((c_re, 1.0, WCre, "WCre"), (c_im, -1.0, WCim, "WCim")):
                    memset(wdst[:, l, :, :], 0.0, [nm])
                    kb.dma("sp", Cn[:], csrc[l].rearrange("(a g) c p -> (g c) a p", g=8), ["Cn"], ["Cn"])
                    for t_ in range(2):
                        ts(Cn2[:, :, 64 * t_:64 * t_ + 64], Cn[:], pmk[:, t_:t_ + 1], ALU.mult, ["Cn", "pmk"], ["Cn2"], s2=sgn, op1=ALU.mult)
                    pb, pk = bank()
                    for a in range(4):
                        tr(pb[:, a * 128:(a + 1) * 128], Cn2[:, a, :], ident[:], ["Cn2", "ident"], [pk])
                    pbv = pb.rearrange("p (a b) -> p a b", a=4)
                    for km in range(4):
                        acopy(wdst[:, l, km::4, 32 * km:32 * km + 32], pbv[:, :, 32 * km:32 * km + 32], [pk], [nm])
                stage(4)
                for j in range(4):
                    kb.dma("sp", cw[:, l, :, j], conv_w[l][j].rearrange("(q p) -> p q", p=128), (), ["cw"])
                kb.dma("sp", cb[:, l, :], conv_b[l].rearrange("(q p) -> p q", p=128), (), ["cb"])
                kb.dma("sp", ba[:, l, :], lru_b_a[l].rearrange("(q p) -> p q", p=128), (), ["ba"])
                kb.dma("sp", bx[:, l, :], lru_b_x[l].rearrange("(q p) -> p q", p=128), (), ["bx"])
                kb.dma("sp", lamt[:], lru_lam[l].rearrange("(q p) -> p q", p=128), ["lamt"], ["lamt"])
                act(lamt[:], lamt[:], AF.Exp, ["lamt"], ["lamt"], scale=-1.0)
                act(lamt[:], lamt[:], AF.Ln, ["lamt"], ["lamt"], bias=1.0)
                ts(c1[:, l, :], lamt[:], -8.0, ALU.mult, ["lamt"], ["c1"])
                ts(c2[:, l, :], lamt[:], -16.0, ALU.mult, ["lamt"], ["c2"])
                for wsrc, wdst, nm in ((lru_w_a, WA, "WA"), (lru_w_x, WX, "WX")):
                    memset(wst[:], 0.0, ["wst"])
                    for hh in range(2):
                        kb.dma("sp", wst[64 * hh:64 * hh + 64, :, 64 * hh:64 * hh + 64],
                               wsrc[l].rearrange("(q t) d e -> t d q e", t=2)[hh], ["wst"], ["wst"])
                    vcopy(wdst[:, l, :, :], wst[:], ["wst"], [nm])
        except _Stop:
            pass
        kb.barrier()

        w_in_v = [w_in[l].rearrange("(k p) c -> p k c", p=128) for l in range(2)]
        w_out_v = [w_out[l].rearrange("(k p) c -> p k c", p=128) for l in range(2)]
        w_br_v = [[w_br[b][l].rearrange("(k p) c -> p k c", p=128) for l in range(2)] for b in range(3)]
        w_glu_v = [w_glu[l].rearrange("(k p) c -> p k c", p=128) for l in range(2)]

        def v8(slot):
            return slot[:].rearrange("p (k c) -> p k c", k=8)

        def v4(slot):
            return slot[:].rearrange("p (k c) -> p k c", k=4)

        def win_spec(l, c0):
            return [(lambda s: v8(s), w_in_v[l][:, :, c0:c0 + 256])]

        def pass_specs(l):
            sp = []
            sp += [("q0", win_spec(l, 0)), ("q1", win_spec(l, 256)), ("kv", win_spec(l, 512))]
            sp += [("kd", [(lambda s, j=j, h=h: v8(s)[:, :, 128 * h + 64 * j:128 * h + 64 * j + 64], w_in_v[l][:, :, 512 + 64 * h:576 + 64 * h])
                           for h in range(2) for j in range(2)])]
            sp += [("ga0", win_spec(l, 768)), ("ga1", win_spec(l, 1024))]
            for b, (c_in, c_gm) in enumerate(((1280, 3328), (2304, 4352), (0, 5376))):
                if b == 1:
                    sp += [("u0", win_spec(l, 1280)), ("u1", win_spec(l, 1536)), ("gb0", win_spec(l, 1792)), ("gb1", win_spec(l, 2048))]
                    sp += [("xc0", win_spec(l, 2304)), ("xc1", win_spec(l, 2560)), ("gc0", win_spec(l, 2816)), ("gc1", win_spec(l, 3072))]
                    sp += [("glu", [(lambda s: v4(s), w_glu_v[l])])]
                g0 = 3328 + 1024 * b
                for h in range(2):
                    sp += [(f"br{b}{h}", [(lambda s: v4(s), w_br_v[b][l][:, :, 512 * h:512 * h + 512])])]
                    sp += [(f"gm{b}{i}", win_spec(l, g0 + 256 * i)) for i in (2 * h, 2 * h + 1)]
            sp += [(f"wo{i}", [(lambda s: v8(s), w_out_v[l][:, :, 256 * i:256 * i + 256])]) for i in range(4)]
            return sp

        all_passes = passes if passes is not None else [(t, l) for t in range(NPT + 1) for l in range(2)]
        NSLOT = len(pass_specs(0))
        wc_t = nc.dram_tensor("wcache", [2, NSLOT, 128, 2048], BF16, kind="Internal")
        wc = wc_t.ap()
        for l in sorted({l for (_, l) in all_passes}):
            for i, (name, pieces) in enumerate(pass_specs(l)):
                for dstf, src in pieces:
                    kb.dma("pool", dstf(wc[l, i]), src, (), [f"wc{l}_{i}"])
        wq = []
        for (t, l) in all_passes:
            wq += [(name, l, i) for i, (name, _) in enumerate(pass_specs(l))]
        wstate = {"issued": 0, "used": 0}

        def wissue():
            n = wstate["issued"]
            name, l_, i_ = wq[n]
            kb.dma("sp", wr[n % RING][:], wc[l_, i_], [f"wc{l_}_{i_}"], [f"wr{n % RING}"])
            wstate["issued"] = n + 1

        def wget(name, ahead=RING):
            n = wstate["used"]
            assert wq[n][0] == name, (wq[n][0], name)
            while wstate["issued"] < min(len(wq), n + ahead):
                wissue()
            wstate["used"] = n + 1
            return wr[n % RING], f"wr{n % RING}"

        def run_pass(t, l, xin, xout):
            sample = (t == NPT)
            ntok = NTS if sample else TILE
            nsub = 1 if sample else 4
            R = 64 if sample else 128
            last_prompt = (t == NPT - 1)
            xi = xtok[xin]; xo = xtok[xout]
            xik = f"xtok{xin}"; xok = f"xtok{xout}"

            for s in range(nsub):
                for k4 in range(2):
                    pb, pk = bank()
                    for kk in range(4):
                        kc = 4 * k4 + kk
                        tr(pb[:, kk * 128:kk * 128 + R], xi[0:R, s, kc * 128:(kc + 1) * 128], ident[0:R, 0:R], [xik, "ident"], [pk])
                    acopy(xT[:, 4 * k4:4 * k4 + 4, s * 128:s * 128 + R], pb.rearrange("p (a b) -> p a b", a=4)[:, :, 0:R], [pk], ["xT"])

            def proj_fm(slot, skey, cchunk, nk, rhs_of_kc, rkeys):
                pb, pk = bank()
                view = v8(slot) if nk == 8 else v4(slot)
                for kc in range(nk):
                    mm(pb[:, 0:ntok], view[:, kc, cchunk * 128:(cchunk + 1) * 128], rhs_of_kc(kc), kc == 0, kc == nk - 1,
                       [skey] + rkeys, [pk])
                return pb, pk

            xTk = lambda kc: xT[:, kc, 0:ntok]

            with contextlib.ExitStack() as pa:
                pa_keys = []

                def PA(name, shape, dt=F32):
                    pa_keys.append(name)
                    return S(name, shape, dt, stack=pa)
                gatedT = PA("gatedT", [128, 4, TILE], BF16)
                mTh = {}
                W7 = TS + 3
                gcT = PA("gcT", [128, 4, ntok], BF16)
                cbuf = PA("cbuf", [128, 4, NS, W7]) if sample else PA("cbuf", [128, 4, TILE + 3])
                xco = PA("xco", [NTS, 512]) if sample else PA("xco", [4, 512])
                lru_st = {}

                def xc_unit(q):
                    if q % 2 == 0:
                        lru_st["slot"] = wget(f"xc{q // 2}")
                        slot, skey = lru_st["slot"]
                        if sample or last_prompt:
                            nr = NTS if sample else 3
                            cstart = 0 if sample else TILE - 3
                            pbx, pkx = bank()
                            for kc in range(8):
                                mm(pbx[0:nr, 0:256], xT[:, kc, cstart:cstart + nr], v8(slot)[:, kc, :], kc == 0, kc == 7, [skey, "xT"], [pkx])
                            acopy(xco[0:nr, 256 * (q // 2):256 * (q // 2) + 256], pbx[0:nr, 0:256], [pkx], ["xco"])
                    slot, skey = lru_st["slot"]
                    pb, pk = proj_fm(slot, skey, q % 2, 8, xTk, ["xT"])
                    if sample:
                        acopy(cbuf[:, q, :, 3:W7], pb[:, 0:ntok].rearrange("p (s t) -> p s t", t=TS), [pk], ["cbuf"])
                    else:
                        acopy(cbuf[:, q, 0:3], chist[:, l, q, :], ["chist"], ["cbuf"])
                        acopy(cbuf[:, q, 3:3 + ntok], pb[:, 0:ntok], [pk], ["cbuf"])
                        acopy(chist[:, l, q, :], cbuf[:, q, ntok:ntok + 3], ["cbuf"], ["chist"])

                def gc_unit(q):
                    if q % 2 == 0:
                        lru_st["slot"] = wget(f"gc{q // 2}")
                    slot, skey = lru_st["slot"]
                    pb, pk = proj_fm(slot, skey, q % 2, 8, xTk, ["xT"])
                    act(gcT[:, q, 0:ntok], pb[:, 0:ntok], AF.Silu, [pk], ["gcT"])
                lru_units = [lambda q=q: xc_unit(q) for q in range(4)] + [lambda q=q: gc_unit(q) for q in range(4)]

                def branch_out(b):
                    with contextlib.ExitStack() as bo:
                        bo_keys = ["gms0", "gms1", "tmpm"]
                        gms = [S("gms0", [128, ntok], F32, stack=bo), S("gms1", [128, ntok], F32, stack=bo)]
                        tmpm = S("tmpm", [128, ntok], F32, stack=bo)
                        _branch_out(b, gms, tmpm)
                        kb.retire(bo_keys)

                def _branch_out(b, gms, tmpm):
                    mT = mTh.get("mT")
                    ypb = []
                    for fo in range(8):
                        if fo % 4 == 0:
                            slot, skey = wget(f"br{b}{fo // 4}")
                        ypb.append(proj_fm(slot, skey, fo % 4, 4, lambda c: gatedT[:, c, 0:ntok], ["gatedT"]))
                        if fo % 4 == 3:
                            for f2 in range(fo - 3, fo + 1):
                                if f2 % 2 == 0:
                                    gslot, gkey = wget(f"gm{b}{f2 // 2}")
                                gpb, gpk = proj_fm(gslot, gkey, f2 % 2, 8, xTk, ["xT"])
                                g = gms[f2 % 2]; gk = f"gms{f2 % 2}"
                                act(g[:, 0:ntok], gpb[:, 0:ntok], AF.Sigmoid, [gpk], [gk])
                                yb, yk = ypb[f2]
                                if b == 0:
                                    tt(merged[:, f2, 0:ntok], yb[:, 0:ntok], g[:, 0:ntok], ALU.mult, [yk, gk], ["merged"])
                                elif b == 1:
                                    tt(tmpm[:, 0:ntok], yb[:, 0:ntok], g[:, 0:ntok], ALU.mult, [yk, gk], ["tmpm"])
                                    tt(merged[:, f2, 0:ntok], merged[:, f2, 0:ntok], tmpm[:, 0:ntok], ALU.add, ["merged", "tmpm"], ["merged"])
                                else:
                                    tt(tmpm[:, 0:ntok], yb[:, 0:ntok], g[:, 0:ntok], ALU.mult, [yk, gk], ["tmpm"])
                                    tt(mT[:, f2, 0:ntok], merged[:, f2, 0:ntok], tmpm[:, 0:ntok], ALU.add, ["merged", "tmpm"], ["mT"])

                with contextlib.ExitStack() as ph:
                    ph_keys = []

                    def PH(name, shape, dt=F32, ph=ph, ph_keys=ph_keys):
                        ph_keys.append(name)
                        return S(name, shape, dt, stack=ph)
                    qT = PH("qT", [128, 4, TILE], BF16)
                    Tb = PH("Tb", [128, 8, 256])
                    kb.dma("sp", Tb[:].rearrange("p h s -> p (h s)"), tb_scr, ["tb_dram"], ["Tb"])
                    gatok = PH("gatok", [128, 4, 512], BF16)
                    kvo = PH("kvo", [128, 256])
                    KW = 132 if sample else 256
                    sc2 = [PH(f"sc{i}", [128, 4, KW]) for i in range(2)]; Pm2 = [PH(f"Pm{i}", [128, 4, KW], BF16) for i in range(2)]
                    PT = PH("PT", [128, 8, 128], BF16)
                    mx2 = [PH(f"mx{i}", [128, 4]) for i in range(2)]; ngm2 = [PH(f"ngm{i}", [128, 4]) for i in range(2)]
                    rs2 = [PH(f"rs{i}", [128, 4]) for i in range(2)]; esk2 = [PH(f"esk{i}", [128, 4]) for i in range(2)]
                    rinv2 = [PH(f"rinv{i}", [128, 4]) for i in range(2)]; att = PH("att", [128, 256])
                    gA2 = [PH(f"gA{i}", [128, 512], BF16) for i in range(2)]
                    if sample:
                        Ks = PH("Ks", [128, NS, 128], BF16); Kd = PH("Kd", [128, 2, 2, 64], BF16)
                        KTs = PH("KTs", [128, NS, 2, 132], BF16); Vs = PH("Vs", [128, NS, 128], BF16)
                        vnew = PH("vnew", [4, NS, 128], BF16)
                        kTn = PH("kTn", [128, 2, NTS], BF16); gas2 = [PH(f"gas{i}", [4, 512], BF16) for i in range(2)]

                    for c2_ in range(2):
                        slot, skey = wget(f"q{c2_}")
                        for cc in range(2):
                            pb, pk = proj_fm(slot, skey, cc, 8, xTk, ["xT"])
                            acopy(qT[:, 2 * c2_ + cc, 0:ntok], pb[:, 0:ntok], [pk], ["qT"])
                    slot, skey = wget("kv")
                    for s in range(nsub):
                        pb, pk = bank()
                        for kc in range(8):
                            mm(pb[0:R, 0:256], xT[:, kc, s * 128:s * 128 + R], v8(slot)[:, kc, :], kc == 0, kc == 7, [skey, "xT"], [pk])
                        if not sample:
                            acopy(vtok[:, l, 1 + s, :], pb[:, 128:256], [pk], ["vtok"])
                            if last_prompt and s == 3:
                                acopy(kvo[:], pb[:, 0:256], [pk], ["kvo"])
                                kb.dma("sp", nkp[l], kvo[:, 0:128], ["kvo"], (), final=True)
                                kb.dma("sp", nvp[l], kvo[:, 128:256], ["kvo"], (), final=True)
                        else:
                            vcopy(kvo[0:R, :], pb[0:R, 0:256], [pk], ["kvo"])
                            for tt_ in range(TS):
                                kb.dma("sp", nks[l][:, 124 + tt_, :], kvo[tt_:NTS:TS, 0:128], ["kvo"], (), final=True)
                                kb.dma("sp", nvs[l][:, 124 + tt_, :], kvo[tt_:NTS:TS, 128:256], ["kvo"], (), final=True)
                            kb.dma("sp", nks[l][:, 0:124, :], ck[l][:, 4:128, :], (), (), final=True)
                            kb.dma("sp", nvs[l][:, 0:124, :], cv[l][:, 4:128, :], (), (), final=True)
                            for i in range(NS):
                                pb2, pk2 = bank()
                                for kc in range(8):
                                    mm(pb2[0:TS, 0:128], xT[:, kc, TS * i:TS * i + TS], v8(slot)[:, kc, 128:256], kc == 0, kc == 7,
                                       [skey, "xT"], [pk2])
                                acopy(vnew[:, i, :], pb2[0:TS, 0:128], [pk2], ["vnew"])
                    slot, skey = wget("kd")
                    for kvh in range(2):
                        pb, pk = proj_fm(slot, skey, kvh, 8, xTk, ["xT"])
                        if not sample:
                            acopy(kTd[:, l, kvh, 128:128 + ntok], pb[:, 0:ntok], [pk], ["kTd"])
                        else:
                            acopy(kTn[:, kvh, :], pb[:, 0:ntok], [pk], ["kTn"])
                    for h2 in range(2):
                        slot, skey = wget(f"ga{h2}")
                        for s in range(nsub):
                            pb, pk = bank()
                            for kc in range(8):
                                mm(pb[0:R, 0:256], xT[:, kc, s * 128:s * 128 + R], v8(slot)[:, kc, :], kc == 0, kc == 7, [skey, "xT"], [pk])
                            act(gatok[0:R, s, h2 * 256:h2 * 256 + 256], pb[0:R, 0:256], AF.Silu, [pk], ["gatok"])

                    def attend_all(blocks):
                        units = [(bi, hf) for bi in range(len(blocks)) for hf in range(2)]
                        stt_ = {}

                        def S1(u):
                            bi, hf = units[u]
                            B = blocks[bi]
                            nq, segs = B["nq"], B["segs"]
                            if hf == 0 and B.get("prep"):
                                B["prep"]()
                            pb2, pks = bank2()
                            scv = pb2.rearrange("p (h s) -> p h s", h=4)
                            for hh in range(4):
                                h = 4 * hf + hh
                                cch, half = h // 2, h % 2
                                off = 0
                                for (kfn, vap, n) in segs:
                                    mm(scv[0:nq, half * 2 + hh // 2, off:off + n], qT[64 * half:64 * half + 64, cch, B["col"]:B["col"] + nq],
                                       kfn(hf)[64 * half:64 * half + 64, :], True, True, ["qT", "kTd", "KTs"], [pks[half]])
                                    off += n
                            stt_[u] = (scv, pks)

                        def S2(u, part):
                            bi, hf = units[u]
                            B = blocks[bi]
                            nq, segs, nkeys, tb_c0 = B["nq"], B["segs"], B["nkeys"], B["tb_c0"]
                            par = u % 2
                            sc_, Pm_, mx_, ngm_, rs_, esk_, rinv_ = sc2[par], Pm2[par], mx2[par], ngm2[par], rs2[par], esk2[par], rinv2[par]
                            ksc, kPm, kmx, kngm, krs, kesk, krinv = (f"{n_}{par}" for n_ in ("sc", "Pm", "mx", "ngm", "rs", "esk", "rinv"))
                            gA_ = gA2[bi % 2]; kgA = f"gA{bi % 2}"
                            if part == "a":
                                scv, pks = stt_.pop(u)
                                for half in range(2):
                                    stt(sc_[0:nq, half:4:2, 0:nkeys], scv[0:nq, 2 * half:2 * half + 2, 0:nkeys], 0.125,
                                        Tb[0:nq, 4 * hf + half:4 * hf + 4:2, tb_c0:tb_c0 + nkeys],
                                        ALU.mult, ALU.add, [pks[half], "Tb"], [ksc])
                                kb.op("dve", lambda e: e.tensor_reduce(out=mx_[0:nq, :], in_=sc_[0:nq, :, 0:nkeys], axis=AX.X, op=ALU.max),
                                      [ksc], [kmx])
                                tt(mx_[0:nq, :], mx_[0:nq, :], skb[0:nq, l, 4 * hf:4 * hf + 4], ALU.max, [kmx, "skb"], [kmx])
                                ts(ngm_[0:nq, :], mx_[0:nq, :], -1.0, ALU.mult, [kmx], [kngm])
                                tt(esk_[0:nq, :], skb[0:nq, l, 4 * hf:4 * hf + 4], mx_[0:nq, :], ALU.subtract, ["skb", kmx], [kesk])
                            elif part == "b":
                                for hh in range(4):
                                    act(Pm_[0:nq, hh, 0:nkeys], sc_[0:nq, hh, 0:nkeys], AF.Exp, [ksc, kngm], [kPm, krs],
                                        bias=ngm_[0:nq, hh:hh + 1], scale=1.0, accum_out=rs_[0:nq, hh:hh + 1])
                                act(esk_[0:nq, :], esk_[0:nq, :], AF.Exp, [kesk], [kesk])
                            else:
                                tt(rinv_[0:nq, :], rs_[0:nq, :], esk_[0:nq, :], ALU.add, [krs, kesk], [krinv])
                                kb.op("dve", lambda e: e.reciprocal(rinv_[0:nq, :], rinv_[0:nq, :]), [krinv], [krinv])

                        def S3(u, part):
                            bi, hf = units[u]
                            B = blocks[bi]
                            nq, segs = B["nq"], B["segs"]
                            par = u % 2
                            Pm_, rinv_ = Pm2[par], rinv2[par]
                            kPm, krinv = f"Pm{par}", f"rinv{par}"
                            gA_ = gA2[bi % 2]; kgA = f"gA{bi % 2}"
                            nseg = len(segs)
                            if part == "a":
                                pbt, pkt = bank()
                                ptv = pbt.bitcast(BF16).rearrange("p (a b) -> p a b", a=8)
                                for hh in range(4):
                                    off = 0
                                    for si, (kfn, vap, n) in enumerate(segs):
                                        tr(ptv[0:n, hh * nseg + si, 0:nq], Pm_[0:nq, hh, off:off + n], identb[0:nq, 0:nq], [kPm, "identb"], [pkt])
                                        off += n
                                for si, (kfn, vap, n) in enumerate(segs):
                                    acopy(PT[0:n, si:4 * nseg:nseg, 0:nq], ptv[0:n, si:4 * nseg:nseg, 0:nq], [pkt], ["PT"])
                                return
                            pbo, pko = bank()
                            for hh in range(4):
                                for si, (kfn, vap, n) in enumerate(segs):
                                    mm(pbo[0:nq, hh * 64:hh * 64 + 64], PT[0:n, hh * nseg + si, 0:nq], vap[0:n, 64 * hf:64 * hf + 64],
                                       si == 0, si == nseg - 1, ["PT", "vtok", "Vs", "vnew"], [pko])
                            tt(att[0:nq, :].rearrange("p (h d) -> p h d", h=4), pbo[0:nq, 0:256].rearrange("p (h d) -> p h d", h=4),
                               rinv_[0:nq, :].unsqueeze(2).to_broadcast([nq, 4, 64]), ALU.mult, [pko, krinv], ["att"])
                            tt(gA_[0:nq, 256 * hf:256 * hf + 256], att[0:nq, :], B["ga"][:, 256 * hf:256 * hf + 256], ALU.mult,
                               ["att", B["ga_key"]], [kgA])
                            if hf == 1:
                                pbt, pkt = bank()
                                ptv = pbt.bitcast(BF16).rearrange("p (a b) -> p a b", a=8)
                                for cch in range(4):
                                    tr(ptv[:, cch, 0:nq], gA_[0:nq, cch * 128:(cch + 1) * 128], identb[0:nq, 0:nq], [kgA, "identb"], [pkt])
                                acopy(gatedT[:, :, B["col"]:B["col"] + nq], ptv[:, 0:4, 0:nq], [pkt], ["gatedT"])

                        nu = len(units)
                        S1(0); S2(0, "a"); S2(0, "b"); S2(0, "c")
                        if nu > 1:
                            S1(1)
                        for u in range(nu):
                            if u + 1 < nu:
                                S2(u + 1, "a")
                            S3(u, "a")
                            if u + 1 < nu:
                                S2(u + 1, "b")
                            S3(u, "b")
                            if u + 1 < nu:
                                S2(u + 1, "c")
                            if u + 2 < nu:
                                S1(u + 2)

                    if not sample:
                        blocks = []
                        for s in range(4):
                            blk = 4 * t + s
                            if blk == 0:
                                segs = [(lambda kvh, s=s: kTd[:, l, kvh, 128 + 128 * s:256 + 128 * s], vtok[:, l, 1 + s, :], 128)]
                                blocks.append(dict(nq=128, segs=segs, nkeys=128, tb_c0=128, ga=gatok[:, s, :], ga_key="gatok", col=128 * s))
                            else:
                                segs = [(lambda kvh, s=s: kTd[:, l, kvh, 128 * s:128 * s + 128], vtok[:, l, s, :], 128),
                                        (lambda kvh, s=s: kTd[:, l, kvh, 128 + 128 * s:256 + 128 * s], vtok[:, l, 1 + s, :], 128)]
                                blocks.append(dict(nq=128, segs=segs, nkeys=256, tb_c0=0, ga=gatok[:, s, :], ga_key="gatok", col=128 * s))
                        attend_all(blocks)
                        vcopy(kTd[:, l, :, 0:128], kTd[:, l, :, TILE:TILE + 128], ["kTd"], ["kTd"])
                        vcopy(vtok[:, l, 0, :], vtok[:, l, 4, :], ["vtok"], ["vtok"])
                    else:
                        kb.dma("pool", Ks[:], ck[l].rearrange("s w f -> w s f"), (), ["Ks"])
                        kb.dma("pool", Vs[:], cv[l].rearrange("s w f -> w s f"), (), ["Vs"])
                        for i in range(NS):
                            pbt, pkt = bank()
                            ptv = pbt.bitcast(BF16).rearrange("p (a b) -> p a b", a=8)
                            vcopy(Kd[:], Ks[:, i, :].rearrange("w (h d) -> w h d", h=2).unsqueeze(2).to_broadcast([128, 2, 2, 64]), ["Ks"], ["Kd"])
                            for kvh in range(2):
                                tr(ptv[:, kvh, :], Kd[:, kvh, :, :].rearrange("w j d -> w (j d)"), identb[:], ["Kd", "identb"], [pkt])
                            acopy(KTs[:, i, :, 0:128], ptv[:, 0:2, :], [pkt], ["KTs"])
                        vcopy(KTs[:, :, :, 128:132], kTn[:].rearrange("p h (s t) -> p s h t", t=TS), ["kTn"], ["KTs"])
                        blocks = []
                        for i in range(NS):
                            def prep(i=i):
                                pbg, pkg = bank()
                                mm(pbg[0:TS, 0:512], identb[0:NTS, TS * i:TS * i + TS], gatok[0:NTS, 0, :], True, True, ["identb", "gatok"], [pkg])
                                acopy(gas2[i % 2][:, :], pbg[0:TS, 0:512], [pkg], [f"gas{i % 2}"])
                            segs = [(lambda kvh, i=i: KTs[:, i, kvh, 0:128], Vs[:, i, :], 128),
                                    (lambda kvh, i=i: KTs[:, i, kvh, 128:132], vnew[:, i, :], TS)]
                            blocks.append(dict(nq=TS, segs=segs, nkeys=132, tb_c0=0, ga=gas2[i % 2][:, :], ga_key=f"gas{i % 2}", col=TS * i, prep=prep))
                        attend_all(blocks)
                    kb.retire(ph_keys)
                if KDBG and t == 0 and l == 0:
                    dstg = PA("dstg", [128, 4, TILE])
                    acopy(dstg[:], gatedT[:], ["gatedT"], ["dstg"])
                    kb.dma("sp", dbg_x[:, 0:2048], dstg[:].rearrange("p a b -> p (a b)"), ["dstg"], (), final=True)
                    kb.dma("sp", dbg_x[:, 2048:3072], xi[:, 0, :], [xik], (), final=True)
                branch_out(0)
                if KDBG and t == 0 and l == 0:
                    kb.dma("sp", dbg_m[0], merged[:].rearrange("p a b -> p (a b)"), ["merged"], (), final=True)

                with contextlib.ExitStack() as ph:
                    ph_keys = []

                    def PH(name, shape, dt=F32, ph=ph, ph_keys=ph_keys):
                        ph_keys.append(name)
                        return S(name, shape, dt, stack=ph)
                    nrot[0] = 6
                    bank_rr[0] = 0
                    uT = PH("uT", [128, 4, ntok], BF16); yz = PH("yz", [128, 4, ntok]); gbT = PH("gbT", [128, 4, ntok], BF16)
                    zb = PH("zb", [128, 4, ntok], BF16)
                    NXS = 1
                    XRs = [PH(f"XR{i}", [128, 16, SJ]) for i in range(NXS)]; XIs = [PH(f"XI{i}", [128, 16, SJ]) for i in range(NXS)]
                    T1 = PH("T1", [128, 16, SJ]); T2 = PH("T2", [128, 16, SJ])
                    HR = PH("HR", [128, 16, SJ], BF16); HI = PH("HI", [128, 16, SJ], BF16)
                    g1 = PH("g1", [128, ntok]); g2b = PH("g2b", [128, ntok])
                    NH = NS if sample else 1
                    hfr = PH("hfr", [128, 16, NH]); hfi = PH("hfi", [128, 16, NH]); s5a = PH("s5a", [128, 16, NH]); s5b = PH("s5b", [128, 16, NH])
                    if sample:
                        h0r = PH("h0r", [128, 16, NS]); h0i = PH("h0i", [128, 16, NS])
                        st_io = PH("st_io", [NS, 2048])
                        for ssrc, sdst, skey_ in ((sre, h0r, "h0r"), (sim, h0i, "h0i")):
                            kb.dma("sp", st_io[:], ssrc[l], ["st_io"], ["st_io"])
                            pbs, pks = bank()
                            for k in range(16):
                                tr(pbs[:, k * NS:(k + 1) * NS], st_io[0:NS, k * 128:(k + 1) * 128], ident[0:NS, 0:NS], ["st_io", "ident"], [pks])
                            acopy(sdst[:].rearrange("p k s -> p (k s)"), pbs[:, 0:16 * NS], [pks], [skey_])
                    for q in range(4):
                        if q % 2 == 0:
                            slot, skey = wget(f"u{q // 2}")
                        pb, pk = proj_fm(slot, skey, q % 2, 8, xTk, ["xT"])
                        acopy(uT[:, q, 0:ntok], pb[:, 0:ntok], [pk], ["uT"])
                        act(yz[:, q, 0:ntok], pb[:, 0:ntok], AF.Copy, [pk, "dS5"], ["yz"], scale=dS5[:, l, q:q + 1])
                    for q in range(4):
                        if q % 2 == 0:
                            slot, skey = wget(f"gb{q // 2}")
                        pb, pk = proj_fm(slot, skey, q % 2, 8, xTk, ["xT"])
                        act(gbT[:, q, 0:ntok], pb[:, 0:ntok], AF.Silu, [pk], ["gbT"])

                    cosl = COS[:, l, :, :]; sinl = SIN[:, l, :, :]
                    nsj = 1 if sample else TILE // SJ
                    def stage1(jt):
                        c0 = jt * SJ
                        XR = XRs[jt % NXS]; XI = XIs[jt % NXS]; kXR = f"XR{jt % NXS}"; kXI = f"XI{jt % NXS}"
                        pxr, kxr = bank2(); pxi, kxi = bank2()
                        for k in range(16):
                            h_ = (k % 4) // 2
                            pk_ = h_ * 8 + (k // 4) * 2 + k % 2
                            mm(pxr[:, pk_ * SJ:(pk_ + 1) * SJ], WBre[64 * h_:64 * h_ + 64, l, k // 4, k % 2, :], uT[64 * h_:64 * h_ + 64, k // 4, c0:c0 + SJ],
                               True, True, ["WBre", "uT"], [kxr[h_]])
                        for k in range(16):
                            h_ = (k % 4) // 2
                            pk_ = h_ * 8 + (k // 4) * 2 + k % 2
                            mm(pxi[:, pk_ * SJ:(pk_ + 1) * SJ], WBim[64 * h_:64 * h_ + 64, l, k // 4, k % 2, :], uT[64 * h_:64 * h_ + 64, k // 4, c0:c0 + SJ],
                               True, True, ["WBim", "uT"], [kxi[h_]])
                        for h_ in range(2):
                            acopy(XR[:].rearrange("p (q h kk) j -> p h q kk j", h=2, kk=2)[:, h_], pxr[:, 512 * h_:512 * h_ + 512].rearrange("p (q kk j) -> p q kk j", q=4, kk=2),
                                  [kxr[h_]], [kXR])
                            acopy(XI[:].rearrange("p (q h kk) j -> p h q kk j", h=2, kk=2)[:, h_], pxi[:, 512 * h_:512 * h_ + 512].rearrange("p (q kk j) -> p q kk j", q=4, kk=2),
                                  [kxi[h_]], [kXI])

                    stage1(0)
                    pend = []
                    for jt in range(nsj):
                        c0 = jt * SJ
                        if NXS == 2 and jt + 1 < nsj:
                            stage1(jt + 1)
                        if lru_units:
                            lru_units.pop(0)()
                        XR = XRs[jt % NXS]; XI = XIs[jt % NXS]; kXR = f"XR{jt % NXS}"; kXI = f"XI{jt % NXS}"
                        if sample:
                            cosv = cosl[:, :, 0:TS].unsqueeze(2).to_broadcast([128, 16, NS, TS])
                            sinv = sinl[:, :, 0:TS].unsqueeze(2).to_broadcast([128, 16, NS, TS])
                            V = lambda a: a[:].rearrange("p k (s t) -> p k s t", t=TS)
                            magt = MAGT[:, l, :, :]
                            memset(MAGT[:, l, :, :].rearrange("p k (s t) -> p k s t", t=TS)[:, :, :, 0:1], 0.0, ["MAGT"])
                        else:
                            cosv = cosl[:, :, 0:SJ]; sinv = sinl[:, :, 0:SJ]
                            V = lambda a: a[:]
                            magt = MAGT[:, l, :, :]
                        tt(V(T1), V(XR), cosv, ALU.mult, [kXR, "COS"], ["T1"])
                        tt(V(T2), V(XI), sinv, ALU.mult, [kXI, "SIN"], ["T2"])
                        tt(T1[:], T1[:], T2[:], ALU.add, ["T1", "T2"], ["T1"])
                        tt(V(T2), V(XR), sinv, ALU.mult, [kXR, "SIN"], ["T2"])
                        tt(V(XI), V(XI), cosv, ALU.mult, [kXI, "COS"], [kXI])
                        tt(XI[:], XI[:], T2[:], ALU.subtract, [kXI, "T2"], [kXI])
                        if sample:
                            tt(s5a[:], h0r[:], AbR[:, l, :].unsqueeze(2).to_broadcast([128, 16, NS]), ALU.mult, ["h0r", "Ab"], ["s5a"])
                            tt(s5b[:], h0i[:], AbI[:, l, :].unsqueeze(2).to_broadcast([128, 16, NS]), ALU.mult, ["h0i", "Ab"], ["s5b"])
                            tt(s5a[:], s5a[:], s5b[:], ALU.subtract, ["s5a", "s5b"], ["s5a"])
                            tt(V(T1)[:, :, :, 0], V(T1)[:, :, :, 0], s5a[:], ALU.add, ["T1", "s5a"], ["T1"])
                            tt(s5a[:], h0r[:], AbI[:, l, :].unsqueeze(2).to_broadcast([128, 16, NS]), ALU.mult, ["h0r", "Ab"], ["s5a"])
                            tt(s5b[:], h0i[:], AbR[:, l, :].unsqueeze(2).to_broadcast([128, 16, NS]), ALU.mult, ["h0i", "Ab"], ["s5b"])
                            tt(s5a[:], s5a[:], s5b[:], ALU.add, ["s5a", "s5b"], ["s5a"])
                            tt(V(XI)[:, :, :, 0], V(XI)[:, :, :, 0], s5a[:], ALU.add, [kXI, "s5a"], [kXI])
                        else:
                            tt(T1[:, :, 0], T1[:, :, 0], Gre[:, l, :], ALU.add, ["T1", "G"], ["T1"])
                            tt(XI[:, :, 0], XI[:, :, 0], Gim[:, l, :], ALU.add, [kXI, "G"], [kXI])
                        flat = lambda a: a[:].rearrange("p k j -> p (k j)")
                        mflat = magt.rearrange("p k j -> p (k j)")
                        kb.op("dve", lambda e, o=flat(XR), a=mflat, b=flat(T1): e.tensor_tensor_scan(out=o, data0=a, data1=b, initial=0.0,
                                                                                                     op0=ALU.mult, op1=ALU.add),
                              ["T1", "MAGT"], [kXR])
                        kb.op("dve", lambda e, o=flat(T2), a=mflat, b=flat(XI): e.tensor_tensor_scan(out=o, data0=a, data1=b, initial=0.0,
                                                                                                     op0=ALU.mult, op1=ALU.add),
                              [kXI, "MAGT"], ["T2"])
                        if pend:
                            pend.pop()()
                        pby, pky = pst[3][:, (jt % 2) * 512:(jt % 2) * 512 + 512], f"pb{6 + jt % 2}"
                        if sample:
                            tt(V(T1), V(XR), cosv, ALU.mult, [kXR, "COS"], ["T1"])
                            tt(V(XI), V(T2), sinv, ALU.mult, ["T2", "SIN"], [kXI])
                            tt(HR[:], T1[:], XI[:], ALU.subtract, ["T1", kXI], ["HR"])
                            tt(V(T1), V(XR), sinv, ALU.mult, [kXR, "SIN"], ["T1"])
                            tt(V(XI), V(T2), cosv, ALU.mult, ["T2", "COS"], [kXI])
                            tt(HI[:], T1[:], XI[:], ALU.add, ["T1", kXI], ["HI"])
                            for q in range(4):
                                for kk in range(4):
                                    k = 4 * q + kk
                                    mm(pby[:, q * SJ:(q + 1) * SJ], WCre[:, l, k, :], HR[:, k, :], kk == 0, False, ["WCre", "HR"], [pky])
                                    mm(pby[:, q * SJ:(q + 1) * SJ], WCim[:, l, k, :], HI[:, k, :], False, kk == 3, ["WCim", "HI"], [pky])
                        else:
                            H2v = zb[:].rearrange("p q t -> p (q t)").rearrange("p (a k j) -> p a k j", a=2, k=16)
                            HR2, HI2 = H2v[:, 0], H2v[:, 1]
                            tt(HR[:], XR[:], cosv, ALU.mult, [kXR, "COS"], ["HR"])
                            stt(HR2, T2[:], -1.0, sinv, ALU.mult, ALU.mult, ["T2", "SIN"], ["zb"])
                            tt(HI[:], XR[:], sinv, ALU.mult, [kXR, "SIN"], ["HI"])
                            tt(HI2, T2[:], cosv, ALU.mult, ["T2", "COS"], ["zb"])
                            for q in range(4):
                                for kk in range(4):
                                    k = 4 * q + kk
                                    mm(pby[:, q * SJ:(q + 1) * SJ], WCre[:, l, k, :], HR[:, k, :], kk == 0, False, ["WCre", "HR"], [pky])
                                    mm(pby[:, q * SJ:(q + 1) * SJ], WCre[:, l, k, :], HR2[:, k, :], False, False, ["WCre", "zb"], [pky])
                                    mm(pby[:, q * SJ:(q + 1) * SJ], WCim[:, l, k, :], HI[:, k, :], False, False, ["WCim", "HI"], [pky])
                                    mm(pby[:, q * SJ:(q + 1) * SJ], WCim[:, l, k, :], HI2[:, k, :], False, kk == 3, ["WCim", "zb"], [pky])
                        pend.append(lambda c0=c0, pby=pby, pky=pky: tt(yz[:, :, c0:c0 + SJ], yz[:, :, c0:c0 + SJ],
                                                                        pby[:, 0:4 * SJ].rearrange("p (q j) -> p q j", q=4), ALU.add, ["yz", pky], ["yz"]))
                        if not sample:
                            tt(s5a[:, :, 0], XR[:, :, SJ - 1], CaR[:, l, :], ALU.mult, [kXR, "Ca"], ["s5a"], eng="pool")
                            tt(s5b[:, :, 0], T2[:, :, SJ - 1], CaI[:, l, :], ALU.mult, ["T2", "Ca"], ["s5b"], eng="pool")
                            tt(Gre[:, l, :], s5a[:, :, 0], s5b[:, :, 0], ALU.subtract, ["s5a", "s5b"], ["G"], eng="pool")
                            tt(s5a[:, :, 0], XR[:, :, SJ - 1], CaI[:, l, :], ALU.mult, [kXR, "Ca"], ["s5a"], eng="pool")
                            tt(s5b[:, :, 0], T2[:, :, SJ - 1], CaR[:, l, :], ALU.mult, ["T2", "Ca"], ["s5b"], eng="pool")
                            tt(Gim[:, l, :], s5a[:, :, 0], s5b[:, :, 0], ALU.add, ["s5a", "s5b"], ["G"], eng="pool")
                        if (last_prompt and jt == nsj - 1) or sample:
                            if sample:
                                gr = V(XR)[:, :, :, TS - 1]; gi = V(T2)[:, :, :, TS - 1]
                                cl = cosl[:, :, TS - 1:TS].to_broadcast([128, 16, NS]); sl = sinl[:, :, TS - 1:TS].to_broadcast([128, 16, NS])
                                o1, o2, a_, b_ = hfr[:], hfi[:], s5a[:], s5b[:]
                            else:
                                gr = XR[:, :, SJ - 1]; gi = T2[:, :, SJ - 1]
                                cl = cosl[:, :, SJ - 1]; sl = sinl[:, :, SJ - 1]
                                o1, o2, a_, b_ = hfr[:, :, 0], hfi[:, :, 0], s5a[:, :, 0], s5b[:, :, 0]
                            tt(a_, gr, cl, ALU.mult, [kXR, "COS"], ["s5a"])
                            tt(b_, gi, sl, ALU.mult, ["T2", "SIN"], ["s5b"])
                            tt(o1, a_, b_, ALU.subtract, ["s5a", "s5b"], ["hfr"])
                            tt(a_, gr, sl, ALU.mult, [kXR, "SIN"], ["s5a"])
                            tt(b_, gi, cl, ALU.mult, ["T2", "COS"], ["s5b"])
                            tt(o2, a_, b_, ALU.add, ["s5a", "s5b"], ["hfi"])
                            if sample:
                                for hsrc, hkey, hdst in ((hfr, "hfr", nres), (hfi, "hfi", nims)):
                                    pbA, pkA = bank2(); pbB, pkB = bank2()
                                    for k in range(16):
                                        pbv, pkv = (pbA, pkA) if k < 8 else (pbB, pkB)
                                        tr(pbv[0:NS, (k % 8) * 128:(k % 8 + 1) * 128], hsrc[:, k, :], ident[:], [hkey, "ident"], [pkv[(k % 8) // 4]])
                                    acopy(st_io[:, 0:1024], pbA[0:NS, :], pkA, ["st_io"])
                                    acopy(st_io[:, 1024:2048], pbB[0:NS, :], pkB, ["st_io"])
                                    kb.dma("sp", hdst[l], st_io[:], ["st_io"], (), final=True)
                            else:
                                kb.dma("sp", nrep[l].rearrange("(k p) -> p k", p=128), hfr[:, :, 0], ["hfr"], (), final=True)
                                kb.dma("sp", nimp[l].rearrange("(k p) -> p k", p=128), hfi[:, :, 0], ["hfi"], (), final=True)
                        if NXS == 1 and jt + 1 < nsj:
                            stage1(jt + 1)
                    while lru_units:
                        lru_units.pop(0)()
                    while pend:
                        pend.pop()()
                    for q in range(4):
                        yv = yz[:, q, 0:ntok]
                        gq, kgq = (g1, "g1") if q % 2 == 0 else (g2b, "g2b")
                        kyz = f"yz{q}"
                        act(gq[:, 0:ntok], yv, AF.Square, ["yz"], [kgq])
                        ts(gq[:, 0:ntok], gq[:, 0:ntok], 0.044715, ALU.mult, [kgq], [kgq], s2=1.0, op1=ALU.add)
                        tt(gq[:, 0:ntok], gq[:, 0:ntok], yv, ALU.mult, [kgq, "yz"], [kgq])
                        act(gq[:, 0:ntok], gq[:, 0:ntok], AF.Sigmoid, [kgq], [kgq], scale=1.5957691216)
                        tt(yv, yv, gq[:, 0:ntok], ALU.mult, ["yz", kgq], [kyz])
                        acopy(zb[:, q, 0:ntok], yv, [kyz], ["zb"])
                    slot, skey = wget("glu")
                    for fo in range(4):
                        pb, pk = proj_fm(slot, skey, fo, 4, lambda c: zb[:, c, 0:ntok], ["zb"])
                        act(g1[:, 0:ntok], pb[:, 0:ntok], AF.Sigmoid, [pk], ["g1"])
                        tt(g2b[:, 0:ntok], yz[:, fo, 0:ntok], g1[:, 0:ntok], ALU.mult, [f"yz{fo}", "g1"], ["g2b"])
                        tt(gatedT[:, fo, 0:ntok], g2b[:, 0:ntok], gbT[:, fo, 0:ntok], ALU.mult, ["g2b", "gbT"], ["gatedT"])
                    nrot[0] = 8
                    ph_keys.extend([f"yz{q}" for q in range(4)])
                    kb.retire(ph_keys)
                branch_out(1)
                if KDBG and t == 0 and l == 0:
                    kb.dma("sp", dbg_m[1], merged[:].rearrange("p a b -> p (a b)"), ["merged"], (), final=True)

                with contextlib.ExitStack() as ph:
                    ph_keys = []

                    def PH(name, shape, dt=F32, ph=ph, ph_keys=ph_keys):
                        ph_keys.append(name)
                        return S(name, shape, dt, stack=ph)
                    if sample:
                        h0l = PH("h0l", [128, 4, NS]); hso = PH("hso", [128, 4, NS])
                        l_io = PH("l_io", [NS, 512]); c_in = PH("c_in", [3 * NS, 512])
                        kb.dma("sp", l_io[:], slru[l], (), ["l_io"])
                        kb.dma("sp", c_in[:], sconv[l].rearrange("s j f -> (s j) f"), (), ["c_in"])
                        pbs, pks = bank()
                        for q in range(4):
                            tr(pbs[:, q * NS:(q + 1) * NS], l_io[0:NS, q * 128:(q + 1) * 128], ident[0:NS, 0:NS], ["l_io", "ident"], [pks])
                        acopy(h0l[:].rearrange("p q s -> p (q s)"), pbs[:, 0:4 * NS], [pks], ["h0l"])
                        pbs, pks = bank()
                        for q in range(4):
                            tr(pbs[:, q * 3 * NS:(q + 1) * 3 * NS], c_in[0:3 * NS, q * 128:(q + 1) * 128], ident[0:3 * NS, 0:3 * NS], ["c_in", "ident"], [pks])
                        for q in range(4):
                            acopy(cbuf[:, q, :, 0:3], pbs[:, q * 3 * NS:(q + 1) * 3 * NS].rearrange("p (s j) -> p s j", j=3), [pks], ["cbuf"])
                    lbuf = {}
                    for nm_, dt_ in (("cvv", F32), ("cvb", BF16), ("rg", F32), ("ig", F32), ("aa", F32), ("sq", F32), ("hT", F32)):
                        lbuf[nm_] = [PH(f"{nm_}{i}", [128, ntok], dt_) for i in range(2)]
                    tl = PH("tl", [128, NS])
                    hfo = PH("hfo", [128, 4])
                    if sample:
                        for tt_ in range(1, TS):
                            kb.dma("sp", ncvs[l][:, tt_ - 1, :], xco[tt_:NTS:TS, :], ["xco"], (), final=True)
                    elif last_prompt:
                        kb.dma("sp", ncvp[l], xco[0:3, :], ["xco"], (), final=True)
                    def lru_chunk(q):
                        cvv, cvb, rg, ig, aa, sq, hT = (lbuf[n_][q % 2] for n_ in ("cvv", "cvb", "rg", "ig", "aa", "sq", "hT"))
                        kcvv, kcvb, krg, kig, kaa, ksq, khT = (f"{n_}{q % 2}" for n_ in ("cvv", "cvb", "rg", "ig", "aa", "sq", "hT"))
                        if sample:
                            win = lambda j: cbuf[:, q, :, j:j + TS]
                            V2 = lambda a: a[:, 0:ntok].rearrange("p (s t) -> p s t", t=TS)
                        else:
                            win = lambda j: cbuf[:, q, j:j + ntok]
                            V2 = lambda a: a[:, 0:ntok]
                        ts(V2(cvv), win(0), cw[:, l, q, 0:1], ALU.mult, ["cbuf", "cw", "cb"], [kcvv], s2=cb[:, l, q:q + 1], op1=ALU.add)
                        for j in range(1, 4):
                            stt(V2(cvv), win(j), cw[:, l, q, j:j + 1], V2(cvv), ALU.mult, ALU.add, ["cbuf", "cw", kcvv], [kcvv])
                        acopy(cvb[:, 0:ntok], cvv[:, 0:ntok], [kcvv], [kcvb])
                        pb, pk = bank()
                        mm(pb[:, 0:ntok], WA[:, l, q, :], cvb[:, 0:ntok], True, True, ["WA", kcvb], [pk])
                        act(rg[:, 0:ntok], pb[:, 0:ntok], AF.Sigmoid, [pk, "ba"], [krg], bias=ba[:, l, q:q + 1], scale=1.0)
                        pb, pk = bank()
                        mm(pb[:, 0:ntok], WX[:, l, q, :], cvb[:, 0:ntok], True, True, ["WX", kcvb], [pk])
                        act(ig[:, 0:ntok], pb[:, 0:ntok], AF.Sigmoid, [pk, "bx"], [kig], bias=bx[:, l, q:q + 1], scale=1.0)
                        act(sq[:, 0:ntok], rg[:, 0:ntok], AF.Exp, [krg, "c2"], [ksq], scale=c2[:, l, q:q + 1])
                        act(sq[:, 0:ntok], sq[:, 0:ntok], AF.Sqrt, [ksq], [ksq], scale=-1.0, bias=1.0)
                        act(aa[:, 0:ntok], rg[:, 0:ntok], AF.Exp, [krg, "c1"], [kaa], scale=c1[:, l, q:q + 1])
                        tt(ig[:, 0:ntok], ig[:, 0:ntok], cvv[:, 0:ntok], ALU.mult, [kig, kcvv], [kig])
                        tt(ig[:, 0:ntok], ig[:, 0:ntok], sq[:, 0:ntok], ALU.mult, [kig, ksq], [kig])
                        if sample:
                            tt(tl[:], V2(aa)[:, :, 0], h0l[:, q, :], ALU.mult, [kaa, "h0l"], ["tl"])
                            tt(V2(ig)[:, :, 0], V2(ig)[:, :, 0], tl[:], ALU.add, [kig, "tl"], [kig])
                            memset(V2(aa)[:, :, 0:1], 0.0, [kaa])
                            kb.op("dve", lambda e: e.tensor_tensor_scan(out=hT[:, 0:ntok], data0=aa[:, 0:ntok], data1=ig[:, 0:ntok], initial=0.0,
                                                                        op0=ALU.mult, op1=ALU.add), [kaa, kig], [khT])
                            vcopy(hso[:, q, :], V2(hT)[:, :, TS - 1], [khT], ["hso"])
                        else:
                            kb.op("dve", lambda e, q=q: e.tensor_tensor_scan(out=hT[:, 0:ntok], data0=aa[:, 0:ntok], data1=ig[:, 0:ntok],
                                                                             initial=hl[:, l, q:q + 1], op0=ALU.mult, op1=ALU.add),
                                  [kaa, kig, "hl"], [khT])
                            vcopy(hl[:, l, q:q + 1], hT[:, ntok - 1:ntok], [khT], ["hl"])
                        tt(gatedT[:, q, 0:ntok], hT[:, 0:ntok], gcT[:, q, 0:ntok], ALU.mult, [khT, "gcT"], ["gatedT"])

                    for q in range(4):
                        lru_chunk(q)
                    if sample:
                        pbs, pks = bank()
                        for q in range(4):
                            tr(pbs[0:NS, q * 128:(q + 1) * 128], hso[:, q, :], ident[:], ["hso", "ident"], [pks])
                        acopy(l_io[:], pbs[0:NS, 0:512], [pks], ["l_io"])
                        kb.dma("sp", nlrus[l], l_io[:], ["l_io"], (), final=True)
                    elif last_prompt:
                        vcopy(hfo[:], hl[:, l, :], ["hl"], ["hfo"])
                        kb.dma("sp", nlrup[l].rearrange("(q p) -> p q", p=128), hfo[:], ["hfo"], (), final=True)
                    kb.retire(ph_keys)
                mTh["mT"] = PA("mT", [128, 8, ntok], BF16)
                branch_out(2)

                with contextlib.ExitStack() as ph:
                    ph_keys = []

                    def PH(name, shape, dt=F32, ph=ph, ph_keys=ph_keys):
                        ph_keys.append(name)
                        return S(name, shape, dt, stack=ph)
                    lng = PH("lng", [128, D]); lnb = PH("lnb", [128, D])
                    st6 = PH("st6", [128, 2, 6]); mv = PH("mv", [128, 2]); rstd = PH("rstd", [128, 1])
                    kb.dma("sp", lng[:], ln_g[l].partition_broadcast(128), (), ["lng"])
                    kb.dma("sp", lnb[:], ln_b[l].partition_broadcast(128), (), ["lnb"])
                    wos = [wget(f"wo{wo}", ahead=RING - wo) for wo in range(4)]
                    for s in range(nsub):
                        for wo in range(4):
                            slot, skey = wos[wo]
                            pb, pk = bank()
                            for fc in range(8):
                                mm(pb[0:R, 0:256], mTh["mT"][:, fc, s * 128:s * 128 + R], v8(slot)[:, fc, :], fc == 0, fc == 7, [skey, "mT"], [pk])
                            stt(xo[0:R, s, wo * 256:wo * 256 + 256], xi[0:R, s, wo * 256:wo * 256 + 256], DN_ALPHA, pb[0:R, 0:256],
                                ALU.mult, ALU.add, [xik, pk], [xok])
                        hv = xo[0:R, s, :]
                        for hh in range(2):
                            kb.op("dve", lambda e, hh=hh, hv=hv: e.bn_stats(out=st6[0:R, hh, :], in_=hv[:, hh * 512:hh * 512 + 512]), [xok], ["st6"])
                        kb.op("dve", lambda e: e.bn_aggr(out=mv[0:R, :], in_=st6[0:R, :, :].rearrange("p a b -> p (a b)")), ["st6"], ["mv"])
                        act(rstd[0:R, :], mv[0:R, 1:2], AF.Sqrt, ["mv", "epsc"], ["rstd"], bias=epsc[0:R, :], scale=1.0)
                        kb.op("dve", lambda e: e.reciprocal(rstd[0:R, :], rstd[0:R, :]), ["rstd"], ["rstd"])
                        ts(hv, hv, mv[0:R, 0:1], ALU.subtract, [xok, "mv", "rstd"], [xok], s2=rstd[0:R, 0:1], op1=ALU.mult)
                        tt(hv, hv, lng[0:R, :], ALU.mult, [xok, "lng"], [xok])
                        tt(hv, hv, lnb[0:R, :], ALU.add, [xok, "lnb"], [xok])
                        if l == DEPTH - 1 or os.environ.get("KDUMPL0"):
                            if sample:
                                kb.dma("sp", ys[:, :], hv, [xok], (), final=True)
                            else:
                                r0 = t * TILE + s * 128
                                kb.dma("sp", yp[r0:r0 + 128, :], hv, [xok], (), final=True)
                    kb.retire(ph_keys)
                kb.retire(pa_keys)

        for (t, l) in all_passes:
            if l == 0:
                if t == NPT:
                    kb.dma("sp", xtok[0][0:NTS, 0, :], xs, (), ["xtok0"])
                else:
                    kb.dma("sp", xtok[0][:], xp[t * TILE:(t + 1) * TILE, :].rearrange("(s p) d -> p s d", p=128), (), ["xtok0"])
            run_pass(t, l, l % 2, (l + 1) % 2)
        nx = int(os.environ.get("KEXTRA", "0"))
        if nx:
            kb.maxops = 10 ** 9
            pbx, pkx = bank()
            for _ in range(nx):
                mm(pbx[:, 0:128], identb[:], identb[:], True, True, ["identb"], [pkx])
        print("recorded ops:", kb.nrec, {k: v for k, v in kb.cnt.items()})
        kb.emit()
    return nc


_NC_CACHE = {}


def kernel(**inputs):
    f = lambda k: np.ascontiguousarray(np.asarray(inputs[k], dtype=np.float32))
    oh, neg = _bias_consts()
    par = (np.arange(128) // 16) % 2
    pm = np.stack([1.0 - par, par], 1).astype(np.float32)
    shared = {
        "rel_bias": f("rel_bias"), "w_in": f("w_in"), "sinks": f("sinks"),
        "w_br_a": f("w_branch_a"), "w_br_b": f("w_branch_b"), "w_br_c": f("w_branch_c"),
        "lam_re": f("ssm_lambda_re").reshape(2, 2048), "lam_im": f("ssm_lambda_im").reshape(2, 2048),
        "log_step": f("ssm_log_step"), "b_re": f("ssm_b_re"), "b_im": f("ssm_b_im"),
        "c_re": f("ssm_c_re"), "c_im": f("ssm_c_im"), "ssm_d": f("ssm_d"), "w_glu": f("ssm_w_glu"),
        "conv_w": f("conv_w"), "conv_b": f("conv_b"), "lru_w_a": f("lru_w_a"), "lru_b_a": f("lru_b_a"),
        "lru_w_x": f("lru_w_x"), "lru_b_x": f("lru_b_x"), "lru_lam": f("lru_lambda"), "w_out": f("w_out"),
        "ln_g": f("ln_g"), "ln_b": f("ln_b"), "oh2": oh, "negmask": neg, "pmask": pm,
    }
    x_prompt = f("x_prompt"); x_sample = f("x_sample")
    cache_k = f("cache_k").reshape(2, 128, 128, 128); cache_v = f("cache_v").reshape(2, 128, 128, 128)
    s_re = f("state_ssm_re").reshape(2, 128, 2048); s_im = f("state_ssm_im").reshape(2, 128, 2048)
    s_lru = f("state_lru"); s_conv = f("state_conv")
    in_maps = []
    for c in range(NCORES):
        sl = slice(NS * c, NS * c + NS)
        m = dict(shared)
        m.update({
            "xp": x_prompt[c], "xs": np.ascontiguousarray(x_sample[sl].reshape(NTS, D)),
            "ck": np.ascontiguousarray(cache_k[:, sl]), "cv": np.ascontiguousarray(cache_v[:, sl]),
            "sre": np.ascontiguousarray(s_re[:, sl]), "sim": np.ascontiguousarray(s_im[:, sl]),
            "slru": np.ascontiguousarray(s_lru[:, sl]), "sconv": np.ascontiguousarray(s_conv[:, sl]),
        })
        in_maps.append(m)
    if "nc" not in _NC_CACHE:
        _NC_CACHE["nc"] = build_nc()
    res = run_bass_kernel_spmd(_NC_CACHE["nc"], in_maps, core_ids=list(range(NCORES)))
    r = res.results
    cat = lambda k, ax: np.concatenate([np.asarray(r[c][k])[None] if ax is None else np.asarray(r[c][k]) for c in range(NCORES)], axis=0 if ax is None else ax)
    y_prompt = np.stack([r[c]["yp"] for c in range(NCORES)], 0).astype(np.float32)
    y_sample = np.concatenate([r[c]["ys"].reshape(NS, TS, D) for c in range(NCORES)], 0).astype(np.float32)
    stk = lambda k, shp: np.stack([np.asarray(r[c][k]).reshape(shp) for c in range(NCORES)], 1).astype(np.float32)
    ccat = lambda k, shp: np.concatenate([np.asarray(r[c][k]).reshape(shp) for c in range(NCORES)], 1).astype(np.float32)
    return (y_prompt, y_sample,
            stk("nkp", (2, 128, 2, 64)), stk("nvp", (2, 128, 2, 64)),
            stk("nrep", (2, 32, 64)), stk("nimp", (2, 32, 64)),
            stk("nlrup", (2, 512)), stk("ncvp", (2, 3, 512)),
            ccat("nks", (2, NS, 128, 2, 64)), ccat("nvs", (2, NS, 128, 2, 64)),
            ccat("nres", (2, NS, 32, 64)), ccat("nims", (2, NS, 32, 64)),
            ccat("nlrus", (2, NS, 512)), ccat("ncvs", (2, NS, 3, 512)))
```

```python
import contextlib
import math
import os
import numpy as np
import concourse.bass as bass
import concourse.mybir as mybir
from concourse.bass_utils import run_bass_kernel_spmd

F32 = mybir.dt.float32
BF16 = mybir.dt.bfloat16
I32 = mybir.dt.int32
AF = mybir.ActivationFunctionType
ALU = mybir.AluOpType
AX = mybir.AxisListType

NCORES = 8
D = 1024
DEPTH = 2
SEQ = 2048
NS = 16
TS = 4
NTS = NS * TS
TILE = 512
NPT = SEQ // TILE
NEG = -1e30
DN_ALPHA = (2 * DEPTH) ** 0.25
LN_EPS = 1e-5
SJ = 64
COMPUTE = ("pe", "dve", "act", "pool")


class KB:
    def __init__(self, nc, n_dma_sems=16):
        self.nc = nc
        self.eng = {"pe": nc.tensor, "dve": nc.vector, "act": nc.scalar, "pool": nc.gpsimd, "sp": nc.sync}
        self.streams = {k: [] for k in self.eng}
        self.sem = {k: nc.alloc_semaphore(name=f"c_{k}") for k in COMPUTE}
        self.cnt = {k: 0 for k in COMPUTE}
        self.waited = {}
        self.lastw = {}
        self.readers = {}
        self.dpool = {}
        for q in ("sp", "pool"):
            self.dpool[q] = {"sems": [nc.alloc_semaphore(name=f"d_{q}{i}") for i in range(n_dma_sems)],
                             "uses": [0] * n_dma_sems, "next": 0}
        self.final_tokens = []
        self.dma_since_barrier = []
        self.nrec = 0
        self.alias_pending = {}
        self.maxops = int(os.environ.get("KMAXOPS", "1000000000"))
        self.dummy = (self.sem["pe"], 0, "pe")

    def _wait(self, e, tok):
        sem, val, owner = tok
        if owner == "pe" and e == "pe":
            return
        key = (e, sem.name)
        if self.waited.get(key, 0) >= val:
            return
        self.waited[key] = val
        eng = self.eng[e]
        self.streams[e].append(lambda eng=eng, sem=sem, val=val: eng.wait_ge(sem, val))

    def retire(self, keys):
        for k in keys:
            toks = []
            if k in self.lastw:
                toks.append(self.lastw.pop(k))
            toks.extend(self.readers.pop(k, []))
            for t in toks:
                cur = self.alias_pending.get(t[0].name)
                if cur is None or cur[1] < t[1]:
                    self.alias_pending[t[0].name] = t

    def _deps(self, e, reads, writes):
        for k in writes:
            if k not in self.lastw and k not in self.readers:
                for t in self.alias_pending.values():
                    self._wait(e, t)
                break
        toks = []
        for k in reads:
            if k in self.lastw:
                toks.append(self.lastw[k])
        for k in writes:
            if k in self.lastw:
                toks.append(self.lastw[k])
            toks.extend(self.readers.get(k, []))
        for t in toks:
            self._wait(e, t)

    def _commit(self, tok, reads, writes):
        for k in writes:
            self.lastw[k] = tok
            self.readers[k] = []
        for k in reads:
            self.readers.setdefault(k, []).append(tok)

    def op(self, e, fn, reads=(), writes=()):
        self.nrec += 1
        if self.nrec > self.maxops:
            return self.dummy
        if self.nrec == int(os.environ.get("KTRACE", "-1")):
            import traceback
            traceback.print_stack()
            for k in list(reads) + list(writes):
                print("KEY", k, "lastw", (self.lastw[k][0].name, self.lastw[k][1]) if k in self.lastw else None,
                      "readers", [(t[0].name, t[1]) for t in self.readers.get(k, [])])
            print("CNT", self.cnt, {q: p["uses"] for q, p in self.dpool.items()})
        self._deps(e, reads, writes)
        self.cnt[e] += 1
        tok = (self.sem[e], self.cnt[e], e)
        sem = self.sem[e]
        eng = self.eng[e]
        self.streams[e].append(lambda eng=eng, fn=fn, sem=sem: fn(eng).then_inc(sem, 1))
        self._commit(tok, reads, writes)
        return tok

    def dma(self, q, out, in_, reads=(), writes=(), final=False, **kw):
        self.nrec += 1
        if self.nrec > self.maxops:
            return self.dummy
        self._deps(q, reads, writes)
        p = self.dpool[q]
        i = p["next"]
        p["next"] = (i + 1) % len(p["sems"])
        sem = p["sems"][i]
        if p["uses"][i] > 0:
            self._wait(q, (sem, 16 * p["uses"][i], None))
        p["uses"][i] += 1
        tok = (sem, 16 * p["uses"][i], None)
        eng = self.eng[q]
        self.streams[q].append(
            lambda eng=eng, out=out, in_=in_, sem=sem, kw=kw: eng.dma_start(out=out, in_=in_, **kw).then_inc(sem, 16))
        self._commit(tok, reads, writes)
        self.dma_since_barrier.append(tok)
        if final:
            self.final_tokens.append(tok)
        return tok

    def barrier(self):
        toks = [(self.sem[k], self.cnt[k], k) for k in COMPUTE if self.cnt[k] > 0] + list(self.dma_since_barrier)
        for e in self.eng:
            for t in toks:
                if t[2] == e:
                    continue
                self._wait(e, t)
        self.dma_since_barrier = []

    def emit(self):
        for t in self.final_tokens:
            self._wait("sp", t)
        nc = self.nc
        st = self.streams
        with nc.Block() as block:
            @block.sync
            def _(e):
                for f in st["sp"]:
                    f()

            @block.tensor
            def _(e):
                for f in st["pe"]:
                    f()

            @block.vector
            def _(e):
                for f in st["dve"]:
                    f()

            @block.scalar
            def _(e):
                for f in st["act"]:
                    f()

            @block.gpsimd
            def _(e):
                for f in st["pool"]:
                    f()


def _t5_bucket(d):
    max_exact = 16
    d = max(d, 0)
    if d < max_exact:
        return d
    v = np.float32(np.log(np.float32(max(d, 1)) / np.float32(max_exact))) / np.float32(math.log(128 / max_exact)) * np.float32(16)
    return min(max_exact + int(np.float32(v)), 31)


def _bias_consts():
    oh = np.zeros((32, 383), np.float32)
    neg = np.zeros((8, 383), np.float32)
    for i in range(383):
        d = 255 - i
        if 0 <= d < 128:
            oh[_t5_bucket(d), i] = 1.0
        else:
            neg[:, i] = NEG
    return oh, neg


def build_nc(passes=None, dbg=False):
    nc = bass.Bass("TRN2", target_bir_lowering=False)

    def din(name, shape):
        return nc.dram_tensor(name, list(shape), F32, kind="ExternalInput").ap()

    def dout(name, shape):
        return nc.dram_tensor(name, list(shape), F32, kind="ExternalOutput").ap()

    xp = din("xp", [SEQ, D]); xs = din("xs", [NTS, D])
    ck = din("ck", [2, NS, 128, 128]); cv = din("cv", [2, NS, 128, 128])
    sre = din("sre", [2, NS, 2048]); sim = din("sim", [2, NS, 2048])
    slru = din("slru", [2, NS, 512]); sconv = din("sconv", [2, NS, 3, 512])
    rel_bias = din("rel_bias", [32, 8]); w_in = din("w_in", [2, D, 6400]); sinks = din("sinks", [2, 8])
    w_br = [din("w_br_a", [2, 512, D]), din("w_br_b", [2, 512, D]), din("w_br_c", [2, 512, D])]
    lam_re = din("lam_re", [2, 2048]); lam_im = din("lam_im", [2, 2048]); log_step = din("log_step", [2, 32])
    b_re = din("b_re", [2, 32, 64, 16]); b_im = din("b_im", [2, 32, 64, 16])
    c_re = din("c_re", [2, 32, 16, 64]); c_im = din("c_im", [2, 32, 16, 64])
    ssm_d = din("ssm_d", [2, 512]); w_glu = din("w_glu", [2, 512, 512])
    conv_w = din("conv_w", [2, 4, 512]); conv_b = din("conv_b", [2, 512])
    lru_w_a = din("lru_w_a", [2, 8, 64, 64]); lru_b_a = din("lru_b_a", [2, 512])
    lru_w_x = din("lru_w_x", [2, 8, 64, 64]); lru_b_x = din("lru_b_x", [2, 512])
    lru_lam = din("lru_lam", [2, 512]); w_out = din("w_out", [2, D, D])
    ln_g = din("ln_g", [2, D]); ln_b = din("ln_b", [2, D])
    oh2 = din("oh2", [32, 383]); negmask = din("negmask", [8, 383]); pmask = din("pmask", [128, 2])

    yp = dout("yp", [SEQ, D]); ys = dout("ys", [NTS, D])
    nkp = dout("nkp", [2, 128, 128]); nvp = dout("nvp", [2, 128, 128])
    nrep = dout("nrep", [2, 2048]); nimp = dout("nimp", [2, 2048])
    nlrup = dout("nlrup", [2, 512]); ncvp = dout("ncvp", [2, 3, 512])
    nks = dout("nks", [2, NS, 128, 128]); nvs = dout("nvs", [2, NS, 128, 128])
    nres = dout("nres", [2, NS, 2048]); nims = dout("nims", [2, NS, 2048])
    nlrus = dout("nlrus", [2, NS, 512]); ncvs = dout("ncvs", [2, NS, 3, 512])
    KDBG = bool(os.environ.get("KDBG"))
    dbg_m = [dout(f"dbg_m{b}", [128, 8 * TILE]) for b in range(2)] if KDBG else None
    dbg_x = dout("dbg_x", [128, 4 * D]) if KDBG else None
    rv_t = nc.dram_tensor("rv_scr", [8, 383], F32, kind="Internal")
    rv = rv_t.ap()
    tb_scr = nc.dram_tensor("tb_scr", [128, 2048], F32, kind="Internal").ap()

    kb = KB(nc)
    ncd = nc.allow_non_contiguous_dma(reason="small strided parameter/state transfers")

    def mm(out, lhsT, rhs, start, stop, reads, writes):
        kb.op("pe", lambda e: e.matmul(out, lhsT=lhsT, rhs=rhs, start=start, stop=stop), reads, writes)

    def tr(out, in_, ident, reads, writes):
        kb.op("pe", lambda e: e.transpose(out, in_, ident), reads, writes)

    def act(out, in_, func, reads, writes, **kw):
        kb.op("act", lambda e: e.activation(out=out, in_=in_, func=func, **kw), reads, writes)

    def acopy(out, in_, reads, writes):
        kb.op("act", lambda e: e.copy(out=out, in_=in_), reads, writes)

    def vcopy(out, in_, reads, writes, eng="dve"):
        kb.op(eng, lambda e: e.tensor_copy(out, in_), reads, writes)

    def tt(out, in0, in1, op, reads, writes, eng="dve"):
        kb.op(eng, lambda e: e.tensor_tensor(out=out, in0=in0, in1=in1, op=op), reads, writes)

    def ts(out, in0, s1, op0, reads, writes, s2=None, op1=None, eng="dve"):
        if op1 is None:
            kb.op(eng, lambda e: e.tensor_scalar(out=out, in0=in0, scalar1=s1, scalar2=None, op0=op0), reads, writes)
        else:
            kb.op(eng, lambda e: e.tensor_scalar(out=out, in0=in0, scalar1=s1, scalar2=s2, op0=op0, op1=op1), reads, writes)

    def stt(out, in0, scalar, in1, op0, op1, reads, writes):
        kb.op("dve", lambda e: e.scalar_tensor_tensor(out=out, in0=in0, scalar=scalar, in1=in1, op0=op0, op1=op1), reads, writes)

    def memset(ap, val, writes, eng="dve"):
        kb.op(eng, lambda e: e.memset(ap, val), (), writes)

    with contextlib.ExitStack() as es:
        es.enter_context(ncd)

        uniq = [0]

        def S(name, shape, dt=F32, stack=es):
            uniq[0] += 1
            return stack.enter_context(nc.sbuf_tensor(f"{name}_{uniq[0]}", list(shape), dt))

        pst = [es.enter_context(nc.psum_tensor(f"ps{i}", [128, 1024], F32)) for i in range(4)]
        bank_rr = [0]
        nrot = [8]

        def bank():
            i = bank_rr[0]
            bank_rr[0] = (i + 1) % nrot[0]
            return pst[i // 2][:, (i % 2) * 512:(i % 2) * 512 + 512], f"pb{i}"

        def bank2():
            i = bank_rr[0]
            if i % 2:
                i = (i + 1) % nrot[0]
            bank_rr[0] = (i + 2) % nrot[0]
            return pst[i // 2][:, :], [f"pb{i}", f"pb{i + 1}"]

        ident = S("ident", [128, 128]); identb = S("identb", [128, 128], BF16)
        xtok = [S("xtokA", [128, 4, D]), S("xtokB", [128, 4, D])]
        xT = S("xT", [128, 8, TILE], BF16)
        RING = 4
        wr = [S(f"wr{i}", [128, 2048], BF16) for i in range(RING)]
        kTd = S("kTd", [128, 2, 2, 128 + TILE], BF16)
        vtok = S("vtok", [128, 2, 5, 128], BF16)
        merged = S("merged", [128, 8, TILE])
        skb = S("skb", [128, 2, 8])
        COS = S("COS", [128, 2, 16, SJ + 1]); SIN = S("SIN", [128, 2, 16, SJ + 1])
        MAGT = S("MAGT", [128, 2, 16, SJ])
        WBre = S("WBre", [128, 2, 4, 2, 128], BF16); WBim = S("WBim", [128, 2, 4, 2, 128], BF16)
        WCre = S("WCre", [128, 2, 16, 128], BF16); WCim = S("WCim", [128, 2, 16, 128], BF16)
        AbR = S("AbR", [128, 2, 16]); AbI = S("AbI", [128, 2, 16])
        CaR = S("CaR", [128, 2, 16]); CaI = S("CaI", [128, 2, 16])
        dS5 = S("dS5", [128, 2, 4])
        Gre = S("Gre", [128, 2, 16]); Gim = S("Gim", [128, 2, 16])
        WA = S("WA", [128, 2, 4, 128], BF16); WX = S("WX", [128, 2, 4, 128], BF16)
        cw = S("cw", [128, 2, 4, 4]); cb = S("cb", [128, 2, 4]); ba = S("ba", [128, 2, 4]); bx = S("bx", [128, 2, 4])
        c1 = S("c1", [128, 2, 4]); c2 = S("c2", [128, 2, 4])
        hl = S("hl", [128, 2, 4]); chist = S("chist", [128, 2, 4, 3])
        epsc = S("epsc", [128, 1])

        kb.op("pool", lambda e: e.memset(ident[:], 0.0), (), ["ident"])
        kb.op("pool", lambda e: e.affine_select(out=ident[:], in_=ident[:], pattern=[[-1, 128]], compare_op=ALU.not_equal,
                                                fill=1.0, base=0, channel_multiplier=1), ["ident"], ["ident"])
        vcopy(identb[:], ident[:], ["ident"], ["identb"])
        memset(epsc[:], LN_EPS, ["epsc"])
        memset(Gre[:], 0.0, ["G"]); memset(Gim[:], 0.0, ["G"])
        memset(hl[:], 0.0, ["hl"]); memset(chist[:], 0.0, ["chist"])
        kb.dma("sp", skb[:].rearrange("p l h -> p (l h)"), sinks.rearrange("l h -> (l h)").partition_broadcast(128), (), ["skb"])

        class _Stop(Exception):
            pass
        STAGE = float(os.environ.get("KSTAGE", "99"))

        def stage(n):
            if STAGE < n:
                raise _Stop()
        try:
          with contextlib.ExitStack() as ss:
            def SS(name, shape, dt=F32):
                return S(name, shape, dt, stack=ss)
            stage(1)
            rb = SS("rb", [32, 8]); ohs = SS("ohs", [32, 383]); ngs = SS("ngs", [8, 383]); rvs = SS("rvs", [8, 383])
            kb.dma("sp", rb[:], rel_bias, (), ["rb"])
            kb.dma("sp", ohs[:], oh2, (), ["ohs"])
            kb.dma("sp", ngs[:], negmask, (), ["ngs"])
            pb, pk = bank()
            mm(pb[0:8, 0:383], rb[:], ohs[:], True, True, ["rb", "ohs"], [pk])
            tt(rvs[:], pb[0:8, 0:383], ngs[:], ALU.add, [pk, "ngs"], ["rvs"])
            kb.dma("sp", rv, rvs[:], ["rvs"], ["rv_dram"])
            stage(1.5)
            Tq = SS("Tq", [128, 2048]); Tf = SS("Tf", [128, 2048]); Jm = SS("Jm", [128, 128])
            kb.dma("sp", Tq[:].rearrange("p (h s) -> p h s", h=8), bass.AP(rv_t, 0, [[1, 128], [383, 8], [1, 256]]), ["rv_dram"], ["Tq"])
            kb.op("pool", lambda e: e.memset(Jm[:], 0.0), (), ["Jm"])
            kb.op("pool", lambda e: e.affine_select(out=Jm[:], in_=Jm[:], pattern=[[1, 128]], compare_op=ALU.not_equal,
                                                    fill=1.0, base=-127, channel_multiplier=1), ["Jm"], ["Jm"])
            for c4 in range(4):
                pb, pk = bank()
                mm(pb[:, :], Jm[:], Tq[:, 512 * c4:512 * c4 + 512], True, True, ["Jm", "Tq"], [pk])
                acopy(Tf[:, 512 * c4:512 * c4 + 512], pb[:, :], [pk], ["Tf"])
            kb.dma("sp", tb_scr, Tf[:], ["Tf"], ["tb_dram"])

            stage(2)
            jj_i = SS("jj_i", [128, SJ + 1], I32); jj = SS("jj", [128, SJ + 1])
            kb.op("pool", lambda e: e.iota(jj_i[:], pattern=[[1, SJ + 1]], base=0, channel_multiplier=0), (), ["jj_i"])
            vcopy(jj[:], jj_i[:], ["jj_i"], ["jj"])
            lre = SS("lre", [128, 2, 16]); lim = SS("lim", [128, 2, 16]); lst = SS("lst", [128, 2, 16])
            Bre = SS("Bre", [128, 16, 16]); Bim = SS("Bim", [128, 16, 16]); Cn = SS("Cn", [128, 4, 64]); Cn2 = SS("Cn2", [128, 4, 128])
            pmk = SS("pmk", [128, 2])
            kb.dma("sp", pmk[:], pmask, (), ["pmk"])
            BBr = SS("BBr", [128, 16, 16]); BBi = SS("BBi", [128, 16, 16]); tB = SS("tB", [128, 16, 16])
            ZB = SS("ZB", [128, 16, 128])
            phi = SS("phi", [128, 16, SJ + 1]); rr = SS("rr", [128, 16, SJ + 1]); ri = SS("ri", [128, 16, SJ + 1], I32)
            rf = SS("rf", [128, 16, SJ + 1]); gg = SS("gg", [128, 16, SJ + 1])
            sm = [SS(f"sm{i}", [128, 16]) for i in range(12)]
            wst = SS("wst", [128, 4, 128]); lamt = SS("lamt", [128, 4])
            for l in range(2):
                kb.dma("sp", lre[:, l, :], lam_re[l].rearrange("(k p) -> p k", p=128), (), ["lre"])
                kb.dma("sp", lim[:, l, :], lam_im[l].rearrange("(k p) -> p k", p=128), (), ["lim"])
                for g2 in range(2):
                    kb.dma("sp", lst[64 * g2:64 * g2 + 64, l, :],
                           log_step[l].rearrange("(k t) -> t k", t=2)[g2].partition_broadcast(64), (), ["lst"])
                kb.dma("sp", dS5[:, l, :], ssm_d[l].rearrange("(q p) -> p q", p=128), (), ["dS5"])
            stage(2.5)
            for l in range(2):
                step, lr, th, mag, den, a1, t0, t1, fre, fim, rden, t2 = sm
                act(step[:], lst[:, l, :], AF.Exp, ["lst"], ["sm0"])
                ts(lr[:], lre[:, l, :], -1e-4, ALU.min, ["lre"], ["sm1"])
                tt(th[:], lim[:, l, :], step[:], ALU.mult, ["lim", "sm0"], ["sm2"])
                tt(t0[:], lr[:], step[:], ALU.mult, ["sm1", "sm0"], ["sm6"])
                act(mag[:], t0[:], AF.Exp, ["sm6"], ["sm3"])
                tt(phi[:], th[:].unsqueeze(2).to_broadcast([128, 16, SJ + 1]), jj[:].unsqueeze(1).to_broadcast([128, 16, SJ + 1]),
                   ALU.mult, ["sm2", "jj"], ["phi"])
                for which, TAB in ((0, SIN), (1, COS)):
                    ts(rr[:], phi[:], 1.0 / (2 * math.pi), ALU.mult, ["phi"], ["rr"], s2=0.25 * which, op1=ALU.add)
                    vcopy(ri[:], rr[:], ["rr"], ["ri"])
                    vcopy(rf[:], ri[:], ["ri"], ["rf"])
                    tt(rr[:], rr[:], rf[:], ALU.subtract, ["rr", "rf"], ["rr"])
                    ts(gg[:], rr[:], 0.5, ALU.is_gt, ["rr"], ["gg"])
                    tt(rr[:], rr[:], gg[:], ALU.subtract, ["rr", "gg"], ["rr"])
                    ts(gg[:], rr[:], -0.5, ALU.is_lt, ["rr"], ["gg"])
                    tt(rr[:], rr[:], gg[:], ALU.add, ["rr", "gg"], ["rr"])
                    ts(rr[:], rr[:], 0.5, ALU.min, ["rr"], ["rr"], s2=-0.5, op1=ALU.max)
                    act(TAB[:, l, :, :], rr[:], AF.Sin, ["rr"], ["COS" if which else "SIN"], scale=6.283185)
                tt(AbR[:, l, :], mag[:], COS[:, l, :, 1], ALU.mult, ["sm3", "COS"], ["Ab"])
                tt(AbI[:, l, :], mag[:], SIN[:, l, :, 1], ALU.mult, ["sm3", "SIN"], ["Ab"])
                tt(CaR[:, l, :], mag[:], COS[:, l, :, SJ], ALU.mult, ["sm3", "COS"], ["Ca"])
                tt(CaI[:, l, :], mag[:], SIN[:, l, :, SJ], ALU.mult, ["sm3", "SIN"], ["Ca"])
                vcopy(MAGT[:, l, :, :], mag[:].unsqueeze(2).to_broadcast([128, 16, SJ]), ["sm3"], ["MAGT"])
                memset(MAGT[:, l, :, 0:1], 0.0, ["MAGT"])
                li_ = lim[:, l, :]
                tt(den[:], lr[:], lr[:], ALU.mult, ["sm1"], ["sm4"])
                tt(t1[:], li_, li_, ALU.mult, ["lim"], ["sm7"])
                tt(den[:], den[:], t1[:], ALU.add, ["sm4", "sm7"], ["sm4"])
                kb.op("dve", lambda e, rden=rden, den=den: e.reciprocal(rden[:], den[:]), ["sm4"], ["sm10"])
                ts(a1[:], AbR[:, l, :], -1.0, ALU.add, ["Ab"], ["sm5"])
                tt(t0[:], a1[:], lr[:], ALU.mult, ["sm5", "sm1"], ["sm6"])
                tt(t1[:], AbI[:, l, :], li_, ALU.mult, ["Ab", "lim"], ["sm7"])
                tt(fre[:], t0[:], t1[:], ALU.add, ["sm6", "sm7"], ["sm8"])
                tt(fre[:], fre[:], rden[:], ALU.mult, ["sm8", "sm10"], ["sm8"])
                tt(t0[:], AbI[:, l, :], lr[:], ALU.mult, ["Ab", "sm1"], ["sm6"])
                tt(t1[:], a1[:], li_, ALU.mult, ["sm5", "lim"], ["sm7"])
                tt(fim[:], t0[:], t1[:], ALU.subtract, ["sm6", "sm7"], ["sm9"])
                tt(fim[:], fim[:], rden[:], ALU.mult, ["sm9", "sm10"], ["sm9"])
                stage(3)
                kb.dma("sp", Bre[:], b_re[l].rearrange("(k t) p c -> (t p) k c", t=2), ["Bre"], ["Bre"])
                kb.dma("sp", Bim[:], b_im[l].rearrange("(k t) p c -> (t p) k c", t=2), ["Bim"], ["Bim"])
                frb = fre[:].unsqueeze(2).to_broadcast([128, 16, 16]); fib = fim[:].unsqueeze(2).to_broadcast([128, 16, 16])
                tt(BBr[:], Bre[:], frb, ALU.mult, ["Bre", "sm8"], ["BBr"])
                tt(tB[:], Bim[:], fib, ALU.mult, ["Bim", "sm9"], ["tB"])
                tt(BBr[:], BBr[:], tB[:], ALU.subtract, ["BBr", "tB"], ["BBr"])
                tt(BBi[:], Bim[:], frb, ALU.mult, ["Bim", "sm8"], ["BBi"])
                tt(tB[:], Bre[:], fib, ALU.mult, ["Bre", "sm9"], ["tB"])
                tt(BBi[:], BBi[:], tB[:], ALU.add, ["BBi", "tB"], ["BBi"])
                for src, skey_, dst, nm in ((BBr, "BBr", WBre, "WBre"), (BBi, "BBi", WBim, "WBim")):
                    memset(ZB[:], 0.0, ["ZB"])
                    for km in range(4):
                        for g2 in range(2):
                            c0 = 16 * (2 * km + g2)
                            vcopy(ZB[64 * g2:64 * g2 + 64, km::4, c0:c0 + 16], src[64 * g2:64 * g2 + 64, km::4, :], [skey_], ["ZB"])
                    for k4 in range(4):
                        pb, pk = bank()
                        for kq in range(4):
                            k = 4 * k4 + kq
                            tr(pb[:, kq * 128:(kq + 1) * 128], ZB[:, k, :], ident[:], ["ZB", "ident"], [pk])
                        for kq in range(4):
                            h_, kk_ = kq // 2, kq % 2
                            acopy(dst[64 * h_:64 * h_ + 64, l, k4, kk_, :], pb[64 * h_:64 * h_ + 64, kq * 128:(kq + 1) * 128], [pk], [nm])
                for csrc, sgn, wdst, nm in ((c_re, 1.0, WCre, "WCre"), (c_im, -1.0, WCim, "WCim")):
                    memset(wdst[:, l, :, :], 0.0, [nm])
                    kb.dma("sp", Cn[:], csrc[l].rearrange("(a g) c p -> (g c) a p", g=8), ["Cn"], ["Cn"])
                    for t_ in range(2):
                        ts(Cn2[:, :, 64 * t_:64 * t_ + 64], Cn[:], pmk[:, t_:t_ + 1], ALU.mult, ["Cn", "pmk"], ["Cn2"], s2=sgn, op1=ALU.mult)
                    pb, pk = bank()
                    for a in range(4):
                        tr(pb[:, a * 128:(a + 1) * 128], Cn2[:, a, :], ident[:], ["Cn2", "ident"], [pk])
                    pbv = pb.rearrange("p (a b) -> p a b", a=4)
                    for km in range(4):
                        acopy(wdst[:, l, km::4, 32 * km:32 * km + 32], pbv[:, :, 32 * km:32 * km + 32], [pk], [nm])
                stage(4)
                for j in range(4):
                    kb.dma("sp", cw[:, l, :, j], conv_w[l][j].rearrange("(q p) -> p q", p=128), (), ["cw"])
                kb.dma("sp", cb[:, l, :], conv_b[l].rearrange("(q p) -> p q", p=128), (), ["cb"])
                kb.dma("sp", ba[:, l, :], lru_b_a[l].rearrange("(q p) -> p q", p=128), (), ["ba"])
                kb.dma("sp", bx[:, l, :], lru_b_x[l].rearrange("(q p) -> p q", p=128), (), ["bx"])
                kb.dma("sp", lamt[:], lru_lam[l].rearrange("(q p) -> p q", p=128), ["lamt"], ["lamt"])
                act(lamt[:], lamt[:], AF.Exp, ["lamt"], ["lamt"], scale=-1.0)
                act(lamt[:], lamt[:], AF.Ln, ["lamt"], ["lamt"], bias=1.0)
                ts(c1[:, l, :], lamt[:], -8.0, ALU.mult, ["lamt"], ["c1"])
                ts(c2[:, l, :], lamt[:], -16.0, ALU.mult, ["lamt"], ["c2"])
                for wsrc, wdst, nm in ((lru_w_a, WA, "WA"), (lru_w_x, WX, "WX")):
                    memset(wst[:], 0.0, ["wst"])
                    for hh in range(2):
                        kb.dma("sp", wst[64 * hh:64 * hh + 64, :, 64 * hh:64 * hh + 64],
                               wsrc[l].rearrange("(q t) d e -> t d q e", t=2)[hh], ["wst"], ["wst"])
                    vcopy(wdst[:, l, :, :], wst[:], ["wst"], [nm])
        except _Stop:
            pass
        kb.barrier()

        w_in_v = [w_in[l].rearrange("(k p) c -> p k c", p=128) for l in range(2)]
        w_out_v = [w_out[l].rearrange("(k p) c -> p k c", p=128) for l in range(2)]
        w_br_v = [[w_br[b][l].rearrange("(k p) c -> p k c", p=128) for l in range(2)] for b in range(3)]
        w_glu_v = [w_glu[l].rearrange("(k p) c -> p k c", p=128) for l in range(2)]

        def v8(slot):
            return slot[:].rearrange("p (k c) -> p k c", k=8)

        def v4(slot):
            return slot[:].rearrange("p (k c) -> p k c", k=4)

        def win_spec(l, c0):
            return [(lambda s: v8(s), w_in_v[l][:, :, c0:c0 + 256])]

        def pass_specs(l):
            sp = []
            sp += [("q0", win_spec(l, 0)), ("q1", win_spec(l, 256)), ("kv", win_spec(l, 512))]
            sp += [("kd", [(lambda s, j=j, h=h: v8(s)[:, :, 128 * h + 64 * j:128 * h + 64 * j + 64], w_in_v[l][:, :, 512 + 64 * h:576 + 64 * h])
                           for h in range(2) for j in range(2)])]
            sp += [("ga0", win_spec(l, 768)), ("ga1", win_spec(l, 1024))]
            for b, (c_in, c_gm) in enumerate(((1280, 3328), (2304, 4352), (0, 5376))):
                if b == 1:
                    sp += [("u0", win_spec(l, 1280)), ("u1", win_spec(l, 1536)), ("gb0", win_spec(l, 1792)), ("gb1", win_spec(l, 2048))]
                    sp += [("xc0", win_spec(l, 2304)), ("xc1", win_spec(l, 2560)), ("gc0", win_spec(l, 2816)), ("gc1", win_spec(l, 3072))]
                    sp += [("glu", [(lambda s: v4(s), w_glu_v[l])])]
                g0 = 3328 + 1024 * b
                for h in range(2):
                    sp += [(f"br{b}{h}", [(lambda s: v4(s), w_br_v[b][l][:, :, 512 * h:512 * h + 512])])]
                    sp += [(f"gm{b}{i}", win_spec(l, g0 + 256 * i)) for i in (2 * h, 2 * h + 1)]
            sp += [(f"wo{i}", [(lambda s: v8(s), w_out_v[l][:, :, 256 * i:256 * i + 256])]) for i in range(4)]
            return sp

        all_passes = passes if passes is not None else [(t, l) for t in range(NPT + 1) for l in range(2)]
        NSLOT = len(pass_specs(0))
        wc_t = nc.dram_tensor("wcache", [2, NSLOT, 128, 2048], BF16, kind="Internal")
        wc = wc_t.ap()
        for l in sorted({l for (_, l) in all_passes}):
            for i, (name, pieces) in enumerate(pass_specs(l)):
                for dstf, src in pieces:
                    kb.dma("pool", dstf(wc[l, i]), src, (), [f"wc{l}_{i}"])
        wq = []
        for (t, l) in all_passes:
            wq += [(name, l, i) for i, (name, _) in enumerate(pass_specs(l))]
        wstate = {"issued": 0, "used": 0}

        def wissue():
            n = wstate["issued"]
            name, l_, i_ = wq[n]
            kb.dma("sp", wr[n % RING][:], wc[l_, i_], [f"wc{l_}_{i_}"], [f"wr{n % RING}"])
            wstate["issued"] = n + 1

        def wget(name, ahead=RING):
            n = wstate["used"]
            assert wq[n][0] == name, (wq[n][0], name)
            while wstate["issued"] < min(len(wq), n + ahead):
                wissue()
            wstate["used"] = n + 1
            return wr[n % RING], f"wr{n % RING}"

        def run_pass(t, l, xin, xout):
            sample = (t == NPT)
            ntok = NTS if sample else TILE
            nsub = 1 if sample else 4
            R = 64 if sample else 128
            last_prompt = (t == NPT - 1)
            xi = xtok[xin]; xo = xtok[xout]
            xik = f"xtok{xin}"; xok = f"xtok{xout}"

            for s in range(nsub):
                for k4 in range(2):
                    pb, pk = bank()
                    for kk in range(4):
                        kc = 4 * k4 + kk
                        tr(pb[:, kk * 128:kk * 128 + R], xi[0:R, s, kc * 128:(kc + 1) * 128], ident[0:R, 0:R], [xik, "ident"], [pk])
                    acopy(xT[:, 4 * k4:4 * k4 + 4, s * 128:s * 128 + R], pb.rearrange("p (a b) -> p a b", a=4)[:, :, 0:R], [pk], ["xT"])

            def proj_fm(slot, skey, cchunk, nk, rhs_of_kc, rkeys):
                pb, pk = bank()
                view = v8(slot) if nk == 8 else v4(slot)
                for kc in range(nk):
                    mm(pb[:, 0:ntok], view[:, kc, cchunk * 128:(cchunk + 1) * 128], rhs_of_kc(kc), kc == 0, kc == nk - 1,
                       [skey] + rkeys, [pk])
                return pb, pk

            xTk = lambda kc: xT[:, kc, 0:ntok]

            with contextlib.ExitStack() as pa:
                pa_keys = []

                def PA(name, shape, dt=F32):
                    pa_keys.append(name)
                    return S(name, shape, dt, stack=pa)
                gatedT = PA("gatedT", [128, 4, TILE], BF16)
                mTh = {}
                W7 = TS + 3
                gcT = PA("gcT", [128, 4, ntok], BF16)
                cbuf = PA("cbuf", [128, 4, NS, W7]) if sample else PA("cbuf", [128, 4, TILE + 3])
                xco = PA("xco", [NTS, 512]) if sample else PA("xco", [4, 512])
                lru_st = {}

                def xc_unit(q):
                    if q % 2 == 0:
                        lru_st["slot"] = wget(f"xc{q // 2}")
                        slot, skey = lru_st["slot"]
                        if sample or last_prompt:
                            nr = NTS if sample else 3
                            cstart = 0 if sample else TILE - 3
                            pbx, pkx = bank()
                            for kc in range(8):
                                mm(pbx[0:nr, 0:256], xT[:, kc, cstart:cstart + nr], v8(slot)[:, kc, :], kc == 0, kc == 7, [skey, "xT"], [pkx])
                            acopy(xco[0:nr, 256 * (q // 2):256 * (q // 2) + 256], pbx[0:nr, 0:256], [pkx], ["xco"])
                    slot, skey = lru_st["slot"]
                    pb, pk = proj_fm(slot, skey, q % 2, 8, xTk, ["xT"])
                    if sample:
                        acopy(cbuf[:, q, :, 3:W7], pb[:, 0:ntok].rearrange("p (s t) -> p s t", t=TS), [pk], ["cbuf"])
                    else:
                        acopy(cbuf[:, q, 0:3], chist[:, l, q, :], ["chist"], ["cbuf"])
                        acopy(cbuf[:, q, 3:3 + ntok], pb[:, 0:ntok], [pk], ["cbuf"])
                        acopy(chist[:, l, q, :], cbuf[:, q, ntok:ntok + 3], ["cbuf"], ["chist"])

                def gc_unit(q):
                    if q % 2 == 0:
                        lru_st["slot"] = wget(f"gc{q // 2}")
                    slot, skey = lru_st["slot"]
                    pb, pk = proj_fm(slot, skey, q % 2, 8, xTk, ["xT"])
                    act(gcT[:, q, 0:ntok], pb[:, 0:ntok], AF.Silu, [pk], ["gcT"])
                lru_units = [lambda q=q: xc_unit(q) for q in range(4)] + [lambda q=q: gc_unit(q) for q in range(4)]

                def branch_out(b):
                    with contextlib.ExitStack() as bo:
                        bo_keys = ["gms0", "gms1", "tmpm"]
                        gms = [S("gms0", [128, ntok], F32, stack=bo), S("gms1", [128, ntok], F32, stack=bo)]
                        tmpm = S("tmpm", [128, ntok], F32, stack=bo)
                        _branch_out(b, gms, tmpm)
                        kb.retire(bo_keys)

                def _branch_out(b, gms, tmpm):
                    mT = mTh.get("mT")
                    ypb = []
                    for fo in range(8):
                        if fo % 4 == 0:
                            slot, skey = wget(f"br{b}{fo // 4}")
                        ypb.append(proj_fm(slot, skey, fo % 4, 4, lambda c: gatedT[:, c, 0:ntok], ["gatedT"]))
                        if fo % 4 == 3:
                            for f2 in range(fo - 3, fo + 1):
                                if f2 % 2 == 0:
                                    gslot, gkey = wget(f"gm{b}{f2 // 2}")
                                gpb, gpk = proj_fm(gslot, gkey, f2 % 2, 8, xTk, ["xT"])
                                g = gms[f2 % 2]; gk = f"gms{f2 % 2}"
                                act(g[:, 0:ntok], gpb[:, 0:ntok], AF.Sigmoid, [gpk], [gk])
                                yb, yk = ypb[f2]
                                if b == 0:
                                    tt(merged[:, f2, 0:ntok], yb[:, 0:ntok], g[:, 0:ntok], ALU.mult, [yk, gk], ["merged"])
                                elif b == 1:
                                    tt(tmpm[:, 0:ntok], yb[:, 0:ntok], g[:, 0:ntok], ALU.mult, [yk, gk], ["tmpm"])
                                    tt(merged[:, f2, 0:ntok], merged[:, f2, 0:ntok], tmpm[:, 0:ntok], ALU.add, ["merged", "tmpm"], ["merged"])
                                else:
                                    tt(tmpm[:, 0:ntok], yb[:, 0:ntok], g[:, 0:ntok], ALU.mult, [yk, gk], ["tmpm"])
                                    tt(mT[:, f2, 0:ntok], merged[:, f2, 0:ntok], tmpm[:, 0:ntok], ALU.add, ["merged", "tmpm"], ["mT"])

                with contextlib.ExitStack() as ph:
                    ph_keys = []

                    def PH(name, shape, dt=F32, ph=ph, ph_keys=ph_keys):
                        ph_keys.append(name)
                        return S(name, shape, dt, stack=ph)
                    qT = PH("qT", [128, 4, TILE], BF16)
                    Tb = PH("Tb", [128, 8, 256])
                    kb.dma("sp", Tb[:].rearrange("p h s -> p (h s)"), tb_scr, ["tb_dram"], ["Tb"])
                    gatok = PH("gatok", [128, 4, 512], BF16)
                    kvo = PH("kvo", [128, 256])
                    KW = 132 if sample else 256
                    sc2 = [PH(f"sc{i}", [128, 4, KW]) for i in range(2)]; Pm2 = [PH(f"Pm{i}", [128, 4, KW], BF16) for i in range(2)]
                    PT = PH("PT", [128, 8, 128], BF16)
                    mx2 = [PH(f"mx{i}", [128, 4]) for i in range(2)]; ngm2 = [PH(f"ngm{i}", [128, 4]) for i in range(2)]
                    rs2 = [PH(f"rs{i}", [128, 4]) for i in range(2)]; esk2 = [PH(f"esk{i}", [128, 4]) for i in range(2)]
                    rinv2 = [PH(f"rinv{i}", [128, 4]) for i in range(2)]; att = PH("att", [128, 256])
                    gA2 = [PH(f"gA{i}", [128, 512], BF16) for i in range(2)]
                    if sample:
                        Ks = PH("Ks", [128, NS, 128], BF16); Kd = PH("Kd", [128, 2, 2, 64], BF16)
                        KTs = PH("KTs", [128, NS, 2, 132], BF16); Vs = PH("Vs", [128, NS, 128], BF16)
                        vnew = PH("vnew", [4, NS, 128], BF16)
                        kTn = PH("kTn", [128, 2, NTS], BF16); gas2 = [PH(f"gas{i}", [4, 512], BF16) for i in range(2)]

                    for c2_ in range(2):
                        slot, skey = wget(f"q{c2_}")
                        for cc in range(2):
                            pb, pk = proj_fm(slot, skey, cc, 8, xTk, ["xT"])
                            acopy(qT[:, 2 * c2_ + cc, 0:ntok], pb[:, 0:ntok], [pk], ["qT"])
                    slot, skey = wget("kv")
                    for s in range(nsub):
                        pb, pk = bank()
                        for kc in range(8):
                            mm(pb[0:R, 0:256], xT[:, kc, s * 128:s * 128 + R], v8(slot)[:, kc, :], kc == 0, kc == 7, [skey, "xT"], [pk])
                        if not sample:
                            acopy(vtok[:, l, 1 + s, :], pb[:, 128:256], [pk], ["vtok"])
                            if last_prompt and s == 3:
                                acopy(kvo[:], pb[:, 0:256], [pk], ["kvo"])
                                kb.dma("sp", nkp[l], kvo[:, 0:128], ["kvo"], (), final=True)
                                kb.dma("sp", nvp[l], kvo[:, 128:256], ["kvo"], (), final=True)
                        else:
                            vcopy(kvo[0:R, :], pb[0:R, 0:256], [pk], ["kvo"])
                            for tt_ in range(TS):
                                kb.dma("sp", nks[l][:, 124 + tt_, :], kvo[tt_:NTS:TS, 0:128], ["kvo"], (), final=True)
                                kb.dma("sp", nvs[l][:, 124 + tt_, :], kvo[tt_:NTS:TS, 128:256], ["kvo"], (), final=True)
                            kb.dma("sp", nks[l][:, 0:124, :], ck[l][:, 4:128, :], (), (), final=True)
                            kb.dma("sp", nvs[l][:, 0:124, :], cv[l][:, 4:128, :], (), (), final=True)
                            for i in range(NS):
                                pb2, pk2 = bank()
                                for kc in range(8):
                                    mm(pb2[0:TS, 0:128], xT[:, kc, TS * i:TS * i + TS], v8(slot)[:, kc, 128:256], kc == 0, kc == 7,
                                       [skey, "xT"], [pk2])
                                acopy(vnew[:, i, :], pb2[0:TS, 0:128], [pk2], ["vnew"])
                    slot, skey = wget("kd")
                    for kvh in range(2):
                        pb, pk = proj_fm(slot, skey, kvh, 8, xTk, ["xT"])
                        if not sample:
                            acopy(kTd[:, l, kvh, 128:128 + ntok], pb[:, 0:ntok], [pk], ["kTd"])
                        else:
                            acopy(kTn[:, kvh, :], pb[:, 0:ntok], [pk], ["kTn"])
                    for h2 in range(2):
                        slot, skey = wget(f"ga{h2}")
                        for s in range(nsub):
                            pb, pk = bank()
                            for kc in range(8):
                                mm(pb[0:R, 0:256], xT[:, kc, s * 128:s * 128 + R], v8(slot)[:, kc, :], kc == 0, kc == 7, [skey, "xT"], [pk])
                            act(gatok[0:R, s, h2 * 256:h2 * 256 + 256], pb[0:R, 0:256], AF.Silu, [pk], ["gatok"])

                    def attend_all(blocks):
                        units = [(bi, hf) for bi in range(len(blocks)) for hf in range(2)]
                        stt_ = {}

                        def S1(u):
                            bi, hf = units[u]
                            B = blocks[bi]
                            nq, segs = B["nq"], B["segs"]
                            if hf == 0 and B.get("prep"):
                                B["prep"]()
                            pb2, pks = bank2()
                            scv = pb2.rearrange("p (h s) -> p h s", h=4)
                            for hh in range(4):
                                h = 4 * hf + hh
                                cch, half = h // 2, h % 2
                                off = 0
                                for (kfn, vap, n) in segs:
                                    mm(scv[0:nq, half * 2 + hh // 2, off:off + n], qT[64 * half:64 * half + 64, cch, B["col"]:B["col"] + nq],
                                       kfn(hf)[64 * half:64 * half + 64, :], True, True, ["qT", "kTd", "KTs"], [pks[half]])
                                    off += n
                            stt_[u] = (scv, pks)

                        def S2(u, part):
                            bi, hf = units[u]
                            B = blocks[bi]
                            nq, segs, nkeys, tb_c0 = B["nq"], B["segs"], B["nkeys"], B["tb_c0"]
                            par = u % 2
                            sc_, Pm_, mx_, ngm_, rs_, esk_, rinv_ = sc2[par], Pm2[par], mx2[par], ngm2[par], rs2[par], esk2[par], rinv2[par]
                            ksc, kPm, kmx, kngm, krs, kesk, krinv = (f"{n_}{par}" for n_ in ("sc", "Pm", "mx", "ngm", "rs", "esk", "rinv"))
                            gA_ = gA2[bi % 2]; kgA = f"gA{bi % 2}"
                            if part == "a":
                                scv, pks = stt_.pop(u)
                                for half in range(2):
                                    stt(sc_[0:nq, half:4:2, 0:nkeys], scv[0:nq, 2 * half:2 * half + 2, 0:nkeys], 0.125,
                                        Tb[0:nq, 4 * hf + half:4 * hf + 4:2, tb_c0:tb_c0 + nkeys],
                                        ALU.mult, ALU.add, [pks[half], "Tb"], [ksc])
                                kb.op("dve", lambda e: e.tensor_reduce(out=mx_[0:nq, :], in_=sc_[0:nq, :, 0:nkeys], axis=AX.X, op=ALU.max),
                                      [ksc], [kmx])
                                tt(mx_[0:nq, :], mx_[0:nq, :], skb[0:nq, l, 4 * hf:4 * hf + 4], ALU.max, [kmx, "skb"], [kmx])
                                ts(ngm_[0:nq, :], mx_[0:nq, :], -1.0, ALU.mult, [kmx], [kngm])
                                tt(esk_[0:nq, :], skb[0:nq, l, 4 * hf:4 * hf + 4], mx_[0:nq, :], ALU.subtract, ["skb", kmx], [kesk])
                            elif part == "b":
                                for hh in range(4):
                                    act(Pm_[0:nq, hh, 0:nkeys], sc_[0:nq, hh, 0:nkeys], AF.Exp, [ksc, kngm], [kPm, krs],
                                        bias=ngm_[0:nq, hh:hh + 1], scale=1.0, accum_out=rs_[0:nq, hh:hh + 1])
                                act(esk_[0:nq, :], esk_[0:nq, :], AF.Exp, [kesk], [kesk])
                            else:
                                tt(rinv_[0:nq, :], rs_[0:nq, :], esk_[0:nq, :], ALU.add, [krs, kesk], [krinv])
                                kb.op("dve", lambda e: e.reciprocal(rinv_[0:nq, :], rinv_[0:nq, :]), [krinv], [krinv])

                        def S3(u, part):
                            bi, hf = units[u]
                            B = blocks[bi]
                            nq, segs = B["nq"], B["segs"]
                            par = u % 2
                            Pm_, rinv_ = Pm2[par], rinv2[par]
                            kPm, krinv = f"Pm{par}", f"rinv{par}"
                            gA_ = gA2[bi % 2]; kgA = f"gA{bi % 2}"
                            nseg = len(segs)
                            if part == "a":
                                pbt, pkt = bank()
                                ptv = pbt.bitcast(BF16).rearrange("p (a b) -> p a b", a=8)
                                for hh in range(4):
                                    off = 0
                                    for si, (kfn, vap, n) in enumerate(segs):
                                        tr(ptv[0:n, hh * nseg + si, 0:nq], Pm_[0:nq, hh, off:off + n], identb[0:nq, 0:nq], [kPm, "identb"], [pkt])
                                        off += n
                                for si, (kfn, vap, n) in enumerate(segs):
                                    acopy(PT[0:n, si:4 * nseg:nseg, 0:nq], ptv[0:n, si:4 * nseg:nseg, 0:nq], [pkt], ["PT"])
                                return
                            pbo, pko = bank()
                            for hh in range(4):
                                for si, (kfn, vap, n) in enumerate(segs):
                                    mm(pbo[0:nq, hh * 64:hh * 64 + 64], PT[0:n, hh * nseg + si, 0:nq], vap[0:n, 64 * hf:64 * hf + 64],
                                       si == 0, si == nseg - 1, ["PT", "vtok", "Vs", "vnew"], [pko])
                            tt(att[0:nq, :].rearrange("p (h d) -> p h d", h=4), pbo[0:nq, 0:256].rearrange("p (h d) -> p h d", h=4),
                               rinv_[0:nq, :].unsqueeze(2).to_broadcast([nq, 4, 64]), ALU.mult, [pko, krinv], ["att"])
                            tt(gA_[0:nq, 256 * hf:256 * hf + 256], att[0:nq, :], B["ga"][:, 256 * hf:256 * hf + 256], ALU.mult,
                               ["att", B["ga_key"]], [kgA])
                            if hf == 1:
                                pbt, pkt = bank()
                                ptv = pbt.bitcast(BF16).rearrange("p (a b) -> p a b", a=8)
                                for cch in range(4):
                                    tr(ptv[:, cch, 0:nq], gA_[0:nq, cch * 128:(cch + 1) * 128], identb[0:nq, 0:nq], [kgA, "identb"], [pkt])
                                acopy(gatedT[:, :, B["col"]:B["col"] + nq], ptv[:, 0:4, 0:nq], [pkt], ["gatedT"])

                        nu = len(units)
                        S1(0); S2(0, "a"); S2(0, "b"); S2(0, "c")
                        if nu > 1:
                            S1(1)
                        for u in range(nu):
                            if u + 1 < nu:
                                S2(u + 1, "a")
                            S3(u, "a")
                            if u + 1 < nu:
                                S2(u + 1, "b")
                            S3(u, "b")
                            if u + 1 < nu:
                                S2(u + 1, "c")
                            if u + 2 < nu:
                                S1(u + 2)

                    if not sample:
                        blocks = []
                        for s in range(4):
                            blk = 4 * t + s
                            if blk == 0:
                                segs = [(lambda kvh, s=s: kTd[:, l, kvh, 128 + 128 * s:256 + 128 * s], vtok[:, l, 1 + s, :], 128)]
                                blocks.append(dict(nq=128, segs=segs, nkeys=128, tb_c0=128, ga=gatok[:, s, :], ga_key="gatok", col=128 * s))
                            else:
                                segs = [(lambda kvh, s=s: kTd[:, l, kvh, 128 * s:128 * s + 128], vtok[:, l, s, :], 128),
                                        (lambda kvh, s=s: kTd[:, l, kvh, 128 + 128 * s:256 + 128 * s], vtok[:, l, 1 + s, :], 128)]
                                blocks.append(dict(nq=128, segs=segs, nkeys=256, tb_c0=0, ga=gatok[:, s, :], ga_key="gatok", col=128 * s))
                        attend_all(blocks)
                        vcopy(kTd[:, l, :, 0:128], kTd[:, l, :, TILE:TILE + 128], ["kTd"], ["kTd"])
                        vcopy(vtok[:, l, 0, :], vtok[:, l, 4, :], ["vtok"], ["vtok"])
                    else:
                        kb.dma("pool", Ks[:], ck[l].rearrange("s w f -> w s f"), (), ["Ks"])
                        kb.dma("pool", Vs[:], cv[l].rearrange("s w f -> w s f"), (), ["Vs"])
                        for i in range(NS):
                            pbt, pkt = bank()
                            ptv = pbt.bitcast(BF16).rearrange("p (a b) -> p a b", a=8)
                            vcopy(Kd[:], Ks[:, i, :].rearrange("w (h d) -> w h d", h=2).unsqueeze(2).to_broadcast([128, 2, 2, 64]), ["Ks"], ["Kd"])
                            for kvh in range(2):
                                tr(ptv[:, kvh, :], Kd[:, kvh, :, :].rearrange("w j d -> w (j d)"), identb[:], ["Kd", "identb"], [pkt])
                            acopy(KTs[:, i, :, 0:128], ptv[:, 0:2, :], [pkt], ["KTs"])
                        vcopy(KTs[:, :, :, 128:132], kTn[:].rearrange("p h (s t) -> p s h t", t=TS), ["kTn"], ["KTs"])
                        blocks = []
                        for i in range(NS):
                            def prep(i=i):
                                pbg, pkg = bank()
                                mm(pbg[0:TS, 0:512], identb[0:NTS, TS * i:TS * i + TS], gatok[0:NTS, 0, :], True, True, ["identb", "gatok"], [pkg])
                                acopy(gas2[i % 2][:, :], pbg[0:TS, 0:512], [pkg], [f"gas{i % 2}"])
                            segs = [(lambda kvh, i=i: KTs[:, i, kvh, 0:128], Vs[:, i, :], 128),
                                    (lambda kvh, i=i: KTs[:, i, kvh, 128:132], vnew[:, i, :], TS)]
                            blocks.append(dict(nq=TS, segs=segs, nkeys=132, tb_c0=0, ga=gas2[i % 2][:, :], ga_key=f"gas{i % 2}", col=TS * i, prep=prep))
                        attend_all(blocks)
                    kb.retire(ph_keys)
                if KDBG and t == 0 and l == 0:
                    dstg = PA("dstg", [128, 4, TILE])
                    acopy(dstg[:], gatedT[:], ["gatedT"], ["dstg"])
                    kb.dma("sp", dbg_x[:, 0:2048], dstg[:].rearrange("p a b -> p (a b)"), ["dstg"], (), final=True)
                    kb.dma("sp", dbg_x[:, 2048:3072], xi[:, 0, :], [xik], (), final=True)
                branch_out(0)
                if KDBG and t == 0 and l == 0:
                    kb.dma("sp", dbg_m[0], merged[:].rearrange("p a b -> p (a b)"), ["merged"], (), final=True)

                with contextlib.ExitStack() as ph:
                    ph_keys = []

                    def PH(name, shape, dt=F32, ph=ph, ph_keys=ph_keys):
                        ph_keys.append(name)
                        return S(name, shape, dt, stack=ph)
                    nrot[0] = 6
                    bank_rr[0] = 0
                    uT = PH("uT", [128, 4, ntok], BF16); yz = PH("yz", [128, 4, ntok]); gbT = PH("gbT", [128, 4, ntok], BF16)
                    zb = PH("zb", [128, 4, ntok], BF16)
                    NXS = 1 if sample else 2
                    XRs = [PH(f"XR{i}", [128, 16, SJ]) for i in range(NXS)]; XIs = [PH(f"XI{i}", [128, 16, SJ]) for i in range(NXS)]
                    T1 = PH("T1", [128, 16, SJ]); T2 = PH("T2", [128, 16, SJ])
                    HR = PH("HR", [128, 16, SJ], BF16); HI = PH("HI", [128, 16, SJ], BF16)
                    g1 = PH("g1", [128, ntok]); g2b = PH("g2b", [128, ntok])
                    NH = NS if sample else 1
                    hfr = PH("hfr", [128, 16, NH]); hfi = PH("hfi", [128, 16, NH]); s5a = PH("s5a", [128, 16, NH]); s5b = PH("s5b", [128, 16, NH])
                    if sample:
                        h0r = PH("h0r", [128, 16, NS]); h0i = PH("h0i", [128, 16, NS])
                        st_io = PH("st_io", [NS, 2048])
                        for ssrc, sdst, skey_ in ((sre, h0r, "h0r"), (sim, h0i, "h0i")):
                            kb.dma("sp", st_io[:], ssrc[l], ["st_io"], ["st_io"])
                            pbs, pks = bank()
                            for k in range(16):
                                tr(pbs[:, k * NS:(k + 1) * NS], st_io[0:NS, k * 128:(k + 1) * 128], ident[0:NS, 0:NS], ["st_io", "ident"], [pks])
                            acopy(sdst[:].rearrange("p k s -> p (k s)"), pbs[:, 0:16 * NS], [pks], [skey_])
                    for q in range(4):
                        if q % 2 == 0:
                            slot, skey = wget(f"u{q // 2}")
                        pb, pk = proj_fm(slot, skey, q % 2, 8, xTk, ["xT"])
                        acopy(uT[:, q, 0:ntok], pb[:, 0:ntok], [pk], ["uT"])
                        act(yz[:, q, 0:ntok], pb[:, 0:ntok], AF.Copy, [pk, "dS5"], ["yz"], scale=dS5[:, l, q:q + 1])
                    for q in range(4):
                        if q % 2 == 0:
                            slot, skey = wget(f"gb{q // 2}")
                        pb, pk = proj_fm(slot, skey, q % 2, 8, xTk, ["xT"])
                        act(gbT[:, q, 0:ntok], pb[:, 0:ntok], AF.Silu, [pk], ["gbT"])

                    cosl = COS[:, l, :, :]; sinl = SIN[:, l, :, :]
                    nsj = 1 if sample else TILE // SJ
                    def stage1(jt):
                        c0 = jt * SJ
                        XR = XRs[jt % NXS]; XI = XIs[jt % NXS]; kXR = f"XR{jt % NXS}"; kXI = f"XI{jt % NXS}"
                        pxr, kxr = bank2(); pxi, kxi = bank2()
                        for k in range(16):
                            h_ = (k % 4) // 2
                            pk_ = h_ * 8 + (k // 4) * 2 + k % 2
                            mm(pxr[:, pk_ * SJ:(pk_ + 1) * SJ], WBre[64 * h_:64 * h_ + 64, l, k // 4, k % 2, :], uT[64 * h_:64 * h_ + 64, k // 4, c0:c0 + SJ],
                               True, True, ["WBre", "uT"], [kxr[h_]])
                        for k in range(16):
                            h_ = (k % 4) // 2
                            pk_ = h_ * 8 + (k // 4) * 2 + k % 2
                            mm(pxi[:, pk_ * SJ:(pk_ + 1) * SJ], WBim[64 * h_:64 * h_ + 64, l, k // 4, k % 2, :], uT[64 * h_:64 * h_ + 64, k // 4, c0:c0 + SJ],
                               True, True, ["WBim", "uT"], [kxi[h_]])
                        for h_ in range(2):
                            acopy(XR[:].rearrange("p (q h kk) j -> p h q kk j", h=2, kk=2)[:, h_], pxr[:, 512 * h_:512 * h_ + 512].rearrange("p (q kk j) -> p q kk j", q=4, kk=2),
                                  [kxr[h_]], [kXR])
                            acopy(XI[:].rearrange("p (q h kk) j -> p h q kk j", h=2, kk=2)[:, h_], pxi[:, 512 * h_:512 * h_ + 512].rearrange("p (q kk j) -> p q kk j", q=4, kk=2),
                                  [kxi[h_]], [kXI])

                    stage1(0)
                    pend = []
                    for jt in range(nsj):
                        c0 = jt * SJ
                        if jt + 1 < nsj:
                            stage1(jt + 1)
                        if lru_units:
                            lru_units.pop(0)()
                        XR = XRs[jt % NXS]; XI = XIs[jt % NXS]; kXR = f"XR{jt % NXS}"; kXI = f"XI{jt % NXS}"
                        if sample:
                            cosv = cosl[:, :, 0:TS].unsqueeze(2).to_broadcast([128, 16, NS, TS])
                            sinv = sinl[:, :, 0:TS].unsqueeze(2).to_broadcast([128, 16, NS, TS])
                            V = lambda a: a[:].rearrange("p k (s t) -> p k s t", t=TS)
                            magt = MAGT[:, l, :, :]
                            memset(MAGT[:, l, :, :].rearrange("p k (s t) -> p k s t", t=TS)[:, :, :, 0:1], 0.0, ["MAGT"])
                        else:
                            cosv = cosl[:, :, 0:SJ]; sinv = sinl[:, :, 0:SJ]
                            V = lambda a: a[:]
                            magt = MAGT[:, l, :, :]
                        tt(V(T1), V(XR), cosv, ALU.mult, [kXR, "COS"], ["T1"])
                        tt(V(T2), V(XI), sinv, ALU.mult, [kXI, "SIN"], ["T2"])
                        tt(T1[:], T1[:], T2[:], ALU.add, ["T1", "T2"], ["T1"])
                        tt(V(T2), V(XR), sinv, ALU.mult, [kXR, "SIN"], ["T2"])
                        tt(V(XI), V(XI), cosv, ALU.mult, [kXI, "COS"], [kXI])
                        tt(XI[:], XI[:], T2[:], ALU.subtract, [kXI, "T2"], [kXI])
                        if sample:
                            tt(s5a[:], h0r[:], AbR[:, l, :].unsqueeze(2).to_broadcast([128, 16, NS]), ALU.mult, ["h0r", "Ab"], ["s5a"])
                            tt(s5b[:], h0i[:], AbI[:, l, :].unsqueeze(2).to_broadcast([128, 16, NS]), ALU.mult, ["h0i", "Ab"], ["s5b"])
                            tt(s5a[:], s5a[:], s5b[:], ALU.subtract, ["s5a", "s5b"], ["s5a"])
                            tt(V(T1)[:, :, :, 0], V(T1)[:, :, :, 0], s5a[:], ALU.add, ["T1", "s5a"], ["T1"])
                            tt(s5a[:], h0r[:], AbI[:, l, :].unsqueeze(2).to_broadcast([128, 16, NS]), ALU.mult, ["h0r", "Ab"], ["s5a"])
                            tt(s5b[:], h0i[:], AbR[:, l, :].unsqueeze(2).to_broadcast([128, 16, NS]), ALU.mult, ["h0i", "Ab"], ["s5b"])
                            tt(s5a[:], s5a[:], s5b[:], ALU.add, ["s5a", "s5b"], ["s5a"])
                            tt(V(XI)[:, :, :, 0], V(XI)[:, :, :, 0], s5a[:], ALU.add, [kXI, "s5a"], [kXI])
                        else:
                            tt(T1[:, :, 0], T1[:, :, 0], Gre[:, l, :], ALU.add, ["T1", "G"], ["T1"])
                            tt(XI[:, :, 0], XI[:, :, 0], Gim[:, l, :], ALU.add, [kXI, "G"], [kXI])
                        flat = lambda a: a[:].rearrange("p k j -> p (k j)")
                        mflat = magt.rearrange("p k j -> p (k j)")
                        kb.op("dve", lambda e, o=flat(XR), a=mflat, b=flat(T1): e.tensor_tensor_scan(out=o, data0=a, data1=b, initial=0.0,
                                                                                                     op0=ALU.mult, op1=ALU.add),
                              ["T1", "MAGT"], [kXR])
                        kb.op("dve", lambda e, o=flat(T2), a=mflat, b=flat(XI): e.tensor_tensor_scan(out=o, data0=a, data1=b, initial=0.0,
                                                                                                     op0=ALU.mult, op1=ALU.add),
                              [kXI, "MAGT"], ["T2"])
                        if pend:
                            pend.pop()()
                        pby, pky = pst[3][:, (jt % 2) * 512:(jt % 2) * 512 + 512], f"pb{6 + jt % 2}"
                        if sample:
                            tt(V(T1), V(XR), cosv, ALU.mult, [kXR, "COS"], ["T1"])
                            tt(V(XI), V(T2), sinv, ALU.mult, ["T2", "SIN"], [kXI])
                            tt(HR[:], T1[:], XI[:], ALU.subtract, ["T1", kXI], ["HR"])
                            tt(V(T1), V(XR), sinv, ALU.mult, [kXR, "SIN"], ["T1"])
                            tt(V(XI), V(T2), cosv, ALU.mult, ["T2", "COS"], [kXI])
                            tt(HI[:], T1[:], XI[:], ALU.add, ["T1", kXI], ["HI"])
                            for q in range(4):
                                for kk in range(4):
                                    k = 4 * q + kk
                                    mm(pby[:, q * SJ:(q + 1) * SJ], WCre[:, l, k, :], HR[:, k, :], kk == 0, False, ["WCre", "HR"], [pky])
                                    mm(pby[:, q * SJ:(q + 1) * SJ], WCim[:, l, k, :], HI[:, k, :], False, kk == 3, ["WCim", "HI"], [pky])
                        else:
                            H2v = zb[:].rearrange("p q t -> p (q t)").rearrange("p (a k j) -> p a k j", a=2, k=16)
                            HR2, HI2 = H2v[:, 0], H2v[:, 1]
                            tt(HR[:], XR[:], cosv, ALU.mult, [kXR, "COS"], ["HR"])
                            stt(HR2, T2[:], -1.0, sinv, ALU.mult, ALU.mult, ["T2", "SIN"], ["zb"])
                            tt(HI[:], XR[:], sinv, ALU.mult, [kXR, "SIN"], ["HI"])
                            tt(HI2, T2[:], cosv, ALU.mult, ["T2", "COS"], ["zb"])
                            for q in range(4):
                                for kk in range(4):
                                    k = 4 * q + kk
                                    mm(pby[:, q * SJ:(q + 1) * SJ], WCre[:, l, k, :], HR[:, k, :], kk == 0, False, ["WCre", "HR"], [pky])
                                    mm(pby[:, q * SJ:(q + 1) * SJ], WCre[:, l, k, :], HR2[:, k, :], False, False, ["WCre", "zb"], [pky])
                                    mm(pby[:, q * SJ:(q + 1) * SJ], WCim[:, l, k, :], HI[:, k, :], False, False, ["WCim", "HI"], [pky])
                                    mm(pby[:, q * SJ:(q + 1) * SJ], WCim[:, l, k, :], HI2[:, k, :], False, kk == 3, ["WCim", "zb"], [pky])
                        pend.append(lambda c0=c0, pby=pby, pky=pky: tt(yz[:, :, c0:c0 + SJ], yz[:, :, c0:c0 + SJ],
                                                                        pby[:, 0:4 * SJ].rearrange("p (q j) -> p q j", q=4), ALU.add, ["yz", pky], ["yz"]))
                        if not sample:
                            tt(s5a[:, :, 0], XR[:, :, SJ - 1], CaR[:, l, :], ALU.mult, [kXR, "Ca"], ["s5a"], eng="pool")
                            tt(s5b[:, :, 0], T2[:, :, SJ - 1], CaI[:, l, :], ALU.mult, ["T2", "Ca"], ["s5b"], eng="pool")
                            tt(Gre[:, l, :], s5a[:, :, 0], s5b[:, :, 0], ALU.subtract, ["s5a", "s5b"], ["G"], eng="pool")
                            tt(s5a[:, :, 0], XR[:, :, SJ - 1], CaI[:, l, :], ALU.mult, [kXR, "Ca"], ["s5a"], eng="pool")
                            tt(s5b[:, :, 0], T2[:, :, SJ - 1], CaR[:, l, :], ALU.mult, ["T2", "Ca"], ["s5b"], eng="pool")
                            tt(Gim[:, l, :], s5a[:, :, 0], s5b[:, :, 0], ALU.add, ["s5a", "s5b"], ["G"], eng="pool")
                        if (last_prompt and jt == nsj - 1) or sample:
                            if sample:
                                gr = V(XR)[:, :, :, TS - 1]; gi = V(T2)[:, :, :, TS - 1]
                                cl = cosl[:, :, TS - 1:TS].to_broadcast([128, 16, NS]); sl = sinl[:, :, TS - 1:TS].to_broadcast([128, 16, NS])
                                o1, o2, a_, b_ = hfr[:], hfi[:], s5a[:], s5b[:]
                            else:
                                gr = XR[:, :, SJ - 1]; gi = T2[:, :, SJ - 1]
                                cl = cosl[:, :, SJ - 1]; sl = sinl[:, :, SJ - 1]
                                o1, o2, a_, b_ = hfr[:, :, 0], hfi[:, :, 0], s5a[:, :, 0], s5b[:, :, 0]
                            tt(a_, gr, cl, ALU.mult, [kXR, "COS"], ["s5a"])
                            tt(b_, gi, sl, ALU.mult, ["T2", "SIN"], ["s5b"])
                            tt(o1, a_, b_, ALU.subtract, ["s5a", "s5b"], ["hfr"])
                            tt(a_, gr, sl, ALU.mult, [kXR, "SIN"], ["s5a"])
                            tt(b_, gi, cl, ALU.mult, ["T2", "COS"], ["s5b"])
                            tt(o2, a_, b_, ALU.add, ["s5a", "s5b"], ["hfi"])
                            if sample:
                                for hsrc, hkey, hdst in ((hfr, "hfr", nres), (hfi, "hfi", nims)):
                                    pbA, pkA = bank2(); pbB, pkB = bank2()
                                    for k in range(16):
                                        pbv, pkv = (pbA, pkA) if k < 8 else (pbB, pkB)
                                        tr(pbv[0:NS, (k % 8) * 128:(k % 8 + 1) * 128], hsrc[:, k, :], ident[:], [hkey, "ident"], [pkv[(k % 8) // 4]])
                                    acopy(st_io[:, 0:1024], pbA[0:NS, :], pkA, ["st_io"])
                                    acopy(st_io[:, 1024:2048], pbB[0:NS, :], pkB, ["st_io"])
                                    kb.dma("sp", hdst[l], st_io[:], ["st_io"], (), final=True)
                            else:
                                kb.dma("sp", nrep[l].rearrange("(k p) -> p k", p=128), hfr[:, :, 0], ["hfr"], (), final=True)
                                kb.dma("sp", nimp[l].rearrange("(k p) -> p k", p=128), hfi[:, :, 0], ["hfi"], (), final=True)
                    while lru_units:
                        lru_units.pop(0)()
                    while pend:
                        pend.pop()()
                    for q in range(4):
                        yv = yz[:, q, 0:ntok]
                        gq, kgq = (g1, "g1") if q % 2 == 0 else (g2b, "g2b")
                        kyz = f"yz{q}"
                        act(gq[:, 0:ntok], yv, AF.Square, ["yz"], [kgq])
                        ts(gq[:, 0:ntok], gq[:, 0:ntok], 0.044715, ALU.mult, [kgq], [kgq], s2=1.0, op1=ALU.add)
                        tt(gq[:, 0:ntok], gq[:, 0:ntok], yv, ALU.mult, [kgq, "yz"], [kgq])
                        act(gq[:, 0:ntok], gq[:, 0:ntok], AF.Sigmoid, [kgq], [kgq], scale=1.5957691216)
                        tt(yv, yv, gq[:, 0:ntok], ALU.mult, ["yz", kgq], [kyz])
                        acopy(zb[:, q, 0:ntok], yv, [kyz], ["zb"])
                    slot, skey = wget("glu")
                    for fo in range(4):
                        pb, pk = proj_fm(slot, skey, fo, 4, lambda c: zb[:, c, 0:ntok], ["zb"])
                        act(g1[:, 0:ntok], pb[:, 0:ntok], AF.Sigmoid, [pk], ["g1"])
                        tt(g2b[:, 0:ntok], yz[:, fo, 0:ntok], g1[:, 0:ntok], ALU.mult, [f"yz{fo}", "g1"], ["g2b"])
                        tt(gatedT[:, fo, 0:ntok], g2b[:, 0:ntok], gbT[:, fo, 0:ntok], ALU.mult, ["g2b", "gbT"], ["gatedT"])
                    nrot[0] = 8
                    ph_keys.extend([f"yz{q}" for q in range(4)])
                    kb.retire(ph_keys)
                branch_out(1)
                if KDBG and t == 0 and l == 0:
                    kb.dma("sp", dbg_m[1], merged[:].rearrange("p a b -> p (a b)"), ["merged"], (), final=True)

                with contextlib.ExitStack() as ph:
                    ph_keys = []

                    def PH(name, shape, dt=F32, ph=ph, ph_keys=ph_keys):
                        ph_keys.append(name)
                        return S(name, shape, dt, stack=ph)
                    if sample:
                        h0l = PH("h0l", [128, 4, NS]); hso = PH("hso", [128, 4, NS])
                        l_io = PH("l_io", [NS, 512]); c_in = PH("c_in", [3 * NS, 512])
                        kb.dma("sp", l_io[:], slru[l], (), ["l_io"])
                        kb.dma("sp", c_in[:], sconv[l].rearrange("s j f -> (s j) f"), (), ["c_in"])
                        pbs, pks = bank()
                        for q in range(4):
                            tr(pbs[:, q * NS:(q + 1) * NS], l_io[0:NS, q * 128:(q + 1) * 128], ident[0:NS, 0:NS], ["l_io", "ident"], [pks])
                        acopy(h0l[:].rearrange("p q s -> p (q s)"), pbs[:, 0:4 * NS], [pks], ["h0l"])
                        pbs, pks = bank()
                        for q in range(4):
                            tr(pbs[:, q * 3 * NS:(q + 1) * 3 * NS], c_in[0:3 * NS, q * 128:(q + 1) * 128], ident[0:3 * NS, 0:3 * NS], ["c_in", "ident"], [pks])
                        for q in range(4):
                            acopy(cbuf[:, q, :, 0:3], pbs[:, q * 3 * NS:(q + 1) * 3 * NS].rearrange("p (s j) -> p s j", j=3), [pks], ["cbuf"])
                    lbuf = {}
                    for nm_, dt_ in (("cvv", F32), ("cvb", BF16), ("rg", F32), ("ig", F32), ("aa", F32), ("sq", F32), ("hT", F32)):
                        lbuf[nm_] = [PH(f"{nm_}{i}", [128, ntok], dt_) for i in range(2)]
                    tl = PH("tl", [128, NS])
                    hfo = PH("hfo", [128, 4])
                    if sample:
                        for tt_ in range(1, TS):
                            kb.dma("sp", ncvs[l][:, tt_ - 1, :], xco[tt_:NTS:TS, :], ["xco"], (), final=True)
                    elif last_prompt:
                        kb.dma("sp", ncvp[l], xco[0:3, :], ["xco"], (), final=True)
                    def lru_chunk(q):
                        cvv, cvb, rg, ig, aa, sq, hT = (lbuf[n_][q % 2] for n_ in ("cvv", "cvb", "rg", "ig", "aa", "sq", "hT"))
                        kcvv, kcvb, krg, kig, kaa, ksq, khT = (f"{n_}{q % 2}" for n_ in ("cvv", "cvb", "rg", "ig", "aa", "sq", "hT"))
                        if sample:
                            win = lambda j: cbuf[:, q, :, j:j + TS]
                            V2 = lambda a: a[:, 0:ntok].rearrange("p (s t) -> p s t", t=TS)
                        else:
                            win = lambda j: cbuf[:, q, j:j + ntok]
                            V2 = lambda a: a[:, 0:ntok]
                        ts(V2(cvv), win(0), cw[:, l, q, 0:1], ALU.mult, ["cbuf", "cw", "cb"], [kcvv], s2=cb[:, l, q:q + 1], op1=ALU.add)
                        for j in range(1, 4):
                            stt(V2(cvv), win(j), cw[:, l, q, j:j + 1], V2(cvv), ALU.mult, ALU.add, ["cbuf", "cw", kcvv], [kcvv])
                        acopy(cvb[:, 0:ntok], cvv[:, 0:ntok], [kcvv], [kcvb])
                        pb, pk = bank()
                        mm(pb[:, 0:ntok], WA[:, l, q, :], cvb[:, 0:ntok], True, True, ["WA", kcvb], [pk])
                        act(rg[:, 0:ntok], pb[:, 0:ntok], AF.Sigmoid, [pk, "ba"], [krg], bias=ba[:, l, q:q + 1], scale=1.0)
                        pb, pk = bank()
                        mm(pb[:, 0:ntok], WX[:, l, q, :], cvb[:, 0:ntok], True, True, ["WX", kcvb], [pk])
                        act(ig[:, 0:ntok], pb[:, 0:ntok], AF.Sigmoid, [pk, "bx"], [kig], bias=bx[:, l, q:q + 1], scale=1.0)
                        act(sq[:, 0:ntok], rg[:, 0:ntok], AF.Exp, [krg, "c2"], [ksq], scale=c2[:, l, q:q + 1])
                        act(sq[:, 0:ntok], sq[:, 0:ntok], AF.Ln, [ksq], [ksq], scale=-1.0, bias=1.0)
                        act(sq[:, 0:ntok], sq[:, 0:ntok], AF.Exp, [ksq], [ksq], scale=0.5)
                        act(aa[:, 0:ntok], rg[:, 0:ntok], AF.Exp, [krg, "c1"], [kaa], scale=c1[:, l, q:q + 1])
                        tt(ig[:, 0:ntok], ig[:, 0:ntok], cvv[:, 0:ntok], ALU.mult, [kig, kcvv], [kig])
                        tt(ig[:, 0:ntok], ig[:, 0:ntok], sq[:, 0:ntok], ALU.mult, [kig, ksq], [kig])
                        if sample:
                            tt(tl[:], V2(aa)[:, :, 0], h0l[:, q, :], ALU.mult, [kaa, "h0l"], ["tl"])
                            tt(V2(ig)[:, :, 0], V2(ig)[:, :, 0], tl[:], ALU.add, [kig, "tl"], [kig])
                            memset(V2(aa)[:, :, 0:1], 0.0, [kaa])
                            kb.op("dve", lambda e: e.tensor_tensor_scan(out=hT[:, 0:ntok], data0=aa[:, 0:ntok], data1=ig[:, 0:ntok], initial=0.0,
                                                                        op0=ALU.mult, op1=ALU.add), [kaa, kig], [khT])
                            vcopy(hso[:, q, :], V2(hT)[:, :, TS - 1], [khT], ["hso"])
                        else:
                            kb.op("dve", lambda e, q=q: e.tensor_tensor_scan(out=hT[:, 0:ntok], data0=aa[:, 0:ntok], data1=ig[:, 0:ntok],
                                                                             initial=hl[:, l, q:q + 1], op0=ALU.mult, op1=ALU.add),
                                  [kaa, kig, "hl"], [khT])
                            vcopy(hl[:, l, q:q + 1], hT[:, ntok - 1:ntok], [khT], ["hl"])
                        tt(gatedT[:, q, 0:ntok], hT[:, 0:ntok], gcT[:, q, 0:ntok], ALU.mult, [khT, "gcT"], ["gatedT"])

                    for q in range(4):
                        lru_chunk(q)
                    if sample:
                        pbs, pks = bank()
                        for q in range(4):
                            tr(pbs[0:NS, q * 128:(q + 1) * 128], hso[:, q, :], ident[:], ["hso", "ident"], [pks])
                        acopy(l_io[:], pbs[0:NS, 0:512], [pks], ["l_io"])
                        kb.dma("sp", nlrus[l], l_io[:], ["l_io"], (), final=True)
                    elif last_prompt:
                        vcopy(hfo[:], hl[:, l, :], ["hl"], ["hfo"])
                        kb.dma("sp", nlrup[l].rearrange("(q p) -> p q", p=128), hfo[:], ["hfo"], (), final=True)
                    kb.retire(ph_keys)
                mTh["mT"] = PA("mT", [128, 8, ntok], BF16)
                branch_out(2)

                with contextlib.ExitStack() as ph:
                    ph_keys = []

                    def PH(name, shape, dt=F32, ph=ph, ph_keys=ph_keys):
                        ph_keys.append(name)
                        return S(name, shape, dt, stack=ph)
                    lng = PH("lng", [128, D]); lnb = PH("lnb", [128, D])
                    st6 = PH("st6", [128, 2, 6]); mv = PH("mv", [128, 2]); rstd = PH("rstd", [128, 1])
                    kb.dma("sp", lng[:], ln_g[l].partition_broadcast(128), (), ["lng"])
                    kb.dma("sp", lnb[:], ln_b[l].partition_broadcast(128), (), ["lnb"])
                    wos = [wget(f"wo{wo}", ahead=RING - wo) for wo in range(4)]
                    for s in range(nsub):
                        for wo in range(4):
                            slot, skey = wos[wo]
                            pb, pk = bank()
                            for fc in range(8):
                                mm(pb[0:R, 0:256], mTh["mT"][:, fc, s * 128:s * 128 + R], v8(slot)[:, fc, :], fc == 0, fc == 7, [skey, "mT"], [pk])
                            stt(xo[0:R, s, wo * 256:wo * 256 + 256], xi[0:R, s, wo * 256:wo * 256 + 256], DN_ALPHA, pb[0:R, 0:256],
                                ALU.mult, ALU.add, [xik, pk], [xok])
                        hv = xo[0:R, s, :]
                        for hh in range(2):
                            kb.op("dve", lambda e, hh=hh, hv=hv: e.bn_stats(out=st6[0:R, hh, :], in_=hv[:, hh * 512:hh * 512 + 512]), [xok], ["st6"])
                        kb.op("dve", lambda e: e.bn_aggr(out=mv[0:R, :], in_=st6[0:R, :, :].rearrange("p a b -> p (a b)")), ["st6"], ["mv"])
                        act(rstd[0:R, :], mv[0:R, 1:2], AF.Sqrt, ["mv", "epsc"], ["rstd"], bias=epsc[0:R, :], scale=1.0)
                        kb.op("dve", lambda e: e.reciprocal(rstd[0:R, :], rstd[0:R, :]), ["rstd"], ["rstd"])
                        ts(hv, hv, mv[0:R, 0:1], ALU.subtract, [xok, "mv", "rstd"], [xok], s2=rstd[0:R, 0:1], op1=ALU.mult)
                        tt(hv, hv, lng[0:R, :], ALU.mult, [xok, "lng"], [xok])
                        tt(hv, hv, lnb[0:R, :], ALU.add, [xok, "lnb"], [xok])
                        if l == DEPTH - 1 or os.environ.get("KDUMPL0"):
                            if sample:
                                kb.dma("sp", ys[:, :], hv, [xok], (), final=True)
                            else:
                                r0 = t * TILE + s * 128
                                kb.dma("sp", yp[r0:r0 + 128, :], hv, [xok], (), final=True)
                    kb.retire(ph_keys)
                kb.retire(pa_keys)

        for (t, l) in all_passes:
            if l == 0:
                if t == NPT:
                    kb.dma("sp", xtok[0][0:NTS, 0, :], xs, (), ["xtok0"])
                else:
                    kb.dma("sp", xtok[0][:], xp[t * TILE:(t + 1) * TILE, :].rearrange("(s p) d -> p s d", p=128), (), ["xtok0"])
            run_pass(t, l, l % 2, (l + 1) % 2)
        nx = int(os.environ.get("KEXTRA", "0"))
        if nx:
            kb.maxops = 10 ** 9
            pbx, pkx = bank()
            for _ in range(nx):
                mm(pbx[:, 0:128], identb[:], identb[:], True, True, ["identb"], [pkx])
        print("recorded ops:", kb.nrec, {k: v for k, v in kb.cnt.items()})
        kb.emit()
    return nc


_NC_CACHE = {}


def kernel(**inputs):
    f = lambda k: np.ascontiguousarray(np.asarray(inputs[k], dtype=np.float32))
    oh, neg = _bias_consts()
    par = (np.arange(128) // 16) % 2
    pm = np.stack([1.0 - par, par], 1).astype(np.float32)
    shared = {
        "rel_bias": f("rel_bias"), "w_in": f("w_in"), "sinks": f("sinks"),
        "w_br_a": f("w_branch_a"), "w_br_b": f("w_branch_b"), "w_br_c": f("w_branch_c"),
        "lam_re": f("ssm_lambda_re").reshape(2, 2048), "lam_im": f("ssm_lambda_im").reshape(2, 2048),
        "log_step": f("ssm_log_step"), "b_re": f("ssm_b_re"), "b_im": f("ssm_b_im"),
        "c_re": f("ssm_c_re"), "c_im": f("ssm_c_im"), "ssm_d": f("ssm_d"), "w_glu": f("ssm_w_glu"),
        "conv_w": f("conv_w"), "conv_b": f("conv_b"), "lru_w_a": f("lru_w_a"), "lru_b_a": f("lru_b_a"),
        "lru_w_x": f("lru_w_x"), "lru_b_x": f("lru_b_x"), "lru_lam": f("lru_lambda"), "w_out": f("w_out"),
        "ln_g": f("ln_g"), "ln_b": f("ln_b"), "oh2": oh, "negmask": neg, "pmask": pm,
    }
    x_prompt = f("x_prompt"); x_sample = f("x_sample")
    cache_k = f("cache_k").reshape(2, 128, 128, 128); cache_v = f("cache_v").reshape(2, 128, 128, 128)
    s_re = f("state_ssm_re").reshape(2, 128, 2048); s_im = f("state_ssm_im").reshape(2, 128, 2048)
    s_lru = f("state_lru"); s_conv = f("state_conv")
    in_maps = []
    for c in range(NCORES):
        sl = slice(NS * c, NS * c + NS)
        m = dict(shared)
        m.update({
            "xp": x_prompt[c], "xs": np.ascontiguousarray(x_sample[sl].reshape(NTS, D)),
            "ck": np.ascontiguousarray(cache_k[:, sl]), "cv": np.ascontiguousarray(cache_v[:, sl]),
            "sre": np.ascontiguousarray(s_re[:, sl]), "sim": np.ascontiguousarray(s_im[:, sl]),
            "slru": np.ascontiguousarray(s_lru[:, sl]), "sconv": np.ascontiguousarray(s_conv[:, sl]),
        })
        in_maps.append(m)
    if "nc" not in _NC_CACHE:
        _NC_CACHE["nc"] = build_nc()
    res = run_bass_kernel_spmd(_NC_CACHE["nc"], in_maps, core_ids=list(range(NCORES)))
    r = res.results
    cat = lambda k, ax: np.concatenate([np.asarray(r[c][k])[None] if ax is None else np.asarray(r[c][k]) for c in range(NCORES)], axis=0 if ax is None else ax)
    y_prompt = np.stack([r[c]["yp"] for c in range(NCORES)], 0).astype(np.float32)
    y_sample = np.concatenate([r[c]["ys"].reshape(NS, TS, D) for c in range(NCORES)], 0).astype(np.float32)
    stk = lambda k, shp: np.stack([np.asarray(r[c][k]).reshape(shp) for c in range(NCORES)], 1).astype(np.float32)
    ccat = lambda k, shp: np.concatenate([np.asarray(r[c][k]).reshape(shp) for c in range(NCORES)], 1).astype(np.float32)
    return (y_prompt, y_sample,
            stk("nkp", (2, 128, 2, 64)), stk("nvp", (2, 128, 2, 64)),
            stk("nrep", (2, 32, 64)), stk("nimp", (2, 32, 64)),
            stk("nlrup", (2, 512)), stk("ncvp", (2, 3, 512)),
            ccat("nks", (2, NS, 128, 2, 64)), ccat("nvs", (2, NS, 128, 2, 64)),
            ccat("nres", (2, NS, 32, 64)), ccat("nims", (2, NS, 32, 64)),
            ccat("nlrus", (2, NS, 512)), ccat("ncvs", (2, NS, 3, 512)))
```

```python
import contextlib
import math
import os
import numpy as np
import concourse.bass as bass
import concourse.mybir as mybir
from concourse.bass_utils import run_bass_kernel_spmd

F32 = mybir.dt.float32
BF16 = mybir.dt.bfloat16
I32 = mybir.dt.int32
AF = mybir.ActivationFunctionType
ALU = mybir.AluOpType
AX = mybir.AxisListType

NCORES = 8
D = 1024
DEPTH = 2
SEQ = 2048
NS = 16
TS = 4
NTS = NS * TS
TILE = 512
NPT = SEQ // TILE
NEG = -1e30
DN_ALPHA = (2 * DEPTH) ** 0.25
LN_EPS = 1e-5
SJ = 64
COMPUTE = ("pe", "dve", "act", "pool")


class KB:
    def __init__(self, nc, n_dma_sems=16):
        self.nc = nc
        self.eng = {"pe": nc.tensor, "dve": nc.vector, "act": nc.scalar, "pool": nc.gpsimd, "sp": nc.sync}
        self.streams = {k: [] for k in self.eng}
        self.sem = {k: nc.alloc_semaphore(name=f"c_{k}") for k in COMPUTE}
        self.cnt = {k: 0 for k in COMPUTE}
        self.waited = {}
        self.lastw = {}
        self.readers = {}
        self.dpool = {}
        for q in ("sp", "pool"):
            self.dpool[q] = {"sems": [nc.alloc_semaphore(name=f"d_{q}{i}") for i in range(n_dma_sems)],
                             "uses": [0] * n_dma_sems, "next": 0}
        self.final_tokens = []
        self.dma_since_barrier = []
        self.nrec = 0
        self.alias_pending = {}
        self.maxops = int(os.environ.get("KMAXOPS", "1000000000"))
        self.dummy = (self.sem["pe"], 0, "pe")

    def _wait(self, e, tok):
        sem, val, owner = tok
        if owner == "pe" and e == "pe":
            return
        key = (e, sem.name)
        if self.waited.get(key, 0) >= val:
            return
        self.waited[key] = val
        eng = self.eng[e]
        self.streams[e].append(lambda eng=eng, sem=sem, val=val: eng.wait_ge(sem, val))

    def retire(self, keys):
        for k in keys:
            toks = []
            if k in self.lastw:
                toks.append(self.lastw.pop(k))
            toks.extend(self.readers.pop(k, []))
            for t in toks:
                cur = self.alias_pending.get(t[0].name)
                if cur is None or cur[1] < t[1]:
                    self.alias_pending[t[0].name] = t

    def _deps(self, e, reads, writes):
        for k in writes:
            if k not in self.lastw and k not in self.readers:
                for t in self.alias_pending.values():
                    self._wait(e, t)
                break
        toks = []
        for k in reads:
            if k in self.lastw:
                toks.append(self.lastw[k])
        for k in writes:
            if k in self.lastw:
                toks.append(self.lastw[k])
            toks.extend(self.readers.get(k, []))
        for t in toks:
            self._wait(e, t)

    def _commit(self, tok, reads, writes):
        for k in writes:
            self.lastw[k] = tok
            self.readers[k] = []
        for k in reads:
            self.readers.setdefault(k, []).append(tok)

    def op(self, e, fn, reads=(), writes=()):
        self.nrec += 1
        if self.nrec > self.maxops:
            return self.dummy
        if self.nrec == int(os.environ.get("KTRACE", "-1")):
            import traceback
            traceback.print_stack()
            for k in list(reads) + list(writes):
                print("KEY", k, "lastw", (self.lastw[k][0].name, self.lastw[k][1]) if k in self.lastw else None,
                      "readers", [(t[0].name, t[1]) for t in self.readers.get(k, [])])
            print("CNT", self.cnt, {q: p["uses"] for q, p in self.dpool.items()})
        self._deps(e, reads, writes)
        self.cnt[e] += 1
        tok = (self.sem[e], self.cnt[e], e)
        sem = self.sem[e]
        eng = self.eng[e]
        self.streams[e].append(lambda eng=eng, fn=fn, sem=sem: fn(eng).then_inc(sem, 1))
        self._commit(tok, reads, writes)
        return tok

    def dma(self, q, out, in_, reads=(), writes=(), final=False, **kw):
        self.nrec += 1
        if self.nrec > self.maxops:
            return self.dummy
        self._deps(q, reads, writes)
        p = self.dpool[q]
        i = p["next"]
        p["next"] = (i + 1) % len(p["sems"])
        sem = p["sems"][i]
        if p["uses"][i] > 0:
            self._wait(q, (sem, 16 * p["uses"][i], None))
        p["uses"][i] += 1
        tok = (sem, 16 * p["uses"][i], None)
        eng = self.eng[q]
        self.streams[q].append(
            lambda eng=eng, out=out, in_=in_, sem=sem, kw=kw: eng.dma_start(out=out, in_=in_, **kw).then_inc(sem, 16))
        self._commit(tok, reads, writes)
        self.dma_since_barrier.append(tok)
        if final:
            self.final_tokens.append(tok)
        return tok

    def barrier(self):
        toks = [(self.sem[k], self.cnt[k], k) for k in COMPUTE if self.cnt[k] > 0] + list(self.dma_since_barrier)
        for e in self.eng:
            for t in toks:
                if t[2] == e:
                    continue
                self._wait(e, t)
        self.dma_since_barrier = []

    def emit(self):
        for t in self.final_tokens:
            self._wait("sp", t)
        nc = self.nc
        st = self.streams
        with nc.Block() as block:
            @block.sync
            def _(e):
                for f in st["sp"]:
                    f()

            @block.tensor
            def _(e):
                for f in st["pe"]:
                    f()

            @block.vector
            def _(e):
                for f in st["dve"]:
                    f()

            @block.scalar
            def _(e):
                for f in st["act"]:
                    f()

            @block.gpsimd
            def _(e):
                for f in st["pool"]:
                    f()


def _t5_bucket(d):
    max_exact = 16
    d = max(d, 0)
    if d < max_exact:
        return d
    v = np.float32(np.log(np.float32(max(d, 1)) / np.float32(max_exact))) / np.float32(math.log(128 / max_exact)) * np.float32(16)
    return min(max_exact + int(np.float32(v)), 31)


def _bias_consts():
    oh = np.zeros((32, 383), np.float32)
    neg = np.zeros((8, 383), np.float32)
    for i in range(383):
        d = 255 - i
        if 0 <= d < 128:
            oh[_t5_bucket(d), i] = 1.0
        else:
            neg[:, i] = NEG
    return oh, neg


def build_nc(passes=None, dbg=False):
    nc = bass.Bass("TRN2", target_bir_lowering=False)

    def din(name, shape):
        return nc.dram_tensor(name, list(shape), F32, kind="ExternalInput").ap()

    def dout(name, shape):
        return nc.dram_tensor(name, list(shape), F32, kind="ExternalOutput").ap()

    xp = din("xp", [SEQ, D]); xs = din("xs", [NTS, D])
    ck = din("ck", [2, NS, 128, 128]); cv = din("cv", [2, NS, 128, 128])
    sre = din("sre", [2, NS, 2048]); sim = din("sim", [2, NS, 2048])
    slru = din("slru", [2, NS, 512]); sconv = din("sconv", [2, NS, 3, 512])
    rel_bias = din("rel_bias", [32, 8]); w_in = din("w_in", [2, D, 6400]); sinks = din("sinks", [2, 8])
    w_br = [din("w_br_a", [2, 512, D]), din("w_br_b", [2, 512, D]), din("w_br_c", [2, 512, D])]
    lam_re = din("lam_re", [2, 2048]); lam_im = din("lam_im", [2, 2048]); log_step = din("log_step", [2, 32])
    b_re = din("b_re", [2, 32, 64, 16]); b_im = din("b_im", [2, 32, 64, 16])
    c_re = din("c_re", [2, 32, 16, 64]); c_im = din("c_im", [2, 32, 16, 64])
    ssm_d = din("ssm_d", [2, 512]); w_glu = din("w_glu", [2, 512, 512])
    conv_w = din("conv_w", [2, 4, 512]); conv_b = din("conv_b", [2, 512])
    lru_w_a = din("lru_w_a", [2, 8, 64, 64]); lru_b_a = din("lru_b_a", [2, 512])
    lru_w_x = din("lru_w_x", [2, 8, 64, 64]); lru_b_x = din("lru_b_x", [2, 512])
    lru_lam = din("lru_lam", [2, 512]); w_out = din("w_out", [2, D, D])
    ln_g = din("ln_g", [2, D]); ln_b = din("ln_b", [2, D])
    oh2 = din("oh2", [32, 383]); negmask = din("negmask", [8, 383]); pmask = din("pmask", [128, 2])

    yp = dout("yp", [SEQ, D]); ys = dout("ys", [NTS, D])
    nkp = dout("nkp", [2, 128, 128]); nvp = dout("nvp", [2, 128, 128])
    nrep = dout("nrep", [2, 2048]); nimp = dout("nimp", [2, 2048])
    nlrup = dout("nlrup", [2, 512]); ncvp = dout("ncvp", [2, 3, 512])
    nks = dout("nks", [2, NS, 128, 128]); nvs = dout("nvs", [2, NS, 128, 128])
    nres = dout("nres", [2, NS, 2048]); nims = dout("nims", [2, NS, 2048])
    nlrus = dout("nlrus", [2, NS, 512]); ncvs = dout("ncvs", [2, NS, 3, 512])
    KDBG = bool(os.environ.get("KDBG"))
    dbg_m = [dout(f"dbg_m{b}", [128, 8 * TILE]) for b in range(2)] if KDBG else None
    dbg_x = dout("dbg_x", [128, 4 * D]) if KDBG else None
    rv_t = nc.dram_tensor("rv_scr", [8, 383], F32, kind="Internal")
    rv = rv_t.ap()
    tb_scr = nc.dram_tensor("tb_scr", [128, 2048], F32, kind="Internal").ap()

    kb = KB(nc)
    ncd = nc.allow_non_contiguous_dma(reason="small strided parameter/state transfers")

    def mm(out, lhsT, rhs, start, stop, reads, writes):
        kb.op("pe", lambda e: e.matmul(out, lhsT=lhsT, rhs=rhs, start=start, stop=stop), reads, writes)

    def tr(out, in_, ident, reads, writes):
        kb.op("pe", lambda e: e.transpose(out, in_, ident), reads, writes)

    def act(out, in_, func, reads, writes, **kw):
        kb.op("act", lambda e: e.activation(out=out, in_=in_, func=func, **kw), reads, writes)

    def acopy(out, in_, reads, writes):
        kb.op("act", lambda e: e.copy(out=out, in_=in_), reads, writes)

    def vcopy(out, in_, reads, writes, eng="dve"):
        kb.op(eng, lambda e: e.tensor_copy(out, in_), reads, writes)

    def tt(out, in0, in1, op, reads, writes, eng="dve"):
        kb.op(eng, lambda e: e.tensor_tensor(out=out, in0=in0, in1=in1, op=op), reads, writes)

    def ts(out, in0, s1, op0, reads, writes, s2=None, op1=None, eng="dve"):
        if op1 is None:
            kb.op(eng, lambda e: e.tensor_scalar(out=out, in0=in0, scalar1=s1, scalar2=None, op0=op0), reads, writes)
        else:
            kb.op(eng, lambda e: e.tensor_scalar(out=out, in0=in0, scalar1=s1, scalar2=s2, op0=op0, op1=op1), reads, writes)

    def stt(out, in0, scalar, in1, op0, op1, reads, writes):
        kb.op("dve", lambda e: e.scalar_tensor_tensor(out=out, in0=in0, scalar=scalar, in1=in1, op0=op0, op1=op1), reads, writes)

    def memset(ap, val, writes, eng="dve"):
        kb.op(eng, lambda e: e.memset(ap, val), (), writes)

    with contextlib.ExitStack() as es:
        es.enter_context(ncd)

        uniq = [0]

        def S(name, shape, dt=F32, stack=es):
            uniq[0] += 1
            return stack.enter_context(nc.sbuf_tensor(f"{name}_{uniq[0]}", list(shape), dt))

        pst = [es.enter_context(nc.psum_tensor(f"ps{i}", [128, 1024], F32)) for i in range(4)]
        bank_rr = [0]
        nrot = [8]

        def bank():
            i = bank_rr[0]
            bank_rr[0] = (i + 1) % nrot[0]
            return pst[i // 2][:, (i % 2) * 512:(i % 2) * 512 + 512], f"pb{i}"

        def bank2():
            i = bank_rr[0]
            if i % 2:
                i = (i + 1) % nrot[0]
            bank_rr[0] = (i + 2) % nrot[0]
            return pst[i // 2][:, :], [f"pb{i}", f"pb{i + 1}"]

        ident = S("ident", [128, 128]); identb = S("identb", [128, 128], BF16)
        xtok = [S("xtokA", [128, 4, D]), S("xtokB", [128, 4, D])]
        xT = S("xT", [128, 8, TILE], BF16)
        RING = 4
        wr = [S(f"wr{i}", [128, 2048], BF16) for i in range(RING)]
        kTd = S("kTd", [128, 2, 2, 128 + TILE], BF16)
        vtok = S("vtok", [128, 2, 5, 128], BF16)
        merged = S("merged", [128, 8, TILE])
        skb = S("skb", [128, 2, 8])
        COS = S("COS", [128, 2, 16, SJ + 1]); SIN = S("SIN", [128, 2, 16, SJ + 1])
        MAGT = S("MAGT", [128, 2, 16, SJ])
        WBre = S("WBre", [128, 2, 4, 2, 128], BF16); WBim = S("WBim", [128, 2, 4, 2, 128], BF16)
        WCre = S("WCre", [128, 2, 16, 128], BF16); WCim = S("WCim", [128, 2, 16, 128], BF16)
        AbR = S("AbR", [128, 2, 16]); AbI = S("AbI", [128, 2, 16])
        CaR = S("CaR", [128, 2, 16]); CaI = S("CaI", [128, 2, 16])
        dS5 = S("dS5", [128, 2, 4])
        Gre = S("Gre", [128, 2, 16]); Gim = S("Gim", [128, 2, 16])
        WA = S("WA", [128, 2, 4, 128], BF16); WX = S("WX", [128, 2, 4, 128], BF16)
        cw = S("cw", [128, 2, 4, 4]); cb = S("cb", [128, 2, 4]); ba = S("ba", [128, 2, 4]); bx = S("bx", [128, 2, 4])
        c1 = S("c1", [128, 2, 4]); c2 = S("c2", [128, 2, 4])
        hl = S("hl", [128, 2, 4]); chist = S("chist", [128, 2, 4, 3])
        epsc = S("epsc", [128, 1])

        kb.op("pool", lambda e: e.memset(ident[:], 0.0), (), ["ident"])
        kb.op("pool", lambda e: e.affine_select(out=ident[:], in_=ident[:], pattern=[[-1, 128]], compare_op=ALU.not_equal,
                                                fill=1.0, base=0, channel_multiplier=1), ["ident"], ["ident"])
        vcopy(identb[:], ident[:], ["ident"], ["identb"])
        memset(epsc[:], LN_EPS, ["epsc"])
        memset(Gre[:], 0.0, ["G"]); memset(Gim[:], 0.0, ["G"])
        memset(hl[:], 0.0, ["hl"]); memset(chist[:], 0.0, ["chist"])
        kb.dma("sp", skb[:].rearrange("p l h -> p (l h)"), sinks.rearrange("l h -> (l h)").partition_broadcast(128), (), ["skb"])

        class _Stop(Exception):
            pass
        STAGE = float(os.environ.get("KSTAGE", "99"))

        def stage(n):
            if STAGE < n:
                raise _Stop()
        try:
          with contextlib.ExitStack() as ss:
            def SS(name, shape, dt=F32):
                return S(name, shape, dt, stack=ss)
            stage(1)
            rb = SS("rb", [32, 8]); ohs = SS("ohs", [32, 383]); ngs = SS("ngs", [8, 383]); rvs = SS("rvs", [8, 383])
            kb.dma("sp", rb[:], rel_bias, (), ["rb"])
            kb.dma("sp", ohs[:], oh2, (), ["ohs"])
            kb.dma("sp", ngs[:], negmask, (), ["ngs"])
            pb, pk = bank()
            mm(pb[0:8, 0:383], rb[:], ohs[:], True, True, ["rb", "ohs"], [pk])
            tt(rvs[:], pb[0:8, 0:383], ngs[:], ALU.add, [pk, "ngs"], ["rvs"])
            kb.dma("sp", rv, rvs[:], ["rvs"], ["rv_dram"])
            stage(1.5)
            Tq = SS("Tq", [128, 2048]); Tf = SS("Tf", [128, 2048]); Jm = SS("Jm", [128, 128])
            kb.dma("sp", Tq[:].rearrange("p (h s) -> p h s", h=8), bass.AP(rv_t, 0, [[1, 128], [383, 8], [1, 256]]), ["rv_dram"], ["Tq"])
            kb.op("pool", lambda e: e.memset(Jm[:], 0.0), (), ["Jm"])
            kb.op("pool", lambda e: e.affine_select(out=Jm[:], in_=Jm[:], pattern=[[1, 128]], compare_op=ALU.not_equal,
                                                    fill=1.0, base=-127, channel_multiplier=1), ["Jm"], ["Jm"])
            for c4 in range(4):
                pb, pk = bank()
                mm(pb[:, :], Jm[:], Tq[:, 512 * c4:512 * c4 + 512], True, True, ["Jm", "Tq"], [pk])
                acopy(Tf[:, 512 * c4:512 * c4 + 512], pb[:, :], [pk], ["Tf"])
            kb.dma("sp", tb_scr, Tf[:], ["Tf"], ["tb_dram"])

            stage(2)
            jj_i = SS("jj_i", [128, SJ + 1], I32); jj = SS("jj", [128, SJ + 1])
            kb.op("pool", lambda e: e.iota(jj_i[:], pattern=[[1, SJ + 1]], base=0, channel_multiplier=0), (), ["jj_i"])
            vcopy(jj[:], jj_i[:], ["jj_i"], ["jj"])
            lre = SS("lre", [128, 2, 16]); lim = SS("lim", [128, 2, 16]); lst = SS("lst", [128, 2, 16])
            Bre = SS("Bre", [128, 16, 16]); Bim = SS("Bim", [128, 16, 16]); Cn = SS("Cn", [128, 4, 64]); Cn2 = SS("Cn2", [128, 4, 128])
            pmk = SS("pmk", [128, 2])
            kb.dma("sp", pmk[:], pmask, (), ["pmk"])
            BBr = SS("BBr", [128, 16, 16]); BBi = SS("BBi", [128, 16, 16]); tB = SS("tB", [128, 16, 16])
            ZB = SS("ZB", [128, 16, 128])
            phi = SS("phi", [128, 16, SJ + 1]); rr = SS("rr", [128, 16, SJ + 1]); ri = SS("ri", [128, 16, SJ + 1], I32)
            rf = SS("rf", [128, 16, SJ + 1]); gg = SS("gg", [128, 16, SJ + 1])
            sm = [SS(f"sm{i}", [128, 16]) for i in range(12)]
            wst = SS("wst", [128, 4, 128]); lamt = SS("lamt", [128, 4])
            for l in range(2):
                kb.dma("sp", lre[:, l, :], lam_re[l].rearrange("(k p) -> p k", p=128), (), ["lre"])
                kb.dma("sp", lim[:, l, :], lam_im[l].rearrange("(k p) -> p k", p=128), (), ["lim"])
                for g2 in range(2):
                    kb.dma("sp", lst[64 * g2:64 * g2 + 64, l, :],
                           log_step[l].rearrange("(k t) -> t k", t=2)[g2].partition_broadcast(64), (), ["lst"])
                kb.dma("sp", dS5[:, l, :], ssm_d[l].rearrange("(q p) -> p q", p=128), (), ["dS5"])
            stage(2.5)
            for l in range(2):
                step, lr, th, mag, den, a1, t0, t1, fre, fim, rden, t2 = sm
                act(step[:], lst[:, l, :], AF.Exp, ["lst"], ["sm0"])
                ts(lr[:], lre[:, l, :], -1e-4, ALU.min, ["lre"], ["sm1"])
                tt(th[:], lim[:, l, :], step[:], ALU.mult, ["lim", "sm0"], ["sm2"])
                tt(t0[:], lr[:], step[:], ALU.mult, ["sm1", "sm0"], ["sm6"])
                act(mag[:], t0[:], AF.Exp, ["sm6"], ["sm3"])
                tt(phi[:], th[:].unsqueeze(2).to_broadcast([128, 16, SJ + 1]), jj[:].unsqueeze(1).to_broadcast([128, 16, SJ + 1]),
                   ALU.mult, ["sm2", "jj"], ["phi"])
                for which, TAB in ((0, SIN), (1, COS)):
                    ts(rr[:], phi[:], 1.0 / (2 * math.pi), ALU.mult, ["phi"], ["rr"], s2=0.25 * which, op1=ALU.add)
                    vcopy(ri[:], rr[:], ["rr"], ["ri"])
                    vcopy(rf[:], ri[:], ["ri"], ["rf"])
                    tt(rr[:], rr[:], rf[:], ALU.subtract, ["rr", "rf"], ["rr"])
                    ts(gg[:], rr[:], 0.5, ALU.is_gt, ["rr"], ["gg"])
                    tt(rr[:], rr[:], gg[:], ALU.subtract, ["rr", "gg"], ["rr"])
                    ts(gg[:], rr[:], -0.5, ALU.is_lt, ["rr"], ["gg"])
                    tt(rr[:], rr[:], gg[:], ALU.add, ["rr", "gg"], ["rr"])
                    ts(rr[:], rr[:], 0.5, ALU.min, ["rr"], ["rr"], s2=-0.5, op1=ALU.max)
                    act(TAB[:, l, :, :], rr[:], AF.Sin, ["rr"], ["COS" if which else "SIN"], scale=6.283185)
                tt(AbR[:, l, :], mag[:], COS[:, l, :, 1], ALU.mult, ["sm3", "COS"], ["Ab"])
                tt(AbI[:, l, :], mag[:], SIN[:, l, :, 1], ALU.mult, ["sm3", "SIN"], ["Ab"])
                tt(CaR[:, l, :], mag[:], COS[:, l, :, SJ], ALU.mult, ["sm3", "COS"], ["Ca"])
                tt(CaI[:, l, :], mag[:], SIN[:, l, :, SJ], ALU.mult, ["sm3", "SIN"], ["Ca"])
                vcopy(MAGT[:, l, :, :], mag[:].unsqueeze(2).to_broadcast([128, 16, SJ]), ["sm3"], ["MAGT"])
                memset(MAGT[:, l, :, 0:1], 0.0, ["MAGT"])
                li_ = lim[:, l, :]
                tt(den[:], lr[:], lr[:], ALU.mult, ["sm1"], ["sm4"])
                tt(t1[:], li_, li_, ALU.mult, ["lim"], ["sm7"])
                tt(den[:], den[:], t1[:], ALU.add, ["sm4", "sm7"], ["sm4"])
                kb.op("dve", lambda e, rden=rden, den=den: e.reciprocal(rden[:], den[:]), ["sm4"], ["sm10"])
                ts(a1[:], AbR[:, l, :], -1.0, ALU.add, ["Ab"], ["sm5"])
                tt(t0[:], a1[:], lr[:], ALU.mult, ["sm5", "sm1"], ["sm6"])
                tt(t1[:], AbI[:, l, :], li_, ALU.mult, ["Ab", "lim"], ["sm7"])
                tt(fre[:], t0[:], t1[:], ALU.add, ["sm6", "sm7"], ["sm8"])
                tt(fre[:], fre[:], rden[:], ALU.mult, ["sm8", "sm10"], ["sm8"])
                tt(t0[:], AbI[:, l, :], lr[:], ALU.mult, ["Ab", "sm1"], ["sm6"])
                tt(t1[:], a1[:], li_, ALU.mult, ["sm5", "lim"], ["sm7"])
                tt(fim[:], t0[:], t1[:], ALU.subtract, ["sm6", "sm7"], ["sm9"])
                tt(fim[:], fim[:], rden[:], ALU.mult, ["sm9", "sm10"], ["sm9"])
                stage(3)
                kb.dma("sp", Bre[:], b_re[l].rearrange("(k t) p c -> (t p) k c", t=2), ["Bre"], ["Bre"])
                kb.dma("sp", Bim[:], b_im[l].rearrange("(k t) p c -> (t p) k c", t=2), ["Bim"], ["Bim"])
                frb = fre[:].unsqueeze(2).to_broadcast([128, 16, 16]); fib = fim[:].unsqueeze(2).to_broadcast([128, 16, 16])
                tt(BBr[:], Bre[:], frb, ALU.mult, ["Bre", "sm8"], ["BBr"])
                tt(tB[:], Bim[:], fib, ALU.mult, ["Bim", "sm9"], ["tB"])
                tt(BBr[:], BBr[:], tB[:], ALU.subtract, ["BBr", "tB"], ["BBr"])
                tt(BBi[:], Bim[:], frb, ALU.mult, ["Bim", "sm8"], ["BBi"])
                tt(tB[:], Bre[:], fib, ALU.mult, ["Bre", "sm9"], ["tB"])
                tt(BBi[:], BBi[:], tB[:], ALU.add, ["BBi", "tB"], ["BBi"])
                for src, skey_, dst, nm in ((BBr, "BBr", WBre, "WBre"), (BBi, "BBi", WBim, "WBim")):
                    memset(ZB[:], 0.0, ["ZB"])
                    for km in range(4):
                        for g2 in range(2):
                            c0 = 16 * (2 * km + g2)
                            vcopy(ZB[64 * g2:64 * g2 + 64, km::4, c0:c0 + 16], src[64 * g2:64 * g2 + 64, km::4, :], [skey_], ["ZB"])
                    for k4 in range(4):
                        pb, pk = bank()
                        for kq in range(4):
                            k = 4 * k4 + kq
                            tr(pb[:, kq * 128:(kq + 1) * 128], ZB[:, k, :], ident[:], ["ZB", "ident"], [pk])
                        for kq in range(4):
                            h_, kk_ = kq // 2, kq % 2
                            acopy(dst[64 * h_:64 * h_ + 64, l, k4, kk_, :], pb[64 * h_:64 * h_ + 64, kq * 128:(kq + 1) * 128], [pk], [nm])
                for csrc, sgn, wdst, nm in ((c_re, 1.0, WCre, "WCre"), (c_im, -1.0, WCim, "WCim")):
                    memset(wdst[:, l, :, :], 0.0, [nm])
                    kb.dma("sp", Cn[:], csrc[l].rearrange("(a g) c p -> (g c) a p", g=8), ["Cn"], ["Cn"])
                    for t_ in range(2):
                        ts(Cn2[:, :, 64 * t_:64 * t_ + 64], Cn[:], pmk[:, t_:t_ + 1], ALU.mult, ["Cn", "pmk"], ["Cn2"], s2=sgn, op1=ALU.mult)
                    pb, pk = bank()
                    for a in range(4):
                        tr(pb[:, a * 128:(a + 1) * 128], Cn2[:, a, :], ident[:], ["Cn2", "ident"], [pk])
                    pbv = pb.rearrange("p (a b) -> p a b", a=4)
                    for km in range(4):
                        acopy(wdst[:, l, km::4, 32 * km:32 * km + 32], pbv[:, :, 32 * km:32 * km + 32], [pk], [nm])
                stage(4)
                for j in range(4):
                    kb.dma("sp", cw[:, l, :, j], conv_w[l][j].rearrange("(q p) -> p q", p=128), (), ["cw"])
                kb.dma("sp", cb[:, l, :], conv_b[l].rearrange("(q p) -> p q", p=128), (), ["cb"])
                kb.dma("sp", ba[:, l, :], lru_b_a[l].rearrange("(q p) -> p q", p=128), (), ["ba"])
                kb.dma("sp", bx[:, l, :], lru_b_x[l].rearrange("(q p) -> p q", p=128), (), ["bx"])
                kb.dma("sp", lamt[:], lru_lam[l].rearrange("(q p) -> p q", p=128), ["lamt"], ["lamt"])
                act(lamt[:], lamt[:], AF.Exp, ["lamt"], ["lamt"], scale=-1.0)
                act(lamt[:], lamt[:], AF.Ln, ["lamt"], ["lamt"], bias=1.0)
                ts(c1[:, l, :], lamt[:], -8.0, ALU.mult, ["lamt"], ["c1"])
                ts(c2[:, l, :], lamt[:], -16.0, ALU.mult, ["lamt"], ["c2"])
                for wsrc, wdst, nm in ((lru_w_a, WA, "WA"), (lru_w_x, WX, "WX")):
                    memset(wst[:], 0.0, ["wst"])
                    for hh in range(2):
                        kb.dma("sp", wst[64 * hh:64 * hh + 64, :, 64 * hh:64 * hh + 64],
                               wsrc[l].rearrange("(q t) d e -> t d q e", t=2)[hh], ["wst"], ["wst"])
                    vcopy(wdst[:, l, :, :], wst[:], ["wst"], [nm])
        except _Stop:
            pass
        kb.barrier()

        w_in_v = [w_in[l].rearrange("(k p) c -> p k c", p=128) for l in range(2)]
        w_out_v = [w_out[l].rearrange("(k p) c -> p k c", p=128) for l in range(2)]
        w_br_v = [[w_br[b][l].rearrange("(k p) c -> p k c", p=128) for l in range(2)] for b in range(3)]
        w_glu_v = [w_glu[l].rearrange("(k p) c -> p k c", p=128) for l in range(2)]

        def v8(slot):
            return slot[:].rearrange("p (k c) -> p k c", k=8)

        def v4(slot):
            return slot[:].rearrange("p (k c) -> p k c", k=4)

        def win_spec(l, c0):
            return [(lambda s: v8(s), w_in_v[l][:, :, c0:c0 + 256])]

        def pass_specs(l):
            sp = []
            sp += [("q0", win_spec(l, 0)), ("q1", win_spec(l, 256)), ("kv", win_spec(l, 512))]
            sp += [("kd", [(lambda s, j=j, h=h: v8(s)[:, :, 128 * h + 64 * j:128 * h + 64 * j + 64], w_in_v[l][:, :, 512 + 64 * h:576 + 64 * h])
                           for h in range(2) for j in range(2)])]
            sp += [("ga0", win_spec(l, 768)), ("ga1", win_spec(l, 1024))]
            for b, (c_in, c_gm) in enumerate(((1280, 3328), (2304, 4352), (0, 5376))):
                if b == 1:
                    sp += [("u0", win_spec(l, 1280)), ("u1", win_spec(l, 1536)), ("gb0", win_spec(l, 1792)), ("gb1", win_spec(l, 2048))]
                    sp += [("xc0", win_spec(l, 2304)), ("xc1", win_spec(l, 2560)), ("gc0", win_spec(l, 2816)), ("gc1", win_spec(l, 3072))]
                    sp += [("glu", [(lambda s: v4(s), w_glu_v[l])])]
                g0 = 3328 + 1024 * b
                for h in range(2):
                    sp += [(f"br{b}{h}", [(lambda s: v4(s), w_br_v[b][l][:, :, 512 * h:512 * h + 512])])]
                    sp += [(f"gm{b}{i}", win_spec(l, g0 + 256 * i)) for i in (2 * h, 2 * h + 1)]
            sp += [(f"wo{i}", [(lambda s: v8(s), w_out_v[l][:, :, 256 * i:256 * i + 256])]) for i in range(4)]
            return sp

        all_passes = passes if passes is not None else [(t, l) for t in range(NPT + 1) for l in range(2)]
        NSLOT = len(pass_specs(0))
        wc_t = nc.dram_tensor("wcache", [2, NSLOT, 128, 2048], BF16, kind="Internal")
        wc = wc_t.ap()
        for l in sorted({l for (_, l) in all_passes}):
            for i, (name, pieces) in enumerate(pass_specs(l)):
                for dstf, src in pieces:
                    kb.dma("pool", dstf(wc[l, i]), src, (), [f"wc{l}_{i}"])
        wq = []
        for (t, l) in all_passes:
            wq += [(name, l, i) for i, (name, _) in enumerate(pass_specs(l))]
        wstate = {"issued": 0, "used": 0}

        def wissue():
            n = wstate["issued"]
            name, l_, i_ = wq[n]
            kb.dma("sp", wr[n % RING][:], wc[l_, i_], [f"wc{l_}_{i_}"], [f"wr{n % RING}"])
            wstate["issued"] = n + 1

        def wget(name, ahead=RING):
            n = wstate["used"]
            assert wq[n][0] == name, (wq[n][0], name)
            while wstate["issued"] < min(len(wq), n + ahead):
                wissue()
            wstate["used"] = n + 1
            return wr[n % RING], f"wr{n % RING}"

        def run_pass(t, l, xin, xout):
            sample = (t == NPT)
            ntok = NTS if sample else TILE
            nsub = 1 if sample else 4
            R = 64 if sample else 128
            last_prompt = (t == NPT - 1)
            xi = xtok[xin]; xo = xtok[xout]
            xik = f"xtok{xin}"; xok = f"xtok{xout}"

            for s in range(nsub):
                for k4 in range(2):
                    pb, pk = bank()
                    for kk in range(4):
                        kc = 4 * k4 + kk
                        tr(pb[:, kk * 128:kk * 128 + R], xi[0:R, s, kc * 128:(kc + 1) * 128], ident[0:R, 0:R], [xik, "ident"], [pk])
                    acopy(xT[:, 4 * k4:4 * k4 + 4, s * 128:s * 128 + R], pb.rearrange("p (a b) -> p a b", a=4)[:, :, 0:R], [pk], ["xT"])

            def proj_fm(slot, skey, cchunk, nk, rhs_of_kc, rkeys):
                pb, pk = bank()
                view = v8(slot) if nk == 8 else v4(slot)
                for kc in range(nk):
                    mm(pb[:, 0:ntok], view[:, kc, cchunk * 128:(cchunk + 1) * 128], rhs_of_kc(kc), kc == 0, kc == nk - 1,
                       [skey] + rkeys, [pk])
                return pb, pk

            xTk = lambda kc: xT[:, kc, 0:ntok]

            with contextlib.ExitStack() as pa:
                pa_keys = []

                def PA(name, shape, dt=F32):
                    pa_keys.append(name)
                    return S(name, shape, dt, stack=pa)
                gatedT = PA("gatedT", [128, 4, TILE], BF16)
                mTh = {}
                W7 = TS + 3
                gcT = PA("gcT", [128, 4, ntok], BF16)
                cbuf = PA("cbuf", [128, 4, NS, W7]) if sample else PA("cbuf", [128, 4, TILE + 3])
                xco = PA("xco", [NTS, 512]) if sample else PA("xco", [4, 512])
                lru_st = {}

                def xc_unit(q):
                    if q % 2 == 0:
                        lru_st["slot"] = wget(f"xc{q // 2}")
                        slot, skey = lru_st["slot"]
                        if sample or last_prompt:
                            nr = NTS if sample else 3
                            cstart = 0 if sample else TILE - 3
                            pbx, pkx = bank()
                            for kc in range(8):
                                mm(pbx[0:nr, 0:256], xT[:, kc, cstart:cstart + nr], v8(slot)[:, kc, :], kc == 0, kc == 7, [skey, "xT"], [pkx])
                            acopy(xco[0:nr, 256 * (q // 2):256 * (q // 2) + 256], pbx[0:nr, 0:256], [pkx], ["xco"])
                    slot, skey = lru_st["slot"]
                    pb, pk = proj_fm(slot, skey, q % 2, 8, xTk, ["xT"])
                    if sample:
                        acopy(cbuf[:, q, :, 3:W7], pb[:, 0:ntok].rearrange("p (s t) -> p s t", t=TS), [pk], ["cbuf"])
                    else:
                        acopy(cbuf[:, q, 0:3], chist[:, l, q, :], ["chist"], ["cbuf"])
                        acopy(cbuf[:, q, 3:3 + ntok], pb[:, 0:ntok], [pk], ["cbuf"])
                        acopy(chist[:, l, q, :], cbuf[:, q, ntok:ntok + 3], ["cbuf"], ["chist"])

                def gc_unit(q):
                    if q % 2 == 0:
                        lru_st["slot"] = wget(f"gc{q // 2}")
                    slot, skey = lru_st["slot"]
                    pb, pk = proj_fm(slot, skey, q % 2, 8, xTk, ["xT"])
                    act(gcT[:, q, 0:ntok], pb[:, 0:ntok], AF.Silu, [pk], ["gcT"])
                lru_units = [lambda q=q: xc_unit(q) for q in range(4)] + [lambda q=q: gc_unit(q) for q in range(4)]

                def branch_out(b):
                    with contextlib.ExitStack() as bo:
                        bo_keys = ["gms0", "gms1", "tmpm"]
                        gms = [S("gms0", [128, ntok], F32, stack=bo), S("gms1", [128, ntok], F32, stack=bo)]
                        tmpm = S("tmpm", [128, ntok], F32, stack=bo)
                        _branch_out(b, gms, tmpm)
                        kb.retire(bo_keys)

                def _branch_out(b, gms, tmpm):
                    mT = mTh.get("mT")
                    ypb = []
                    for fo in range(8):
                        if fo % 4 == 0:
                            slot, skey = wget(f"br{b}{fo // 4}")
                        ypb.append(proj_fm(slot, skey, fo % 4, 4, lambda c: gatedT[:, c, 0:ntok], ["gatedT"]))
                        if fo % 4 == 3:
                            for f2 in range(fo - 3, fo + 1):
                                if f2 % 2 == 0:
                                    gslot, gkey = wget(f"gm{b}{f2 // 2}")
                                gpb, gpk = proj_fm(gslot, gkey, f2 % 2, 8, xTk, ["xT"])
                                g = gms[f2 % 2]; gk = f"gms{f2 % 2}"
                                act(g[:, 0:ntok], gpb[:, 0:ntok], AF.Sigmoid, [gpk], [gk])
                                yb, yk = ypb[f2]
                                if b == 0:
                                    tt(merged[:, f2, 0:ntok], yb[:, 0:ntok], g[:, 0:ntok], ALU.mult, [yk, gk], ["merged"])
                                elif b == 1:
                                    tt(tmpm[:, 0:ntok], yb[:, 0:ntok], g[:, 0:ntok], ALU.mult, [yk, gk], ["tmpm"])
                                    tt(merged[:, f2, 0:ntok], merged[:, f2, 0:ntok], tmpm[:, 0:ntok], ALU.add, ["merged", "tmpm"], ["merged"])
                                else:
                                    tt(tmpm[:, 0:ntok], yb[:, 0:ntok], g[:, 0:ntok], ALU.mult, [yk, gk], ["tmpm"])
                                    tt(mT[:, f2, 0:ntok], merged[:, f2, 0:ntok], tmpm[:, 0:ntok], ALU.add, ["merged", "tmpm"], ["mT"])

                with contextlib.ExitStack() as ph:
                    ph_keys = []

                    def PH(name, shape, dt=F32, ph=ph, ph_keys=ph_keys):
                        ph_keys.append(name)
                        return S(name, shape, dt, stack=ph)
                    qT = PH("qT", [128, 4, TILE], BF16)
                    Tb = PH("Tb", [128, 8, 256])
                    kb.dma("sp", Tb[:].rearrange("p h s -> p (h s)"), tb_scr, ["tb_dram"], ["Tb"])
                    gatok = PH("gatok", [128, 4, 512], BF16)
                    kvo = PH("kvo", [128, 256])
                    KW = 132 if sample else 256
                    sc2 = [PH(f"sc{i}", [128, 4, KW]) for i in range(2)]; Pm2 = [PH(f"Pm{i}", [128, 4, KW], BF16) for i in range(2)]
                    PT = PH("PT", [128, 8, 128], BF16)
                    mx2 = [PH(f"mx{i}", [128, 4]) for i in range(2)]; ngm2 = [PH(f"ngm{i}", [128, 4]) for i in range(2)]
                    rs2 = [PH(f"rs{i}", [128, 4]) for i in range(2)]; esk2 = [PH(f"esk{i}", [128, 4]) for i in range(2)]
                    rinv2 = [PH(f"rinv{i}", [128, 4]) for i in range(2)]; att = PH("att", [128, 256])
                    gA2 = [PH(f"gA{i}", [128, 512], BF16) for i in range(2)]
                    if sample:
                        Ks = PH("Ks", [128, NS, 128], BF16); Kd = PH("Kd", [128, 2, 2, 64], BF16)
                        KTs = PH("KTs", [128, NS, 2, 132], BF16); Vs = PH("Vs", [128, NS, 128], BF16)
                        vnew = PH("vnew", [4, NS, 128], BF16)
                        kTn = PH("kTn", [128, 2, NTS], BF16); gas2 = [PH(f"gas{i}", [4, 512], BF16) for i in range(2)]

                    for c2_ in range(2):
                        slot, skey = wget(f"q{c2_}")
                        for cc in range(2):
                            pb, pk = proj_fm(slot, skey, cc, 8, xTk, ["xT"])
                            acopy(qT[:, 2 * c2_ + cc, 0:ntok], pb[:, 0:ntok], [pk], ["qT"])
                    slot, skey = wget("kv")
                    for s in range(nsub):
                        pb, pk = bank()
                        for kc in range(8):
                            mm(pb[0:R, 0:256], xT[:, kc, s * 128:s * 128 + R], v8(slot)[:, kc, :], kc == 0, kc == 7, [skey, "xT"], [pk])
                        if not sample:
                            acopy(vtok[:, l, 1 + s, :], pb[:, 128:256], [pk], ["vtok"])
                            if last_prompt and s == 3:
                                acopy(kvo[:], pb[:, 0:256], [pk], ["kvo"])
                                kb.dma("sp", nkp[l], kvo[:, 0:128], ["kvo"], (), final=True)
                                kb.dma("sp", nvp[l], kvo[:, 128:256], ["kvo"], (), final=True)
                        else:
                            vcopy(kvo[0:R, :], pb[0:R, 0:256], [pk], ["kvo"])
                            for tt_ in range(TS):
                                kb.dma("sp", nks[l][:, 124 + tt_, :], kvo[tt_:NTS:TS, 0:128], ["kvo"], (), final=True)
                                kb.dma("sp", nvs[l][:, 124 + tt_, :], kvo[tt_:NTS:TS, 128:256], ["kvo"], (), final=True)
                            kb.dma("sp", nks[l][:, 0:124, :], ck[l][:, 4:128, :], (), (), final=True)
                            kb.dma("sp", nvs[l][:, 0:124, :], cv[l][:, 4:128, :], (), (), final=True)
                            for i in range(NS):
                                pb2, pk2 = bank()
                                for kc in range(8):
                                    mm(pb2[0:TS, 0:128], xT[:, kc, TS * i:TS * i + TS], v8(slot)[:, kc, 128:256], kc == 0, kc == 7,
                                       [skey, "xT"], [pk2])
                                acopy(vnew[:, i, :], pb2[0:TS, 0:128], [pk2], ["vnew"])
                    slot, skey = wget("kd")
                    for kvh in range(2):
                        pb, pk = proj_fm(slot, skey, kvh, 8, xTk, ["xT"])
                        if not sample:
                            acopy(kTd[:, l, kvh, 128:128 + ntok], pb[:, 0:ntok], [pk], ["kTd"])
                        else:
                            acopy(kTn[:, kvh, :], pb[:, 0:ntok], [pk], ["kTn"])
                    for h2 in range(2):
                        slot, skey = wget(f"ga{h2}")
                        for s in range(nsub):
                            pb, pk = bank()
                            for kc in range(8):
                                mm(pb[0:R, 0:256], xT[:, kc, s * 128:s * 128 + R], v8(slot)[:, kc, :], kc == 0, kc == 7, [skey, "xT"], [pk])
                            act(gatok[0:R, s, h2 * 256:h2 * 256 + 256], pb[0:R, 0:256], AF.Silu, [pk], ["gatok"])

                    def attend_all(blocks):
                        units = [(bi, hf) for bi in range(len(blocks)) for hf in range(2)]
                        stt_ = {}

                        def S1(u):
                            bi, hf = units[u]
                            B = blocks[bi]
                            nq, segs = B["nq"], B["segs"]
                            if hf == 0 and B.get("prep"):
                                B["prep"]()
                            pb2, pks = bank2()
                            scv = pb2.rearrange("p (h s) -> p h s", h=4)
                            for hh in range(4):
                                h = 4 * hf + hh
                                cch, half = h // 2, h % 2
                                off = 0
                                for (kfn, vap, n) in segs:
                                    mm(scv[0:nq, half * 2 + hh // 2, off:off + n], qT[64 * half:64 * half + 64, cch, B["col"]:B["col"] + nq],
                                       kfn(hf)[64 * half:64 * half + 64, :], True, True, ["qT", "kTd", "KTs"], [pks[half]])
                                    off += n
                            stt_[u] = (scv, pks)

                        def S2(u, part):
                            bi, hf = units[u]
                            B = blocks[bi]
                            nq, segs, nkeys, tb_c0 = B["nq"], B["segs"], B["nkeys"], B["tb_c0"]
                            par = u % 2
                            sc_, Pm_, mx_, ngm_, rs_, esk_, rinv_ = sc2[par], Pm2[par], mx2[par], ngm2[par], rs2[par], esk2[par], rinv2[par]
                            ksc, kPm, kmx, kngm, krs, kesk, krinv = (f"{n_}{par}" for n_ in ("sc", "Pm", "mx", "ngm", "rs", "esk", "rinv"))
                            gA_ = gA2[bi % 2]; kgA = f"gA{bi % 2}"
                            if part == "a":
                                scv, pks = stt_.pop(u)
                                for half in range(2):
                                    stt(sc_[0:nq, half:4:2, 0:nkeys], scv[0:nq, 2 * half:2 * half + 2, 0:nkeys], 0.125,
                                        Tb[0:nq, 4 * hf + half:4 * hf + 4:2, tb_c0:tb_c0 + nkeys],
                                        ALU.mult, ALU.add, [pks[half], "Tb"], [ksc])
                                kb.op("dve", lambda e: e.tensor_reduce(out=mx_[0:nq, :], in_=sc_[0:nq, :, 0:nkeys], axis=AX.X, op=ALU.max),
                                      [ksc], [kmx])
                                tt(mx_[0:nq, :], mx_[0:nq, :], skb[0:nq, l, 4 * hf:4 * hf + 4], ALU.max, [kmx, "skb"], [kmx])
                                ts(ngm_[0:nq, :], mx_[0:nq, :], -1.0, ALU.mult, [kmx], [kngm])
                                tt(esk_[0:nq, :], skb[0:nq, l, 4 * hf:4 * hf + 4], mx_[0:nq, :], ALU.subtract, ["skb", kmx], [kesk])
                            elif part == "b":
                                for hh in range(4):
                                    act(Pm_[0:nq, hh, 0:nkeys], sc_[0:nq, hh, 0:nkeys], AF.Exp, [ksc, kngm], [kPm, krs],
                                        bias=ngm_[0:nq, hh:hh + 1], scale=1.0, accum_out=rs_[0:nq, hh:hh + 1])
                                act(esk_[0:nq, :], esk_[0:nq, :], AF.Exp, [kesk], [kesk])
                            else:
                                tt(rinv_[0:nq, :], rs_[0:nq, :], esk_[0:nq, :], ALU.add, [krs, kesk], [krinv])
                                kb.op("dve", lambda e: e.reciprocal(rinv_[0:nq, :], rinv_[0:nq, :]), [krinv], [krinv])

                        def S3(u, part):
                            bi, hf = units[u]
                            B = blocks[bi]
                            nq, segs = B["nq"], B["segs"]
                            par = u % 2
                            Pm_, rinv_ = Pm2[par], rinv2[par]
                            kPm, krinv = f"Pm{par}", f"rinv{par}"
                            gA_ = gA2[bi % 2]; kgA = f"gA{bi % 2}"
                            nseg = len(segs)
                            if part == "a":
                                pbt, pkt = bank()
                                ptv = pbt.bitcast(BF16).rearrange("p (a b) -> p a b", a=8)
                                for hh in range(4):
                                    off = 0
                                    for si, (kfn, vap, n) in enumerate(segs):
                                        tr(ptv[0:n, hh * nseg + si, 0:nq], Pm_[0:nq, hh, off:off + n], identb[0:nq, 0:nq], [kPm, "identb"], [pkt])
                                        off += n
                                for si, (kfn, vap, n) in enumerate(segs):
                                    acopy(PT[0:n, si:4 * nseg:nseg, 0:nq], ptv[0:n, si:4 * nseg:nseg, 0:nq], [pkt], ["PT"])
                                return
                            pbo, pko = bank()
                            for hh in range(4):
                                for si, (kfn, vap, n) in enumerate(segs):
                                    mm(pbo[0:nq, hh * 64:hh * 64 + 64], PT[0:n, hh * nseg + si, 0:nq], vap[0:n, 64 * hf:64 * hf + 64],
                                       si == 0, si == nseg - 1, ["PT", "vtok", "Vs", "vnew"], [pko])
                            tt(att[0:nq, :].rearrange("p (h d) -> p h d", h=4), pbo[0:nq, 0:256].rearrange("p (h d) -> p h d", h=4),
                               rinv_[0:nq, :].unsqueeze(2).to_broadcast([nq, 4, 64]), ALU.mult, [pko, krinv], ["att"])
                            tt(gA_[0:nq, 256 * hf:256 * hf + 256], att[0:nq, :], B["ga"][:, 256 * hf:256 * hf + 256], ALU.mult,
                               ["att", B["ga_key"]], [kgA])
                            if hf == 1:
                                pbt, pkt = bank()
                                ptv = pbt.bitcast(BF16).rearrange("p (a b) -> p a b", a=8)
                                for cch in range(4):
                                    tr(ptv[:, cch, 0:nq], gA_[0:nq, cch * 128:(cch + 1) * 128], identb[0:nq, 0:nq], [kgA, "identb"], [pkt])
                                acopy(gatedT[:, :, B["col"]:B["col"] + nq], ptv[:, 0:4, 0:nq], [pkt], ["gatedT"])

                        nu = len(units)
                        S1(0); S2(0, "a"); S2(0, "b"); S2(0, "c")
                        if nu > 1:
                            S1(1)
                        for u in range(nu):
                            if u + 1 < nu:
                                S2(u + 1, "a")
                            S3(u, "a")
                            if u + 1 < nu:
                                S2(u + 1, "b")
                            S3(u, "b")
                            if u + 1 < nu:
                                S2(u + 1, "c")
                            if u + 2 < nu:
                                S1(u + 2)

                    if not sample:
                        blocks = []
                        for s in range(4):
                            blk = 4 * t + s
                            if blk == 0:
                                segs = [(lambda kvh, s=s: kTd[:, l, kvh, 128 + 128 * s:256 + 128 * s], vtok[:, l, 1 + s, :], 128)]
                                blocks.append(dict(nq=128, segs=segs, nkeys=128, tb_c0=128, ga=gatok[:, s, :], ga_key="gatok", col=128 * s))
                            else:
                                segs = [(lambda kvh, s=s: kTd[:, l, kvh, 128 * s:128 * s + 128], vtok[:, l, s, :], 128),
                                        (lambda kvh, s=s: kTd[:, l, kvh, 128 + 128 * s:256 + 128 * s], vtok[:, l, 1 + s, :], 128)]
                                blocks.append(dict(nq=128, segs=segs, nkeys=256, tb_c0=0, ga=gatok[:, s, :], ga_key="gatok", col=128 * s))
                        attend_all(blocks)
                        vcopy(kTd[:, l, :, 0:128], kTd[:, l, :, TILE:TILE + 128], ["kTd"], ["kTd"])
                        vcopy(vtok[:, l, 0, :], vtok[:, l, 4, :], ["vtok"], ["vtok"])
                    else:
                        kb.dma("pool", Ks[:], ck[l].rearrange("s w f -> w s f"), (), ["Ks"])
                        kb.dma("pool", Vs[:], cv[l].rearrange("s w f -> w s f"), (), ["Vs"])
                        for i in range(NS):
                            pbt, pkt = bank()
                            ptv = pbt.bitcast(BF16).rearrange("p (a b) -> p a b", a=8)
                            vcopy(Kd[:], Ks[:, i, :].rearrange("w (h d) -> w h d", h=2).unsqueeze(2).to_broadcast([128, 2, 2, 64]), ["Ks"], ["Kd"])
                            for kvh in range(2):
                                tr(ptv[:, kvh, :], Kd[:, kvh, :, :].rearrange("w j d -> w (j d)"), identb[:], ["Kd", "identb"], [pkt])
                            acopy(KTs[:, i, :, 0:128], ptv[:, 0:2, :], [pkt], ["KTs"])
                        vcopy(KTs[:, :, :, 128:132], kTn[:].rearrange("p h (s t) -> p s h t", t=TS), ["kTn"], ["KTs"])
                        blocks = []
                        for i in range(NS):
                            def prep(i=i):
                                pbg, pkg = bank()
                                mm(pbg[0:TS, 0:512], identb[0:NTS, TS * i:TS * i + TS], gatok[0:NTS, 0, :], True, True, ["identb", "gatok"], [pkg])
                                acopy(gas2[i % 2][:, :], pbg[0:TS, 0:512], [pkg], [f"gas{i % 2}"])
                            segs = [(lambda kvh, i=i: KTs[:, i, kvh, 0:128], Vs[:, i, :], 128),
                                    (lambda kvh, i=i: KTs[:, i, kvh, 128:132], vnew[:, i, :], TS)]
                            blocks.append(dict(nq=TS, segs=segs, nkeys=132, tb_c0=0, ga=gas2[i % 2][:, :], ga_key=f"gas{i % 2}", col=TS * i, prep=prep))
                        attend_all(blocks)
                    kb.retire(ph_keys)
                if KDBG and t == 0 and l == 0:
                    dstg = PA("dstg", [128, 4, TILE])
                    acopy(dstg[:], gatedT[:], ["gatedT"], ["dstg"])
                    kb.dma("sp", dbg_x[:, 0:2048], dstg[:].rearrange("p a b -> p (a b)"), ["dstg"], (), final=True)
                    kb.dma("sp", dbg_x[:, 2048:3072], xi[:, 0, :], [xik], (), final=True)
                branch_out(0)
                if KDBG and t == 0 and l == 0:
                    kb.dma("sp", dbg_m[0], merged[:].rearrange("p a b -> p (a b)"), ["merged"], (), final=True)

                with contextlib.ExitStack() as ph:
                    ph_keys = []

                    def PH(name, shape, dt=F32, ph=ph, ph_keys=ph_keys):
                        ph_keys.append(name)
                        return S(name, shape, dt, stack=ph)
                    nrot[0] = 6
                    bank_rr[0] = 0
                    uT = PH("uT", [128, 4, ntok], BF16); yz = PH("yz", [128, 4, ntok]); gbT = PH("gbT", [128, 4, ntok], BF16)
                    zb = PH("zb", [128, 4, ntok], BF16)
                    NXS = 1 if sample else 2
                    XRs = [PH(f"XR{i}", [128, 16, SJ]) for i in range(NXS)]; XIs = [PH(f"XI{i}", [128, 16, SJ]) for i in range(NXS)]
                    T1 = PH("T1", [128, 16, SJ]); T2 = PH("T2", [128, 16, SJ])
                    HR = PH("HR", [128, 16, SJ], BF16); HI = PH("HI", [128, 16, SJ], BF16)
                    g1 = PH("g1", [128, ntok]); g2b = PH("g2b", [128, ntok])
                    NH = NS if sample else 1
                    hfr = PH("hfr", [128, 16, NH]); hfi = PH("hfi", [128, 16, NH]); s5a = PH("s5a", [128, 16, NH]); s5b = PH("s5b", [128, 16, NH])
                    if sample:
                        h0r = PH("h0r", [128, 16, NS]); h0i = PH("h0i", [128, 16, NS])
                        st_io = PH("st_io", [NS, 2048])
                        for ssrc, sdst, skey_ in ((sre, h0r, "h0r"), (sim, h0i, "h0i")):
                            kb.dma("sp", st_io[:], ssrc[l], ["st_io"], ["st_io"])
                            pbs, pks = bank()
                            for k in range(16):
                                tr(pbs[:, k * NS:(k + 1) * NS], st_io[0:NS, k * 128:(k + 1) * 128], ident[0:NS, 0:NS], ["st_io", "ident"], [pks])
                            acopy(sdst[:].rearrange("p k s -> p (k s)"), pbs[:, 0:16 * NS], [pks], [skey_])
                    for q in range(4):
                        if q % 2 == 0:
                            slot, skey = wget(f"u{q // 2}")
                        pb, pk = proj_fm(slot, skey, q % 2, 8, xTk, ["xT"])
                        acopy(uT[:, q, 0:ntok], pb[:, 0:ntok], [pk], ["uT"])
                        act(yz[:, q, 0:ntok], pb[:, 0:ntok], AF.Copy, [pk, "dS5"], ["yz"], scale=dS5[:, l, q:q + 1])
                    for q in range(4):
                        if q % 2 == 0:
                            slot, skey = wget(f"gb{q // 2}")
                        pb, pk = proj_fm(slot, skey, q % 2, 8, xTk, ["xT"])
                        act(gbT[:, q, 0:ntok], pb[:, 0:ntok], AF.Silu, [pk], ["gbT"])

                    cosl = COS[:, l, :, :]; sinl = SIN[:, l, :, :]
                    nsj = 1 if sample else TILE // SJ
                    def stage1(jt):
                        c0 = jt * SJ
                        XR = XRs[jt % NXS]; XI = XIs[jt % NXS]; kXR = f"XR{jt % NXS}"; kXI = f"XI{jt % NXS}"
                        pxr, kxr = bank2(); pxi, kxi = bank2()
                        for k in range(16):
                            h_ = (k % 4) // 2
                            pk_ = h_ * 8 + (k // 4) * 2 + k % 2
                            mm(pxr[:, pk_ * SJ:(pk_ + 1) * SJ], WBre[64 * h_:64 * h_ + 64, l, k // 4, k % 2, :], uT[64 * h_:64 * h_ + 64, k // 4, c0:c0 + SJ],
                               True, True, ["WBre", "uT"], [kxr[h_]])
                        for k in range(16):
                            h_ = (k % 4) // 2
                            pk_ = h_ * 8 + (k // 4) * 2 + k % 2
                            mm(pxi[:, pk_ * SJ:(pk_ + 1) * SJ], WBim[64 * h_:64 * h_ + 64, l, k // 4, k % 2, :], uT[64 * h_:64 * h_ + 64, k // 4, c0:c0 + SJ],
                               True, True, ["WBim", "uT"], [kxi[h_]])
                        for h_ in range(2):
                            acopy(XR[:].rearrange("p (q h kk) j -> p h q kk j", h=2, kk=2)[:, h_], pxr[:, 512 * h_:512 * h_ + 512].rearrange("p (q kk j) -> p q kk j", q=4, kk=2),
                                  [kxr[h_]], [kXR])
                            acopy(XI[:].rearrange("p (q h kk) j -> p h q kk j", h=2, kk=2)[:, h_], pxi[:, 512 * h_:512 * h_ + 512].rearrange("p (q kk j) -> p q kk j", q=4, kk=2),
                                  [kxi[h_]], [kXI])

                    stage1(0)
                    pend = []
                    for jt in range(nsj):
                        c0 = jt * SJ
                        if jt + 1 < nsj:
                            stage1(jt + 1)
                        if lru_units:
                            lru_units.pop(0)()
                        XR = XRs[jt % NXS]; XI = XIs[jt % NXS]; kXR = f"XR{jt % NXS}"; kXI = f"XI{jt % NXS}"
                        if sample:
                            cosv = cosl[:, :, 0:TS].unsqueeze(2).to_broadcast([128, 16, NS, TS])
                            sinv = sinl[:, :, 0:TS].unsqueeze(2).to_broadcast([128, 16, NS, TS])
                            V = lambda a: a[:].rearrange("p k (s t) -> p k s t", t=TS)
                            magt = MAGT[:, l, :, :]
                            memset(MAGT[:, l, :, :].rearrange("p k (s t) -> p k s t", t=TS)[:, :, :, 0:1], 0.0, ["MAGT"])
                        else:
                            cosv = cosl[:, :, 0:SJ]; sinv = sinl[:, :, 0:SJ]
                            V = lambda a: a[:]
                            magt = MAGT[:, l, :, :]
                        tt(V(T1), V(XR), cosv, ALU.mult, [kXR, "COS"], ["T1"])
                        tt(V(T2), V(XI), sinv, ALU.mult, [kXI, "SIN"], ["T2"])
                        tt(T1[:], T1[:], T2[:], ALU.add, ["T1", "T2"], ["T1"])
                        tt(V(T2), V(XR), sinv, ALU.mult, [kXR, "SIN"], ["T2"])
                        tt(V(XI), V(XI), cosv, ALU.mult, [kXI, "COS"], [kXI])
                        tt(XI[:], XI[:], T2[:], ALU.subtract, [kXI, "T2"], [kXI])
                        if sample:
                            tt(s5a[:], h0r[:], AbR[:, l, :].unsqueeze(2).to_broadcast([128, 16, NS]), ALU.mult, ["h0r", "Ab"], ["s5a"])
                            tt(s5b[:], h0i[:], AbI[:, l, :].unsqueeze(2).to_broadcast([128, 16, NS]), ALU.mult, ["h0i", "Ab"], ["s5b"])
                            tt(s5a[:], s5a[:], s5b[:], ALU.subtract, ["s5a", "s5b"], ["s5a"])
                            tt(V(T1)[:, :, :, 0], V(T1)[:, :, :, 0], s5a[:], ALU.add, ["T1", "s5a"], ["T1"])
                            tt(s5a[:], h0r[:], AbI[:, l, :].unsqueeze(2).to_broadcast([128, 16, NS]), ALU.mult, ["h0r", "Ab"], ["s5a"])
                            tt(s5b[:], h0i[:], AbR[:, l, :].unsqueeze(2).to_broadcast([128, 16, NS]), ALU.mult, ["h0i", "Ab"], ["s5b"])
                            tt(s5a[:], s5a[:], s5b[:], ALU.add, ["s5a", "s5b"], ["s5a"])
                            tt(V(XI)[:, :, :, 0], V(XI)[:, :, :, 0], s5a[:], ALU.add, [kXI, "s5a"], [kXI])
                        else:
                            tt(T1[:, :, 0], T1[:, :, 0], Gre[:, l, :], ALU.add, ["T1", "G"], ["T1"])
                            tt(XI[:, :, 0], XI[:, :, 0], Gim[:, l, :], ALU.add, [kXI, "G"], [kXI])
                        flat = lambda a: a[:].rearrange("p k j -> p (k j)")
                        mflat = magt.rearrange("p k j -> p (k j)")
                        kb.op("dve", lambda e, o=flat(XR), a=mflat, b=flat(T1): e.tensor_tensor_scan(out=o, data0=a, data1=b, initial=0.0,
                                                                                                     op0=ALU.mult, op1=ALU.add),
                              ["T1", "MAGT"], [kXR])
                        kb.op("dve", lambda e, o=flat(T2), a=mflat, b=flat(XI): e.tensor_tensor_scan(out=o, data0=a, data1=b, initial=0.0,
                                                                                                     op0=ALU.mult, op1=ALU.add),
                              [kXI, "MAGT"], ["T2"])
                        if pend:
                            pend.pop()()
                        pby, pky = pst[3][:, (jt % 2) * 512:(jt % 2) * 512 + 512], f"pb{6 + jt % 2}"
                        if sample:
                            tt(V(T1), V(XR), cosv, ALU.mult, [kXR, "COS"], ["T1"])
                            tt(V(XI), V(T2), sinv, ALU.mult, ["T2", "SIN"], [kXI])
                            tt(HR[:], T1[:], XI[:], ALU.subtract, ["T1", kXI], ["HR"])
                            tt(V(T1), V(XR), sinv, ALU.mult, [kXR, "SIN"], ["T1"])
                            tt(V(XI), V(T2), cosv, ALU.mult, ["T2", "COS"], [kXI])
                            tt(HI[:], T1[:], XI[:], ALU.add, ["T1", kXI], ["HI"])
                            for q in range(4):
                                for kk in range(4):
                                    k = 4 * q + kk
                                    mm(pby[:, q * SJ:(q + 1) * SJ], WCre[:, l, k, :], HR[:, k, :], kk == 0, False, ["WCre", "HR"], [pky])
                                    mm(pby[:, q * SJ:(q + 1) * SJ], WCim[:, l, k, :], HI[:, k, :], False, kk == 3, ["WCim", "HI"], [pky])
                        else:
                            H2v = zb[:].rearrange("p q t -> p (q t)").rearrange("p (a k j) -> p a k j", a=2, k=16)
                            HR2, HI2 = H2v[:, 0], H2v[:, 1]
                            tt(HR[:], XR[:], cosv, ALU.mult, [kXR, "COS"], ["HR"])
                            stt(HR2, T2[:], -1.0, sinv, ALU.mult, ALU.mult, ["T2", "SIN"], ["zb"])
                            tt(HI[:], XR[:], sinv, ALU.mult, [kXR, "SIN"], ["HI"])
                            tt(HI2, T2[:], cosv, ALU.mult, ["T2", "COS"], ["zb"])
                            for q in range(4):
                                for kk in range(4):
                                    k = 4 * q + kk
                                    mm(pby[:, q * SJ:(q + 1) * SJ], WCre[:, l, k, :], HR[:, k, :], kk == 0, False, ["WCre", "HR"], [pky])
                                    mm(pby[:, q * SJ:(q + 1) * SJ], WCre[:, l, k, :], HR2[:, k, :], False, False, ["WCre", "zb"], [pky])
                                    mm(pby[:, q * SJ:(q + 1) * SJ], WCim[:, l, k, :], HI[:, k, :], False, False, ["WCim", "HI"], [pky])
                                    mm(pby[:, q * SJ:(q + 1) * SJ], WCim[:, l, k, :], HI2[:, k, :], False, kk == 3, ["WCim", "zb"], [pky])
                        pend.append(lambda c0=c0, pby=pby, pky=pky: tt(yz[:, :, c0:c0 + SJ], yz[:, :, c0:c0 + SJ],
                                                                        pby[:, 0:4 * SJ].rearrange("p (q j) -> p q j", q=4), ALU.add, ["yz", pky], ["yz"]))
                        if not sample:
                            tt(s5a[:, :, 0], XR[:, :, SJ - 1], CaR[:, l, :], ALU.mult, [kXR, "Ca"], ["s5a"], eng="pool")
                            tt(s5b[:, :, 0], T2[:, :, SJ - 1], CaI[:, l, :], ALU.mult, ["T2", "Ca"], ["s5b"], eng="pool")
                            tt(Gre[:, l, :], s5a[:, :, 0], s5b[:, :, 0], ALU.subtract, ["s5a", "s5b"], ["G"], eng="pool")
                            tt(s5a[:, :, 0], XR[:, :, SJ - 1], CaI[:, l, :], ALU.mult, [kXR, "Ca"], ["s5a"], eng="pool")
                            tt(s5b[:, :, 0], T2[:, :, SJ - 1], CaR[:, l, :], ALU.mult, ["T2", "Ca"], ["s5b"], eng="pool")
                            tt(Gim[:, l, :], s5a[:, :, 0], s5b[:, :, 0], ALU.add, ["s5a", "s5b"], ["G"], eng="pool")
                        if (last_prompt and jt == nsj - 1) or sample:
                            if sample:
                                gr = V(XR)[:, :, :, TS - 1]; gi = V(T2)[:, :, :, TS - 1]
                                cl = cosl[:, :, TS - 1:TS].to_broadcast([128, 16, NS]); sl = sinl[:, :, TS - 1:TS].to_broadcast([128, 16, NS])
                                o1, o2, a_, b_ = hfr[:], hfi[:], s5a[:], s5b[:]
                            else:
                                gr = XR[:, :, SJ - 1]; gi = T2[:, :, SJ - 1]
                                cl = cosl[:, :, SJ - 1]; sl = sinl[:, :, SJ - 1]
                                o1, o2, a_, b_ = hfr[:, :, 0], hfi[:, :, 0], s5a[:, :, 0], s5b[:, :, 0]
                            tt(a_, gr, cl, ALU.mult, [kXR, "COS"], ["s5a"])
                            tt(b_, gi, sl, ALU.mult, ["T2", "SIN"], ["s5b"])
                            tt(o1, a_, b_, ALU.subtract, ["s5a", "s5b"], ["hfr"])
                            tt(a_, gr, sl, ALU.mult, [kXR, "SIN"], ["s5a"])
                            tt(b_, gi, cl, ALU.mult, ["T2", "COS"], ["s5b"])
                            tt(o2, a_, b_, ALU.add, ["s5a", "s5b"], ["hfi"])
                            if sample:
                                for hsrc, hkey, hdst in ((hfr, "hfr", nres), (hfi, "hfi", nims)):
                                    pbA, pkA = bank2(); pbB, pkB = bank2()
                                    for k in range(16):
                                        pbv, pkv = (pbA, pkA) if k < 8 else (pbB, pkB)
                                        tr(pbv[0:NS, (k % 8) * 128:(k % 8 + 1) * 128], hsrc[:, k, :], ident[:], [hkey, "ident"], [pkv[(k % 8) // 4]])
                                    acopy(st_io[:, 0:1024], pbA[0:NS, :], pkA, ["st_io"])
                                    acopy(st_io[:, 1024:2048], pbB[0:NS, :], pkB, ["st_io"])
                                    kb.dma("sp", hdst[l], st_io[:], ["st_io"], (), final=True)
                            else:
                                kb.dma("sp", nrep[l].rearrange("(k p) -> p k", p=128), hfr[:, :, 0], ["hfr"], (), final=True)
                                kb.dma("sp", nimp[l].rearrange("(k p) -> p k", p=128), hfi[:, :, 0], ["hfi"], (), final=True)
                    while lru_units:
                        lru_units.pop(0)()
                    while pend:
                        pend.pop()()
                    for q in range(4):
                        yv = yz[:, q, 0:ntok]
                        gq, kgq = (g1, "g1") if q % 2 == 0 else (g2b, "g2b")
                        kyz = f"yz{q}"
                        tt(gq[:, 0:ntok], yv, yv, ALU.mult, ["yz"], [kgq])
                        ts(gq[:, 0:ntok], gq[:, 0:ntok], 0.044715, ALU.mult, [kgq], [kgq], s2=1.0, op1=ALU.add)
                        tt(gq[:, 0:ntok], gq[:, 0:ntok], yv, ALU.mult, [kgq, "yz"], [kgq])
                        act(gq[:, 0:ntok], gq[:, 0:ntok], AF.Sigmoid, [kgq], [kgq], scale=1.5957691216)
                        tt(yv, yv, gq[:, 0:ntok], ALU.mult, ["yz", kgq], [kyz])
                        acopy(zb[:, q, 0:ntok], yv, [kyz], ["zb"])
                    slot, skey = wget("glu")
                    for fo in range(4):
                        pb, pk = proj_fm(slot, skey, fo, 4, lambda c: zb[:, c, 0:ntok], ["zb"])
                        act(g1[:, 0:ntok], pb[:, 0:ntok], AF.Sigmoid, [pk], ["g1"])
                        tt(g2b[:, 0:ntok], yz[:, fo, 0:ntok], g1[:, 0:ntok], ALU.mult, [f"yz{fo}", "g1"], ["g2b"])
                        tt(gatedT[:, fo, 0:ntok], g2b[:, 0:ntok], gbT[:, fo, 0:ntok], ALU.mult, ["g2b", "gbT"], ["gatedT"])
                    nrot[0] = 8
                    ph_keys.extend([f"yz{q}" for q in range(4)])
                    kb.retire(ph_keys)
                branch_out(1)
                if KDBG and t == 0 and l == 0:
                    kb.dma("sp", dbg_m[1], merged[:].rearrange("p a b -> p (a b)"), ["merged"], (), final=True)

                with contextlib.ExitStack() as ph:
                    ph_keys = []

                    def PH(name, shape, dt=F32, ph=ph, ph_keys=ph_keys):
                        ph_keys.append(name)
                        return S(name, shape, dt, stack=ph)
                    if sample:
                        h0l = PH("h0l", [128, 4, NS]); hso = PH("hso", [128, 4, NS])
                        l_io = PH("l_io", [NS, 512]); c_in = PH("c_in", [3 * NS, 512])
                        kb.dma("sp", l_io[:], slru[l], (), ["l_io"])
                        kb.dma("sp", c_in[:], sconv[l].rearrange("s j f -> (s j) f"), (), ["c_in"])
                        pbs, pks = bank()
                        for q in range(4):
                            tr(pbs[:, q * NS:(q + 1) * NS], l_io[0:NS, q * 128:(q + 1) * 128], ident[0:NS, 0:NS], ["l_io", "ident"], [pks])
                        acopy(h0l[:].rearrange("p q s -> p (q s)"), pbs[:, 0:4 * NS], [pks], ["h0l"])
                        pbs, pks = bank()
                        for q in range(4):
                            tr(pbs[:, q * 3 * NS:(q + 1) * 3 * NS], c_in[0:3 * NS, q * 128:(q + 1) * 128], ident[0:3 * NS, 0:3 * NS], ["c_in", "ident"], [pks])
                        for q in range(4):
                            acopy(cbuf[:, q, :, 0:3], pbs[:, q * 3 * NS:(q + 1) * 3 * NS].rearrange("p (s j) -> p s j", j=3), [pks], ["cbuf"])
                    lbuf = {}
                    for nm_, dt_ in (("cvv", F32), ("cvb", BF16), ("rg", F32), ("ig", F32), ("aa", F32), ("sq", F32), ("hT", F32)):
                        lbuf[nm_] = [PH(f"{nm_}{i}", [128, ntok], dt_) for i in range(2)]
                    tl = PH("tl", [128, NS])
                    hfo = PH("hfo", [128, 4])
                    if sample:
                        for tt_ in range(1, TS):
                            kb.dma("sp", ncvs[l][:, tt_ - 1, :], xco[tt_:NTS:TS, :], ["xco"], (), final=True)
                    elif last_prompt:
                        kb.dma("sp", ncvp[l], xco[0:3, :], ["xco"], (), final=True)
                    def lru_chunk(q):
                        cvv, cvb, rg, ig, aa, sq, hT = (lbuf[n_][q % 2] for n_ in ("cvv", "cvb", "rg", "ig", "aa", "sq", "hT"))
                        kcvv, kcvb, krg, kig, kaa, ksq, khT = (f"{n_}{q % 2}" for n_ in ("cvv", "cvb", "rg", "ig", "aa", "sq", "hT"))
                        if sample:
                            win = lambda j: cbuf[:, q, :, j:j + TS]
                            V2 = lambda a: a[:, 0:ntok].rearrange("p (s t) -> p s t", t=TS)
                        else:
                            win = lambda j: cbuf[:, q, j:j + ntok]
                            V2 = lambda a: a[:, 0:ntok]
                        ts(V2(cvv), win(0), cw[:, l, q, 0:1], ALU.mult, ["cbuf", "cw", "cb"], [kcvv], s2=cb[:, l, q:q + 1], op1=ALU.add)
                        for j in range(1, 4):
                            stt(V2(cvv), win(j), cw[:, l, q, j:j + 1], V2(cvv), ALU.mult, ALU.add, ["cbuf", "cw", kcvv], [kcvv])
                        acopy(cvb[:, 0:ntok], cvv[:, 0:ntok], [kcvv], [kcvb])
                        pb, pk = bank()
                        mm(pb[:, 0:ntok], WA[:, l, q, :], cvb[:, 0:ntok], True, True, ["WA", kcvb], [pk])
                        act(rg[:, 0:ntok], pb[:, 0:ntok], AF.Sigmoid, [pk, "ba"], [krg], bias=ba[:, l, q:q + 1], scale=1.0)
                        pb, pk = bank()
                        mm(pb[:, 0:ntok], WX[:, l, q, :], cvb[:, 0:ntok], True, True, ["WX", kcvb], [pk])
                        act(ig[:, 0:ntok], pb[:, 0:ntok], AF.Sigmoid, [pk, "bx"], [kig], bias=bx[:, l, q:q + 1], scale=1.0)
                        act(sq[:, 0:ntok], rg[:, 0:ntok], AF.Exp, [krg, "c2"], [ksq], scale=c2[:, l, q:q + 1])
                        act(sq[:, 0:ntok], sq[:, 0:ntok], AF.Ln, [ksq], [ksq], scale=-1.0, bias=1.0)
                        act(sq[:, 0:ntok], sq[:, 0:ntok], AF.Exp, [ksq], [ksq], scale=0.5)
                        act(aa[:, 0:ntok], rg[:, 0:ntok], AF.Exp, [krg, "c1"], [kaa], scale=c1[:, l, q:q + 1])
                        tt(ig[:, 0:ntok], ig[:, 0:ntok], cvv[:, 0:ntok], ALU.mult, [kig, kcvv], [kig])
                        tt(ig[:, 0:ntok], ig[:, 0:ntok], sq[:, 0:ntok], ALU.mult, [kig, ksq], [kig])
                        if sample:
                            tt(tl[:], V2(aa)[:, :, 0], h0l[:, q, :], ALU.mult, [kaa, "h0l"], ["tl"])
                            tt(V2(ig)[:, :, 0], V2(ig)[:, :, 0], tl[:], ALU.add, [kig, "tl"], [kig])
                            memset(V2(aa)[:, :, 0:1], 0.0, [kaa])
                            kb.op("dve", lambda e: e.tensor_tensor_scan(out=hT[:, 0:ntok], data0=aa[:, 0:ntok], data1=ig[:, 0:ntok], initial=0.0,
                                                                        op0=ALU.mult, op1=ALU.add), [kaa, kig], [khT])
                            vcopy(hso[:, q, :], V2(hT)[:, :, TS - 1], [khT], ["hso"])
                        else:
                            kb.op("dve", lambda e, q=q: e.tensor_tensor_scan(out=hT[:, 0:ntok], data0=aa[:, 0:ntok], data1=ig[:, 0:ntok],
                                                                             initial=hl[:, l, q:q + 1], op0=ALU.mult, op1=ALU.add),
                                  [kaa, kig, "hl"], [khT])
                            vcopy(hl[:, l, q:q + 1], hT[:, ntok - 1:ntok], [khT], ["hl"])
                        tt(gatedT[:, q, 0:ntok], hT[:, 0:ntok], gcT[:, q, 0:ntok], ALU.mult, [khT, "gcT"], ["gatedT"])

                    for q in range(4):
                        lru_chunk(q)
                    if sample:
                        pbs, pks = bank()
                        for q in range(4):
                            tr(pbs[0:NS, q * 128:(q + 1) * 128], hso[:, q, :], ident[:], ["hso", "ident"], [pks])
                        acopy(l_io[:], pbs[0:NS, 0:512], [pks], ["l_io"])
                        kb.dma("sp", nlrus[l], l_io[:], ["l_io"], (), final=True)
                    elif last_prompt:
                        vcopy(hfo[:], hl[:, l, :], ["hl"], ["hfo"])
                        kb.dma("sp", nlrup[l].rearrange("(q p) -> p q", p=128), hfo[:], ["hfo"], (), final=True)
                    kb.retire(ph_keys)
                mTh["mT"] = PA("mT", [128, 8, ntok], BF16)
                branch_out(2)

                with contextlib.ExitStack() as ph:
                    ph_keys = []

                    def PH(name, shape, dt=F32, ph=ph, ph_keys=ph_keys):
                        ph_keys.append(name)
                        return S(name, shape, dt, stack=ph)
                    lng = PH("lng", [128, D]); lnb = PH("lnb", [128, D])
                    st6 = PH("st6", [128, 2, 6]); mv = PH("mv", [128, 2]); rstd = PH("rstd", [128, 1])
                    kb.dma("sp", lng[:], ln_g[l].partition_broadcast(128), (), ["lng"])
                    kb.dma("sp", lnb[:], ln_b[l].partition_broadcast(128), (), ["lnb"])
                    wos = [wget(f"wo{wo}", ahead=RING - wo) for wo in range(4)]
                    for s in range(nsub):
                        for wo in range(4):
                            slot, skey = wos[wo]
                            pb, pk = bank()
                            for fc in range(8):
                                mm(pb[0:R, 0:256], mTh["mT"][:, fc, s * 128:s * 128 + R], v8(slot)[:, fc, :], fc == 0, fc == 7, [skey, "mT"], [pk])
                            stt(xo[0:R, s, wo * 256:wo * 256 + 256], xi[0:R, s, wo * 256:wo * 256 + 256], DN_ALPHA, pb[0:R, 0:256],
                                ALU.mult, ALU.add, [xik, pk], [xok])
                        hv = xo[0:R, s, :]
                        for hh in range(2):
                            kb.op("dve", lambda e, hh=hh, hv=hv: e.bn_stats(out=st6[0:R, hh, :], in_=hv[:, hh * 512:hh * 512 + 512]), [xok], ["st6"])
                        kb.op("dve", lambda e: e.bn_aggr(out=mv[0:R, :], in_=st6[0:R, :, :].rearrange("p a b -> p (a b)")), ["st6"], ["mv"])
                        act(rstd[0:R, :], mv[0:R, 1:2], AF.Sqrt, ["mv", "epsc"], ["rstd"], bias=epsc[0:R, :], scale=1.0)
                        kb.op("dve", lambda e: e.reciprocal(rstd[0:R, :], rstd[0:R, :]), ["rstd"], ["rstd"])
                        ts(hv, hv, mv[0:R, 0:1], ALU.subtract, [xok, "mv", "rstd"], [xok], s2=rstd[0:R, 0:1], op1=ALU.mult)
                        tt(hv, hv, lng[0:R, :], ALU.mult, [xok, "lng"], [xok])
                        tt(hv, hv, lnb[0:R, :], ALU.add, [xok, "lnb"], [xok])
                        if l == DEPTH - 1 or os.environ.get("KDUMPL0"):
                            if sample:
                                kb.dma("sp", ys[:, :], hv, [xok], (), final=True)
                            else:
                                r0 = t * TILE + s * 128
                                kb.dma("sp", yp[r0:r0 + 128, :], hv, [xok], (), final=True)
                    kb.retire(ph_keys)
                kb.retire(pa_keys)

        for (t, l) in all_passes:
            if l == 0:
                if t == NPT:
                    kb.dma("sp", xtok[0][0:NTS, 0, :], xs, (), ["xtok0"])
                else:
                    kb.dma("sp", xtok[0][:], xp[t * TILE:(t + 1) * TILE, :].rearrange("(s p) d -> p s d", p=128), (), ["xtok0"])
            run_pass(t, l, l % 2, (l + 1) % 2)
        nx = int(os.environ.get("KEXTRA", "0"))
        if nx:
            kb.maxops = 10 ** 9
            pbx, pkx = bank()
            for _ in range(nx):
                mm(pbx[:, 0:128], identb[:], identb[:], True, True, ["identb"], [pkx])
        print("recorded ops:", kb.nrec, {k: v for k, v in kb.cnt.items()})
        kb.emit()
    return nc


_NC_CACHE = {}


def kernel(**inputs):
    f = lambda k: np.ascontiguousarray(np.asarray(inputs[k], dtype=np.float32))
    oh, neg = _bias_consts()
    par = (np.arange(128) // 16) % 2
    pm = np.stack([1.0 - par, par], 1).astype(np.float32)
    shared = {
        "rel_bias": f("rel_bias"), "w_in": f("w_in"), "sinks": f("sinks"),
        "w_br_a": f("w_branch_a"), "w_br_b": f("w_branch_b"), "w_br_c": f("w_branch_c"),
        "lam_re": f("ssm_lambda_re").reshape(2, 2048), "lam_im": f("ssm_lambda_im").reshape(2, 2048),
        "log_step": f("ssm_log_step"), "b_re": f("ssm_b_re"), "b_im": f("ssm_b_im"),
        "c_re": f("ssm_c_re"), "c_im": f("ssm_c_im"), "ssm_d": f("ssm_d"), "w_glu": f("ssm_w_glu"),
        "conv_w": f("conv_w"), "conv_b": f("conv_b"), "lru_w_a": f("lru_w_a"), "lru_b_a": f("lru_b_a"),
        "lru_w_x": f("lru_w_x"), "lru_b_x": f("lru_b_x"), "lru_lam": f("lru_lambda"), "w_out": f("w_out"),
        "ln_g": f("ln_g"), "ln_b": f("ln_b"), "oh2": oh, "negmask": neg, "pmask": pm,
    }
    x_prompt = f("x_prompt"); x_sample = f("x_sample")
    cache_k = f("cache_k").reshape(2, 128, 128, 128); cache_v = f("cache_v").reshape(2, 128, 128, 128)
    s_re = f("state_ssm_re").reshape(2, 128, 2048); s_im = f("state_ssm_im").reshape(2, 128, 2048)
    s_lru = f("state_lru"); s_conv = f("state_conv")
    in_maps = []
    for c in range(NCORES):
        sl = slice(NS * c, NS * c + NS)
        m = dict(shared)
        m.update({
            "xp": x_prompt[c], "xs": np.ascontiguousarray(x_sample[sl].reshape(NTS, D)),
            "ck": np.ascontiguousarray(cache_k[:, sl]), "cv": np.ascontiguousarray(cache_v[:, sl]),
            "sre": np.ascontiguousarray(s_re[:, sl]), "sim": np.ascontiguousarray(s_im[:, sl]),
            "slru": np.ascontiguousarray(s_lru[:, sl]), "sconv": np.ascontiguousarray(s_conv[:, sl]),
        })
        in_maps.append(m)
    if "nc" not in _NC_CACHE:
        _NC_CACHE["nc"] = build_nc()
    res = run_bass_kernel_spmd(_NC_CACHE["nc"], in_maps, core_ids=list(range(NCORES)))
    r = res.results
    cat = lambda k, ax: np.concatenate([np.asarray(r[c][k])[None] if ax is None else np.asarray(r[c][k]) for c in range(NCORES)], axis=0 if ax is None else ax)
    y_prompt = np.stack([r[c]["yp"] for c in range(NCORES)], 0).astype(np.float32)
    y_sample = np.concatenate([r[c]["ys"].reshape(NS, TS, D) for c in range(NCORES)], 0).astype(np.float32)
    stk = lambda k, shp: np.stack([np.asarray(r[c][k]).reshape(shp) for c in range(NCORES)], 1).astype(np.float32)
    ccat = lambda k, shp: np.concatenate([np.asarray(r[c][k]).reshape(shp) for c in range(NCORES)], 1).astype(np.float32)
    return (y_prompt, y_sample,
            stk("nkp", (2, 128, 2, 64)), stk("nvp", (2, 128, 2, 64)),
            stk("nrep", (2, 32, 64)), stk("nimp", (2, 32, 64)),
            stk("nlrup", (2, 512)), stk("ncvp", (2, 3, 512)),
            ccat("nks", (2, NS, 128, 2, 64)), ccat("nvs", (2, NS, 128, 2, 64)),
            ccat("nres", (2, NS, 32, 64)), ccat("nims", (2, NS, 32, 64)),
            ccat("nlrus", (2, NS, 512)), ccat("ncvs", (2, NS, 3, 512)))
```

```python
import contextlib
import math
import os
import numpy as np
import concourse.bass as bass
import concourse.mybir as mybir
from concourse.bass_utils import run_bass_kernel_spmd

F32 = mybir.dt.float32
BF16 = mybir.dt.bfloat16
I32 = mybir.dt.int32
AF = mybir.ActivationFunctionType
ALU = mybir.AluOpType
AX = mybir.AxisListType

NCORES = 8
D = 1024
DEPTH = 2
SEQ = 2048
NS = 16
TS = 4
NTS = NS * TS
TILE = 512
NPT = SEQ // TILE
NEG = -1e30
DN_ALPHA = (2 * DEPTH) ** 0.25
LN_EPS = 1e-5
SJ = 64
COMPUTE = ("pe", "dve", "act", "pool")


class KB:
    def __init__(self, nc, n_dma_sems=16):
        self.nc = nc
        self.eng = {"pe": nc.tensor, "dve": nc.vector, "act": nc.scalar, "pool": nc.gpsimd, "sp": nc.sync}
        self.streams = {k: [] for k in self.eng}
        self.sem = {k: nc.alloc_semaphore(name=f"c_{k}") for k in COMPUTE}
        self.cnt = {k: 0 for k in COMPUTE}
        self.waited = {}
        self.lastw = {}
        self.readers = {}
        self.dpool = {}
        for q in ("sp", "pool"):
            self.dpool[q] = {"sems": [nc.alloc_semaphore(name=f"d_{q}{i}") for i in range(n_dma_sems)],
                             "uses": [0] * n_dma_sems, "next": 0}
        self.final_tokens = []
        self.dma_since_barrier = []
        self.nrec = 0
        self.alias_pending = {}
        self.maxops = int(os.environ.get("KMAXOPS", "1000000000"))
        self.dummy = (self.sem["pe"], 0, "pe")

    def _wait(self, e, tok):
        sem, val, owner = tok
        if owner == "pe" and e == "pe":
            return
        key = (e, sem.name)
        if self.waited.get(key, 0) >= val:
            return
        self.waited[key] = val
        eng = self.eng[e]
        self.streams[e].append(lambda eng=eng, sem=sem, val=val: eng.wait_ge(sem, val))

    def retire(self, keys):
        for k in keys:
            toks = []
            if k in self.lastw:
                toks.append(self.lastw.pop(k))
            toks.extend(self.readers.pop(k, []))
            for t in toks:
                cur = self.alias_pending.get(t[0].name)
                if cur is None or cur[1] < t[1]:
                    self.alias_pending[t[0].name] = t

    def _deps(self, e, reads, writes):
        for k in writes:
            if k not in self.lastw and k not in self.readers:
                for t in self.alias_pending.values():
                    self._wait(e, t)
                break
        toks = []
        for k in reads:
            if k in self.lastw:
                toks.append(self.lastw[k])
        for k in writes:
            if k in self.lastw:
                toks.append(self.lastw[k])
            toks.extend(self.readers.get(k, []))
        for t in toks:
            self._wait(e, t)

    def _commit(self, tok, reads, writes):
        for k in writes:
            self.lastw[k] = tok
            self.readers[k] = []
        for k in reads:
            self.readers.setdefault(k, []).append(tok)

    def op(self, e, fn, reads=(), writes=()):
        self.nrec += 1
        if self.nrec > self.maxops:
            return self.dummy
        if self.nrec == int(os.environ.get("KTRACE", "-1")):
            import traceback
            traceback.print_stack()
            for k in list(reads) + list(writes):
                print("KEY", k, "lastw", (self.lastw[k][0].name, self.lastw[k][1]) if k in self.lastw else None,
                      "readers", [(t[0].name, t[1]) for t in self.readers.get(k, [])])
            print("CNT", self.cnt, {q: p["uses"] for q, p in self.dpool.items()})
        self._deps(e, reads, writes)
        self.cnt[e] += 1
        tok = (self.sem[e], self.cnt[e], e)
        sem = self.sem[e]
        eng = self.eng[e]
        self.streams[e].append(lambda eng=eng, fn=fn, sem=sem: fn(eng).then_inc(sem, 1))
        self._commit(tok, reads, writes)
        return tok

    def dma(self, q, out, in_, reads=(), writes=(), final=False, **kw):
        self.nrec += 1
        if self.nrec > self.maxops:
            return self.dummy
        self._deps(q, reads, writes)
        p = self.dpool[q]
        i = p["next"]
        p["next"] = (i + 1) % len(p["sems"])
        sem = p["sems"][i]
        if p["uses"][i] > 0:
            self._wait(q, (sem, 16 * p["uses"][i], None))
        p["uses"][i] += 1
        tok = (sem, 16 * p["uses"][i], None)
        eng = self.eng[q]
        self.streams[q].append(
            lambda eng=eng, out=out, in_=in_, sem=sem, kw=kw: eng.dma_start(out=out, in_=in_, **kw).then_inc(sem, 16))
        self._commit(tok, reads, writes)
        self.dma_since_barrier.append(tok)
        if final:
            self.final_tokens.append(tok)
        return tok

    def barrier(self):
        toks = [(self.sem[k], self.cnt[k], k) for k in COMPUTE if self.cnt[k] > 0] + list(self.dma_since_barrier)
        for e in self.eng:
            for t in toks:
                if t[2] == e:
                    continue
                self._wait(e, t)
        self.dma_since_barrier = []

    def emit(self):
        for t in self.final_tokens:
            self._wait("sp", t)
        nc = self.nc
        st = self.streams
        with nc.Block() as block:
            @block.sync
            def _(e):
                for f in st["sp"]:
                    f()

            @block.tensor
            def _(e):
                for f in st["pe"]:
                    f()

            @block.vector
            def _(e):
                for f in st["dve"]:
                    f()

            @block.scalar
            def _(e):
                for f in st["act"]:
                    f()

            @block.gpsimd
            def _(e):
                for f in st["pool"]:
                    f()


def _t5_bucket(d):
    max_exact = 16
    d = max(d, 0)
    if d < max_exact:
        return d
    v = np.float32(np.log(np.float32(max(d, 1)) / np.float32(max_exact))) / np.float32(math.log(128 / max_exact)) * np.float32(16)
    return min(max_exact + int(np.float32(v)), 31)


def _bias_consts():
    oh = np.zeros((32, 383), np.float32)
    neg = np.zeros((8, 383), np.float32)
    for i in range(383):
        d = 255 - i
        if 0 <= d < 128:
            oh[_t5_bucket(d), i] = 1.0
        else:
            neg[:, i] = NEG
    return oh, neg


def build_nc(passes=None, dbg=False):
    nc = bass.Bass("TRN2", target_bir_lowering=False)

    def din(name, shape):
        return nc.dram_tensor(name, list(shape), F32, kind="ExternalInput").ap()

    def dout(name, shape):
        return nc.dram_tensor(name, list(shape), F32, kind="ExternalOutput").ap()

    xp = din("xp", [SEQ, D]); xs = din("xs", [NTS, D])
    ck = din("ck", [2, NS, 128, 128]); cv = din("cv", [2, NS, 128, 128])
    sre = din("sre", [2, NS, 2048]); sim = din("sim", [2, NS, 2048])
    slru = din("slru", [2, NS, 512]); sconv = din("sconv", [2, NS, 3, 512])
    rel_bias = din("rel_bias", [32, 8]); w_in = din("w_in", [2, D, 6400]); sinks = din("sinks", [2, 8])
    w_br = [din("w_br_a", [2, 512, D]), din("w_br_b", [2, 512, D]), din("w_br_c", [2, 512, D])]
    lam_re = din("lam_re", [2, 2048]); lam_im = din("lam_im", [2, 2048]); log_step = din("log_step", [2, 32])
    b_re = din("b_re", [2, 32, 64, 16]); b_im = din("b_im", [2, 32, 64, 16])
    c_re = din("c_re", [2, 32, 16, 64]); c_im = din("c_im", [2, 32, 16, 64])
    ssm_d = din("ssm_d", [2, 512]); w_glu = din("w_glu", [2, 512, 512])
    conv_w = din("conv_w", [2, 4, 512]); conv_b = din("conv_b", [2, 512])
    lru_w_a = din("lru_w_a", [2, 8, 64, 64]); lru_b_a = din("lru_b_a", [2, 512])
    lru_w_x = din("lru_w_x", [2, 8, 64, 64]); lru_b_x = din("lru_b_x", [2, 512])
    lru_lam = din("lru_lam", [2, 512]); w_out = din("w_out", [2, D, D])
    ln_g = din("ln_g", [2, D]); ln_b = din("ln_b", [2, D])
    oh2 = din("oh2", [32, 383]); negmask = din("negmask", [8, 383]); pmask = din("pmask", [128, 2])

    yp = dout("yp", [SEQ, D]); ys = dout("ys", [NTS, D])
    nkp = dout("nkp", [2, 128, 128]); nvp = dout("nvp", [2, 128, 128])
    nrep = dout("nrep", [2, 2048]); nimp = dout("nimp", [2, 2048])
    nlrup = dout("nlrup", [2, 512]); ncvp = dout("ncvp", [2, 3, 512])
    nks = dout("nks", [2, NS, 128, 128]); nvs = dout("nvs", [2, NS, 128, 128])
    nres = dout("nres", [2, NS, 2048]); nims = dout("nims", [2, NS, 2048])
    nlrus = dout("nlrus", [2, NS, 512]); ncvs = dout("ncvs", [2, NS, 3, 512])
    KDBG = bool(os.environ.get("KDBG"))
    dbg_m = [dout(f"dbg_m{b}", [128, 8 * TILE]) for b in range(2)] if KDBG else None
    dbg_x = dout("dbg_x", [128, 4 * D]) if KDBG else None
    rv_t = nc.dram_tensor("rv_scr", [8, 383], F32, kind="Internal")
    rv = rv_t.ap()
    tb_scr = nc.dram_tensor("tb_scr", [128, 2048], F32, kind="Internal").ap()

    kb = KB(nc)
    ncd = nc.allow_non_contiguous_dma(reason="small strided parameter/state transfers")

    def mm(out, lhsT, rhs, start, stop, reads, writes):
        kb.op("pe", lambda e: e.matmul(out, lhsT=lhsT, rhs=rhs, start=start, stop=stop), reads, writes)

    def tr(out, in_, ident, reads, writes):
        kb.op("pe", lambda e: e.transpose(out, in_, ident), reads, writes)

    def act(out, in_, func, reads, writes, **kw):
        kb.op("act", lambda e: e.activation(out=out, in_=in_, func=func, **kw), reads, writes)

    def acopy(out, in_, reads, writes):
        kb.op("act", lambda e: e.copy(out=out, in_=in_), reads, writes)

    def vcopy(out, in_, reads, writes, eng="dve"):
        kb.op(eng, lambda e: e.tensor_copy(out, in_), reads, writes)

    def tt(out, in0, in1, op, reads, writes, eng="dve"):
        kb.op(eng, lambda e: e.tensor_tensor(out=out, in0=in0, in1=in1, op=op), reads, writes)

    def ts(out, in0, s1, op0, reads, writes, s2=None, op1=None, eng="dve"):
        if op1 is None:
            kb.op(eng, lambda e: e.tensor_scalar(out=out, in0=in0, scalar1=s1, scalar2=None, op0=op0), reads, writes)
        else:
            kb.op(eng, lambda e: e.tensor_scalar(out=out, in0=in0, scalar1=s1, scalar2=s2, op0=op0, op1=op1), reads, writes)

    def stt(out, in0, scalar, in1, op0, op1, reads, writes):
        kb.op("dve", lambda e: e.scalar_tensor_tensor(out=out, in0=in0, scalar=scalar, in1=in1, op0=op0, op1=op1), reads, writes)

    def memset(ap, val, writes, eng="dve"):
        kb.op(eng, lambda e: e.memset(ap, val), (), writes)

    with contextlib.ExitStack() as es:
        es.enter_context(ncd)

        uniq = [0]

        def S(name, shape, dt=F32, stack=es):
            uniq[0] += 1
            return stack.enter_context(nc.sbuf_tensor(f"{name}_{uniq[0]}", list(shape), dt))

        pst = [es.enter_context(nc.psum_tensor(f"ps{i}", [128, 1024], F32)) for i in range(4)]
        bank_rr = [0]
        nrot = [8]

        def bank():
            i = bank_rr[0]
            bank_rr[0] = (i + 1) % nrot[0]
            return pst[i // 2][:, (i % 2) * 512:(i % 2) * 512 + 512], f"pb{i}"

        def bank2():
            i = bank_rr[0]
            if i % 2:
                i = (i + 1) % nrot[0]
            bank_rr[0] = (i + 2) % nrot[0]
            return pst[i // 2][:, :], [f"pb{i}", f"pb{i + 1}"]

        ident = S("ident", [128, 128]); identb = S("identb", [128, 128], BF16)
        xtok = [S("xtokA", [128, 4, D]), S("xtokB", [128, 4, D])]
        xT = S("xT", [128, 8, TILE], BF16)
        RING = 4
        wr = [S(f"wr{i}", [128, 2048], BF16) for i in range(RING)]
        kTd = S("kTd", [128, 2, 2, 128 + TILE], BF16)
        vtok = S("vtok", [128, 2, 5, 128], BF16)
        merged = S("merged", [128, 8, TILE])
        skb = S("skb", [128, 2, 8])
        COS = S("COS", [128, 2, 16, SJ + 1]); SIN = S("SIN", [128, 2, 16, SJ + 1])
        MAGT = S("MAGT", [128, 2, 16, SJ])
        WBre = S("WBre", [128, 2, 4, 2, 128], BF16); WBim = S("WBim", [128, 2, 4, 2, 128], BF16)
        WCre = S("WCre", [128, 2, 16, 128], BF16); WCim = S("WCim", [128, 2, 16, 128], BF16)
        AbR = S("AbR", [128, 2, 16]); AbI = S("AbI", [128, 2, 16])
        CaR = S("CaR", [128, 2, 16]); CaI = S("CaI", [128, 2, 16])
        dS5 = S("dS5", [128, 2, 4])
        Gre = S("Gre", [128, 2, 16]); Gim = S("Gim", [128, 2, 16])
        WA = S("WA", [128, 2, 4, 128], BF16); WX = S("WX", [128, 2, 4, 128], BF16)
        cw = S("cw", [128, 2, 4, 4]); cb = S("cb", [128, 2, 4]); ba = S("ba", [128, 2, 4]); bx = S("bx", [128, 2, 4])
        c1 = S("c1", [128, 2, 4]); c2 = S("c2", [128, 2, 4])
        hl = S("hl", [128, 2, 4]); chist = S("chist", [128, 2, 4, 3])
        epsc = S("epsc", [128, 1])

        kb.op("pool", lambda e: e.memset(ident[:], 0.0), (), ["ident"])
        kb.op("pool", lambda e: e.affine_select(out=ident[:], in_=ident[:], pattern=[[-1, 128]], compare_op=ALU.not_equal,
                                                fill=1.0, base=0, channel_multiplier=1), ["ident"], ["ident"])
        vcopy(identb[:], ident[:], ["ident"], ["identb"])
        memset(epsc[:], LN_EPS, ["epsc"])
        memset(Gre[:], 0.0, ["G"]); memset(Gim[:], 0.0, ["G"])
        memset(hl[:], 0.0, ["hl"]); memset(chist[:], 0.0, ["chist"])
        kb.dma("sp", skb[:].rearrange("p l h -> p (l h)"), sinks.rearrange("l h -> (l h)").partition_broadcast(128), (), ["skb"])

        class _Stop(Exception):
            pass
        STAGE = float(os.environ.get("KSTAGE", "99"))

        def stage(n):
            if STAGE < n:
                raise _Stop()
        try:
          with contextlib.ExitStack() as ss:
            def SS(name, shape, dt=F32):
                return S(name, shape, dt, stack=ss)
            stage(1)
            rb = SS("rb", [32, 8]); ohs = SS("ohs", [32, 383]); ngs = SS("ngs", [8, 383]); rvs = SS("rvs", [8, 383])
            kb.dma("sp", rb[:], rel_bias, (), ["rb"])
            kb.dma("sp", ohs[:], oh2, (), ["ohs"])
            kb.dma("sp", ngs[:], negmask, (), ["ngs"])
            pb, pk = bank()
            mm(pb[0:8, 0:383], rb[:], ohs[:], True, True, ["rb", "ohs"], [pk])
            tt(rvs[:], pb[0:8, 0:383], ngs[:], ALU.add, [pk, "ngs"], ["rvs"])
            kb.dma("sp", rv, rvs[:], ["rvs"], ["rv_dram"])
            stage(1.5)
            Tq = SS("Tq", [128, 2048]); Tf = SS("Tf", [128, 2048]); Jm = SS("Jm", [128, 128])
            kb.dma("sp", Tq[:].rearrange("p (h s) -> p h s", h=8), bass.AP(rv_t, 0, [[1, 128], [383, 8], [1, 256]]), ["rv_dram"], ["Tq"])
            kb.op("pool", lambda e: e.memset(Jm[:], 0.0), (), ["Jm"])
            kb.op("pool", lambda e: e.affine_select(out=Jm[:], in_=Jm[:], pattern=[[1, 128]], compare_op=ALU.not_equal,
                                                    fill=1.0, base=-127, channel_multiplier=1), ["Jm"], ["Jm"])
            for c4 in range(4):
                pb, pk = bank()
                mm(pb[:, :], Jm[:], Tq[:, 512 * c4:512 * c4 + 512], True, True, ["Jm", "Tq"], [pk])
                acopy(Tf[:, 512 * c4:512 * c4 + 512], pb[:, :], [pk], ["Tf"])
            kb.dma("sp", tb_scr, Tf[:], ["Tf"], ["tb_dram"])

            stage(2)
            jj_i = SS("jj_i", [128, SJ + 1], I32); jj = SS("jj", [128, SJ + 1])
            kb.op("pool", lambda e: e.iota(jj_i[:], pattern=[[1, SJ + 1]], base=0, channel_multiplier=0), (), ["jj_i"])
            vcopy(jj[:], jj_i[:], ["jj_i"], ["jj"])
            lre = SS("lre", [128, 2, 16]); lim = SS("lim", [128, 2, 16]); lst = SS("lst", [128, 2, 16])
            Bre = SS("Bre", [128, 16, 16]); Bim = SS("Bim", [128, 16, 16]); Cn = SS("Cn", [128, 4, 64]); Cn2 = SS("Cn2", [128, 4, 128])
            pmk = SS("pmk", [128, 2])
            kb.dma("sp", pmk[:], pmask, (), ["pmk"])
            BBr = SS("BBr", [128, 16, 16]); BBi = SS("BBi", [128, 16, 16]); tB = SS("tB", [128, 16, 16])
            ZB = SS("ZB", [128, 16, 128])
            phi = SS("phi", [128, 16, SJ + 1]); rr = SS("rr", [128, 16, SJ + 1]); ri = SS("ri", [128, 16, SJ + 1], I32)
            rf = SS("rf", [128, 16, SJ + 1]); gg = SS("gg", [128, 16, SJ + 1])
            sm = [SS(f"sm{i}", [128, 16]) for i in range(12)]
            wst = SS("wst", [128, 4, 128]); lamt = SS("lamt", [128, 4])
            for l in range(2):
                kb.dma("sp", lre[:, l, :], lam_re[l].rearrange("(k p) -> p k", p=128), (), ["lre"])
                kb.dma("sp", lim[:, l, :], lam_im[l].rearrange("(k p) -> p k", p=128), (), ["lim"])
                for g2 in range(2):
                    kb.dma("sp", lst[64 * g2:64 * g2 + 64, l, :],
                           log_step[l].rearrange("(k t) -> t k", t=2)[g2].partition_broadcast(64), (), ["lst"])
                kb.dma("sp", dS5[:, l, :], ssm_d[l].rearrange("(q p) -> p q", p=128), (), ["dS5"])
            stage(2.5)
            for l in range(2):
                step, lr, th, mag, den, a1, t0, t1, fre, fim, rden, t2 = sm
                act(step[:], lst[:, l, :], AF.Exp, ["lst"], ["sm0"])
                ts(lr[:], lre[:, l, :], -1e-4, ALU.min, ["lre"], ["sm1"])
                tt(th[:], lim[:, l, :], step[:], ALU.mult, ["lim", "sm0"], ["sm2"])
                tt(t0[:], lr[:], step[:], ALU.mult, ["sm1", "sm0"], ["sm6"])
                act(mag[:], t0[:], AF.Exp, ["sm6"], ["sm3"])
                tt(phi[:], th[:].unsqueeze(2).to_broadcast([128, 16, SJ + 1]), jj[:].unsqueeze(1).to_broadcast([128, 16, SJ + 1]),
                   ALU.mult, ["sm2", "jj"], ["phi"])
                for which, TAB in ((0, SIN), (1, COS)):
                    ts(rr[:], phi[:], 1.0 / (2 * math.pi), ALU.mult, ["phi"], ["rr"], s2=0.25 * which, op1=ALU.add)
                    vcopy(ri[:], rr[:], ["rr"], ["ri"])
                    vcopy(rf[:], ri[:], ["ri"], ["rf"])
                    tt(rr[:], rr[:], rf[:], ALU.subtract, ["rr", "rf"], ["rr"])
                    ts(gg[:], rr[:], 0.5, ALU.is_gt, ["rr"], ["gg"])
                    tt(rr[:], rr[:], gg[:], ALU.subtract, ["rr", "gg"], ["rr"])
                    ts(gg[:], rr[:], -0.5, ALU.is_lt, ["rr"], ["gg"])
                    tt(rr[:], rr[:], gg[:], ALU.add, ["rr", "gg"], ["rr"])
                    ts(rr[:], rr[:], 0.5, ALU.min, ["rr"], ["rr"], s2=-0.5, op1=ALU.max)
                    act(TAB[:, l, :, :], rr[:], AF.Sin, ["rr"], ["COS" if which else "SIN"], scale=6.283185)
                tt(AbR[:, l, :], mag[:], COS[:, l, :, 1], ALU.mult, ["sm3", "COS"], ["Ab"])
                tt(AbI[:, l, :], mag[:], SIN[:, l, :, 1], ALU.mult, ["sm3", "SIN"], ["Ab"])
                tt(CaR[:, l, :], mag[:], COS[:, l, :, SJ], ALU.mult, ["sm3", "COS"], ["Ca"])
                tt(CaI[:, l, :], mag[:], SIN[:, l, :, SJ], ALU.mult, ["sm3", "SIN"], ["Ca"])
                vcopy(MAGT[:, l, :, :], mag[:].unsqueeze(2).to_broadcast([128, 16, SJ]), ["sm3"], ["MAGT"])
                memset(MAGT[:, l, :, 0:1], 0.0, ["MAGT"])
                li_ = lim[:, l, :]
                tt(den[:], lr[:], lr[:], ALU.mult, ["sm1"], ["sm4"])
                tt(t1[:], li_, li_, ALU.mult, ["lim"], ["sm7"])
                tt(den[:], den[:], t1[:], ALU.add, ["sm4", "sm7"], ["sm4"])
                kb.op("dve", lambda e, rden=rden, den=den: e.reciprocal(rden[:], den[:]), ["sm4"], ["sm10"])
                ts(a1[:], AbR[:, l, :], -1.0, ALU.add, ["Ab"], ["sm5"])
                tt(t0[:], a1[:], lr[:], ALU.mult, ["sm5", "sm1"], ["sm6"])
                tt(t1[:], AbI[:, l, :], li_, ALU.mult, ["Ab", "lim"], ["sm7"])
                tt(fre[:], t0[:], t1[:], ALU.add, ["sm6", "sm7"], ["sm8"])
                tt(fre[:], fre[:], rden[:], ALU.mult, ["sm8", "sm10"], ["sm8"])
                tt(t0[:], AbI[:, l, :], lr[:], ALU.mult, ["Ab", "sm1"], ["sm6"])
                tt(t1[:], a1[:], li_, ALU.mult, ["sm5", "lim"], ["sm7"])
                tt(fim[:], t0[:], t1[:], ALU.subtract, ["sm6", "sm7"], ["sm9"])
                tt(fim[:], fim[:], rden[:], ALU.mult, ["sm9", "sm10"], ["sm9"])
                stage(3)
                kb.dma("sp", Bre[:], b_re[l].rearrange("(k t) p c -> (t p) k c", t=2), ["Bre"], ["Bre"])
                kb.dma("sp", Bim[:], b_im[l].rearrange("(k t) p c -> (t p) k c", t=2), ["Bim"], ["Bim"])
                frb = fre[:].unsqueeze(2).to_broadcast([128, 16, 16]); fib = fim[:].unsqueeze(2).to_broadcast([128, 16, 16])
                tt(BBr[:], Bre[:], frb, ALU.mult, ["Bre", "sm8"], ["BBr"])
                tt(tB[:], Bim[:], fib, ALU.mult, ["Bim", "sm9"], ["tB"])
                tt(BBr[:], BBr[:], tB[:], ALU.subtract, ["BBr", "tB"], ["BBr"])
                tt(BBi[:], Bim[:], frb, ALU.mult, ["Bim", "sm8"], ["BBi"])
                tt(tB[:], Bre[:], fib, ALU.mult, ["Bre", "sm9"], ["tB"])
                tt(BBi[:], BBi[:], tB[:], ALU.add, ["BBi", "tB"], ["BBi"])
                for src, skey_, dst, nm in ((BBr, "BBr", WBre, "WBre"), (BBi, "BBi", WBim, "WBim")):
                    memset(ZB[:], 0.0, ["ZB"])
                    for km in range(4):
                        for g2 in range(2):
                            c0 = 16 * (2 * km + g2)
                            vcopy(ZB[64 * g2:64 * g2 + 64, km::4, c0:c0 + 16], src[64 * g2:64 * g2 + 64, km::4, :], [skey_], ["ZB"])
                    for k4 in range(4):
                        pb, pk = bank()
                        for kq in range(4):
                            k = 4 * k4 + kq
                            tr(pb[:, kq * 128:(kq + 1) * 128], ZB[:, k, :], ident[:], ["ZB", "ident"], [pk])
                        for kq in range(4):
                            h_, kk_ = kq // 2, kq % 2
                            acopy(dst[64 * h_:64 * h_ + 64, l, k4, kk_, :], pb[64 * h_:64 * h_ + 64, kq * 128:(kq + 1) * 128], [pk], [nm])
                for csrc, sgn, wdst, nm in ((c_re, 1.0, WCre, "WCre"), (c_im, -1.0, WCim, "WCim")):
                    memset(wdst[:, l, :, :], 0.0, [nm])
                    kb.dma("sp", Cn[:], csrc[l].rearrange("(a g) c p -> (g c) a p", g=8), ["Cn"], ["Cn"])
                    for t_ in range(2):
                        ts(Cn2[:, :, 64 * t_:64 * t_ + 64], Cn[:], pmk[:, t_:t_ + 1], ALU.mult, ["Cn", "pmk"], ["Cn2"], s2=sgn, op1=ALU.mult)
                    pb, pk = bank()
                    for a in range(4):
                        tr(pb[:, a * 128:(a + 1) * 128], Cn2[:, a, :], ident[:], ["Cn2", "ident"], [pk])
                    pbv = pb.rearrange("p (a b) -> p a b", a=4)
                    for km in range(4):
                        acopy(wdst[:, l, km::4, 32 * km:32 * km + 32], pbv[:, :, 32 * km:32 * km + 32], [pk], [nm])
                stage(4)
                for j in range(4):
                    kb.dma("sp", cw[:, l, :, j], conv_w[l][j].rearrange("(q p) -> p q", p=128), (), ["cw"])
                kb.dma("sp", cb[:, l, :], conv_b[l].rearrange("(q p) -> p q", p=128), (), ["cb"])
                kb.dma("sp", ba[:, l, :], lru_b_a[l].rearrange("(q p) -> p q", p=128), (), ["ba"])
                kb.dma("sp", bx[:, l, :], lru_b_x[l].rearrange("(q p) -> p q", p=128), (), ["bx"])
                kb.dma("sp", lamt[:], lru_lam[l].rearrange("(q p) -> p q", p=128), ["lamt"], ["lamt"])
                act(lamt[:], lamt[:], AF.Exp, ["lamt"], ["lamt"], scale=-1.0)
                act(lamt[:], lamt[:], AF.Ln, ["lamt"], ["lamt"], bias=1.0)
                ts(c1[:, l, :], lamt[:], -8.0, ALU.mult, ["lamt"], ["c1"])
                ts(c2[:, l, :], lamt[:], -16.0, ALU.mult, ["lamt"], ["c2"])
                for wsrc, wdst, nm in ((lru_w_a, WA, "WA"), (lru_w_x, WX, "WX")):
                    memset(wst[:], 0.0, ["wst"])
                    for hh in range(2):
                        kb.dma("sp", wst[64 * hh:64 * hh + 64, :, 64 * hh:64 * hh + 64],
                               wsrc[l].rearrange("(q t) d e -> t d q e", t=2)[hh], ["wst"], ["wst"])
                    vcopy(wdst[:, l, :, :], wst[:], ["wst"], [nm])
        except _Stop:
            pass
        kb.barrier()

        w_in_v = [w_in[l].rearrange("(k p) c -> p k c", p=128) for l in range(2)]
        w_out_v = [w_out[l].rearrange("(k p) c -> p k c", p=128) for l in range(2)]
        w_br_v = [[w_br[b][l].rearrange("(k p) c -> p k c", p=128) for l in range(2)] for b in range(3)]
        w_glu_v = [w_glu[l].rearrange("(k p) c -> p k c", p=128) for l in range(2)]

        def v8(slot):
            return slot[:].rearrange("p (k c) -> p k c", k=8)

        def v4(slot):
            return slot[:].rearrange("p (k c) -> p k c", k=4)

        def win_spec(l, c0):
            return [(lambda s: v8(s), w_in_v[l][:, :, c0:c0 + 256])]

        def pass_specs(l):
            sp = []
            sp += [("q0", win_spec(l, 0)), ("q1", win_spec(l, 256)), ("kv", win_spec(l, 512))]
            sp += [("kd", [(lambda s, j=j, h=h: v8(s)[:, :, 128 * h + 64 * j:128 * h + 64 * j + 64], w_in_v[l][:, :, 512 + 64 * h:576 + 64 * h])
                           for h in range(2) for j in range(2)])]
            sp += [("ga0", win_spec(l, 768)), ("ga1", win_spec(l, 1024))]
            for b, (c_in, c_gm) in enumerate(((1280, 3328), (2304, 4352), (0, 5376))):
                if b == 1:
                    sp += [("u0", win_spec(l, 1280)), ("u1", win_spec(l, 1536)), ("gb0", win_spec(l, 1792)), ("gb1", win_spec(l, 2048))]
                    sp += [("xc0", win_spec(l, 2304)), ("xc1", win_spec(l, 2560)), ("gc0", win_spec(l, 2816)), ("gc1", win_spec(l, 3072))]
                    sp += [("glu", [(lambda s: v4(s), w_glu_v[l])])]
                g0 = 3328 + 1024 * b
                for h in range(2):
                    sp += [(f"br{b}{h}", [(lambda s: v4(s), w_br_v[b][l][:, :, 512 * h:512 * h + 512])])]
                    sp += [(f"gm{b}{i}", win_spec(l, g0 + 256 * i)) for i in (2 * h, 2 * h + 1)]
            sp += [(f"wo{i}", [(lambda s: v8(s), w_out_v[l][:, :, 256 * i:256 * i + 256])]) for i in range(4)]
            return sp

        all_passes = passes if passes is not None else [(t, l) for t in range(NPT + 1) for l in range(2)]
        NSLOT = len(pass_specs(0))
        wc_t = nc.dram_tensor("wcache", [2, NSLOT, 128, 2048], BF16, kind="Internal")
        wc = wc_t.ap()
        for l in sorted({l for (_, l) in all_passes}):
            for i, (name, pieces) in enumerate(pass_specs(l)):
                for dstf, src in pieces:
                    kb.dma("pool", dstf(wc[l, i]), src, (), [f"wc{l}_{i}"])
        wq = []
        for (t, l) in all_passes:
            wq += [(name, l, i) for i, (name, _) in enumerate(pass_specs(l))]
        wstate = {"issued": 0, "used": 0}

        def wissue():
            n = wstate["issued"]
            name, l_, i_ = wq[n]
            kb.dma("sp", wr[n % RING][:], wc[l_, i_], [f"wc{l_}_{i_}"], [f"wr{n % RING}"])
            wstate["issued"] = n + 1

        def wget(name, ahead=RING):
            n = wstate["used"]
            assert wq[n][0] == name, (wq[n][0], name)
            while wstate["issued"] < min(len(wq), n + ahead):
                wissue()
            wstate["used"] = n + 1
            return wr[n % RING], f"wr{n % RING}"

        def run_pass(t, l, xin, xout):
            sample = (t == NPT)
            ntok = NTS if sample else TILE
            nsub = 1 if sample else 4
            R = 64 if sample else 128
            last_prompt = (t == NPT - 1)
            xi = xtok[xin]; xo = xtok[xout]
            xik = f"xtok{xin}"; xok = f"xtok{xout}"

            for s in range(nsub):
                for k4 in range(2):
                    pb, pk = bank()
                    for kk in range(4):
                        kc = 4 * k4 + kk
                        tr(pb[:, kk * 128:kk * 128 + R], xi[0:R, s, kc * 128:(kc + 1) * 128], ident[0:R, 0:R], [xik, "ident"], [pk])
                    acopy(xT[:, 4 * k4:4 * k4 + 4, s * 128:s * 128 + R], pb.rearrange("p (a b) -> p a b", a=4)[:, :, 0:R], [pk], ["xT"])

            def proj_fm(slot, skey, cchunk, nk, rhs_of_kc, rkeys):
                pb, pk = bank()
                view = v8(slot) if nk == 8 else v4(slot)
                for kc in range(nk):
                    mm(pb[:, 0:ntok], view[:, kc, cchunk * 128:(cchunk + 1) * 128], rhs_of_kc(kc), kc == 0, kc == nk - 1,
                       [skey] + rkeys, [pk])
                return pb, pk

            xTk = lambda kc: xT[:, kc, 0:ntok]

            with contextlib.ExitStack() as pa:
                pa_keys = []

                def PA(name, shape, dt=F32):
                    pa_keys.append(name)
                    return S(name, shape, dt, stack=pa)
                gatedT = PA("gatedT", [128, 4, TILE], BF16)
                mTh = {}
                W7 = TS + 3
                gcT = PA("gcT", [128, 4, ntok], BF16)
                cbuf = PA("cbuf", [128, 4, NS, W7]) if sample else PA("cbuf", [128, 4, TILE + 3])
                xco = PA("xco", [NTS, 512]) if sample else PA("xco", [4, 512])
                lru_st = {}

                def xc_unit(q):
                    if q % 2 == 0:
                        lru_st["slot"] = wget(f"xc{q // 2}")
                        slot, skey = lru_st["slot"]
                        if sample or last_prompt:
                            nr = NTS if sample else 3
                            cstart = 0 if sample else TILE - 3
                            pbx, pkx = bank()
                            for kc in range(8):
                                mm(pbx[0:nr, 0:256], xT[:, kc, cstart:cstart + nr], v8(slot)[:, kc, :], kc == 0, kc == 7, [skey, "xT"], [pkx])
                            acopy(xco[0:nr, 256 * (q // 2):256 * (q // 2) + 256], pbx[0:nr, 0:256], [pkx], ["xco"])
                    slot, skey = lru_st["slot"]
                    pb, pk = proj_fm(slot, skey, q % 2, 8, xTk, ["xT"])
                    if sample:
                        acopy(cbuf[:, q, :, 3:W7], pb[:, 0:ntok].rearrange("p (s t) -> p s t", t=TS), [pk], ["cbuf"])
                    else:
                        acopy(cbuf[:, q, 0:3], chist[:, l, q, :], ["chist"], ["cbuf"])
                        acopy(cbuf[:, q, 3:3 + ntok], pb[:, 0:ntok], [pk], ["cbuf"])
                        acopy(chist[:, l, q, :], cbuf[:, q, ntok:ntok + 3], ["cbuf"], ["chist"])

                def gc_unit(q):
                    if q % 2 == 0:
                        lru_st["slot"] = wget(f"gc{q // 2}")
                    slot, skey = lru_st["slot"]
                    pb, pk = proj_fm(slot, skey, q % 2, 8, xTk, ["xT"])
                    act(gcT[:, q, 0:ntok], pb[:, 0:ntok], AF.Silu, [pk], ["gcT"])
                lru_units = [lambda q=q: xc_unit(q) for q in range(4)] + [lambda q=q: gc_unit(q) for q in range(4)]

                def branch_out(b):
                    with contextlib.ExitStack() as bo:
                        bo_keys = ["gms0", "gms1", "tmpm"]
                        gms = [S("gms0", [128, ntok], F32, stack=bo), S("gms1", [128, ntok], F32, stack=bo)]
                        tmpm = S("tmpm", [128, ntok], F32, stack=bo)
                        _branch_out(b, gms, tmpm)
                        kb.retire(bo_keys)

                def _branch_out(b, gms, tmpm):
                    mT = mTh.get("mT")
                    ypb = []
                    for fo in range(8):
                        if fo % 4 == 0:
                            slot, skey = wget(f"br{b}{fo // 4}")
                        ypb.append(proj_fm(slot, skey, fo % 4, 4, lambda c: gatedT[:, c, 0:ntok], ["gatedT"]))
                        if fo % 4 == 3:
                            for f2 in range(fo - 3, fo + 1):
                                if f2 % 2 == 0:
                                    gslot, gkey = wget(f"gm{b}{f2 // 2}")
                                gpb, gpk = proj_fm(gslot, gkey, f2 % 2, 8, xTk, ["xT"])
                                g = gms[f2 % 2]; gk = f"gms{f2 % 2}"
                                act(g[:, 0:ntok], gpb[:, 0:ntok], AF.Sigmoid, [gpk], [gk])
                                yb, yk = ypb[f2]
                                if b == 0:
                                    tt(merged[:, f2, 0:ntok], yb[:, 0:ntok], g[:, 0:ntok], ALU.mult, [yk, gk], ["merged"])
                                elif b == 1:
                                    tt(tmpm[:, 0:ntok], yb[:, 0:ntok], g[:, 0:ntok], ALU.mult, [yk, gk], ["tmpm"])
                                    tt(merged[:, f2, 0:ntok], merged[:, f2, 0:ntok], tmpm[:, 0:ntok], ALU.add, ["merged", "tmpm"], ["merged"])
                                else:
                                    tt(tmpm[:, 0:ntok], yb[:, 0:ntok], g[:, 0:ntok], ALU.mult, [yk, gk], ["tmpm"])
                                    tt(mT[:, f2, 0:ntok], merged[:, f2, 0:ntok], tmpm[:, 0:ntok], ALU.add, ["merged", "tmpm"], ["mT"])

                with contextlib.ExitStack() as ph:
                    ph_keys = []

                    def PH(name, shape, dt=F32, ph=ph, ph_keys=ph_keys):
                        ph_keys.append(name)
                        return S(name, shape, dt, stack=ph)
                    qT = PH("qT", [128, 4, TILE], BF16)
                    Tb = PH("Tb", [128, 8, 256])
                    kb.dma("sp", Tb[:].rearrange("p h s -> p (h s)"), tb_scr, ["tb_dram"], ["Tb"])
                    gatok = PH("gatok", [128, 4, 512], BF16)
                    kvo = PH("kvo", [128, 256])
                    KW = 132 if sample else 256
                    sc2 = [PH(f"sc{i}", [128, 4, KW]) for i in range(2)]; Pm2 = [PH(f"Pm{i}", [128, 4, KW], BF16) for i in range(2)]
                    PT = PH("PT", [128, 8, 128], BF16)
                    mx2 = [PH(f"mx{i}", [128, 4]) for i in range(2)]; ngm2 = [PH(f"ngm{i}", [128, 4]) for i in range(2)]
                    rs2 = [PH(f"rs{i}", [128, 4]) for i in range(2)]; esk2 = [PH(f"esk{i}", [128, 4]) for i in range(2)]
                    rinv2 = [PH(f"rinv{i}", [128, 4]) for i in range(2)]; att = PH("att", [128, 256])
                    gA2 = [PH(f"gA{i}", [128, 512], BF16) for i in range(2)]
                    if sample:
                        Ks = PH("Ks", [128, NS, 128], BF16); Kd = PH("Kd", [128, 2, 2, 64], BF16)
                        KTs = PH("KTs", [128, NS, 2, 132], BF16); Vs = PH("Vs", [128, NS, 128], BF16)
                        vnew = PH("vnew", [4, NS, 128], BF16)
                        kTn = PH("kTn", [128, 2, NTS], BF16); gas2 = [PH(f"gas{i}", [4, 512], BF16) for i in range(2)]

                    for c2_ in range(2):
                        slot, skey = wget(f"q{c2_}")
                        for cc in range(2):
                            pb, pk = proj_fm(slot, skey, cc, 8, xTk, ["xT"])
                            acopy(qT[:, 2 * c2_ + cc, 0:ntok], pb[:, 0:ntok], [pk], ["qT"])
                    slot, skey = wget("kv")
                    for s in range(nsub):
                        pb, pk = bank()
                        for kc in range(8):
                            mm(pb[0:R, 0:256], xT[:, kc, s * 128:s * 128 + R], v8(slot)[:, kc, :], kc == 0, kc == 7, [skey, "xT"], [pk])
                        if not sample:
                            acopy(vtok[:, l, 1 + s, :], pb[:, 128:256], [pk], ["vtok"])
                            if last_prompt and s == 3:
                                acopy(kvo[:], pb[:, 0:256], [pk], ["kvo"])
                                kb.dma("sp", nkp[l], kvo[:, 0:128], ["kvo"], (), final=True)
                                kb.dma("sp", nvp[l], kvo[:, 128:256], ["kvo"], (), final=True)
                        else:
                            vcopy(kvo[0:R, :], pb[0:R, 0:256], [pk], ["kvo"])
                            for tt_ in range(TS):
                                kb.dma("sp", nks[l][:, 124 + tt_, :], kvo[tt_:NTS:TS, 0:128], ["kvo"], (), final=True)
                                kb.dma("sp", nvs[l][:, 124 + tt_, :], kvo[tt_:NTS:TS, 128:256], ["kvo"], (), final=True)
                            kb.dma("sp", nks[l][:, 0:124, :], ck[l][:, 4:128, :], (), (), final=True)
                            kb.dma("sp", nvs[l][:, 0:124, :], cv[l][:, 4:128, :], (), (), final=True)
                            for i in range(NS):
                                pb2, pk2 = bank()
                                for kc in range(8):
                                    mm(pb2[0:TS, 0:128], xT[:, kc, TS * i:TS * i + TS], v8(slot)[:, kc, 128:256], kc == 0, kc == 7,
                                       [skey, "xT"], [pk2])
                                acopy(vnew[:, i, :], pb2[0:TS, 0:128], [pk2], ["vnew"])
                    slot, skey = wget("kd")
                    for kvh in range(2):
                        pb, pk = proj_fm(slot, skey, kvh, 8, xTk, ["xT"])
                        if not sample:
                            acopy(kTd[:, l, kvh, 128:128 + ntok], pb[:, 0:ntok], [pk], ["kTd"])
                        else:
                            acopy(kTn[:, kvh, :], pb[:, 0:ntok], [pk], ["kTn"])
                    for h2 in range(2):
                        slot, skey = wget(f"ga{h2}")
                        for s in range(nsub):
                            pb, pk = bank()
                            for kc in range(8):
                                mm(pb[0:R, 0:256], xT[:, kc, s * 128:s * 128 + R], v8(slot)[:, kc, :], kc == 0, kc == 7, [skey, "xT"], [pk])
                            act(gatok[0:R, s, h2 * 256:h2 * 256 + 256], pb[0:R, 0:256], AF.Silu, [pk], ["gatok"])

                    def attend_all(blocks):
                        units = [(bi, hf) for bi in range(len(blocks)) for hf in range(2)]
                        stt_ = {}

                        def S1(u):
                            bi, hf = units[u]
                            B = blocks[bi]
                            nq, segs = B["nq"], B["segs"]
                            if hf == 0 and B.get("prep"):
                                B["prep"]()
                            pb2, pks = bank2()
                            scv = pb2.rearrange("p (h s) -> p h s", h=4)
                            for hh in range(4):
                                h = 4 * hf + hh
                                cch, half = h // 2, h % 2
                                off = 0
                                for (kfn, vap, n) in segs:
                                    mm(scv[0:nq, half * 2 + hh // 2, off:off + n], qT[64 * half:64 * half + 64, cch, B["col"]:B["col"] + nq],
                                       kfn(hf)[64 * half:64 * half + 64, :], True, True, ["qT", "kTd", "KTs"], [pks[half]])
                                    off += n
                            stt_[u] = (scv, pks)

                        def S2(u, part):
                            bi, hf = units[u]
                            B = blocks[bi]
                            nq, segs, nkeys, tb_c0 = B["nq"], B["segs"], B["nkeys"], B["tb_c0"]
                            par = u % 2
                            sc_, Pm_, mx_, ngm_, rs_, esk_, rinv_ = sc2[par], Pm2[par], mx2[par], ngm2[par], rs2[par], esk2[par], rinv2[par]
                            ksc, kPm, kmx, kngm, krs, kesk, krinv = (f"{n_}{par}" for n_ in ("sc", "Pm", "mx", "ngm", "rs", "esk", "rinv"))
                            gA_ = gA2[bi % 2]; kgA = f"gA{bi % 2}"
                            if part == "a":
                                scv, pks = stt_.pop(u)
                                for half in range(2):
                                    stt(sc_[0:nq, half:4:2, 0:nkeys], scv[0:nq, 2 * half:2 * half + 2, 0:nkeys], 0.125,
                                        Tb[0:nq, 4 * hf + half:4 * hf + 4:2, tb_c0:tb_c0 + nkeys],
                                        ALU.mult, ALU.add, [pks[half], "Tb"], [ksc])
                                kb.op("dve", lambda e: e.tensor_reduce(out=mx_[0:nq, :], in_=sc_[0:nq, :, 0:nkeys], axis=AX.X, op=ALU.max),
                                      [ksc], [kmx])
                                tt(mx_[0:nq, :], mx_[0:nq, :], skb[0:nq, l, 4 * hf:4 * hf + 4], ALU.max, [kmx, "skb"], [kmx])
                                ts(ngm_[0:nq, :], mx_[0:nq, :], -1.0, ALU.mult, [kmx], [kngm])
                                tt(esk_[0:nq, :], skb[0:nq, l, 4 * hf:4 * hf + 4], mx_[0:nq, :], ALU.subtract, ["skb", kmx], [kesk])
                            elif part == "b":
                                for hh in range(4):
                                    act(Pm_[0:nq, hh, 0:nkeys], sc_[0:nq, hh, 0:nkeys], AF.Exp, [ksc, kngm], [kPm, krs],
                                        bias=ngm_[0:nq, hh:hh + 1], scale=1.0, accum_out=rs_[0:nq, hh:hh + 1])
                                act(esk_[0:nq, :], esk_[0:nq, :], AF.Exp, [kesk], [kesk])
                            else:
                                tt(rinv_[0:nq, :], rs_[0:nq, :], esk_[0:nq, :], ALU.add, [krs, kesk], [krinv])
                                kb.op("dve", lambda e: e.reciprocal(rinv_[0:nq, :], rinv_[0:nq, :]), [krinv], [krinv])

                        def S3(u, part):
                            bi, hf = units[u]
                            B = blocks[bi]
                            nq, segs = B["nq"], B["segs"]
                            par = u % 2
                            Pm_, rinv_ = Pm2[par], rinv2[par]
                            kPm, krinv = f"Pm{par}", f"rinv{par}"
                            gA_ = gA2[bi % 2]; kgA = f"gA{bi % 2}"
                            nseg = len(segs)
                            if part == "a":
                                pbt, pkt = bank()
                                ptv = pbt.bitcast(BF16).rearrange("p (a b) -> p a b", a=8)
                                for hh in range(4):
                                    off = 0
                                    for si, (kfn, vap, n) in enumerate(segs):
                                        tr(ptv[0:n, hh * nseg + si, 0:nq], Pm_[0:nq, hh, off:off + n], identb[0:nq, 0:nq], [kPm, "identb"], [pkt])
                                        off += n
                                for si, (kfn, vap, n) in enumerate(segs):
                                    acopy(PT[0:n, si:4 * nseg:nseg, 0:nq], ptv[0:n, si:4 * nseg:nseg, 0:nq], [pkt], ["PT"])
                                return
                            pbo, pko = bank()
                            for hh in range(4):
                                for si, (kfn, vap, n) in enumerate(segs):
                                    mm(pbo[0:nq, hh * 64:hh * 64 + 64], PT[0:n, hh * nseg + si, 0:nq], vap[0:n, 64 * hf:64 * hf + 64],
                                       si == 0, si == nseg - 1, ["PT", "vtok", "Vs", "vnew"], [pko])
                            tt(att[0:nq, :].rearrange("p (h d) -> p h d", h=4), pbo[0:nq, 0:256].rearrange("p (h d) -> p h d", h=4),
                               rinv_[0:nq, :].unsqueeze(2).to_broadcast([nq, 4, 64]), ALU.mult, [pko, krinv], ["att"])
                            tt(gA_[0:nq, 256 * hf:256 * hf + 256], att[0:nq, :], B["ga"][:, 256 * hf:256 * hf + 256], ALU.mult,
                               ["att", B["ga_key"]], [kgA])
                            if hf == 1:
                                pbt, pkt = bank()
                                ptv = pbt.bitcast(BF16).rearrange("p (a b) -> p a b", a=8)
                                for cch in range(4):
                                    tr(ptv[:, cch, 0:nq], gA_[0:nq, cch * 128:(cch + 1) * 128], identb[0:nq, 0:nq], [kgA, "identb"], [pkt])
                                acopy(gatedT[:, :, B["col"]:B["col"] + nq], ptv[:, 0:4, 0:nq], [pkt], ["gatedT"])

                        nu = len(units)
                        S1(0); S2(0, "a"); S2(0, "b"); S2(0, "c")
                        if nu > 1:
                            S1(1)
                        for u in range(nu):
                            if u + 1 < nu:
                                S2(u + 1, "a")
                            S3(u, "a")
                            if u + 1 < nu:
                                S2(u + 1, "b")
                            S3(u, "b")
                            if u + 1 < nu:
                                S2(u + 1, "c")
                            if u + 2 < nu:
                                S1(u + 2)

                    if not sample:
                        blocks = []
                        for s in range(4):
                            blk = 4 * t + s
                            if blk == 0:
                                segs = [(lambda kvh, s=s: kTd[:, l, kvh, 128 + 128 * s:256 + 128 * s], vtok[:, l, 1 + s, :], 128)]
                                blocks.append(dict(nq=128, segs=segs, nkeys=128, tb_c0=128, ga=gatok[:, s, :], ga_key="gatok", col=128 * s))
                            else:
                                segs = [(lambda kvh, s=s: kTd[:, l, kvh, 128 * s:128 * s + 128], vtok[:, l, s, :], 128),
                                        (lambda kvh, s=s: kTd[:, l, kvh, 128 + 128 * s:256 + 128 * s], vtok[:, l, 1 + s, :], 128)]
                                blocks.append(dict(nq=128, segs=segs, nkeys=256, tb_c0=0, ga=gatok[:, s, :], ga_key="gatok", col=128 * s))
                        attend_all(blocks)
                        vcopy(kTd[:, l, :, 0:128], kTd[:, l, :, TILE:TILE + 128], ["kTd"], ["kTd"])
                        vcopy(vtok[:, l, 0, :], vtok[:, l, 4, :], ["vtok"], ["vtok"])
                    else:
                        kb.dma("pool", Ks[:], ck[l].rearrange("s w f -> w s f"), (), ["Ks"])
                        kb.dma("pool", Vs[:], cv[l].rearrange("s w f -> w s f"), (), ["Vs"])
                        for i in range(NS):
                            pbt, pkt = bank()
                            ptv = pbt.bitcast(BF16).rearrange("p (a b) -> p a b", a=8)
                            vcopy(Kd[:], Ks[:, i, :].rearrange("w (h d) -> w h d", h=2).unsqueeze(2).to_broadcast([128, 2, 2, 64]), ["Ks"], ["Kd"])
                            for kvh in range(2):
                                tr(ptv[:, kvh, :], Kd[:, kvh, :, :].rearrange("w j d -> w (j d)"), identb[:], ["Kd", "identb"], [pkt])
                            acopy(KTs[:, i, :, 0:128], ptv[:, 0:2, :], [pkt], ["KTs"])
                        vcopy(KTs[:, :, :, 128:132], kTn[:].rearrange("p h (s t) -> p s h t", t=TS), ["kTn"], ["KTs"])
                        blocks = []
                        for i in range(NS):
                            def prep(i=i):
                                pbg, pkg = bank()
                                mm(pbg[0:TS, 0:512], identb[0:NTS, TS * i:TS * i + TS], gatok[0:NTS, 0, :], True, True, ["identb", "gatok"], [pkg])
                                acopy(gas2[i % 2][:, :], pbg[0:TS, 0:512], [pkg], [f"gas{i % 2}"])
                            segs = [(lambda kvh, i=i: KTs[:, i, kvh, 0:128], Vs[:, i, :], 128),
                                    (lambda kvh, i=i: KTs[:, i, kvh, 128:132], vnew[:, i, :], TS)]
                            blocks.append(dict(nq=TS, segs=segs, nkeys=132, tb_c0=0, ga=gas2[i % 2][:, :], ga_key=f"gas{i % 2}", col=TS * i, prep=prep))
                        attend_all(blocks)
                    kb.retire(ph_keys)
                if KDBG and t == 0 and l == 0:
                    dstg = PA("dstg", [128, 4, TILE])
                    acopy(dstg[:], gatedT[:], ["gatedT"], ["dstg"])
                    kb.dma("sp", dbg_x[:, 0:2048], dstg[:].rearrange("p a b -> p (a b)"), ["dstg"], (), final=True)
                    kb.dma("sp", dbg_x[:, 2048:3072], xi[:, 0, :], [xik], (), final=True)
                branch_out(0)
                if KDBG and t == 0 and l == 0:
                    kb.dma("sp", dbg_m[0], merged[:].rearrange("p a b -> p (a b)"), ["merged"], (), final=True)

                with contextlib.ExitStack() as ph:
                    ph_keys = []

                    def PH(name, shape, dt=F32, ph=ph, ph_keys=ph_keys):
                        ph_keys.append(name)
                        return S(name, shape, dt, stack=ph)
                    nrot[0] = 6
                    bank_rr[0] = 0
                    uT = PH("uT", [128, 4, ntok], BF16); yz = PH("yz", [128, 4, ntok]); gbT = PH("gbT", [128, 4, ntok], BF16)
                    zb = PH("zb", [128, 4, ntok], BF16)
                    NXS = 1 if sample else 2
                    XRs = [PH(f"XR{i}", [128, 16, SJ]) for i in range(NXS)]; XIs = [PH(f"XI{i}", [128, 16, SJ]) for i in range(NXS)]
                    T1 = PH("T1", [128, 16, SJ]); T2 = PH("T2", [128, 16, SJ])
                    HR = PH("HR", [128, 16, SJ], BF16); HI = PH("HI", [128, 16, SJ], BF16)
                    g1 = PH("g1", [128, ntok]); g2b = PH("g2b", [128, ntok])
                    NH = NS if sample else 1
                    hfr = PH("hfr", [128, 16, NH]); hfi = PH("hfi", [128, 16, NH]); s5a = PH("s5a", [128, 16, NH]); s5b = PH("s5b", [128, 16, NH])
                    if sample:
                        h0r = PH("h0r", [128, 16, NS]); h0i = PH("h0i", [128, 16, NS])
                        st_io = PH("st_io", [NS, 2048])
                        for ssrc, sdst, skey_ in ((sre, h0r, "h0r"), (sim, h0i, "h0i")):
                            kb.dma("sp", st_io[:], ssrc[l], ["st_io"], ["st_io"])
                            pbs, pks = bank()
                            for k in range(16):
                                tr(pbs[:, k * NS:(k + 1) * NS], st_io[0:NS, k * 128:(k + 1) * 128], ident[0:NS, 0:NS], ["st_io", "ident"], [pks])
                            acopy(sdst[:].rearrange("p k s -> p (k s)"), pbs[:, 0:16 * NS], [pks], [skey_])
                    for q in range(4):
                        if q % 2 == 0:
                            slot, skey = wget(f"u{q // 2}")
                        pb, pk = proj_fm(slot, skey, q % 2, 8, xTk, ["xT"])
                        acopy(uT[:, q, 0:ntok], pb[:, 0:ntok], [pk], ["uT"])
                        act(yz[:, q, 0:ntok], pb[:, 0:ntok], AF.Copy, [pk, "dS5"], ["yz"], scale=dS5[:, l, q:q + 1])
                    for q in range(4):
                        if q % 2 == 0:
                            slot, skey = wget(f"gb{q // 2}")
                        pb, pk = proj_fm(slot, skey, q % 2, 8, xTk, ["xT"])
                        act(gbT[:, q, 0:ntok], pb[:, 0:ntok], AF.Silu, [pk], ["gbT"])

                    cosl = COS[:, l, :, :]; sinl = SIN[:, l, :, :]
                    nsj = 1 if sample else TILE // SJ
                    def stage1(jt):
                        c0 = jt * SJ
                        XR = XRs[jt % NXS]; XI = XIs[jt % NXS]; kXR = f"XR{jt % NXS}"; kXI = f"XI{jt % NXS}"
                        pxr, kxr = bank2(); pxi, kxi = bank2()
                        for k in range(16):
                            h_ = (k % 4) // 2
                            pk_ = h_ * 8 + (k // 4) * 2 + k % 2
                            mm(pxr[:, pk_ * SJ:(pk_ + 1) * SJ], WBre[64 * h_:64 * h_ + 64, l, k // 4, k % 2, :], uT[64 * h_:64 * h_ + 64, k // 4, c0:c0 + SJ],
                               True, True, ["WBre", "uT"], [kxr[h_]])
                        for k in range(16):
                            h_ = (k % 4) // 2
                            pk_ = h_ * 8 + (k // 4) * 2 + k % 2
                            mm(pxi[:, pk_ * SJ:(pk_ + 1) * SJ], WBim[64 * h_:64 * h_ + 64, l, k // 4, k % 2, :], uT[64 * h_:64 * h_ + 64, k // 4, c0:c0 + SJ],
                               True, True, ["WBim", "uT"], [kxi[h_]])
                        for h_ in range(2):
                            acopy(XR[:].rearrange("p (q h kk) j -> p h q kk j", h=2, kk=2)[:, h_], pxr[:, 512 * h_:512 * h_ + 512].rearrange("p (q kk j) -> p q kk j", q=4, kk=2),
                                  [kxr[h_]], [kXR])
                            acopy(XI[:].rearrange("p (q h kk) j -> p h q kk j", h=2, kk=2)[:, h_], pxi[:, 512 * h_:512 * h_ + 512].rearrange("p (q kk j) -> p q kk j", q=4, kk=2),
                                  [kxi[h_]], [kXI])

                    stage1(0)
                    pend = []
                    for jt in range(nsj):
                        c0 = jt * SJ
                        if jt + 1 < nsj:
                            stage1(jt + 1)
                        if lru_units:
                            lru_units.pop(0)()
                        XR = XRs[jt % NXS]; XI = XIs[jt % NXS]; kXR = f"XR{jt % NXS}"; kXI = f"XI{jt % NXS}"
                        if sample:
                            cosv = cosl[:, :, 0:TS].unsqueeze(2).to_broadcast([128, 16, NS, TS])
                            sinv = sinl[:, :, 0:TS].unsqueeze(2).to_broadcast([128, 16, NS, TS])
                            V = lambda a: a[:].rearrange("p k (s t) -> p k s t", t=TS)
                            magt = MAGT[:, l, :, :]
                            memset(MAGT[:, l, :, :].rearrange("p k (s t) -> p k s t", t=TS)[:, :, :, 0:1], 0.0, ["MAGT"])
                        else:
                            cosv = cosl[:, :, 0:SJ]; sinv = sinl[:, :, 0:SJ]
                            V = lambda a: a[:]
                            magt = MAGT[:, l, :, :]
                        tt(V(T1), V(XR), cosv, ALU.mult, [kXR, "COS"], ["T1"])
                        tt(V(T2), V(XI), sinv, ALU.mult, [kXI, "SIN"], ["T2"])
                        tt(T1[:], T1[:], T2[:], ALU.add, ["T1", "T2"], ["T1"])
                        tt(V(T2), V(XR), sinv, ALU.mult, [kXR, "SIN"], ["T2"])
                        tt(V(XI), V(XI), cosv, ALU.mult, [kXI, "COS"], [kXI])
                        tt(XI[:], XI[:], T2[:], ALU.subtract, [kXI, "T2"], [kXI])
                        if sample:
                            tt(s5a[:], h0r[:], AbR[:, l, :].unsqueeze(2).to_broadcast([128, 16, NS]), ALU.mult, ["h0r", "Ab"], ["s5a"])
                            tt(s5b[:], h0i[:], AbI[:, l, :].unsqueeze(2).to_broadcast([128, 16, NS]), ALU.mult, ["h0i", "Ab"], ["s5b"])
                            tt(s5a[:], s5a[:], s5b[:], ALU.subtract, ["s5a", "s5b"], ["s5a"])
                            tt(V(T1)[:, :, :, 0], V(T1)[:, :, :, 0], s5a[:], ALU.add, ["T1", "s5a"], ["T1"])
                            tt(s5a[:], h0r[:], AbI[:, l, :].unsqueeze(2).to_broadcast([128, 16, NS]), ALU.mult, ["h0r", "Ab"], ["s5a"])
                            tt(s5b[:], h0i[:], AbR[:, l, :].unsqueeze(2).to_broadcast([128, 16, NS]), ALU.mult, ["h0i", "Ab"], ["s5b"])
                            tt(s5a[:], s5a[:], s5b[:], ALU.add, ["s5a", "s5b"], ["s5a"])
                            tt(V(XI)[:, :, :, 0], V(XI)[:, :, :, 0], s5a[:], ALU.add, [kXI, "s5a"], [kXI])
                        else:
                            tt(T1[:, :, 0], T1[:, :, 0], Gre[:, l, :], ALU.add, ["T1", "G"], ["T1"])
                            tt(XI[:, :, 0], XI[:, :, 0], Gim[:, l, :], ALU.add, [kXI, "G"], [kXI])
                        flat = lambda a: a[:].rearrange("p k j -> p (k j)")
                        mflat = magt.rearrange("p k j -> p (k j)")
                        kb.op("dve", lambda e, o=flat(XR), a=mflat, b=flat(T1): e.tensor_tensor_scan(out=o, data0=a, data1=b, initial=0.0,
                                                                                                     op0=ALU.mult, op1=ALU.add),
                              ["T1", "MAGT"], [kXR])
                        kb.op("dve", lambda e, o=flat(T2), a=mflat, b=flat(XI): e.tensor_tensor_scan(out=o, data0=a, data1=b, initial=0.0,
                                                                                                     op0=ALU.mult, op1=ALU.add),
                              [kXI, "MAGT"], ["T2"])
                        if pend:
                            pend.pop()()
                        pby, pky = pst[3][:, (jt % 2) * 512:(jt % 2) * 512 + 512], f"pb{6 + jt % 2}"
                        if sample:
                            tt(V(T1), V(XR), cosv, ALU.mult, [kXR, "COS"], ["T1"])
                            tt(V(XI), V(T2), sinv, ALU.mult, ["T2", "SIN"], [kXI])
                            tt(HR[:], T1[:], XI[:], ALU.subtract, ["T1", kXI], ["HR"])
                            tt(V(T1), V(XR), sinv, ALU.mult, [kXR, "SIN"], ["T1"])
                            tt(V(XI), V(T2), cosv, ALU.mult, ["T2", "COS"], [kXI])
                            tt(HI[:], T1[:], XI[:], ALU.add, ["T1", kXI], ["HI"])
                            for q in range(4):
                                for kk in range(4):
                                    k = 4 * q + kk
                                    mm(pby[:, q * SJ:(q + 1) * SJ], WCre[:, l, k, :], HR[:, k, :], kk == 0, False, ["WCre", "HR"], [pky])
                                    mm(pby[:, q * SJ:(q + 1) * SJ], WCim[:, l, k, :], HI[:, k, :], False, kk == 3, ["WCim", "HI"], [pky])
                        else:
                            H2v = zb[:].rearrange("p q t -> p (q t)").rearrange("p (a k j) -> p a k j", a=2, k=16)
                            HR2, HI2 = H2v[:, 0], H2v[:, 1]
                            tt(HR[:], XR[:], cosv, ALU.mult, [kXR, "COS"], ["HR"])
                            stt(HR2, T2[:], -1.0, sinv, ALU.mult, ALU.mult, ["T2", "SIN"], ["zb"])
                            tt(HI[:], XR[:], sinv, ALU.mult, [kXR, "SIN"], ["HI"])
                            tt(HI2, T2[:], cosv, ALU.mult, ["T2", "COS"], ["zb"])
                            for q in range(4):
                                for kk in range(4):
                                    k = 4 * q + kk
                                    mm(pby[:, q * SJ:(q + 1) * SJ], WCre[:, l, k, :], HR[:, k, :], kk == 0, False, ["WCre", "HR"], [pky])
                                    mm(pby[:, q * SJ:(q + 1) * SJ], WCre[:, l, k, :], HR2[:, k, :], False, False, ["WCre", "zb"], [pky])
                                    mm(pby[:, q * SJ:(q + 1) * SJ], WCim[:, l, k, :], HI[:, k, :], False, False, ["WCim", "HI"], [pky])
                                    mm(pby[:, q * SJ:(q + 1) * SJ], WCim[:, l, k, :], HI2[:, k, :], False, kk == 3, ["WCim", "zb"], [pky])
                        pend.append(lambda c0=c0, pby=pby, pky=pky: tt(yz[:, :, c0:c0 + SJ], yz[:, :, c0:c0 + SJ],
                                                                        pby[:, 0:4 * SJ].rearrange("p (q j) -> p q j", q=4), ALU.add, ["yz", pky], ["yz"]))
                        if not sample:
                            tt(s5a[:, :, 0], XR[:, :, SJ - 1], CaR[:, l, :], ALU.mult, [kXR, "Ca"], ["s5a"], eng="pool")
                            tt(s5b[:, :, 0], T2[:, :, SJ - 1], CaI[:, l, :], ALU.mult, ["T2", "Ca"], ["s5b"], eng="pool")
                            tt(Gre[:, l, :], s5a[:, :, 0], s5b[:, :, 0], ALU.subtract, ["s5a", "s5b"], ["G"], eng="pool")
                            tt(s5a[:, :, 0], XR[:, :, SJ - 1], CaI[:, l, :], ALU.mult, [kXR, "Ca"], ["s5a"], eng="pool")
                            tt(s5b[:, :, 0], T2[:, :, SJ - 1], CaR[:, l, :], ALU.mult, ["T2", "Ca"], ["s5b"], eng="pool")
                            tt(Gim[:, l, :], s5a[:, :, 0], s5b[:, :, 0], ALU.add, ["s5a", "s5b"], ["G"], eng="pool")
                        if (last_prompt and jt == nsj - 1) or sample:
                            if sample:
                                gr = V(XR)[:, :, :, TS - 1]; gi = V(T2)[:, :, :, TS - 1]
                                cl = cosl[:, :, TS - 1:TS].to_broadcast([128, 16, NS]); sl = sinl[:, :, TS - 1:TS].to_broadcast([128, 16, NS])
                                o1, o2, a_, b_ = hfr[:], hfi[:], s5a[:], s5b[:]
                            else:
                                gr = XR[:, :, SJ - 1]; gi = T2[:, :, SJ - 1]
                                cl = cosl[:, :, SJ - 1]; sl = sinl[:, :, SJ - 1]
                                o1, o2, a_, b_ = hfr[:, :, 0], hfi[:, :, 0], s5a[:, :, 0], s5b[:, :, 0]
                            tt(a_, gr, cl, ALU.mult, [kXR, "COS"], ["s5a"])
                            tt(b_, gi, sl, ALU.mult, ["T2", "SIN"], ["s5b"])
                            tt(o1, a_, b_, ALU.subtract, ["s5a", "s5b"], ["hfr"])
                            tt(a_, gr, sl, ALU.mult, [kXR, "SIN"], ["s5a"])
                            tt(b_, gi, cl, ALU.mult, ["T2", "COS"], ["s5b"])
                            tt(o2, a_, b_, ALU.add, ["s5a", "s5b"], ["hfi"])
                            if sample:
                                for hsrc, hkey, hdst in ((hfr, "hfr", nres), (hfi, "hfi", nims)):
                                    pbA, pkA = bank2(); pbB, pkB = bank2()
                                    for k in range(16):
                                        pbv, pkv = (pbA, pkA) if k < 8 else (pbB, pkB)
                                        tr(pbv[0:NS, (k % 8) * 128:(k % 8 + 1) * 128], hsrc[:, k, :], ident[:], [hkey, "ident"], [pkv[(k % 8) // 4]])
                                    acopy(st_io[:, 0:1024], pbA[0:NS, :], pkA, ["st_io"])
                                    acopy(st_io[:, 1024:2048], pbB[0:NS, :], pkB, ["st_io"])
                                    kb.dma("sp", hdst[l], st_io[:], ["st_io"], (), final=True)
                            else:
                                kb.dma("sp", nrep[l].rearrange("(k p) -> p k", p=128), hfr[:, :, 0], ["hfr"], (), final=True)
                                kb.dma("sp", nimp[l].rearrange("(k p) -> p k", p=128), hfi[:, :, 0], ["hfi"], (), final=True)
                    while lru_units:
                        lru_units.pop(0)()
                    while pend:
                        pend.pop()()
                    for q in range(4):
                        yv = yz[:, q, 0:ntok]
                        gq, kgq = (g1, "g1") if q % 2 == 0 else (g2b, "g2b")
                        kyz = f"yz{q}"
                        act(gq[:, 0:ntok], yv, AF.Square, ["yz"], [kgq])
                        ts(gq[:, 0:ntok], gq[:, 0:ntok], 0.044715, ALU.mult, [kgq], [kgq], s2=1.0, op1=ALU.add)
                        tt(gq[:, 0:ntok], gq[:, 0:ntok], yv, ALU.mult, [kgq, "yz"], [kgq])
                        act(gq[:, 0:ntok], gq[:, 0:ntok], AF.Sigmoid, [kgq], [kgq], scale=1.5957691216)
                        tt(yv, yv, gq[:, 0:ntok], ALU.mult, ["yz", kgq], [kyz])
                        acopy(zb[:, q, 0:ntok], yv, [kyz], ["zb"])
                    slot, skey = wget("glu")
                    for fo in range(4):
                        pb, pk = proj_fm(slot, skey, fo, 4, lambda c: zb[:, c, 0:ntok], ["zb"])
                        act(g1[:, 0:ntok], pb[:, 0:ntok], AF.Sigmoid, [pk], ["g1"])
                        tt(g2b[:, 0:ntok], yz[:, fo, 0:ntok], g1[:, 0:ntok], ALU.mult, [f"yz{fo}", "g1"], ["g2b"])
                        tt(gatedT[:, fo, 0:ntok], g2b[:, 0:ntok], gbT[:, fo, 0:ntok], ALU.mult, ["g2b", "gbT"], ["gatedT"])
                    nrot[0] = 8
                    ph_keys.extend([f"yz{q}" for q in range(4)])
                    kb.retire(ph_keys)
                branch_out(1)
                if KDBG and t == 0 and l == 0:
                    kb.dma("sp", dbg_m[1], merged[:].rearrange("p a b -> p (a b)"), ["merged"], (), final=True)

                with contextlib.ExitStack() as ph:
                    ph_keys = []

                    def PH(name, shape, dt=F32, ph=ph, ph_keys=ph_keys):
                        ph_keys.append(name)
                        return S(name, shape, dt, stack=ph)
                    if sample:
                        h0l = PH("h0l", [128, 4, NS]); hso = PH("hso", [128, 4, NS])
                        l_io = PH("l_io", [NS, 512]); c_in = PH("c_in", [3 * NS, 512])
                        kb.dma("sp", l_io[:], slru[l], (), ["l_io"])
                        kb.dma("sp", c_in[:], sconv[l].rearrange("s j f -> (s j) f"), (), ["c_in"])
                        pbs, pks = bank()
                        for q in range(4):
                            tr(pbs[:, q * NS:(q + 1) * NS], l_io[0:NS, q * 128:(q + 1) * 128], ident[0:NS, 0:NS], ["l_io", "ident"], [pks])
                        acopy(h0l[:].rearrange("p q s -> p (q s)"), pbs[:, 0:4 * NS], [pks], ["h0l"])
                        pbs, pks = bank()
                        for q in range(4):
                            tr(pbs[:, q * 3 * NS:(q + 1) * 3 * NS], c_in[0:3 * NS, q * 128:(q + 1) * 128], ident[0:3 * NS, 0:3 * NS], ["c_in", "ident"], [pks])
                        for q in range(4):
                            acopy(cbuf[:, q, :, 0:3], pbs[:, q * 3 * NS:(q + 1) * 3 * NS].rearrange("p (s j) -> p s j", j=3), [pks], ["cbuf"])
                    lbuf = {}
                    for nm_, dt_ in (("cvv", F32), ("cvb", BF16), ("rg", F32), ("ig", F32), ("aa", F32), ("sq", F32), ("hT", F32)):
                        lbuf[nm_] = [PH(f"{nm_}{i}", [128, ntok], dt_) for i in range(2)]
                    tl = PH("tl", [128, NS])
                    hfo = PH("hfo", [128, 4])
                    if sample:
                        for tt_ in range(1, TS):
                            kb.dma("sp", ncvs[l][:, tt_ - 1, :], xco[tt_:NTS:TS, :], ["xco"], (), final=True)
                    elif last_prompt:
                        kb.dma("sp", ncvp[l], xco[0:3, :], ["xco"], (), final=True)
                    def lru_chunk(q, part):
                        cvv, cvb, rg, ig, aa, sq, hT = (lbuf[n_][q % 2] for n_ in ("cvv", "cvb", "rg", "ig", "aa", "sq", "hT"))
                        kcvv, kcvb, krg, kig, kaa, ksq, khT = (f"{n_}{q % 2}" for n_ in ("cvv", "cvb", "rg", "ig", "aa", "sq", "hT"))
                        if sample:
                            win = lambda j: cbuf[:, q, :, j:j + TS]
                            V2 = lambda a: a[:, 0:ntok].rearrange("p (s t) -> p s t", t=TS)
                        else:
                            win = lambda j: cbuf[:, q, j:j + ntok]
                            V2 = lambda a: a[:, 0:ntok]
                        if part == 1:
                            ts(V2(cvv), win(0), cw[:, l, q, 0:1], ALU.mult, ["cbuf", "cw", "cb"], [kcvv], s2=cb[:, l, q:q + 1], op1=ALU.add)
                            for j in range(1, 4):
                                stt(V2(cvv), win(j), cw[:, l, q, j:j + 1], V2(cvv), ALU.mult, ALU.add, ["cbuf", "cw", kcvv], [kcvv])
                            acopy(cvb[:, 0:ntok], cvv[:, 0:ntok], [kcvv], [kcvb])
                            pb, pk = bank()
                            mm(pb[:, 0:ntok], WA[:, l, q, :], cvb[:, 0:ntok], True, True, ["WA", kcvb], [pk])
                            act(rg[:, 0:ntok], pb[:, 0:ntok], AF.Sigmoid, [pk, "ba"], [krg], bias=ba[:, l, q:q + 1], scale=1.0)
                            pb, pk = bank()
                            mm(pb[:, 0:ntok], WX[:, l, q, :], cvb[:, 0:ntok], True, True, ["WX", kcvb], [pk])
                            act(ig[:, 0:ntok], pb[:, 0:ntok], AF.Sigmoid, [pk, "bx"], [kig], bias=bx[:, l, q:q + 1], scale=1.0)
                            return
                        act(sq[:, 0:ntok], rg[:, 0:ntok], AF.Exp, [krg, "c2"], [ksq], scale=c2[:, l, q:q + 1])
                        act(sq[:, 0:ntok], sq[:, 0:ntok], AF.Ln, [ksq], [ksq], scale=-1.0, bias=1.0)
                        act(sq[:, 0:ntok], sq[:, 0:ntok], AF.Exp, [ksq], [ksq], scale=0.5)
                        act(aa[:, 0:ntok], rg[:, 0:ntok], AF.Exp, [krg, "c1"], [kaa], scale=c1[:, l, q:q + 1])
                        tt(ig[:, 0:ntok], ig[:, 0:ntok], cvv[:, 0:ntok], ALU.mult, [kig, kcvv], [kig])
                        tt(ig[:, 0:ntok], ig[:, 0:ntok], sq[:, 0:ntok], ALU.mult, [kig, ksq], [kig])
                        if sample:
                            tt(tl[:], V2(aa)[:, :, 0], h0l[:, q, :], ALU.mult, [kaa, "h0l"], ["tl"])
                            tt(V2(ig)[:, :, 0], V2(ig)[:, :, 0], tl[:], ALU.add, [kig, "tl"], [kig])
                            memset(V2(aa)[:, :, 0:1], 0.0, [kaa])
                            kb.op("dve", lambda e: e.tensor_tensor_scan(out=hT[:, 0:ntok], data0=aa[:, 0:ntok], data1=ig[:, 0:ntok], initial=0.0,
                                                                        op0=ALU.mult, op1=ALU.add), [kaa, kig], [khT])
                            vcopy(hso[:, q, :], V2(hT)[:, :, TS - 1], [khT], ["hso"])
                        else:
                            kb.op("dve", lambda e, q=q: e.tensor_tensor_scan(out=hT[:, 0:ntok], data0=aa[:, 0:ntok], data1=ig[:, 0:ntok],
                                                                             initial=hl[:, l, q:q + 1], op0=ALU.mult, op1=ALU.add),
                                  [kaa, kig, "hl"], [khT])
                            vcopy(hl[:, l, q:q + 1], hT[:, ntok - 1:ntok], [khT], ["hl"])
                        tt(gatedT[:, q, 0:ntok], hT[:, 0:ntok], gcT[:, q, 0:ntok], ALU.mult, [khT, "gcT"], ["gatedT"])

                    for q0 in (0, 2):
                        lru_chunk(q0, 1); lru_chunk(q0 + 1, 1)
                        lru_chunk(q0, 2); lru_chunk(q0 + 1, 2)
                    if sample:
                        pbs, pks = bank()
                        for q in range(4):
                            tr(pbs[0:NS, q * 128:(q + 1) * 128], hso[:, q, :], ident[:], ["hso", "ident"], [pks])
                        acopy(l_io[:], pbs[0:NS, 0:512], [pks], ["l_io"])
                        kb.dma("sp", nlrus[l], l_io[:], ["l_io"], (), final=True)
                    elif last_prompt:
                        vcopy(hfo[:], hl[:, l, :], ["hl"], ["hfo"])
                        kb.dma("sp", nlrup[l].rearrange("(q p) -> p q", p=128), hfo[:], ["hfo"], (), final=True)
                    kb.retire(ph_keys)
                mTh["mT"] = PA("mT", [128, 8, ntok], BF16)
                branch_out(2)

                with contextlib.ExitStack() as ph:
                    ph_keys = []

                    def PH(name, shape, dt=F32, ph=ph, ph_keys=ph_keys):
                        ph_keys.append(name)
                        return S(name, shape, dt, stack=ph)
                    lng = PH("lng", [128, D]); lnb = PH("lnb", [128, D])
                    st6 = PH("st6", [128, 2, 6]); mv = PH("mv", [128, 2]); rstd = PH("rstd", [128, 1])
                    kb.dma("sp", lng[:], ln_g[l].partition_broadcast(128), (), ["lng"])
                    kb.dma("sp", lnb[:], ln_b[l].partition_broadcast(128), (), ["lnb"])
                    wos = [wget(f"wo{wo}", ahead=RING - wo) for wo in range(4)]
                    for s in range(nsub):
                        for wo in range(4):
                            slot, skey = wos[wo]
                            pb, pk = bank()
                            for fc in range(8):
                                mm(pb[0:R, 0:256], mTh["mT"][:, fc, s * 128:s * 128 + R], v8(slot)[:, fc, :], fc == 0, fc == 7, [skey, "mT"], [pk])
                            stt(xo[0:R, s, wo * 256:wo * 256 + 256], xi[0:R, s, wo * 256:wo * 256 + 256], DN_ALPHA, pb[0:R, 0:256],
                                ALU.mult, ALU.add, [xik, pk], [xok])
                        hv = xo[0:R, s, :]
                        for hh in range(2):
                            kb.op("dve", lambda e, hh=hh, hv=hv: e.bn_stats(out=st6[0:R, hh, :], in_=hv[:, hh * 512:hh * 512 + 512]), [xok], ["st6"])
                        kb.op("dve", lambda e: e.bn_aggr(out=mv[0:R, :], in_=st6[0:R, :, :].rearrange("p a b -> p (a b)")), ["st6"], ["mv"])
                        act(rstd[0:R, :], mv[0:R, 1:2], AF.Sqrt, ["mv", "epsc"], ["rstd"], bias=epsc[0:R, :], scale=1.0)
                        kb.op("dve", lambda e: e.reciprocal(rstd[0:R, :], rstd[0:R, :]), ["rstd"], ["rstd"])
                        ts(hv, hv, mv[0:R, 0:1], ALU.subtract, [xok, "mv", "rstd"], [xok], s2=rstd[0:R, 0:1], op1=ALU.mult)
                        tt(hv, hv, lng[0:R, :], ALU.mult, [xok, "lng"], [xok])
                        tt(hv, hv, lnb[0:R, :], ALU.add, [xok, "lnb"], [xok])
                        if l == DEPTH - 1 or os.environ.get("KDUMPL0"):
                            if sample:
                                kb.dma("sp", ys[:, :], hv, [xok], (), final=True)
                            else:
                                r0 = t * TILE + s * 128
                                kb.dma("sp", yp[r0:r0 + 128, :], hv, [xok], (), final=True)
                    kb.retire(ph_keys)
                kb.retire(pa_keys)

        for (t, l) in all_passes:
            if l == 0:
                if t == NPT:
                    kb.dma("sp", xtok[0][0:NTS, 0, :], xs, (), ["xtok0"])
                else:
                    kb.dma("sp", xtok[0][:], xp[t * TILE:(t + 1) * TILE, :].rearrange("(s p) d -> p s d", p=128), (), ["xtok0"])
            run_pass(t, l, l % 2, (l + 1) % 2)
        nx = int(os.environ.get("KEXTRA", "0"))
        if nx:
            kb.maxops = 10 ** 9
            pbx, pkx = bank()
            for _ in range(nx):
                mm(pbx[:, 0:128], identb[:], identb[:], True, True, ["identb"], [pkx])
        print("recorded ops:", kb.nrec, {k: v for k, v in kb.cnt.items()})
        kb.emit()
    return nc


_NC_CACHE = {}


def kernel(**inputs):
    f = lambda k: np.ascontiguousarray(np.asarray(inputs[k], dtype=np.float32))
    oh, neg = _bias_consts()
    par = (np.arange(128) // 16) % 2
    pm = np.stack([1.0 - par, par], 1).astype(np.float32)
    shared = {
        "rel_bias": f("rel_bias"), "w_in": f("w_in"), "sinks": f("sinks"),
        "w_br_a": f("w_branch_a"), "w_br_b": f("w_branch_b"), "w_br_c": f("w_branch_c"),
        "lam_re": f("ssm_lambda_re").reshape(2, 2048), "lam_im": f("ssm_lambda_im").reshape(2, 2048),
        "log_step": f("ssm_log_step"), "b_re": f("ssm_b_re"), "b_im": f("ssm_b_im"),
        "c_re": f("ssm_c_re"), "c_im": f("ssm_c_im"), "ssm_d": f("ssm_d"), "w_glu": f("ssm_w_glu"),
        "conv_w": f("conv_w"), "conv_b": f("conv_b"), "lru_w_a": f("lru_w_a"), "lru_b_a": f("lru_b_a"),
        "lru_w_x": f("lru_w_x"), "lru_b_x": f("lru_b_x"), "lru_lam": f("lru_lambda"), "w_out": f("w_out"),
        "ln_g": f("ln_g"), "ln_b": f("ln_b"), "oh2": oh, "negmask": neg, "pmask": pm,
    }
    x_prompt = f("x_prompt"); x_sample = f("x_sample")
    cache_k = f("cache_k").reshape(2, 128, 128, 128); cache_v = f("cache_v").reshape(2, 128, 128, 128)
    s_re = f("state_ssm_re").reshape(2, 128, 2048); s_im = f("state_ssm_im").reshape(2, 128, 2048)
    s_lru = f("state_lru"); s_conv = f("state_conv")
    in_maps = []
    for c in range(NCORES):
        sl = slice(NS * c, NS * c + NS)
        m = dict(shared)
        m.update({
            "xp": x_prompt[c], "xs": np.ascontiguousarray(x_sample[sl].reshape(NTS, D)),
            "ck": np.ascontiguousarray(cache_k[:, sl]), "cv": np.ascontiguousarray(cache_v[:, sl]),
            "sre": np.ascontiguousarray(s_re[:, sl]), "sim": np.ascontiguousarray(s_im[:, sl]),
            "slru": np.ascontiguousarray(s_lru[:, sl]), "sconv": np.ascontiguousarray(s_conv[:, sl]),
        })
        in_maps.append(m)
    if "nc" not in _NC_CACHE:
        _NC_CACHE["nc"] = build_nc()
    res = run_bass_kernel_spmd(_NC_CACHE["nc"], in_maps, core_ids=list(range(NCORES)))
    r = res.results
    cat = lambda k, ax: np.concatenate([np.asarray(r[c][k])[None] if ax is None else np.asarray(r[c][k]) for c in range(NCORES)], axis=0 if ax is None else ax)
    y_prompt = np.stack([r[c]["yp"] for c in range(NCORES)], 0).astype(np.float32)
    y_sample = np.concatenate([r[c]["ys"].reshape(NS, TS, D) for c in range(NCORES)], 0).astype(np.float32)
    stk = lambda k, shp: np.stack([np.asarray(r[c][k]).reshape(shp) for c in range(NCORES)], 1).astype(np.float32)
    ccat = lambda k, shp: np.concatenate([np.asarray(r[c][k]).reshape(shp) for c in range(NCORES)], 1).astype(np.float32)
    return (y_prompt, y_sample,
            stk("nkp", (2, 128, 2, 64)), stk("nvp", (2, 128, 2, 64)),
            stk("nrep", (2, 32, 64)), stk("nimp", (2, 32, 64)),
            stk("nlrup", (2, 512)), stk("ncvp", (2, 3, 512)),
            ccat("nks", (2, NS, 128, 2, 64)), ccat("nvs", (2, NS, 128, 2, 64)),
            ccat("nres", (2, NS, 32, 64)), ccat("nims", (2, NS, 32, 64)),
            ccat("nlrus", (2, NS, 512)), ccat("ncvs", (2, NS, 3, 512)))
```

```python
import contextlib
import math
import os
import numpy as np
import concourse.bass as bass
import concourse.mybir as mybir
from concourse.bass_utils import run_bass_kernel_spmd

F32 = mybir.dt.float32
BF16 = mybir.dt.bfloat16
I32 = mybir.dt.int32
AF = mybir.ActivationFunctionType
ALU = mybir.AluOpType
AX = mybir.AxisListType

NCORES = 8
D = 1024
DEPTH = 2
SEQ = 2048
NS = 16
TS = 4
NTS = NS * TS
TILE = 512
NPT = SEQ // TILE
NEG = -1e30
DN_ALPHA = (2 * DEPTH) ** 0.25
LN_EPS = 1e-5
SJ = 64
COMPUTE = ("pe", "dve", "act", "pool")


class KB:
    def __init__(self, nc, n_dma_sems=16):
        self.nc = nc
        self.eng = {"pe": nc.tensor, "dve": nc.vector, "act": nc.scalar, "pool": nc.gpsimd, "sp": nc.sync}
        self.streams = {k: [] for k in self.eng}
        self.sem = {k: nc.alloc_semaphore(name=f"c_{k}") for k in COMPUTE}
        self.cnt = {k: 0 for k in COMPUTE}
        self.waited = {}
        self.lastw = {}
        self.readers = {}
        self.dpool = {}
        for q in ("sp", "pool"):
            self.dpool[q] = {"sems": [nc.alloc_semaphore(name=f"d_{q}{i}") for i in range(n_dma_sems)],
                             "uses": [0] * n_dma_sems, "next": 0}
        self.final_tokens = []
        self.dma_since_barrier = []
        self.nrec = 0
        self.alias_pending = {}
        self.maxops = int(os.environ.get("KMAXOPS", "1000000000"))
        self.dummy = (self.sem["pe"], 0, "pe")

    def _wait(self, e, tok):
        sem, val, owner = tok
        if owner == "pe" and e == "pe":
            return
        key = (e, sem.name)
        if self.waited.get(key, 0) >= val:
            return
        self.waited[key] = val
        eng = self.eng[e]
        self.streams[e].append(lambda eng=eng, sem=sem, val=val: eng.wait_ge(sem, val))

    def retire(self, keys):
        for k in keys:
            toks = []
            if k in self.lastw:
                toks.append(self.lastw.pop(k))
            toks.extend(self.readers.pop(k, []))
            for t in toks:
                cur = self.alias_pending.get(t[0].name)
                if cur is None or cur[1] < t[1]:
                    self.alias_pending[t[0].name] = t

    def _deps(self, e, reads, writes):
        for k in writes:
            if k not in self.lastw and k not in self.readers:
                for t in self.alias_pending.values():
                    self._wait(e, t)
                break
        toks = []
        for k in reads:
            if k in self.lastw:
                toks.append(self.lastw[k])
        for k in writes:
            if k in self.lastw:
                toks.append(self.lastw[k])
            toks.extend(self.readers.get(k, []))
        for t in toks:
            self._wait(e, t)

    def _commit(self, tok, reads, writes):
        for k in writes:
            self.lastw[k] = tok
            self.readers[k] = []
        for k in reads:
            self.readers.setdefault(k, []).append(tok)

    def op(self, e, fn, reads=(), writes=()):
        self.nrec += 1
        if self.nrec > self.maxops:
            return self.dummy
        if self.nrec == int(os.environ.get("KTRACE", "-1")):
            import traceback
            traceback.print_stack()
            for k in list(reads) + list(writes):
                print("KEY", k, "lastw", (self.lastw[k][0].name, self.lastw[k][1]) if k in self.lastw else None,
                      "readers", [(t[0].name, t[1]) for t in self.readers.get(k, [])])
            print("CNT", self.cnt, {q: p["uses"] for q, p in self.dpool.items()})
        self._deps(e, reads, writes)
        self.cnt[e] += 1
        tok = (self.sem[e], self.cnt[e], e)
        sem = self.sem[e]
        eng = self.eng[e]
        self.streams[e].append(lambda eng=eng, fn=fn, sem=sem: fn(eng).then_inc(sem, 1))
        self._commit(tok, reads, writes)
        return tok

    def dma(self, q, out, in_, reads=(), writes=(), final=False, **kw):
        self.nrec += 1
        if self.nrec > self.maxops:
            return self.dummy
        self._deps(q, reads, writes)
        p = self.dpool[q]
        i = p["next"]
        p["next"] = (i + 1) % len(p["sems"])
        sem = p["sems"][i]
        if p["uses"][i] > 0:
            self._wait(q, (sem, 16 * p["uses"][i], None))
        p["uses"][i] += 1
        tok = (sem, 16 * p["uses"][i], None)
        eng = self.eng[q]
        self.streams[q].append(
            lambda eng=eng, out=out, in_=in_, sem=sem, kw=kw: eng.dma_start(out=out, in_=in_, **kw).then_inc(sem, 16))
        self._commit(tok, reads, writes)
        self.dma_since_barrier.append(tok)
        if final:
            self.final_tokens.append(tok)
        return tok

    def barrier(self):
        toks = [(self.sem[k], self.cnt[k], k) for k in COMPUTE if self.cnt[k] > 0] + list(self.dma_since_barrier)
        for e in self.eng:
            for t in toks:
                if t[2] == e:
                    continue
                self._wait(e, t)
        self.dma_since_barrier = []

    def emit(self):
        for t in self.final_tokens:
            self._wait("sp", t)
        nc = self.nc
        st = self.streams
        with nc.Block() as block:
            @block.sync
            def _(e):
                for f in st["sp"]:
                    f()

            @block.tensor
            def _(e):
                for f in st["pe"]:
                    f()

            @block.vector
            def _(e):
                for f in st["dve"]:
                    f()

            @block.scalar
            def _(e):
                for f in st["act"]:
                    f()

            @block.gpsimd
            def _(e):
                for f in st["pool"]:
                    f()


def _t5_bucket(d):
    max_exact = 16
    d = max(d, 0)
    if d < max_exact:
        return d
    v = np.float32(np.log(np.float32(max(d, 1)) / np.float32(max_exact))) / np.float32(math.log(128 / max_exact)) * np.float32(16)
    return min(max_exact + int(np.float32(v)), 31)


def _bias_consts():
    oh = np.zeros((32, 383), np.float32)
    neg = np.zeros((8, 383), np.float32)
    for i in range(383):
        d = 255 - i
        if 0 <= d < 128:
            oh[_t5_bucket(d), i] = 1.0
        else:
            neg[:, i] = NEG
    return oh, neg


def build_nc(passes=None, dbg=False):
    nc = bass.Bass("TRN2", target_bir_lowering=False)

    def din(name, shape):
        return nc.dram_tensor(name, list(shape), F32, kind="ExternalInput").ap()

    def dout(name, shape):
        return nc.dram_tensor(name, list(shape), F32, kind="ExternalOutput").ap()

    xp = din("xp", [SEQ, D]); xs = din("xs", [NTS, D])
    ck = din("ck", [2, NS, 128, 128]); cv = din("cv", [2, NS, 128, 128])
    sre = din("sre", [2, NS, 2048]); sim = din("sim", [2, NS, 2048])
    slru = din("slru", [2, NS, 512]); sconv = din("sconv", [2, NS, 3, 512])
    rel_bias = din("rel_bias", [32, 8]); w_in = din("w_in", [2, D, 6400]); sinks = din("sinks", [2, 8])
    w_br = [din("w_br_a", [2, 512, D]), din("w_br_b", [2, 512, D]), din("w_br_c", [2, 512, D])]
    lam_re = din("lam_re", [2, 2048]); lam_im = din("lam_im", [2, 2048]); log_step = din("log_step", [2, 32])
    b_re = din("b_re", [2, 32, 64, 16]); b_im = din("b_im", [2, 32, 64, 16])
    c_re = din("c_re", [2, 32, 16, 64]); c_im = din("c_im", [2, 32, 16, 64])
    ssm_d = din("ssm_d", [2, 512]); w_glu = din("w_glu", [2, 512, 512])
    conv_w = din("conv_w", [2, 4, 512]); conv_b = din("conv_b", [2, 512])
    lru_w_a = din("lru_w_a", [2, 8, 64, 64]); lru_b_a = din("lru_b_a", [2, 512])
    lru_w_x = din("lru_w_x", [2, 8, 64, 64]); lru_b_x = din("lru_b_x", [2, 512])
    lru_lam = din("lru_lam", [2, 512]); w_out = din("w_out", [2, D, D])
    ln_g = din("ln_g", [2, D]); ln_b = din("ln_b", [2, D])
    oh2 = din("oh2", [32, 383]); negmask = din("negmask", [8, 383]); pmask = din("pmask", [128, 2])

    yp = dout("yp", [SEQ, D]); ys = dout("ys", [NTS, D])
    nkp = dout("nkp", [2, 128, 128]); nvp = dout("nvp", [2, 128, 128])
    nrep = dout("nrep", [2, 2048]); nimp = dout("nimp", [2, 2048])
    nlrup = dout("nlrup", [2, 512]); ncvp = dout("ncvp", [2, 3, 512])
    nks = dout("nks", [2, NS, 128, 128]); nvs = dout("nvs", [2, NS, 128, 128])
    nres = dout("nres", [2, NS, 2048]); nims = dout("nims", [2, NS, 2048])
    nlrus = dout("nlrus", [2, NS, 512]); ncvs = dout("ncvs", [2, NS, 3, 512])
    KDBG = bool(os.environ.get("KDBG"))
    dbg_m = [dout(f"dbg_m{b}", [128, 8 * TILE]) for b in range(2)] if KDBG else None
    dbg_x = dout("dbg_x", [128, 4 * D]) if KDBG else None
    rv_t = nc.dram_tensor("rv_scr", [8, 383], F32, kind="Internal")
    rv = rv_t.ap()
    tb_scr = nc.dram_tensor("tb_scr", [128, 2048], F32, kind="Internal").ap()

    kb = KB(nc)
    ncd = nc.allow_non_contiguous_dma(reason="small strided parameter/state transfers")

    def mm(out, lhsT, rhs, start, stop, reads, writes):
        kb.op("pe", lambda e: e.matmul(out, lhsT=lhsT, rhs=rhs, start=start, stop=stop), reads, writes)

    def tr(out, in_, ident, reads, writes):
        kb.op("pe", lambda e: e.transpose(out, in_, ident), reads, writes)

    def act(out, in_, func, reads, writes, **kw):
        kb.op("act", lambda e: e.activation(out=out, in_=in_, func=func, **kw), reads, writes)

    def acopy(out, in_, reads, writes):
        kb.op("act", lambda e: e.copy(out=out, in_=in_), reads, writes)

    def vcopy(out, in_, reads, writes, eng="dve"):
        kb.op(eng, lambda e: e.tensor_copy(out, in_), reads, writes)

    def tt(out, in0, in1, op, reads, writes, eng="dve"):
        kb.op(eng, lambda e: e.tensor_tensor(out=out, in0=in0, in1=in1, op=op), reads, writes)

    def ts(out, in0, s1, op0, reads, writes, s2=None, op1=None, eng="dve"):
        if op1 is None:
            kb.op(eng, lambda e: e.tensor_scalar(out=out, in0=in0, scalar1=s1, scalar2=None, op0=op0), reads, writes)
        else:
            kb.op(eng, lambda e: e.tensor_scalar(out=out, in0=in0, scalar1=s1, scalar2=s2, op0=op0, op1=op1), reads, writes)

    def stt(out, in0, scalar, in1, op0, op1, reads, writes):
        kb.op("dve", lambda e: e.scalar_tensor_tensor(out=out, in0=in0, scalar=scalar, in1=in1, op0=op0, op1=op1), reads, writes)

    def memset(ap, val, writes, eng="dve"):
        kb.op(eng, lambda e: e.memset(ap, val), (), writes)

    with contextlib.ExitStack() as es:
        es.enter_context(ncd)

        uniq = [0]

        def S(name, shape, dt=F32, stack=es):
            uniq[0] += 1
            return stack.enter_context(nc.sbuf_tensor(f"{name}_{uniq[0]}", list(shape), dt))

        pst = [es.enter_context(nc.psum_tensor(f"ps{i}", [128, 1024], F32)) for i in range(4)]
        bank_rr = [0]
        nrot = [8]

        def bank():
            i = bank_rr[0]
            bank_rr[0] = (i + 1) % nrot[0]
            return pst[i // 2][:, (i % 2) * 512:(i % 2) * 512 + 512], f"pb{i}"

        def bank2():
            i = bank_rr[0]
            if i % 2:
                i = (i + 1) % nrot[0]
            bank_rr[0] = (i + 2) % nrot[0]
            return pst[i // 2][:, :], [f"pb{i}", f"pb{i + 1}"]

        ident = S("ident", [128, 128]); identb = S("identb", [128, 128], BF16)
        xtok = [S("xtokA", [128, 4, D]), S("xtokB", [128, 4, D])]
        xT = S("xT", [128, 8, TILE], BF16)
        RING = 4
        wr = [S(f"wr{i}", [128, 2048], BF16) for i in range(RING)]
        kTd = S("kTd", [128, 2, 2, 128 + TILE], BF16)
        vtok = S("vtok", [128, 2, 5, 128], BF16)
        merged = S("merged", [128, 8, TILE])
        skb = S("skb", [128, 2, 8])
        COS = S("COS", [128, 2, 16, SJ + 1]); SIN = S("SIN", [128, 2, 16, SJ + 1])
        MAGT = S("MAGT", [128, 2, 16, SJ])
        WBre = S("WBre", [128, 2, 4, 2, 128], BF16); WBim = S("WBim", [128, 2, 4, 2, 128], BF16)
        WCre = S("WCre", [128, 2, 16, 128], BF16); WCim = S("WCim", [128, 2, 16, 128], BF16)
        AbR = S("AbR", [128, 2, 16]); AbI = S("AbI", [128, 2, 16])
        CaR = S("CaR", [128, 2, 16]); CaI = S("CaI", [128, 2, 16])
        dS5 = S("dS5", [128, 2, 4])
        Gre = S("Gre", [128, 2, 16]); Gim = S("Gim", [128, 2, 16])
        WA = S("WA", [128, 2, 4, 128], BF16); WX = S("WX", [128, 2, 4, 128], BF16)
        cw = S("cw", [128, 2, 4, 4]); cb = S("cb", [128, 2, 4]); ba = S("ba", [128, 2, 4]); bx = S("bx", [128, 2, 4])
        c1 = S("c1", [128, 2, 4]); c2 = S("c2", [128, 2, 4])
        hl = S("hl", [128, 2, 4]); chist = S("chist", [128, 2, 4, 3])
        epsc = S("epsc", [128, 1])

        kb.op("pool", lambda e: e.memset(ident[:], 0.0), (), ["ident"])
        kb.op("pool", lambda e: e.affine_select(out=ident[:], in_=ident[:], pattern=[[-1, 128]], compare_op=ALU.not_equal,
                                                fill=1.0, base=0, channel_multiplier=1), ["ident"], ["ident"])
        vcopy(identb[:], ident[:], ["ident"], ["identb"])
        memset(epsc[:], LN_EPS, ["epsc"])
        memset(Gre[:], 0.0, ["G"]); memset(Gim[:], 0.0, ["G"])
        memset(hl[:], 0.0, ["hl"]); memset(chist[:], 0.0, ["chist"])
        kb.dma("sp", skb[:].rearrange("p l h -> p (l h)"), sinks.rearrange("l h -> (l h)").partition_broadcast(128), (), ["skb"])

        class _Stop(Exception):
            pass
        STAGE = float(os.environ.get("KSTAGE", "99"))

        def stage(n):
            if STAGE < n:
                raise _Stop()
        try:
          with contextlib.ExitStack() as ss:
            def SS(name, shape, dt=F32):
                return S(name, shape, dt, stack=ss)
            stage(1)
            rb = SS("rb", [32, 8]); ohs = SS("ohs", [32, 383]); ngs = SS("ngs", [8, 383]); rvs = SS("rvs", [8, 383])
            kb.dma("sp", rb[:], rel_bias, (), ["rb"])
            kb.dma("sp", ohs[:], oh2, (), ["ohs"])
            kb.dma("sp", ngs[:], negmask, (), ["ngs"])
            pb, pk = bank()
            mm(pb[0:8, 0:383], rb[:], ohs[:], True, True, ["rb", "ohs"], [pk])
            tt(rvs[:], pb[0:8, 0:383], ngs[:], ALU.add, [pk, "ngs"], ["rvs"])
            kb.dma("sp", rv, rvs[:], ["rvs"], ["rv_dram"])
            stage(1.5)
            Tq = SS("Tq", [128, 2048]); Tf = SS("Tf", [128, 2048]); Jm = SS("Jm", [128, 128])
            kb.dma("sp", Tq[:].rearrange("p (h s) -> p h s", h=8), bass.AP(rv_t, 0, [[1, 128], [383, 8], [1, 256]]), ["rv_dram"], ["Tq"])
            kb.op("pool", lambda e: e.memset(Jm[:], 0.0), (), ["Jm"])
            kb.op("pool", lambda e: e.affine_select(out=Jm[:], in_=Jm[:], pattern=[[1, 128]], compare_op=ALU.not_equal,
                                                    fill=1.0, base=-127, channel_multiplier=1), ["Jm"], ["Jm"])
            for c4 in range(4):
                pb, pk = bank()
                mm(pb[:, :], Jm[:], Tq[:, 512 * c4:512 * c4 + 512], True, True, ["Jm", "Tq"], [pk])
                acopy(Tf[:, 512 * c4:512 * c4 + 512], pb[:, :], [pk], ["Tf"])
            kb.dma("sp", tb_scr, Tf[:], ["Tf"], ["tb_dram"])

            stage(2)
            jj_i = SS("jj_i", [128, SJ + 1], I32); jj = SS("jj", [128, SJ + 1])
            kb.op("pool", lambda e: e.iota(jj_i[:], pattern=[[1, SJ + 1]], base=0, channel_multiplier=0), (), ["jj_i"])
            vcopy(jj[:], jj_i[:], ["jj_i"], ["jj"])
            lre = SS("lre", [128, 2, 16]); lim = SS("lim", [128, 2, 16]); lst = SS("lst", [128, 2, 16])
            Bre = SS("Bre", [128, 16, 16]); Bim = SS("Bim", [128, 16, 16]); Cn = SS("Cn", [128, 4, 64]); Cn2 = SS("Cn2", [128, 4, 128])
            pmk = SS("pmk", [128, 2])
            kb.dma("sp", pmk[:], pmask, (), ["pmk"])
            BBr = SS("BBr", [128, 16, 16]); BBi = SS("BBi", [128, 16, 16]); tB = SS("tB", [128, 16, 16])
            ZB = SS("ZB", [128, 16, 128])
            phi = SS("phi", [128, 16, SJ + 1]); rr = SS("rr", [128, 16, SJ + 1]); ri = SS("ri", [128, 16, SJ + 1], I32)
            rf = SS("rf", [128, 16, SJ + 1]); gg = SS("gg", [128, 16, SJ + 1])
            sm = [SS(f"sm{i}", [128, 16]) for i in range(12)]
            wst = SS("wst", [128, 4, 128]); lamt = SS("lamt", [128, 4])
            for l in range(2):
                kb.dma("sp", lre[:, l, :], lam_re[l].rearrange("(k p) -> p k", p=128), (), ["lre"])
                kb.dma("sp", lim[:, l, :], lam_im[l].rearrange("(k p) -> p k", p=128), (), ["lim"])
                for g2 in range(2):
                    kb.dma("sp", lst[64 * g2:64 * g2 + 64, l, :],
                           log_step[l].rearrange("(k t) -> t k", t=2)[g2].partition_broadcast(64), (), ["lst"])
                kb.dma("sp", dS5[:, l, :], ssm_d[l].rearrange("(q p) -> p q", p=128), (), ["dS5"])
            stage(2.5)
            for l in range(2):
                step, lr, th, mag, den, a1, t0, t1, fre, fim, rden, t2 = sm
                act(step[:], lst[:, l, :], AF.Exp, ["lst"], ["sm0"])
                ts(lr[:], lre[:, l, :], -1e-4, ALU.min, ["lre"], ["sm1"])
                tt(th[:], lim[:, l, :], step[:], ALU.mult, ["lim", "sm0"], ["sm2"])
                tt(t0[:], lr[:], step[:], ALU.mult, ["sm1", "sm0"], ["sm6"])
                act(mag[:], t0[:], AF.Exp, ["sm6"], ["sm3"])
                tt(phi[:], th[:].unsqueeze(2).to_broadcast([128, 16, SJ + 1]), jj[:].unsqueeze(1).to_broadcast([128, 16, SJ + 1]),
                   ALU.mult, ["sm2", "jj"], ["phi"])
                for which, TAB in ((0, SIN), (1, COS)):
                    ts(rr[:], phi[:], 1.0 / (2 * math.pi), ALU.mult, ["phi"], ["rr"], s2=0.25 * which, op1=ALU.add)
                    vcopy(ri[:], rr[:], ["rr"], ["ri"])
                    vcopy(rf[:], ri[:], ["ri"], ["rf"])
                    tt(rr[:], rr[:], rf[:], ALU.subtract, ["rr", "rf"], ["rr"])
                    ts(gg[:], rr[:], 0.5, ALU.is_gt, ["rr"], ["gg"])
                    tt(rr[:], rr[:], gg[:], ALU.subtract, ["rr", "gg"], ["rr"])
                    ts(gg[:], rr[:], -0.5, ALU.is_lt, ["rr"], ["gg"])
                    tt(rr[:], rr[:], gg[:], ALU.add, ["rr", "gg"], ["rr"])
                    ts(rr[:], rr[:], 0.5, ALU.min, ["rr"], ["rr"], s2=-0.5, op1=ALU.max)
                    act(TAB[:, l, :, :], rr[:], AF.Sin, ["rr"], ["COS" if which else "SIN"], scale=6.283185)
                tt(AbR[:, l, :], mag[:], COS[:, l, :, 1], ALU.mult, ["sm3", "COS"], ["Ab"])
                tt(AbI[:, l, :], mag[:], SIN[:, l, :, 1], ALU.mult, ["sm3", "SIN"], ["Ab"])
                tt(CaR[:, l, :], mag[:], COS[:, l, :, SJ], ALU.mult, ["sm3", "COS"], ["Ca"])
                tt(CaI[:, l, :], mag[:], SIN[:, l, :, SJ], ALU.mult, ["sm3", "SIN"], ["Ca"])
                vcopy(MAGT[:, l, :, :], mag[:].unsqueeze(2).to_broadcast([128, 16, SJ]), ["sm3"], ["MAGT"])
                memset(MAGT[:, l, :, 0:1], 0.0, ["MAGT"])
                li_ = lim[:, l, :]
                tt(den[:], lr[:], lr[:], ALU.mult, ["sm1"], ["sm4"])
                tt(t1[:], li_, li_, ALU.mult, ["lim"], ["sm7"])
                tt(den[:], den[:], t1[:], ALU.add, ["sm4", "sm7"], ["sm4"])
                kb.op("dve", lambda e, rden=rden, den=den: e.reciprocal(rden[:], den[:]), ["sm4"], ["sm10"])
                ts(a1[:], AbR[:, l, :], -1.0, ALU.add, ["Ab"], ["sm5"])
                tt(t0[:], a1[:], lr[:], ALU.mult, ["sm5", "sm1"], ["sm6"])
                tt(t1[:], AbI[:, l, :], li_, ALU.mult, ["Ab", "lim"], ["sm7"])
                tt(fre[:], t0[:], t1[:], ALU.add, ["sm6", "sm7"], ["sm8"])
                tt(fre[:], fre[:], rden[:], ALU.mult, ["sm8", "sm10"], ["sm8"])
                tt(t0[:], AbI[:, l, :], lr[:], ALU.mult, ["Ab", "sm1"], ["sm6"])
                tt(t1[:], a1[:], li_, ALU.mult, ["sm5", "lim"], ["sm7"])
                tt(fim[:], t0[:], t1[:], ALU.subtract, ["sm6", "sm7"], ["sm9"])
                tt(fim[:], fim[:], rden[:], ALU.mult, ["sm9", "sm10"], ["sm9"])
                stage(3)
                kb.dma("sp", Bre[:], b_re[l].rearrange("(k t) p c -> (t p) k c", t=2), ["Bre"], ["Bre"])
                kb.dma("sp", Bim[:], b_im[l].rearrange("(k t) p c -> (t p) k c", t=2), ["Bim"], ["Bim"])
                frb = fre[:].unsqueeze(2).to_broadcast([128, 16, 16]); fib = fim[:].unsqueeze(2).to_broadcast([128, 16, 16])
                tt(BBr[:], Bre[:], frb, ALU.mult, ["Bre", "sm8"], ["BBr"])
                tt(tB[:], Bim[:], fib, ALU.mult, ["Bim", "sm9"], ["tB"])
                tt(BBr[:], BBr[:], tB[:], ALU.subtract, ["BBr", "tB"], ["BBr"])
                tt(BBi[:], Bim[:], frb, ALU.mult, ["Bim", "sm8"], ["BBi"])
                tt(tB[:], Bre[:], fib, ALU.mult, ["Bre", "sm9"], ["tB"])
                tt(BBi[:], BBi[:], tB[:], ALU.add, ["BBi", "tB"], ["BBi"])
                for src, skey_, dst, nm in ((BBr, "BBr", WBre, "WBre"), (BBi, "BBi", WBim, "WBim")):
                    memset(ZB[:], 0.0, ["ZB"])
                    for km in range(4):
                        for g2 in range(2):
                            c0 = 16 * (2 * km + g2)
                            vcopy(ZB[64 * g2:64 * g2 + 64, km::4, c0:c0 + 16], src[64 * g2:64 * g2 + 64, km::4, :], [skey_], ["ZB"])
                    for k4 in range(4):
                        pb, pk = bank()
                        for kq in range(4):
                            k = 4 * k4 + kq
                            tr(pb[:, kq * 128:(kq + 1) * 128], ZB[:, k, :], ident[:], ["ZB", "ident"], [pk])
                        for kq in range(4):
                            h_, kk_ = kq // 2, kq % 2
                            acopy(dst[64 * h_:64 * h_ + 64, l, k4, kk_, :], pb[64 * h_:64 * h_ + 64, kq * 128:(kq + 1) * 128], [pk], [nm])
                for csrc, sgn, wdst, nm in ((c_re, 1.0, WCre, "WCre"), (c_im, -1.0, WCim, "WCim")):
                    memset(wdst[:, l, :, :], 0.0, [nm])
                    kb.dma("sp", Cn[:], csrc[l].rearrange("(a g) c p -> (g c) a p", g=8), ["Cn"], ["Cn"])
                    for t_ in range(2):
                        ts(Cn2[:, :, 64 * t_:64 * t_ + 64], Cn[:], pmk[:, t_:t_ + 1], ALU.mult, ["Cn", "pmk"], ["Cn2"], s2=sgn, op1=ALU.mult)
                    pb, pk = bank()
                    for a in range(4):
                        tr(pb[:, a * 128:(a + 1) * 128], Cn2[:, a, :], ident[:], ["Cn2", "ident"], [pk])
                    pbv = pb.rearrange("p (a b) -> p a b", a=4)
                    for km in range(4):
                        acopy(wdst[:, l, km::4, 32 * km:32 * km + 32], pbv[:, :, 32 * km:32 * km + 32], [pk], [nm])
                stage(4)
                for j in range(4):
                    kb.dma("sp", cw[:, l, :, j], conv_w[l][j].rearrange("(q p) -> p q", p=128), (), ["cw"])
                kb.dma("sp", cb[:, l, :], conv_b[l].rearrange("(q p) -> p q", p=128), (), ["cb"])
                kb.dma("sp", ba[:, l, :], lru_b_a[l].rearrange("(q p) -> p q", p=128), (), ["ba"])
                kb.dma("sp", bx[:, l, :], lru_b_x[l].rearrange("(q p) -> p q", p=128), (), ["bx"])
                kb.dma("sp", lamt[:], lru_lam[l].rearrange("(q p) -> p q", p=128), ["lamt"], ["lamt"])
                act(lamt[:], lamt[:], AF.Exp, ["lamt"], ["lamt"], scale=-1.0)
                act(lamt[:], lamt[:], AF.Ln, ["lamt"], ["lamt"], bias=1.0)
                ts(c1[:, l, :], lamt[:], -8.0, ALU.mult, ["lamt"], ["c1"])
                ts(c2[:, l, :], lamt[:], -16.0, ALU.mult, ["lamt"], ["c2"])
                for wsrc, wdst, nm in ((lru_w_a, WA, "WA"), (lru_w_x, WX, "WX")):
                    memset(wst[:], 0.0, ["wst"])
                    for hh in range(2):
                        kb.dma("sp", wst[64 * hh:64 * hh + 64, :, 64 * hh:64 * hh + 64],
                               wsrc[l].rearrange("(q t) d e -> t d q e", t=2)[hh], ["wst"], ["wst"])
                    vcopy(wdst[:, l, :, :], wst[:], ["wst"], [nm])
        except _Stop:
            pass
        kb.barrier()

        w_in_v = [w_in[l].rearrange("(k p) c -> p k c", p=128) for l in range(2)]
        w_out_v = [w_out[l].rearrange("(k p) c -> p k c", p=128) for l in range(2)]
        w_br_v = [[w_br[b][l].rearrange("(k p) c -> p k c", p=128) for l in range(2)] for b in range(3)]
        w_glu_v = [w_glu[l].rearrange("(k p) c -> p k c", p=128) for l in range(2)]

        def v8(slot):
            return slot[:].rearrange("p (k c) -> p k c", k=8)

        def v4(slot):
            return slot[:].rearrange("p (k c) -> p k c", k=4)

        def win_spec(l, c0):
            return [(lambda s: v8(s), w_in_v[l][:, :, c0:c0 + 256])]

        def pass_specs(l):
            sp = []
            sp += [("q0", win_spec(l, 0)), ("q1", win_spec(l, 256)), ("kv", win_spec(l, 512))]
            sp += [("kd", [(lambda s, j=j, h=h: v8(s)[:, :, 128 * h + 64 * j:128 * h + 64 * j + 64], w_in_v[l][:, :, 512 + 64 * h:576 + 64 * h])
                           for h in range(2) for j in range(2)])]
            sp += [("ga0", win_spec(l, 768)), ("ga1", win_spec(l, 1024))]
            for b, (c_in, c_gm) in enumerate(((1280, 3328), (2304, 4352), (0, 5376))):
                if b == 1:
                    sp += [("u0", win_spec(l, 1280)), ("u1", win_spec(l, 1536)), ("gb0", win_spec(l, 1792)), ("gb1", win_spec(l, 2048))]
                    sp += [("xc0", win_spec(l, 2304)), ("xc1", win_spec(l, 2560)), ("gc0", win_spec(l, 2816)), ("gc1", win_spec(l, 3072))]
                    sp += [("glu", [(lambda s: v4(s), w_glu_v[l])])]
                g0 = 3328 + 1024 * b
                for h in range(2):
                    sp += [(f"br{b}{h}", [(lambda s: v4(s), w_br_v[b][l][:, :, 512 * h:512 * h + 512])])]
                    sp += [(f"gm{b}{i}", win_spec(l, g0 + 256 * i)) for i in (2 * h, 2 * h + 1)]
            sp += [(f"wo{i}", [(lambda s: v8(s), w_out_v[l][:, :, 256 * i:256 * i + 256])]) for i in range(4)]
            return sp

        all_passes = passes if passes is not None else [(t, l) for t in range(NPT + 1) for l in range(2)]
        NSLOT = len(pass_specs(0))
        wc_t = nc.dram_tensor("wcache", [2, NSLOT, 128, 2048], BF16, kind="Internal")
        wc = wc_t.ap()
        for l in sorted({l for (_, l) in all_passes}):
            for i, (name, pieces) in enumerate(pass_specs(l)):
                for dstf, src in pieces:
                    kb.dma("pool", dstf(wc[l, i]), src, (), [f"wc{l}_{i}"])
        wq = []
        for (t, l) in all_passes:
            wq += [(name, l, i) for i, (name, _) in enumerate(pass_specs(l))]
        wstate = {"issued": 0, "used": 0}

        def wissue():
            n = wstate["issued"]
            name, l_, i_ = wq[n]
            kb.dma("sp", wr[n % RING][:], wc[l_, i_], [f"wc{l_}_{i_}"], [f"wr{n % RING}"])
            wstate["issued"] = n + 1

        def wget(name, ahead=RING):
            n = wstate["used"]
            assert wq[n][0] == name, (wq[n][0], name)
            while wstate["issued"] < min(len(wq), n + ahead):
                wissue()
            wstate["used"] = n + 1
            return wr[n % RING], f"wr{n % RING}"

        def run_pass(t, l, xin, xout):
            sample = (t == NPT)
            ntok = NTS if sample else TILE
            nsub = 1 if sample else 4
            R = 64 if sample else 128
            last_prompt = (t == NPT - 1)
            xi = xtok[xin]; xo = xtok[xout]
            xik = f"xtok{xin}"; xok = f"xtok{xout}"

            for s in range(nsub):
                for k4 in range(2):
                    pb, pk = bank()
                    for kk in range(4):
                        kc = 4 * k4 + kk
                        tr(pb[:, kk * 128:kk * 128 + R], xi[0:R, s, kc * 128:(kc + 1) * 128], ident[0:R, 0:R], [xik, "ident"], [pk])
                    acopy(xT[:, 4 * k4:4 * k4 + 4, s * 128:s * 128 + R], pb.rearrange("p (a b) -> p a b", a=4)[:, :, 0:R], [pk], ["xT"])

            def proj_fm(slot, skey, cchunk, nk, rhs_of_kc, rkeys):
                pb, pk = bank()
                view = v8(slot) if nk == 8 else v4(slot)
                for kc in range(nk):
                    mm(pb[:, 0:ntok], view[:, kc, cchunk * 128:(cchunk + 1) * 128], rhs_of_kc(kc), kc == 0, kc == nk - 1,
                       [skey] + rkeys, [pk])
                return pb, pk

            xTk = lambda kc: xT[:, kc, 0:ntok]

            with contextlib.ExitStack() as pa:
                pa_keys = []

                def PA(name, shape, dt=F32):
                    pa_keys.append(name)
                    return S(name, shape, dt, stack=pa)
                gatedT = PA("gatedT", [128, 4, TILE], BF16)
                mTh = {}
                W7 = TS + 3
                gcT = PA("gcT", [128, 4, ntok], BF16)
                cbuf = PA("cbuf", [128, 4, NS, W7]) if sample else PA("cbuf", [128, 4, TILE + 3])
                xco = PA("xco", [NTS, 512]) if sample else PA("xco", [4, 512])
                lru_st = {}

                def xc_unit(q):
                    if q % 2 == 0:
                        lru_st["slot"] = wget(f"xc{q // 2}")
                        slot, skey = lru_st["slot"]
                        if sample or last_prompt:
                            nr = NTS if sample else 3
                            cstart = 0 if sample else TILE - 3
                            pbx, pkx = bank()
                            for kc in range(8):
                                mm(pbx[0:nr, 0:256], xT[:, kc, cstart:cstart + nr], v8(slot)[:, kc, :], kc == 0, kc == 7, [skey, "xT"], [pkx])
                            acopy(xco[0:nr, 256 * (q // 2):256 * (q // 2) + 256], pbx[0:nr, 0:256], [pkx], ["xco"])
                    slot, skey = lru_st["slot"]
                    pb, pk = proj_fm(slot, skey, q % 2, 8, xTk, ["xT"])
                    if sample:
                        acopy(cbuf[:, q, :, 3:W7], pb[:, 0:ntok].rearrange("p (s t) -> p s t", t=TS), [pk], ["cbuf"])
                    else:
                        acopy(cbuf[:, q, 0:3], chist[:, l, q, :], ["chist"], ["cbuf"])
                        acopy(cbuf[:, q, 3:3 + ntok], pb[:, 0:ntok], [pk], ["cbuf"])
                        acopy(chist[:, l, q, :], cbuf[:, q, ntok:ntok + 3], ["cbuf"], ["chist"])

                def gc_unit(q):
                    if q % 2 == 0:
                        lru_st["slot"] = wget(f"gc{q // 2}")
                    slot, skey = lru_st["slot"]
                    pb, pk = proj_fm(slot, skey, q % 2, 8, xTk, ["xT"])
                    act(gcT[:, q, 0:ntok], pb[:, 0:ntok], AF.Silu, [pk], ["gcT"])
                lru_units = [lambda q=q: xc_unit(q) for q in range(4)] + [lambda q=q: gc_unit(q) for q in range(4)]

                def branch_out(b):
                    with contextlib.ExitStack() as bo:
                        bo_keys = ["gms0", "gms1", "tmpm"]
                        gms = [S("gms0", [128, ntok], F32, stack=bo), S("gms1", [128, ntok], F32, stack=bo)]
                        tmpm = S("tmpm", [128, ntok], F32, stack=bo)
                        _branch_out(b, gms, tmpm)
                        kb.retire(bo_keys)

                def _branch_out(b, gms, tmpm):
                    mT = mTh.get("mT")
                    ypb = []
                    for fo in range(8):
                        if fo % 4 == 0:
                            slot, skey = wget(f"br{b}{fo // 4}")
                        ypb.append(proj_fm(slot, skey, fo % 4, 4, lambda c: gatedT[:, c, 0:ntok], ["gatedT"]))
                        if fo % 4 == 3:
                            for f2 in range(fo - 3, fo + 1):
                                if f2 % 2 == 0:
                                    gslot, gkey = wget(f"gm{b}{f2 // 2}")
                                gpb, gpk = proj_fm(gslot, gkey, f2 % 2, 8, xTk, ["xT"])
                                g = gms[f2 % 2]; gk = f"gms{f2 % 2}"
                                act(g[:, 0:ntok], gpb[:, 0:ntok], AF.Sigmoid, [gpk], [gk])
                                yb, yk = ypb[f2]
                                if b == 0:
                                    tt(merged[:, f2, 0:ntok], yb[:, 0:ntok], g[:, 0:ntok], ALU.mult, [yk, gk], ["merged"])
                                elif b == 1:
                                    tt(tmpm[:, 0:ntok], yb[:, 0:ntok], g[:, 0:ntok], ALU.mult, [yk, gk], ["tmpm"])
                                    tt(merged[:, f2, 0:ntok], merged[:, f2, 0:ntok], tmpm[:, 0:ntok], ALU.add, ["merged", "tmpm"], ["merged"])
                                else:
                                    tt(tmpm[:, 0:ntok], yb[:, 0:ntok], g[:, 0:ntok], ALU.mult, [yk, gk], ["tmpm"])
                                    tt(mT[:, f2, 0:ntok], merged[:, f2, 0:ntok], tmpm[:, 0:ntok], ALU.add, ["merged", "tmpm"], ["mT"])

                with contextlib.ExitStack() as ph:
                    ph_keys = []

                    def PH(name, shape, dt=F32, ph=ph, ph_keys=ph_keys):
                        ph_keys.append(name)
                        return S(name, shape, dt, stack=ph)
                    qT = PH("qT", [128, 4, TILE], BF16)
                    Tb = PH("Tb", [128, 8, 256])
                    kb.dma("sp", Tb[:].rearrange("p h s -> p (h s)"), tb_scr, ["tb_dram"], ["Tb"])
                    gatok = PH("gatok", [128, 4, 512], BF16)
                    kvo = PH("kvo", [128, 256])
                    KW = 132 if sample else 256
                    sc2 = [PH(f"sc{i}", [128, 4, KW]) for i in range(2)]; Pm2 = [PH(f"Pm{i}", [128, 4, KW], BF16) for i in range(2)]
                    PT = PH("PT", [128, 8, 128], BF16)
                    mx2 = [PH(f"mx{i}", [128, 4]) for i in range(2)]; ngm2 = [PH(f"ngm{i}", [128, 4]) for i in range(2)]
                    rs2 = [PH(f"rs{i}", [128, 4]) for i in range(2)]; esk2 = [PH(f"esk{i}", [128, 4]) for i in range(2)]
                    rinv2 = [PH(f"rinv{i}", [128, 4]) for i in range(2)]; att = PH("att", [128, 256])
                    gA2 = [PH(f"gA{i}", [128, 512], BF16) for i in range(2)]
                    if sample:
                        Ks = PH("Ks", [128, NS, 128], BF16); Kd = PH("Kd", [128, 2, 2, 64], BF16)
                        KTs = PH("KTs", [128, NS, 2, 132], BF16); Vs = PH("Vs", [128, NS, 128], BF16)
                        vnew = PH("vnew", [4, NS, 128], BF16)
                        kTn = PH("kTn", [128, 2, NTS], BF16); gas2 = [PH(f"gas{i}", [4, 512], BF16) for i in range(2)]

                    for c2_ in range(2):
                        slot, skey = wget(f"q{c2_}")
                        for cc in range(2):
                            pb, pk = proj_fm(slot, skey, cc, 8, xTk, ["xT"])
                            acopy(qT[:, 2 * c2_ + cc, 0:ntok], pb[:, 0:ntok], [pk], ["qT"])
                    slot, skey = wget("kv")
                    for s in range(nsub):
                        pb, pk = bank()
                        for kc in range(8):
                            mm(pb[0:R, 0:256], xT[:, kc, s * 128:s * 128 + R], v8(slot)[:, kc, :], kc == 0, kc == 7, [skey, "xT"], [pk])
                        if not sample:
                            acopy(vtok[:, l, 1 + s, :], pb[:, 128:256], [pk], ["vtok"])
                            if last_prompt and s == 3:
                                acopy(kvo[:], pb[:, 0:256], [pk], ["kvo"])
                                kb.dma("sp", nkp[l], kvo[:, 0:128], ["kvo"], (), final=True)
                                kb.dma("sp", nvp[l], kvo[:, 128:256], ["kvo"], (), final=True)
                        else:
                            vcopy(kvo[0:R, :], pb[0:R, 0:256], [pk], ["kvo"])
                            for tt_ in range(TS):
                                kb.dma("sp", nks[l][:, 124 + tt_, :], kvo[tt_:NTS:TS, 0:128], ["kvo"], (), final=True)
                                kb.dma("sp", nvs[l][:, 124 + tt_, :], kvo[tt_:NTS:TS, 128:256], ["kvo"], (), final=True)
                            kb.dma("sp", nks[l][:, 0:124, :], ck[l][:, 4:128, :], (), (), final=True)
                            kb.dma("sp", nvs[l][:, 0:124, :], cv[l][:, 4:128, :], (), (), final=True)
                            for i in range(NS):
                                pb2, pk2 = bank()
                                for kc in range(8):
                                    mm(pb2[0:TS, 0:128], xT[:, kc, TS * i:TS * i + TS], v8(slot)[:, kc, 128:256], kc == 0, kc == 7,
                                       [skey, "xT"], [pk2])
                                acopy(vnew[:, i, :], pb2[0:TS, 0:128], [pk2], ["vnew"])
                    slot, skey = wget("kd")
                    for kvh in range(2):
                        pb, pk = proj_fm(slot, skey, kvh, 8, xTk, ["xT"])
                        if not sample:
                            acopy(kTd[:, l, kvh, 128:128 + ntok], pb[:, 0:ntok], [pk], ["kTd"])
                        else:
                            acopy(kTn[:, kvh, :], pb[:, 0:ntok], [pk], ["kTn"])
                    for h2 in range(2):
                        slot, skey = wget(f"ga{h2}")
                        for s in range(nsub):
                            pb, pk = bank()
                            for kc in range(8):
                                mm(pb[0:R, 0:256], xT[:, kc, s * 128:s * 128 + R], v8(slot)[:, kc, :], kc == 0, kc == 7, [skey, "xT"], [pk])
                            act(gatok[0:R, s, h2 * 256:h2 * 256 + 256], pb[0:R, 0:256], AF.Silu, [pk], ["gatok"])

                    def attend_all(blocks):
                        units = [(bi, hf) for bi in range(len(blocks)) for hf in range(2)]
                        stt_ = {}

                        def S1(u):
                            bi, hf = units[u]
                            B = blocks[bi]
                            nq, segs = B["nq"], B["segs"]
                            if hf == 0 and B.get("prep"):
                                B["prep"]()
                            pb2, pks = bank2()
                            scv = pb2.rearrange("p (h s) -> p h s", h=4)
                            for hh in range(4):
                                h = 4 * hf + hh
                                cch, half = h // 2, h % 2
                                off = 0
                                for (kfn, vap, n) in segs:
                                    mm(scv[0:nq, half * 2 + hh // 2, off:off + n], qT[64 * half:64 * half + 64, cch, B["col"]:B["col"] + nq],
                                       kfn(hf)[64 * half:64 * half + 64, :], True, True, ["qT", "kTd", "KTs"], [pks[half]])
                                    off += n
                            stt_[u] = (scv, pks)

                        def S2(u, part):
                            bi, hf = units[u]
                            B = blocks[bi]
                            nq, segs, nkeys, tb_c0 = B["nq"], B["segs"], B["nkeys"], B["tb_c0"]
                            par = u % 2
                            sc_, Pm_, mx_, ngm_, rs_, esk_, rinv_ = sc2[par], Pm2[par], mx2[par], ngm2[par], rs2[par], esk2[par], rinv2[par]
                            ksc, kPm, kmx, kngm, krs, kesk, krinv = (f"{n_}{par}" for n_ in ("sc", "Pm", "mx", "ngm", "rs", "esk", "rinv"))
                            gA_ = gA2[bi % 2]; kgA = f"gA{bi % 2}"
                            if part == "a":
                                scv, pks = stt_.pop(u)
                                for half in range(2):
                                    stt(sc_[0:nq, half:4:2, 0:nkeys], scv[0:nq, 2 * half:2 * half + 2, 0:nkeys], 0.125,
                                        Tb[0:nq, 4 * hf + half:4 * hf + 4:2, tb_c0:tb_c0 + nkeys],
                                        ALU.mult, ALU.add, [pks[half], "Tb"], [ksc])
                                kb.op("dve", lambda e: e.tensor_reduce(out=mx_[0:nq, :], in_=sc_[0:nq, :, 0:nkeys], axis=AX.X, op=ALU.max),
                                      [ksc], [kmx])
                                tt(mx_[0:nq, :], mx_[0:nq, :], skb[0:nq, l, 4 * hf:4 * hf + 4], ALU.max, [kmx, "skb"], [kmx])
                                ts(ngm_[0:nq, :], mx_[0:nq, :], -1.0, ALU.mult, [kmx], [kngm])
                                tt(esk_[0:nq, :], skb[0:nq, l, 4 * hf:4 * hf + 4], mx_[0:nq, :], ALU.subtract, ["skb", kmx], [kesk])
                            elif part == "b":
                                for hh in range(4):
                                    act(Pm_[0:nq, hh, 0:nkeys], sc_[0:nq, hh, 0:nkeys], AF.Exp, [ksc, kngm], [kPm, krs],
                                        bias=ngm_[0:nq, hh:hh + 1], scale=1.0, accum_out=rs_[0:nq, hh:hh + 1])
                                act(esk_[0:nq, :], esk_[0:nq, :], AF.Exp, [kesk], [kesk])
                            else:
                                tt(rinv_[0:nq, :], rs_[0:nq, :], esk_[0:nq, :], ALU.add, [krs, kesk], [krinv])
                                kb.op("dve", lambda e: e.reciprocal(rinv_[0:nq, :], rinv_[0:nq, :]), [krinv], [krinv])

                        def S3(u, part):
                            bi, hf = units[u]
                            B = blocks[bi]
                            nq, segs = B["nq"], B["segs"]
                            par = u % 2
                            Pm_, rinv_ = Pm2[par], rinv2[par]
                            kPm, krinv = f"Pm{par}", f"rinv{par}"
                            gA_ = gA2[bi % 2]; kgA = f"gA{bi % 2}"
                            nseg = len(segs)
                            if part == "a":
                                pbt, pkt = bank()
                                ptv = pbt.bitcast(BF16).rearrange("p (a b) -> p a b", a=8)
                                for hh in range(4):
                                    off = 0
                                    for si, (kfn, vap, n) in enumerate(segs):
                                        tr(ptv[0:n, hh * nseg + si, 0:nq], Pm_[0:nq, hh, off:off + n], identb[0:nq, 0:nq], [kPm, "identb"], [pkt])
                                        off += n
                                for si, (kfn, vap, n) in enumerate(segs):
                                    acopy(PT[0:n, si:4 * nseg:nseg, 0:nq], ptv[0:n, si:4 * nseg:nseg, 0:nq], [pkt], ["PT"])
                                return
                            pbo, pko = bank()
                            for hh in range(4):
                                for si, (kfn, vap, n) in enumerate(segs):
                                    mm(pbo[0:nq, hh * 64:hh * 64 + 64], PT[0:n, hh * nseg + si, 0:nq], vap[0:n, 64 * hf:64 * hf + 64],
                                       si == 0, si == nseg - 1, ["PT", "vtok", "Vs", "vnew"], [pko])
                            tt(att[0:nq, :].rearrange("p (h d) -> p h d", h=4), pbo[0:nq, 0:256].rearrange("p (h d) -> p h d", h=4),
                               rinv_[0:nq, :].unsqueeze(2).to_broadcast([nq, 4, 64]), ALU.mult, [pko, krinv], ["att"])
                            tt(gA_[0:nq, 256 * hf:256 * hf + 256], att[0:nq, :], B["ga"][:, 256 * hf:256 * hf + 256], ALU.mult,
                               ["att", B["ga_key"]], [kgA])
                            if hf == 1:
                                pbt, pkt = bank()
                                ptv = pbt.bitcast(BF16).rearrange("p (a b) -> p a b", a=8)
                                for cch in range(4):
                                    tr(ptv[:, cch, 0:nq], gA_[0:nq, cch * 128:(cch + 1) * 128], identb[0:nq, 0:nq], [kgA, "identb"], [pkt])
                                acopy(gatedT[:, :, B["col"]:B["col"] + nq], ptv[:, 0:4, 0:nq], [pkt], ["gatedT"])

                        nu = len(units)
                        S1(0); S2(0, "a"); S2(0, "b"); S2(0, "c")
                        if nu > 1:
                            S1(1)
                        for u in range(nu):
                            if u + 1 < nu:
                                S2(u + 1, "a")
                            S3(u, "a")
                            if u + 1 < nu:
                                S2(u + 1, "b")
                            S3(u, "b")
                            if u + 1 < nu:
                                S2(u + 1, "c")
                            if u + 2 < nu:
                                S1(u + 2)

                    if not sample:
                        blocks = []
                        for s in range(4):
                            blk = 4 * t + s
                            if blk == 0:
                                segs = [(lambda kvh, s=s: kTd[:, l, kvh, 128 + 128 * s:256 + 128 * s], vtok[:, l, 1 + s, :], 128)]
                                blocks.append(dict(nq=128, segs=segs, nkeys=128, tb_c0=128, ga=gatok[:, s, :], ga_key="gatok", col=128 * s))
                            else:
                                segs = [(lambda kvh, s=s: kTd[:, l, kvh, 128 * s:128 * s + 128], vtok[:, l, s, :], 128),
                                        (lambda kvh, s=s: kTd[:, l, kvh, 128 + 128 * s:256 + 128 * s], vtok[:, l, 1 + s, :], 128)]
                                blocks.append(dict(nq=128, segs=segs, nkeys=256, tb_c0=0, ga=gatok[:, s, :], ga_key="gatok", col=128 * s))
                        attend_all(blocks)
                        vcopy(kTd[:, l, :, 0:128], kTd[:, l, :, TILE:TILE + 128], ["kTd"], ["kTd"])
                        vcopy(vtok[:, l, 0, :], vtok[:, l, 4, :], ["vtok"], ["vtok"])
                    else:
                        kb.dma("pool", Ks[:], ck[l].rearrange("s w f -> w s f"), (), ["Ks"])
                        kb.dma("pool", Vs[:], cv[l].rearrange("s w f -> w s f"), (), ["Vs"])
                        for i in range(NS):
                            pbt, pkt = bank()
                            ptv = pbt.bitcast(BF16).rearrange("p (a b) -> p a b", a=8)
                            vcopy(Kd[:], Ks[:, i, :].rearrange("w (h d) -> w h d", h=2).unsqueeze(2).to_broadcast([128, 2, 2, 64]), ["Ks"], ["Kd"])
                            for kvh in range(2):
                                tr(ptv[:, kvh, :], Kd[:, kvh, :, :].rearrange("w j d -> w (j d)"), identb[:], ["Kd", "identb"], [pkt])
                            acopy(KTs[:, i, :, 0:128], ptv[:, 0:2, :], [pkt], ["KTs"])
                        vcopy(KTs[:, :, :, 128:132], kTn[:].rearrange("p h (s t) -> p s h t", t=TS), ["kTn"], ["KTs"])
                        blocks = []
                        for i in range(NS):
                            def prep(i=i):
                                pbg, pkg = bank()
                                mm(pbg[0:TS, 0:512], identb[0:NTS, TS * i:TS * i + TS], gatok[0:NTS, 0, :], True, True, ["identb", "gatok"], [pkg])
                                acopy(gas2[i % 2][:, :], pbg[0:TS, 0:512], [pkg], [f"gas{i % 2}"])
                            segs = [(lambda kvh, i=i: KTs[:, i, kvh, 0:128], Vs[:, i, :], 128),
                                    (lambda kvh, i=i: KTs[:, i, kvh, 128:132], vnew[:, i, :], TS)]
                            blocks.append(dict(nq=TS, segs=segs, nkeys=132, tb_c0=0, ga=gas2[i % 2][:, :], ga_key=f"gas{i % 2}", col=TS * i, prep=prep))
                        attend_all(blocks)
                    kb.retire(ph_keys)
                if KDBG and t == 0 and l == 0:
                    dstg = PA("dstg", [128, 4, TILE])
                    acopy(dstg[:], gatedT[:], ["gatedT"], ["dstg"])
                    kb.dma("sp", dbg_x[:, 0:2048], dstg[:].rearrange("p a b -> p (a b)"), ["dstg"], (), final=True)
                    kb.dma("sp", dbg_x[:, 2048:3072], xi[:, 0, :], [xik], (), final=True)
                branch_out(0)
                if KDBG and t == 0 and l == 0:
                    kb.dma("sp", dbg_m[0], merged[:].rearrange("p a b -> p (a b)"), ["merged"], (), final=True)

                with contextlib.ExitStack() as ph:
                    ph_keys = []

                    def PH(name, shape, dt=F32, ph=ph, ph_keys=ph_keys):
                        ph_keys.append(name)
                        return S(name, shape, dt, stack=ph)
                    nrot[0] = 6
                    bank_rr[0] = 0
                    uT = PH("uT", [128, 4, ntok], BF16); yz = PH("yz", [128, 4, ntok]); gbT = PH("gbT", [128, 4, ntok], BF16)
                    zb = PH("zb", [128, 4, ntok], BF16)
                    NXS = 1 if sample else 2
                    XRs = [PH(f"XR{i}", [128, 16, SJ]) for i in range(NXS)]; XIs = [PH(f"XI{i}", [128, 16, SJ]) for i in range(NXS)]
                    T1 = PH("T1", [128, 16, SJ]); T2 = PH("T2", [128, 16, SJ])
                    HR = PH("HR", [128, 16, SJ], BF16); HI = PH("HI", [128, 16, SJ], BF16)
                    g1 = PH("g1", [128, ntok]); g2b = PH("g2b", [128, ntok])
                    NH = NS if sample else 1
                    hfr = PH("hfr", [128, 16, NH]); hfi = PH("hfi", [128, 16, NH]); s5a = PH("s5a", [128, 16, NH]); s5b = PH("s5b", [128, 16, NH])
                    if sample:
                        h0r = PH("h0r", [128, 16, NS]); h0i = PH("h0i", [128, 16, NS])
                        st_io = PH("st_io", [NS, 2048])
                        for ssrc, sdst, skey_ in ((sre, h0r, "h0r"), (sim, h0i, "h0i")):
                            kb.dma("sp", st_io[:], ssrc[l], ["st_io"], ["st_io"])
                            pbs, pks = bank()
                            for k in range(16):
                                tr(pbs[:, k * NS:(k + 1) * NS], st_io[0:NS, k * 128:(k + 1) * 128], ident[0:NS, 0:NS], ["st_io", "ident"], [pks])
                            acopy(sdst[:].rearrange("p k s -> p (k s)"), pbs[:, 0:16 * NS], [pks], [skey_])
                    for q in range(4):
                        if q % 2 == 0:
                            slot, skey = wget(f"u{q // 2}")
                        pb, pk = proj_fm(slot, skey, q % 2, 8, xTk, ["xT"])
                        acopy(uT[:, q, 0:ntok], pb[:, 0:ntok], [pk], ["uT"])
                        act(yz[:, q, 0:ntok], pb[:, 0:ntok], AF.Copy, [pk, "dS5"], ["yz"], scale=dS5[:, l, q:q + 1])
                    for q in range(4):
                        if q % 2 == 0:
                            slot, skey = wget(f"gb{q // 2}")
                        pb, pk = proj_fm(slot, skey, q % 2, 8, xTk, ["xT"])
                        act(gbT[:, q, 0:ntok], pb[:, 0:ntok], AF.Silu, [pk], ["gbT"])

                    cosl = COS[:, l, :, :]; sinl = SIN[:, l, :, :]
                    nsj = 1 if sample else TILE // SJ
                    def stage1(jt):
                        c0 = jt * SJ
                        XR = XRs[jt % NXS]; XI = XIs[jt % NXS]; kXR = f"XR{jt % NXS}"; kXI = f"XI{jt % NXS}"
                        pxr, kxr = bank2(); pxi, kxi = bank2()
                        for k in range(16):
                            h_ = (k % 4) // 2
                            pk_ = h_ * 8 + (k // 4) * 2 + k % 2
                            mm(pxr[:, pk_ * SJ:(pk_ + 1) * SJ], WBre[64 * h_:64 * h_ + 64, l, k // 4, k % 2, :], uT[64 * h_:64 * h_ + 64, k // 4, c0:c0 + SJ],
                               True, True, ["WBre", "uT"], [kxr[h_]])
                        for k in range(16):
                            h_ = (k % 4) // 2
                            pk_ = h_ * 8 + (k // 4) * 2 + k % 2
                            mm(pxi[:, pk_ * SJ:(pk_ + 1) * SJ], WBim[64 * h_:64 * h_ + 64, l, k // 4, k % 2, :], uT[64 * h_:64 * h_ + 64, k // 4, c0:c0 + SJ],
                               True, True, ["WBim", "uT"], [kxi[h_]])
                        for h_ in range(2):
                            acopy(XR[:].rearrange("p (q h kk) j -> p h q kk j", h=2, kk=2)[:, h_], pxr[:, 512 * h_:512 * h_ + 512].rearrange("p (q kk j) -> p q kk j", q=4, kk=2),
                                  [kxr[h_]], [kXR])
                            acopy(XI[:].rearrange("p (q h kk) j -> p h q kk j", h=2, kk=2)[:, h_], pxi[:, 512 * h_:512 * h_ + 512].rearrange("p (q kk j) -> p q kk j", q=4, kk=2),
                                  [kxi[h_]], [kXI])

                    stage1(0)
                    pend = []
                    for jt in range(nsj):
                        c0 = jt * SJ
                        if jt + 1 < nsj:
                            stage1(jt + 1)
                        if lru_units:
                            lru_units.pop(0)()
                        XR = XRs[jt % NXS]; XI = XIs[jt % NXS]; kXR = f"XR{jt % NXS}"; kXI = f"XI{jt % NXS}"
                        if sample:
                            cosv = cosl[:, :, 0:TS].unsqueeze(2).to_broadcast([128, 16, NS, TS])
                            sinv = sinl[:, :, 0:TS].unsqueeze(2).to_broadcast([128, 16, NS, TS])
                            V = lambda a: a[:].rearrange("p k (s t) -> p k s t", t=TS)
                            magt = MAGT[:, l, :, :]
                            memset(MAGT[:, l, :, :].rearrange("p k (s t) -> p k s t", t=TS)[:, :, :, 0:1], 0.0, ["MAGT"])
                        else:
                            cosv = cosl[:, :, 0:SJ]; sinv = sinl[:, :, 0:SJ]
                            V = lambda a: a[:]
                            magt = MAGT[:, l, :, :]
                        tt(V(T1), V(XR), cosv, ALU.mult, [kXR, "COS"], ["T1"])
                        tt(V(T2), V(XI), sinv, ALU.mult, [kXI, "SIN"], ["T2"])
                        tt(T1[:], T1[:], T2[:], ALU.add, ["T1", "T2"], ["T1"])
                        tt(V(T2), V(XR), sinv, ALU.mult, [kXR, "SIN"], ["T2"])
                        tt(V(XI), V(XI), cosv, ALU.mult, [kXI, "COS"], [kXI])
                        tt(XI[:], XI[:], T2[:], ALU.subtract, [kXI, "T2"], [kXI])
                        if sample:
                            tt(s5a[:], h0r[:], AbR[:, l, :].unsqueeze(2).to_broadcast([128, 16, NS]), ALU.mult, ["h0r", "Ab"], ["s5a"])
                            tt(s5b[:], h0i[:], AbI[:, l, :].unsqueeze(2).to_broadcast([128, 16, NS]), ALU.mult, ["h0i", "Ab"], ["s5b"])
                            tt(s5a[:], s5a[:], s5b[:], ALU.subtract, ["s5a", "s5b"], ["s5a"])
                            tt(V(T1)[:, :, :, 0], V(T1)[:, :, :, 0], s5a[:], ALU.add, ["T1", "s5a"], ["T1"])
                            tt(s5a[:], h0r[:], AbI[:, l, :].unsqueeze(2).to_broadcast([128, 16, NS]), ALU.mult, ["h0r", "Ab"], ["s5a"])
                            tt(s5b[:], h0i[:], AbR[:, l, :].unsqueeze(2).to_broadcast([128, 16, NS]), ALU.mult, ["h0i", "Ab"], ["s5b"])
                            tt(s5a[:], s5a[:], s5b[:], ALU.add, ["s5a", "s5b"], ["s5a"])
                            tt(V(XI)[:, :, :, 0], V(XI)[:, :, :, 0], s5a[:], ALU.add, [kXI, "s5a"], [kXI])
                        else:
                            tt(T1[:, :, 0], T1[:, :, 0], Gre[:, l, :], ALU.add, ["T1", "G"], ["T1"])
                            tt(XI[:, :, 0], XI[:, :, 0], Gim[:, l, :], ALU.add, [kXI, "G"], [kXI])
                        flat = lambda a: a[:].rearrange("p k j -> p (k j)")
                        mflat = magt.rearrange("p k j -> p (k j)")
                        kb.op("dve", lambda e, o=flat(XR), a=mflat, b=flat(T1): e.tensor_tensor_scan(out=o, data0=a, data1=b, initial=0.0,
                                                                                                     op0=ALU.mult, op1=ALU.add),
                              ["T1", "MAGT"], [kXR])
                        kb.op("dve", lambda e, o=flat(T2), a=mflat, b=flat(XI): e.tensor_tensor_scan(out=o, data0=a, data1=b, initial=0.0,
                                                                                                     op0=ALU.mult, op1=ALU.add),
                              [kXI, "MAGT"], ["T2"])
                        if pend:
                            pend.pop()()
                        pby, pky = pst[3][:, (jt % 2) * 512:(jt % 2) * 512 + 512], f"pb{6 + jt % 2}"
                        if sample:
                            tt(V(T1), V(XR), cosv, ALU.mult, [kXR, "COS"], ["T1"])
                            tt(V(XI), V(T2), sinv, ALU.mult, ["T2", "SIN"], [kXI])
                            tt(HR[:], T1[:], XI[:], ALU.subtract, ["T1", kXI], ["HR"])
                            tt(V(T1), V(XR), sinv, ALU.mult, [kXR, "SIN"], ["T1"])
                            tt(V(XI), V(T2), cosv, ALU.mult, ["T2", "COS"], [kXI])
                            tt(HI[:], T1[:], XI[:], ALU.add, ["T1", kXI], ["HI"])
                            for q in range(4):
                                for kk in range(4):
                                    k = 4 * q + kk
                                    mm(pby[:, q * SJ:(q + 1) * SJ], WCre[:, l, k, :], HR[:, k, :], kk == 0, False, ["WCre", "HR"], [pky])
                                    mm(pby[:, q * SJ:(q + 1) * SJ], WCim[:, l, k, :], HI[:, k, :], False, kk == 3, ["WCim", "HI"], [pky])
                        else:
                            H2v = zb[:].rearrange("p q t -> p (q t)").rearrange("p (a k j) -> p a k j", a=2, k=16)
                            HR2, HI2 = H2v[:, 0], H2v[:, 1]
                            tt(HR[:], XR[:], cosv, ALU.mult, [kXR, "COS"], ["HR"])
                            stt(HR2, T2[:], -1.0, sinv, ALU.mult, ALU.mult, ["T2", "SIN"], ["zb"])
                            tt(HI[:], XR[:], sinv, ALU.mult, [kXR, "SIN"], ["HI"])
                            tt(HI2, T2[:], cosv, ALU.mult, ["T2", "COS"], ["zb"])
                            for q in range(4):
                                for kk in range(4):
                                    k = 4 * q + kk
                                    mm(pby[:, q * SJ:(q + 1) * SJ], WCre[:, l, k, :], HR[:, k, :], kk == 0, False, ["WCre", "HR"], [pky])
                                    mm(pby[:, q * SJ:(q + 1) * SJ], WCre[:, l, k, :], HR2[:, k, :], False, False, ["WCre", "zb"], [pky])
                                    mm(pby[:, q * SJ:(q + 1) * SJ], WCim[:, l, k, :], HI[:, k, :], False, False, ["WCim", "HI"], [pky])
                                    mm(pby[:, q * SJ:(q + 1) * SJ], WCim[:, l, k, :], HI2[:, k, :], False, kk == 3, ["WCim", "zb"], [pky])
                        pend.append(lambda c0=c0, pby=pby, pky=pky: tt(yz[:, :, c0:c0 + SJ], yz[:, :, c0:c0 + SJ],
                                                                        pby[:, 0:4 * SJ].rearrange("p (q j) -> p q j", q=4), ALU.add, ["yz", pky], ["yz"]))
                        if not sample:
                            tt(s5a[:, :, 0], XR[:, :, SJ - 1], CaR[:, l, :], ALU.mult, [kXR, "Ca"], ["s5a"], eng="pool")
                            tt(s5b[:, :, 0], T2[:, :, SJ - 1], CaI[:, l, :], ALU.mult, ["T2", "Ca"], ["s5b"], eng="pool")
                            tt(Gre[:, l, :], s5a[:, :, 0], s5b[:, :, 0], ALU.subtract, ["s5a", "s5b"], ["G"], eng="pool")
                            tt(s5a[:, :, 0], XR[:, :, SJ - 1], CaI[:, l, :], ALU.mult, [kXR, "Ca"], ["s5a"], eng="pool")
                            tt(s5b[:, :, 0], T2[:, :, SJ - 1], CaR[:, l, :], ALU.mult, ["T2", "Ca"], ["s5b"], eng="pool")
                            tt(Gim[:, l, :], s5a[:, :, 0], s5b[:, :, 0], ALU.add, ["s5a", "s5b"], ["G"], eng="pool")
                        if (last_prompt and jt == nsj - 1) or sample:
                            if sample:
                                gr = V(XR)[:, :, :, TS - 1]; gi = V(T2)[:, :, :, TS - 1]
                                cl = cosl[:, :, TS - 1:TS].to_broadcast([128, 16, NS]); sl = sinl[:, :, TS - 1:TS].to_broadcast([128, 16, NS])
                                o1, o2, a_, b_ = hfr[:], hfi[:], s5a[:], s5b[:]
                            else:
                                gr = XR[:, :, SJ - 1]; gi = T2[:, :, SJ - 1]
                                cl = cosl[:, :, SJ - 1]; sl = sinl[:, :, SJ - 1]
                                o1, o2, a_, b_ = hfr[:, :, 0], hfi[:, :, 0], s5a[:, :, 0], s5b[:, :, 0]
                            tt(a_, gr, cl, ALU.mult, [kXR, "COS"], ["s5a"])
                            tt(b_, gi, sl, ALU.mult, ["T2", "SIN"], ["s5b"])
                            tt(o1, a_, b_, ALU.subtract, ["s5a", "s5b"], ["hfr"])
                            tt(a_, gr, sl, ALU.mult, [kXR, "SIN"], ["s5a"])
                            tt(b_, gi, cl, ALU.mult, ["T2", "COS"], ["s5b"])
                            tt(o2, a_, b_, ALU.add, ["s5a", "s5b"], ["hfi"])
                            if sample:
                                for hsrc, hkey, hdst in ((hfr, "hfr", nres), (hfi, "hfi", nims)):
                                    pbA, pkA = bank2(); pbB, pkB = bank2()
                                    for k in range(16):
                                        pbv, pkv = (pbA, pkA) if k < 8 else (pbB, pkB)
                                        tr(pbv[0:NS, (k % 8) * 128:(k % 8 + 1) * 128], hsrc[:, k, :], ident[:], [hkey, "ident"], [pkv[(k % 8) // 4]])
                                    acopy(st_io[:, 0:1024], pbA[0:NS, :], pkA, ["st_io"])
                                    acopy(st_io[:, 1024:2048], pbB[0:NS, :], pkB, ["st_io"])
                                    kb.dma("sp", hdst[l], st_io[:], ["st_io"], (), final=True)
                            else:
                                kb.dma("sp", nrep[l].rearrange("(k p) -> p k", p=128), hfr[:, :, 0], ["hfr"], (), final=True)
                                kb.dma("sp", nimp[l].rearrange("(k p) -> p k", p=128), hfi[:, :, 0], ["hfi"], (), final=True)
                    while lru_units:
                        lru_units.pop(0)()
                    while pend:
                        pend.pop()()
                    for q in range(4):
                        yv = yz[:, q, 0:ntok]
                        gq, kgq = (g1, "g1") if q % 2 == 0 else (g2b, "g2b")
                        kyz = f"yz{q}"
                        act(gq[:, 0:ntok], yv, AF.Square, ["yz"], [kgq])
                        ts(gq[:, 0:ntok], gq[:, 0:ntok], 0.044715, ALU.mult, [kgq], [kgq], s2=1.0, op1=ALU.add)
                        tt(gq[:, 0:ntok], gq[:, 0:ntok], yv, ALU.mult, [kgq, "yz"], [kgq])
                        act(gq[:, 0:ntok], gq[:, 0:ntok], AF.Sigmoid, [kgq], [kgq], scale=1.5957691216)
                        tt(yv, yv, gq[:, 0:ntok], ALU.mult, ["yz", kgq], [kyz])
                        acopy(zb[:, q, 0:ntok], yv, [kyz], ["zb"])
                    slot, skey = wget("glu")
                    for fo in range(4):
                        pb, pk = proj_fm(slot, skey, fo, 4, lambda c: zb[:, c, 0:ntok], ["zb"])
                        act(g1[:, 0:ntok], pb[:, 0:ntok], AF.Sigmoid, [pk], ["g1"])
                        tt(g2b[:, 0:ntok], yz[:, fo, 0:ntok], g1[:, 0:ntok], ALU.mult, [f"yz{fo}", "g1"], ["g2b"])
                        tt(gatedT[:, fo, 0:ntok], g2b[:, 0:ntok], gbT[:, fo, 0:ntok], ALU.mult, ["g2b", "gbT"], ["gatedT"])
                    nrot[0] = 8
                    ph_keys.extend([f"yz{q}" for q in range(4)])
                    kb.retire(ph_keys)
                branch_out(1)
                if KDBG and t == 0 and l == 0:
                    kb.dma("sp", dbg_m[1], merged[:].rearrange("p a b -> p (a b)"), ["merged"], (), final=True)

                with contextlib.ExitStack() as ph:
                    ph_keys = []

                    def PH(name, shape, dt=F32, ph=ph, ph_keys=ph_keys):
                        ph_keys.append(name)
                        return S(name, shape, dt, stack=ph)
                    if sample:
                        h0l = PH("h0l", [128, 4, NS]); hso = PH("hso", [128, 4, NS])
                        l_io = PH("l_io", [NS, 512]); c_in = PH("c_in", [3 * NS, 512])
                        kb.dma("sp", l_io[:], slru[l], (), ["l_io"])
                        kb.dma("sp", c_in[:], sconv[l].rearrange("s j f -> (s j) f"), (), ["c_in"])
                        pbs, pks = bank()
                        for q in range(4):
                            tr(pbs[:, q * NS:(q + 1) * NS], l_io[0:NS, q * 128:(q + 1) * 128], ident[0:NS, 0:NS], ["l_io", "ident"], [pks])
                        acopy(h0l[:].rearrange("p q s -> p (q s)"), pbs[:, 0:4 * NS], [pks], ["h0l"])
                        pbs, pks = bank()
                        for q in range(4):
                            tr(pbs[:, q * 3 * NS:(q + 1) * 3 * NS], c_in[0:3 * NS, q * 128:(q + 1) * 128], ident[0:3 * NS, 0:3 * NS], ["c_in", "ident"], [pks])
                        for q in range(4):
                            acopy(cbuf[:, q, :, 0:3], pbs[:, q * 3 * NS:(q + 1) * 3 * NS].rearrange("p (s j) -> p s j", j=3), [pks], ["cbuf"])
                    lbuf = {}
                    for nm_, dt_ in (("cvv", F32), ("cvb", BF16), ("rg", F32), ("ig", F32), ("aa", F32), ("sq", F32), ("hT", F32)):
                        lbuf[nm_] = [PH(f"{nm_}{i}", [128, ntok], dt_) for i in range(4)]
                    tl = PH("tl", [128, NS])
                    hfo = PH("hfo", [128, 4])
                    if sample:
                        for tt_ in range(1, TS):
                            kb.dma("sp", ncvs[l][:, tt_ - 1, :], xco[tt_:NTS:TS, :], ["xco"], (), final=True)
                    elif last_prompt:
                        kb.dma("sp", ncvp[l], xco[0:3, :], ["xco"], (), final=True)
                    def lru_chunk(q, part):
                        cvv, cvb, rg, ig, aa, sq, hT = (lbuf[n_][q] for n_ in ("cvv", "cvb", "rg", "ig", "aa", "sq", "hT"))
                        kcvv, kcvb, krg, kig, kaa, ksq, khT = (f"{n_}{q}" for n_ in ("cvv", "cvb", "rg", "ig", "aa", "sq", "hT"))
                        if sample:
                            win = lambda j: cbuf[:, q, :, j:j + TS]
                            V2 = lambda a: a[:, 0:ntok].rearrange("p (s t) -> p s t", t=TS)
                        else:
                            win = lambda j: cbuf[:, q, j:j + ntok]
                            V2 = lambda a: a[:, 0:ntok]
                        if part == 1:
                            ts(V2(cvv), win(0), cw[:, l, q, 0:1], ALU.mult, ["cbuf", "cw", "cb"], [kcvv], s2=cb[:, l, q:q + 1], op1=ALU.add)
                            for j in range(1, 4):
                                stt(V2(cvv), win(j), cw[:, l, q, j:j + 1], V2(cvv), ALU.mult, ALU.add, ["cbuf", "cw", kcvv], [kcvv])
                            acopy(cvb[:, 0:ntok], cvv[:, 0:ntok], [kcvv], [kcvb])
                            pb, pk = bank()
                            mm(pb[:, 0:ntok], WA[:, l, q, :], cvb[:, 0:ntok], True, True, ["WA", kcvb], [pk])
                            act(rg[:, 0:ntok], pb[:, 0:ntok], AF.Sigmoid, [pk, "ba"], [krg], bias=ba[:, l, q:q + 1], scale=1.0)
                            pb, pk = bank()
                            mm(pb[:, 0:ntok], WX[:, l, q, :], cvb[:, 0:ntok], True, True, ["WX", kcvb], [pk])
                            act(ig[:, 0:ntok], pb[:, 0:ntok], AF.Sigmoid, [pk, "bx"], [kig], bias=bx[:, l, q:q + 1], scale=1.0)
                            return
                        act(sq[:, 0:ntok], rg[:, 0:ntok], AF.Exp, [krg, "c2"], [ksq], scale=c2[:, l, q:q + 1])
                        act(sq[:, 0:ntok], sq[:, 0:ntok], AF.Ln, [ksq], [ksq], scale=-1.0, bias=1.0)
                        act(sq[:, 0:ntok], sq[:, 0:ntok], AF.Exp, [ksq], [ksq], scale=0.5)
                        act(aa[:, 0:ntok], rg[:, 0:ntok], AF.Exp, [krg, "c1"], [kaa], scale=c1[:, l, q:q + 1])
                        tt(ig[:, 0:ntok], ig[:, 0:ntok], cvv[:, 0:ntok], ALU.mult, [kig, kcvv], [kig])
                        tt(ig[:, 0:ntok], ig[:, 0:ntok], sq[:, 0:ntok], ALU.mult, [kig, ksq], [kig])
                        if sample:
                            tt(tl[:], V2(aa)[:, :, 0], h0l[:, q, :], ALU.mult, [kaa, "h0l"], ["tl"])
                            tt(V2(ig)[:, :, 0], V2(ig)[:, :, 0], tl[:], ALU.add, [kig, "tl"], [kig])
                            memset(V2(aa)[:, :, 0:1], 0.0, [kaa])
                            kb.op("dve", lambda e: e.tensor_tensor_scan(out=hT[:, 0:ntok], data0=aa[:, 0:ntok], data1=ig[:, 0:ntok], initial=0.0,
                                                                        op0=ALU.mult, op1=ALU.add), [kaa, kig], [khT])
                            vcopy(hso[:, q, :], V2(hT)[:, :, TS - 1], [khT], ["hso"])
                        else:
                            kb.op("dve", lambda e, q=q: e.tensor_tensor_scan(out=hT[:, 0:ntok], data0=aa[:, 0:ntok], data1=ig[:, 0:ntok],
                                                                             initial=hl[:, l, q:q + 1], op0=ALU.mult, op1=ALU.add),
                                  [kaa, kig, "hl"], [khT])
                            vcopy(hl[:, l, q:q + 1], hT[:, ntok - 1:ntok], [khT], ["hl"])
                        tt(gatedT[:, q, 0:ntok], hT[:, 0:ntok], gcT[:, q, 0:ntok], ALU.mult, [khT, "gcT"], ["gatedT"])

                    for q in range(4):
                        lru_chunk(q, 1)
                    for q in range(4):
                        lru_chunk(q, 2)
                    if sample:
                        pbs, pks = bank()
                        for q in range(4):
                            tr(pbs[0:NS, q * 128:(q + 1) * 128], hso[:, q, :], ident[:], ["hso", "ident"], [pks])
                        acopy(l_io[:], pbs[0:NS, 0:512], [pks], ["l_io"])
                        kb.dma("sp", nlrus[l], l_io[:], ["l_io"], (), final=True)
                    elif last_prompt:
                        vcopy(hfo[:], hl[:, l, :], ["hl"], ["hfo"])
                        kb.dma("sp", nlrup[l].rearrange("(q p) -> p q", p=128), hfo[:], ["hfo"], (), final=True)
                    kb.retire(ph_keys)
                mTh["mT"] = PA("mT", [128, 8, ntok], BF16)
                branch_out(2)

                with contextlib.ExitStack() as ph:
                    ph_keys = []

                    def PH(name, shape, dt=F32, ph=ph, ph_keys=ph_keys):
                        ph_keys.append(name)
                        return S(name, shape, dt, stack=ph)
                    lng = PH("lng", [128, D]); lnb = PH("lnb", [128, D])
                    st6 = PH("st6", [128, 2, 6]); mv = PH("mv", [128, 2]); rstd = PH("rstd", [128, 1])
                    kb.dma("sp", lng[:], ln_g[l].partition_broadcast(128), (), ["lng"])
                    kb.dma("sp", lnb[:], ln_b[l].partition_broadcast(128), (), ["lnb"])
                    wos = [wget(f"wo{wo}", ahead=RING - wo) for wo in range(4)]
                    for s in range(nsub):
                        for wo in range(4):
                            slot, skey = wos[wo]
                            pb, pk = bank()
                            for fc in range(8):
                                mm(pb[0:R, 0:256], mTh["mT"][:, fc, s * 128:s * 128 + R], v8(slot)[:, fc, :], fc == 0, fc == 7, [skey, "mT"], [pk])
                            stt(xo[0:R, s, wo * 256:wo * 256 + 256], xi[0:R, s, wo * 256:wo * 256 + 256], DN_ALPHA, pb[0:R, 0:256],
                                ALU.mult, ALU.add, [xik, pk], [xok])
                        hv = xo[0:R, s, :]
                        for hh in range(2):
                            kb.op("dve", lambda e, hh=hh, hv=hv: e.bn_stats(out=st6[0:R, hh, :], in_=hv[:, hh * 512:hh * 512 + 512]), [xok], ["st6"])
                        kb.op("dve", lambda e: e.bn_aggr(out=mv[0:R, :], in_=st6[0:R, :, :].rearrange("p a b -> p (a b)")), ["st6"], ["mv"])
                        act(rstd[0:R, :], mv[0:R, 1:2], AF.Sqrt, ["mv", "epsc"], ["rstd"], bias=epsc[0:R, :], scale=1.0)
                        kb.op("dve", lambda e: e.reciprocal(rstd[0:R, :], rstd[0:R, :]), ["rstd"], ["rstd"])
                        ts(hv, hv, mv[0:R, 0:1], ALU.subtract, [xok, "mv", "rstd"], [xok], s2=rstd[0:R, 0:1], op1=ALU.mult)
                        tt(hv, hv, lng[0:R, :], ALU.mult, [xok, "lng"], [xok])
                        tt(hv, hv, lnb[0:R, :], ALU.add, [xok, "lnb"], [xok])
                        if l == DEPTH - 1 or os.environ.get("KDUMPL0"):
                            if sample:
                                kb.dma("sp", ys[:, :], hv, [xok], (), final=True)
                            else:
                                r0 = t * TILE + s * 128
                                kb.dma("sp", yp[r0:r0 + 128, :], hv, [xok], (), final=True)
                    kb.retire(ph_keys)
                kb.retire(pa_keys)

        for (t, l) in all_passes:
            if l == 0:
                if t == NPT:
                    kb.dma("sp", xtok[0][0:NTS, 0, :], xs, (), ["xtok0"])
                else:
                    kb.dma("sp", xtok[0][:], xp[t * TILE:(t + 1) * TILE, :].rearrange("(s p) d -> p s d", p=128), (), ["xtok0"])
            run_pass(t, l, l % 2, (l + 1) % 2)
        nx = int(os.environ.get("KEXTRA", "0"))
        if nx:
            kb.maxops = 10 ** 9
            pbx, pkx = bank()
            for _ in range(nx):
                mm(pbx[:, 0:128], identb[:], identb[:], True, True, ["identb"], [pkx])
        print("recorded ops:", kb.nrec, {k: v for k, v in kb.cnt.items()})
        kb.emit()
    return nc


_NC_CACHE = {}


def kernel(**inputs):
    f = lambda k: np.ascontiguousarray(np.asarray(inputs[k], dtype=np.float32))
    oh, neg = _bias_consts()
    par = (np.arange(128) // 16) % 2
    pm = np.stack([1.0 - par, par], 1).astype(np.float32)
    shared = {
        "rel_bias": f("rel_bias"), "w_in": f("w_in"), "sinks": f("sinks"),
        "w_br_a": f("w_branch_a"), "w_br_b": f("w_branch_b"), "w_br_c": f("w_branch_c"),
        "lam_re": f("ssm_lambda_re").reshape(2, 2048), "lam_im": f("ssm_lambda_im").reshape(2, 2048),
        "log_step": f("ssm_log_step"), "b_re": f("ssm_b_re"), "b_im": f("ssm_b_im"),
        "c_re": f("ssm_c_re"), "c_im": f("ssm_c_im"), "ssm_d": f("ssm_d"), "w_glu": f("ssm_w_glu"),
        "conv_w": f("conv_w"), "conv_b": f("conv_b"), "lru_w_a": f("lru_w_a"), "lru_b_a": f("lru_b_a"),
        "lru_w_x": f("lru_w_x"), "lru_b_x": f("lru_b_x"), "lru_lam": f("lru_lambda"), "w_out": f("w_out"),
        "ln_g": f("ln_g"), "ln_b": f("ln_b"), "oh2": oh, "negmask": neg, "pmask": pm,
    }
    x_prompt = f("x_prompt"); x_sample = f("x_sample")
    cache_k = f("cache_k").reshape(2, 128, 128, 128); cache_v = f("cache_v").reshape(2, 128, 128, 128)
    s_re = f("state_ssm_re").reshape(2, 128, 2048); s_im = f("state_ssm_im").reshape(2, 128, 2048)
    s_lru = f("state_lru"); s_conv = f("state_conv")
    in_maps = []
    for c in range(NCORES):
        sl = slice(NS * c, NS * c + NS)
        m = dict(shared)
        m.update({
            "xp": x_prompt[c], "xs": np.ascontiguousarray(x_sample[sl].reshape(NTS, D)),
            "ck": np.ascontiguousarray(cache_k[:, sl]), "cv": np.ascontiguousarray(cache_v[:, sl]),
            "sre": np.ascontiguousarray(s_re[:, sl]), "sim": np.ascontiguousarray(s_im[:, sl]),
            "slru": np.ascontiguousarray(s_lru[:, sl]), "sconv": np.ascontiguousarray(s_conv[:, sl]),
        })
        in_maps.append(m)
    if "nc" not in _NC_CACHE:
        _NC_CACHE["nc"] = build_nc()
    res = run_bass_kernel_spmd(_NC_CACHE["nc"], in_maps, core_ids=list(range(NCORES)))
    r = res.results
    cat = lambda k, ax: np.concatenate([np.asarray(r[c][k])[None] if ax is None else np.asarray(r[c][k]) for c in range(NCORES)], axis=0 if ax is None else ax)
    y_prompt = np.stack([r[c]["yp"] for c in range(NCORES)], 0).astype(np.float32)
    y_sample = np.concatenate([r[c]["ys"].reshape(NS, TS, D) for c in range(NCORES)], 0).astype(np.float32)
    stk = lambda k, shp: np.stack([np.asarray(r[c][k]).reshape(shp) for c in range(NCORES)], 1).astype(np.float32)
    ccat = lambda k, shp: np.concatenate([np.asarray(r[c][k]).reshape(shp) for c in range(NCORES)], 1).astype(np.float32)
    return (y_prompt, y_sample,
            stk("nkp", (2, 128, 2, 64)), stk("nvp", (2, 128, 2, 64)),
            stk("nrep", (2, 32, 64)), stk("nimp", (2, 32, 64)),
            stk("nlrup", (2, 512)), stk("ncvp", (2, 3, 512)),
            ccat("nks", (2, NS, 128, 2, 64)), ccat("nvs", (2, NS, 128, 2, 64)),
            ccat("nres", (2, NS, 32, 64)), ccat("nims", (2, NS, 32, 64)),
            ccat("nlrus", (2, NS, 512)), ccat("ncvs", (2, NS, 3, 512)))
```
